# Optimizing a Trainium2 kernel written in Bass

```python
import jax, jax.numpy as jnp
from jax import lax
import numpy as np

D_MODEL = 1024
BATCH = 8
SEQ = 2048
DEPTH = 2

D_MIX = D_MODEL
HEAD_DIM = 64
RWKV_HEADS = 6
D_RWKV = RWKV_HEADS * HEAD_DIM
DECAY_LORA = 64
AAA_LORA = 64
GATE_LORA = 128
RWKV_LNX_EPS = 64e-5
MOBA_HEADS = 6
D_MOBA = MOBA_HEADS * HEAD_DIM
MOBA_BLOCK = 256
MOBA_TOPK = 3
MOBA_QCHUNK = 32
GMLP_GROUPS = 4
D_GMLP = GMLP_GROUPS * HEAD_DIM
GMLP_CHUNK = 128
D_FF = ((8 * D_MODEL // 3 + 255) // 256) * 256
NORM_EPS = 1e-6

A_COLS = 3 * D_RWKV + DECAY_LORA + AAA_LORA + GATE_LORA
B_COLS = 3 * D_MOBA
C_COLS = 2 * D_GMLP
D_IN = A_COLS + B_COLS + C_COLS

kernel_name = "hymba_rwkv7_moba_gmlp_block"


def rmsnorm(x, g):
    xf = x.astype(jnp.float32)
    y = xf * lax.rsqrt(jnp.mean(xf * xf, axis=-1, keepdims=True) + NORM_EPS)
    return (y * g.astype(jnp.float32)).astype(x.dtype)


def rwkv7_mixer(ya, mu, w0, w2, a0, a2, g2, k_k, k_a, r_k, lnx_g, lnx_b):
    B, S, _ = ya.shape
    H, D = RWKV_HEADS, HEAD_DIM
    prev = jnp.pad(ya, ((0, 0), (1, 0), (0, 0)))[:, :-1]
    ya = ya + mu * (prev - ya)
    r, k, v, wd, ad, gd = jnp.split(
        ya, [D_RWKV, 2 * D_RWKV, 3 * D_RWKV, 3 * D_RWKV + DECAY_LORA,
             3 * D_RWKV + DECAY_LORA + AAA_LORA], axis=-1)
    w = -jax.nn.softplus(-(w0 + jnp.tanh(wd) @ w2).astype(jnp.float32)) - 0.5
    decay = jnp.exp(-jnp.exp(w))
    a = jax.nn.sigmoid((a0 + ad @ a2).astype(jnp.float32))
    g = jax.nn.sigmoid(gd) @ g2
    heads = lambda t: t.astype(jnp.float32).reshape(B, S, H, D)
    kk = heads(k * k_k)
    kk = kk / jnp.maximum(jnp.linalg.norm(kk, axis=-1, keepdims=True), 1e-12)
    k = k * (1.0 + (a - 1.0) * k_a)
    rh, kh, vh, ah, dh = heads(r), heads(k), heads(v), heads(a), heads(decay)

    def step(state, inp):
        r_t, w_t, k_t, v_t, kk_t, a_t = inp
        sa = jnp.einsum('bhij,bhj->bhi', state, -kk_t)
        state = (state * w_t[:, :, None, :]
                 + sa[..., :, None] * (kk_t * a_t)[:, :, None, :]
                 + v_t[..., :, None] * k_t[:, :, None, :])
        return state, jnp.einsum('bhij,bhj->bhi', state, r_t)

    xs = tuple(jnp.moveaxis(t, 1, 0) for t in (rh, dh, kh, vh, kk, ah))
    s0 = jnp.zeros((B, H, D, D), jnp.float32)
    _, y = lax.scan(step, s0, xs)
    y = jnp.moveaxis(y, 0, 1)
    mean = jnp.mean(y, axis=-1, keepdims=True)
    var = jnp.mean(jnp.square(y - mean), axis=-1, keepdims=True)
    y = ((y - mean) * lax.rsqrt(var + RWKV_LNX_EPS)).reshape(B, S, D_RWKV)
    y = y * lnx_g + lnx_b
    bonus = jnp.sum(rh * kh * r_k.astype(jnp.float32), axis=-1, keepdims=True) * vh
    out = (y + bonus.reshape(B, S, D_RWKV)) * g.astype(jnp.float32)
    return out.astype(ya.dtype)


def moba_mixer(q, k, v):
    B, S, _ = q.shape
    H, D = MOBA_HEADS, HEAD_DIM
    q, k, v = (t.reshape(B, S, H, D).transpose(0, 2, 1, 3) for t in (q, k, v))
    nb = -(-S // MOBA_BLOCK)
    pad = nb * MOBA_BLOCK - S
    kb = jnp.pad(k, ((0, 0), (0, 0), (0, pad), (0, 0))).reshape(B, H, nb, MOBA_BLOCK, D)
    vb = jnp.pad(v, ((0, 0), (0, 0), (0, pad), (0, 0))).reshape(B, H, nb, MOBA_BLOCK, D)
    pos = jnp.arange(S)
    qblk = pos // MOBA_BLOCK
    own = jnp.broadcast_to(qblk[None, None, :, None], (B, H, S, 1))
    n_top = min(MOBA_TOPK, nb - 1)
    if n_top > 0:
        kbar = jnp.mean(kb.astype(jnp.float32), axis=3)
        blk_scores = jnp.einsum('bhsd,bhnd->bhsn', q.astype(jnp.float32), kbar)
        past = jnp.arange(nb)[None, :] < qblk[:, None]
        blk_scores = jnp.where(past, blk_scores, -jnp.inf)
        _, top = lax.top_k(blk_scores, n_top)
        sel = jnp.concatenate([top, own], axis=-1)
        sel_valid = jnp.concatenate(
            [top < qblk[None, None, :, None], jnp.ones_like(own, dtype=bool)], axis=-1)
    else:
        sel = own
        sel_valid = jnp.ones_like(own, dtype=bool)
    nsel = sel.shape[-1]
    nq = S // MOBA_QCHUNK
    q_c = jnp.moveaxis(q.reshape(B, H, nq, MOBA_QCHUNK, D), 2, 0)
    sel_c = jnp.moveaxis(sel.reshape(B, H, nq, MOBA_QCHUNK, nsel), 2, 0)
    val_c = jnp.moveaxis(sel_valid.reshape(B, H, nq, MOBA_QCHUNK, nsel), 2, 0)
    pos_c = pos.reshape(nq, MOBA_QCHUNK)
    bi = jnp.arange(B)[:, None, None, None]
    hi = jnp.arange(H)[None, :, None, None]
    scale = HEAD_DIM ** -0.5

    def attend(args):
        qq, ss, vv, pp = args
        kg = kb[bi, hi, ss]
        vg = vb[bi, hi, ss]
        logits = jnp.einsum('bhqd,bhqnkd->bhqnk', qq, kg).astype(jnp.float32) * scale
        kpos = ss[..., None] * MOBA_BLOCK + jnp.arange(MOBA_BLOCK)
        mask = vv[..., None] & (kpos <= pp[None, None, :, None, None])
        logits = jnp.where(mask, logits, -jnp.inf)
        p = jax.nn.softmax(logits.reshape(B, H, MOBA_QCHUNK, nsel * MOBA_BLOCK), axis=-1)
        p = p.reshape(B, H, MOBA_QCHUNK, nsel, MOBA_BLOCK).astype(vg.dtype)
        return jnp.einsum('bhqnk,bhqnkd->bhqd', p, vg)

    o = lax.map(attend, (q_c, sel_c, val_c, pos_c))
    o = jnp.moveaxis(o, 0, 2).reshape(B, H, S, D).transpose(0, 2, 1, 3)
    return o.reshape(B, S, D_MOBA)


def gmlp_mixer(u, v, ln_g, ln_b, w_s, b_s):
    B, S, _ = u.shape
    u = jax.nn.gelu(u)
    v = jax.nn.gelu(v)
    vf = v.astype(jnp.float32)
    mean = jnp.mean(vf, axis=-1, keepdims=True)
    var = jnp.mean(jnp.square(vf - mean), axis=-1, keepdims=True)
    v = (((vf - mean) * lax.rsqrt(var + NORM_EPS)) * ln_g + ln_b).astype(u.dtype)
    nc = S // GMLP_CHUNK
    vc = v.reshape(B, nc, GMLP_CHUNK, GMLP_GROUPS, HEAD_DIM)
    tril = jnp.tril(jnp.ones((GMLP_CHUNK, GMLP_CHUNK), dtype=bool))
    w_causal = jnp.where(tril[None], w_s, 0.0)
    s = jnp.einsum('gts,bcsgd->bctgd', w_causal, vc) + b_s.T[:, :, None]
    return u * s.reshape(B, S, D_GMLP)


def setup_inputs(seed: int = 0) -> dict:
    key = jax.random.key(seed)
    ks = jax.random.split(key, 24)
    n = lambda k, shape, s: jax.random.normal(k, shape, jnp.float32) * s
    L = DEPTH
    return {
        "x": n(ks[0], (BATCH, SEQ, D_MODEL), 1.0),
        "pre_mix_g": 1.0 + n(ks[1], (L, D_MODEL), 0.02),
        "w_in": n(ks[2], (L, D_MODEL, D_IN), D_MODEL ** -0.5),
        "rwkv_mu": jax.random.uniform(ks[3], (L, A_COLS), jnp.float32),
        "rwkv_w0": jax.random.uniform(ks[4], (L, D_RWKV), jnp.float32, -5.0, 1.0),
        "rwkv_w2": n(ks[5], (L, DECAY_LORA, D_RWKV), 0.1),
        "rwkv_a0": n(ks[6], (L, D_RWKV), 0.1),
        "rwkv_a2": n(ks[7], (L, AAA_LORA, D_RWKV), 0.5 * AAA_LORA ** -0.5),
        "rwkv_g2": n(ks[8], (L, GATE_LORA, D_RWKV), GATE_LORA ** -0.5),
        "rwkv_k_k": 0.85 + n(ks[9], (L, D_RWKV), 0.02),
        "rwkv_k_a": 1.0 + n(ks[10], (L, D_RWKV), 0.02),
        "rwkv_r_k": n(ks[11], (L, RWKV_HEADS, HEAD_DIM), 0.1),
        "rwkv_lnx_g": 1.0 + n(ks[12], (L, D_RWKV), 0.02),
        "rwkv_lnx_b": n(ks[13], (L, D_RWKV), 0.02),
        "gmlp_ln_g": 1.0 + n(ks[14], (L, D_GMLP), 0.02),
        "gmlp_ln_b": n(ks[15], (L, D_GMLP), 0.02),
        "gmlp_w_s": n(ks[16], (L, GMLP_GROUPS, GMLP_CHUNK, GMLP_CHUNK), GMLP_CHUNK ** -0.5),
        "gmlp_b_s": 1.0 + n(ks[17], (L, GMLP_GROUPS, GMLP_CHUNK), 0.01),
        "w_out": n(ks[18], (L, D_MIX, D_MODEL), D_MIX ** -0.5),
        "post_mix_g": 1.0 + n(ks[19], (L, D_MODEL), 0.02),
        "pre_ffn_g": 1.0 + n(ks[20], (L, D_MODEL), 0.02),
        "w_ffn_in": n(ks[21], (L, D_MODEL, 2 * D_FF), D_MODEL ** -0.5),
        "w_ffn_out": n(ks[22], (L, D_FF, D_MODEL), D_FF ** -0.5),
        "post_ffn_g": 1.0 + n(ks[23], (L, D_MODEL), 0.02),
    }


def reference(x, pre_mix_g, w_in, rwkv_mu, rwkv_w0, rwkv_w2, rwkv_a0, rwkv_a2, rwkv_g2,
              rwkv_k_k, rwkv_k_a, rwkv_r_k, rwkv_lnx_g, rwkv_lnx_b, gmlp_ln_g, gmlp_ln_b,
              gmlp_w_s, gmlp_b_s, w_out, post_mix_g, pre_ffn_g, w_ffn_in, w_ffn_out,
              post_ffn_g):
    for l in range(DEPTH):
        h = rmsnorm(x, pre_mix_g[l])
        proj = h @ w_in[l]
        ya, yb, yc = jnp.split(proj, [A_COLS, A_COLS + B_COLS], axis=-1)
        a_out = rwkv7_mixer(ya, rwkv_mu[l], rwkv_w0[l], rwkv_w2[l], rwkv_a0[l],
                            rwkv_a2[l], rwkv_g2[l], rwkv_k_k[l], rwkv_k_a[l],
                            rwkv_r_k[l], rwkv_lnx_g[l], rwkv_lnx_b[l])
        qb, kb, vb = jnp.split(yb, 3, axis=-1)
        b_out = moba_mixer(qb, kb, vb)
        u, v = jnp.split(yc, 2, axis=-1)
        c_out = gmlp_mixer(u, v, gmlp_ln_g[l], gmlp_ln_b[l], gmlp_w_s[l], gmlp_b_s[l])
        mix = jnp.concatenate([a_out, b_out, c_out], axis=-1) @ w_out[l]
        x = x + rmsnorm(mix, post_mix_g[l])
        h = rmsnorm(x, pre_ffn_g[l])
        gate, up = jnp.split(h @ w_ffn_in[l], 2, axis=-1)
        f = (jax.nn.silu(gate) * up) @ w_ffn_out[l]
        x = x + rmsnorm(f, post_ffn_g[l])
    return x
```

```python
import numpy as np
from contextlib import ExitStack
import concourse.bass as bass
import concourse.mybir as mybir
from concourse.bass_utils import run_bass_kernel_spmd

F32 = mybir.dt.float32
AF = mybir.ActivationFunctionType
ALU = mybir.AluOpType
AX = mybir.AxisListType

S = 2048
D = 1024
L = 2
DFF = 2816
NFF = 22
H = 6
C0 = float(np.exp(-0.5))
BIG = 30000.0


class Sched:
    def __init__(self, nc, n_dma=40):
        self.nc = nc
        self.names = ["pe", "act", "dve", "pool", "sp"]
        self.sem = {e: nc.alloc_semaphore("s_" + e) for e in ["pe", "act", "dve", "pool"]}
        self.cnt = {e: 0 for e in self.sem}
        self.dsem = [nc.alloc_semaphore("d%d" % i) for i in range(n_dma)]
        self.dcnt = [0] * n_dma
        self.drr = 0
        self.q = {e: [] for e in self.names}
        self.waited = {e: {} for e in self.names}
        self.lastw = {}
        self.readers = {}

    def _deps(self, reads, writes):
        deps = {}

        def add(k, v):
            if deps.get(k, 0) < v:
                deps[k] = v

        for r in reads:
            ev = self.lastw.get(r)
            if ev is not None:
                add(*ev)
        for w in writes:
            ev = self.lastw.get(w)
            if ev is not None:
                add(*ev)
            for k, v in self.readers.get(w, {}).items():
                add(k, v)
        return deps

    def _commit(self, ev, reads, writes):
        k, v = ev
        for r in reads:
            d = self.readers.setdefault(r, {})
            if d.get(k, 0) < v:
                d[k] = v
        for w in writes:
            self.lastw[w] = ev
            self.readers[w] = {}

    def _waits(self, eng, deps):
        waits = []
        for k, v in deps.items():
            if eng == "pe" and k == ("e", "pe"):
                continue
            if self.waited[eng].get(k, 0) >= v:
                continue
            self.waited[eng][k] = v
            waits.append((k, v))
        return waits

    def op(self, eng, fn, reads=(), writes=()):
        banks = {("psx", k[1]) for k in list(reads) + list(writes) if isinstance(k, tuple) and k and k[0] == "ps"}
        if banks:
            writes = list(writes) + list(banks)
        deps = self._deps(reads, writes)
        waits = self._waits(eng, deps)
        self.cnt[eng] += 1
        ev = (("e", eng), self.cnt[eng])
        self.q[eng].append((waits, fn, "e"))
        self._commit(ev, reads, writes)

    def dma(self, qeng, out, in_, reads=(), writes=(), **kw):
        deps = self._deps(reads, writes)
        idx = self.drr
        self.drr = (self.drr + 1) % len(self.dsem)
        if self.dcnt[idx] > 0:
            k = ("d", idx)
            deps[k] = max(deps.get(k, 0), self.dcnt[idx])
        waits = self._waits(qeng, deps)
        self.dcnt[idx] += 16
        ev = (("d", idx), self.dcnt[idx])
        self.q[qeng].append((waits, lambda e: e.dma_start(out=out, in_=in_, **kw), idx))
        self._commit(ev, reads, writes)

    def barrier(self):
        allev = [(("e", e), c) for e, c in self.cnt.items() if c > 0]
        allev += [(("d", i), c) for i, c in enumerate(self.dcnt) if c > 0]
        for eng in self.names:
            waits = self._waits(eng, dict(allev))
            if waits:
                self.q[eng].append((waits, None, None))
        self.lastw = {}
        self.readers = {}

    def emit(self):
        nc = self.nc
        engs = {"pe": "tensor", "act": "scalar", "dve": "vector", "pool": "gpsimd", "sp": "sync"}
        with nc.Block() as block:
            for name in self.names:
                def body(eng, name=name):
                    for waits, fn, kind in self.q[name]:
                        for k, v in waits:
                            s = self.sem[k[1]] if k[0] == "e" else self.dsem[k[1]]
                            eng.wait_ge(s, v)
                        if fn is None:
                            continue
                        ins = fn(eng)
                        if kind == "e":
                            ins.then_inc(self.sem[name], 1)
                        else:
                            ins.then_inc(self.dsem[kind], 16)
                getattr(block, engs[name])(body)


def make_consts():
    c = {}
    i128 = np.arange(128)
    blk = (i128[:, None] // 64) == (i128[None, :] // 64)
    ident = np.eye(128, dtype=np.float32)
    ones = np.ones((128, 128), np.float32)
    SL = ((i128[:, None] > i128[None, :]) & blk).astype(np.float32)
    SU = ((i128[:, None] < i128[None, :]) & blk).astype(np.float32)
    IU = ((i128[:, None] <= i128[None, :]) & blk).astype(np.float32)
    IUfull = (i128[:, None] <= i128[None, :]).astype(np.float32)
    idst = np.concatenate([np.eye(64), np.eye(64)], 0).astype(np.float32)
    reset = np.ones((128, 256), np.float32)
    reset[:, ::64] = 0.0
    rowm = np.zeros((128, 2), np.float32)
    rowm[:64, 0] = 1.0
    rowm[64:, 1] = 1.0
    q512 = np.arange(512)
    cm = np.stack([(q512[None, :] >= (j * 128 + i128[:, None])).astype(np.float32) for j in range(4)], 1)
    parts = [ident, ones, SU, IU, SL, SU, IU, IUfull, idst, reset, rowm, cm.reshape(128, 2048)]
    offs = {}
    o = 0
    for nm, p in zip(["ident", "ones", "mE", "_1", "_2", "mY", "_3", "iuf", "idst", "reset", "rowm", "cm"], parts):
        offs[nm] = o
        o += p.shape[1]
    c["cst"] = np.ascontiguousarray(np.concatenate(parts, 1))
    c["offs"] = offs
    mb = np.zeros((128, 3, 16, 6, 8), np.float32)
    for t in range(16):
        b = t // 2
        for n in range(8):
            mb[:, 0, t, :, n] = 0.0 if n < b else -1e30
            mb[:, 1, t, :, n] = 1.0 if n < b else 0.0
            mb[:, 2, t, :, n] = 1.0 if n == b else 0.0
    c["mobac"] = mb.reshape(128, 3 * 16 * 48)
    bi = np.zeros((8, S), np.float32)
    for n in range(8):
        bi[n, n * 256:(n + 1) * 256] = 1.0
    c["blkind"] = bi
    return c


CONSTS = make_consts()
PARAM_NAMES = ["pre_mix_g", "w_in", "rwkv_mu", "rwkv_w0", "rwkv_w2", "rwkv_a0", "rwkv_a2", "rwkv_g2",
               "rwkv_k_k", "rwkv_k_a", "rwkv_r_k", "rwkv_lnx_g", "rwkv_lnx_b", "gmlp_ln_g", "gmlp_ln_b",
               "gmlp_w_s", "gmlp_b_s", "w_out", "post_mix_g", "pre_ffn_g", "w_ffn_in", "w_ffn_out", "post_ffn_g"]
PARAM_SHAPES = {"pre_mix_g": (L, D), "w_in": (L, D, 3072), "rwkv_mu": (L, 1408), "rwkv_w0": (L, 384),
                "rwkv_w2": (L, 64, 384), "rwkv_a0": (L, 384), "rwkv_a2": (L, 64, 384), "rwkv_g2": (L, 128, 384),
                "rwkv_k_k": (L, 384), "rwkv_k_a": (L, 384), "rwkv_r_k": (L, 6, 64), "rwkv_lnx_g": (L, 384),
                "rwkv_lnx_b": (L, 384), "gmlp_ln_g": (L, 256), "gmlp_ln_b": (L, 256), "gmlp_w_s": (L, 4, 128, 128),
                "gmlp_b_s": (L, 4, 128), "w_out": (L, D, D), "post_mix_g": (L, D), "pre_ffn_g": (L, D),
                "w_ffn_in": (L, D, 2 * DFF), "w_ffn_out": (L, DFF, D), "post_ffn_g": (L, D)}


OPTS = {}


def build(dbg=None, nlayers=L, stop=None):
    dbg = dbg or []
    nc = bass.Bass("TRN2", target_bir_lowering=False)
    sc = Sched(nc)
    OF = CONSTS["offs"]

    def dram(name, shape, kind="Internal"):
        if name in dbg:
            kind = "ExternalOutput"
        return nc.dram_tensor(name, list(shape), F32, kind=kind).ap()

    x_in = dram("x", [S, D], "ExternalInput")
    y_out = dram("y", [S, D], "ExternalOutput")
    cst_d = dram("cst", CONSTS["cst"].shape, "ExternalInput")
    if not OPTS.get("noparams"):
        PR = {n: dram(n, PARAM_SHAPES[n], "ExternalInput") for n in PARAM_NAMES}
        mobac_d = dram("mobac", CONSTS["mobac"].shape, "ExternalInput")
        blkind_d = dram("blkind", CONSTS["blkind"].shape, "ExternalInput")
    if OPTS.get("noscratch"):
        glob_scr = None
    rwkvT = dram("rwkvT", [1408, S]) if not OPTS.get("noscratch") else None
    qkT = dram("qkT", [768, S]) if not OPTS.get("noscratch") else None
    uT = dram("uT", [256, S]) if not OPTS.get("noscratch") else None
    vm_tm = dram("vm_tm", [S, 384]) if not OPTS.get("noscratch") else None
    vg_tm = dram("vg_tm", [S, 256]) if not OPTS.get("noscratch") else None
    catT = dram("catT", [D, S]) if not OPTS.get("noscratch") else None
    xdbg = dram("xdbg", [D, S]) if not OPTS.get("noscratch") else None

    uid = [0]

    def sb(es, name, shape):
        uid[0] += 1
        return es.enter_context(nc.sbuf_tensor("%s_%d" % (name, uid[0]), list(shape), F32))

    glob = ExitStack()
    xT = sb(glob, "xT", [128, 8, S])
    cst = sb(glob, "cst_sb", [128, CONSTS["cst"].shape[1]])
    ps = [glob.enter_context(nc.psum_tensor("ps%d" % i, [128, 512], F32)) for i in range(8)]
    ident = cst[:, OF["ident"]:OF["ident"] + 128]
    ones = cst[:, OF["ones"]:OF["ones"] + 128]
    mE = cst[:, OF["mE"]:OF["mE"] + 384]
    mY = cst[:, OF["mY"]:OF["mY"] + 256]
    iuf = cst[:, OF["iuf"]:OF["iuf"] + 128]
    idst = cst[:, OF["idst"]:OF["idst"] + 64]
    resetm = cst[:, OF["reset"]:OF["reset"] + 256]
    cm = cst[:, OF["cm"]:OF["cm"] + 2048]
    epsc = sb(glob, "epsc", [128, 4])

    def ACT(out, in_, func, reads, writes, **kw):
        sc.op("act", lambda e: e.activation(out=out, in_=in_, func=func, **kw), reads, writes)

    def MM(out, lhsT, rhs, start, stop, reads, writes):
        sc.op("pe", lambda e: e.matmul(out, lhsT=lhsT, rhs=rhs, start=start, stop=stop), reads, writes)

    def TR(out, in_, idn, reads, writes):
        sc.op("pe", lambda e: e.transpose(out, in_, idn), reads, writes)

    def TT(eng, out, in0, in1, op, reads, writes):
        sc.op(eng, lambda e: e.tensor_tensor(out=out, in0=in0, in1=in1, op=op), reads, writes)

    def TS(eng, out, in0, s1, s2, op0, op1, reads, writes):
        if s2 is None:
            sc.op(eng, lambda e: e.tensor_scalar(out=out, in0=in0, scalar1=s1, scalar2=None, op0=op0), reads, writes)
        else:
            sc.op(eng, lambda e: e.tensor_scalar(out=out, in0=in0, scalar1=s1, scalar2=s2, op0=op0, op1=op1), reads, writes)

    def STT(out, in0, scalar, in1, op0, op1, reads, writes):
        sc.op("dve", lambda e: e.scalar_tensor_tensor(out=out, in0=in0, scalar=scalar, in1=in1, op0=op0, op1=op1), reads, writes)

    def CP(eng, out, in_, reads, writes):
        if eng == "act":
            sc.op("act", lambda e: e.copy(out=out, in_=in_), reads, writes)
        else:
            sc.op(eng, lambda e: e.tensor_copy(out=out, in_=in_), reads, writes)

    def RECIP(out, in_, reads, writes):
        sc.op("dve", lambda e: e.reciprocal(out=out, in_=in_), reads, writes)

    def DMA(out, in_, reads, writes, q="sp", **kw):
        sc.dma(q, out, in_, reads, writes, **kw)

    def colvec(es, name, src_1d, ncol, p=128):
        t = sb(es, name, [p, ncol])
        DMA(t[:, :], src_1d.rearrange("(c p) -> p c", p=p), [], [name], allow_slow_non_contiguous=True)
        return t

    DMA(cst[:, :], cst_d[:, :], [], ["cst"])
    sc.op("dve", lambda e: e.memset(epsc[:, 0:1], 1e-6), [], ["epsc"])
    sc.op("dve", lambda e: e.memset(epsc[:, 1:2], 64e-5), [], ["epsc"])
    sc.op("dve", lambda e: e.memset(epsc[:, 2:3], 0.0), [], ["epsc"])
    with ExitStack() as es:
        xin = sb(es, "xin", [128, 2, D])
        for t in range(OPTS.get('nt', 16)):
            sl = t % 2 if not OPTS.get('sl0') else 0
            DMA(xin[:, sl, :], x_in[t * 128:(t + 1) * 128, :], [], [("xin", sl)])
            for g in range(2):
                bank = ps[(t * 2 + g) % OPTS.get('nb', 4)]
                bk = ("ps", (t * 2 + g) % OPTS.get('nb', 4))
                for kk in range(4):
                    k = g * 4 + kk
                    TR(bank[:, kk * 128:(kk + 1) * 128], xin[:, sl, k * 128:(k + 1) * 128], ident, [("xin", sl), "cst"], [bk])
                CP("act" if g else "dve", xT[:, g * 4:(g + 1) * 4, t * 128:(t + 1) * 128],
                   bank[:, :].rearrange("p (k c) -> p k c", k=4), [bk], [("x", g * 4 + kk, t // 4) for kk in range(4)])
        if not OPTS.get("nobar"):
            sc.barrier()

    def rms_stats(src_fn, src_keys, tt, es_tiles, pbank, pkey):
        sq, rstd = es_tiles
        for k in range(8):
            ACT(sq[:, k % 2, :], src_fn(k), AF.Square, [src_keys(k)], [("sq", k % 2)])
            MM(pbank[:, :], ones, sq[:, k % 2, :], k == 0, k == 7, [("sq", k % 2), "cst"], [pkey])
        ACT(rstd[:, tt % 2, :], pbank[:, :], AF.Sqrt, [pkey, "epsc"], [("rstd", tt % 2)], scale=1.0 / D, bias=epsc[:, 0:1])
        RECIP(rstd[:, tt % 2, :], rstd[:, tt % 2, :], [("rstd", tt % 2)], [("rstd", tt % 2)])
        return rstd[:, tt % 2, :]

    for l in range(nlayers if stop != 'setup' else 0):
        with ExitStack() as es:
            hbuf = sb(es, "hbuf", [128, 8, S])
            sq = sb(es, "sq", [128, 2, 512])
            rstd = sb(es, "rstd", [128, 2, 512])
            gA = colvec(es, "gA", PR["pre_mix_g"][l], 8)
            muA = colvec(es, "muA", PR["rwkv_mu"][l], 11)
            for tt in range(4):
                tsl = slice(tt * 512, (tt + 1) * 512)
                r = rms_stats(lambda k: xT[:, k, tsl], lambda k: ("x", k, tt), tt, (sq, rstd), ps[4 + tt % 2], ("ps", 4 + tt % 2))
                for k in range(8):
                    STT(hbuf[:, k, tsl], xT[:, k, tsl], gA[:, k:k + 1], r, ALU.mult, ALU.mult,
                        [("x", k, tt), ("rstd", tt % 2), "gA"], [("h", k, tt)])
            es_main = es
            es = ExitStack()
            wA = sb(es, "wA", [128, 2, 8, 128])
            stg = sb(es, "stg", [128, 2, S])
            stg2 = sb(es, "stg2", [128, 2, S])
            w_in_l = PR["w_in"][l].rearrange("(k p) c -> p k c", p=128)
            fm_chunks = [(c * 128, rwkvT, c * 128, True) for c in range(11)]
            fm_chunks += [(1408 + c * 128, qkT, c * 128, False) for c in range(6)]
            fm_chunks += [(2560 + c * 128, uT, c * 128, False) for c in range(2)]
            for ci, (col0, dst, row0, shift) in enumerate(fm_chunks):
                sl = ci % 2
                DMA(wA[:, sl, :, :], w_in_l[:, :, col0:col0 + 128], [], [("wA", sl)])
                for tt in range(4):
                    tsl = slice(tt * 512, (tt + 1) * 512)
                    bi = (ci * 4 + tt) % 4
                    for k in range(8):
                        MM(ps[bi][:, :], wA[:, sl, k, :], hbuf[:, k, tsl], k == 0, k == 7,
                           [("wA", sl), ("h", k, tt)], [("ps", bi)])
                    CP("act" if tt % 2 else "dve", stg[:, sl, tsl], ps[bi][:, :], [("ps", bi)], [("stg", sl, tt)])
                allst = [("stg", sl, tt) for tt in range(4)]
                if shift:
                    TT("pool", stg2[:, sl, 1:S], stg[:, sl, 0:S - 1], stg[:, sl, 1:S], ALU.subtract, allst, [("stg2", sl)])
                    TS("pool", stg2[:, sl, 0:1], stg[:, sl, 0:1], -1.0, None, ALU.mult, None, allst, [("stg2", sl)])
                    STT(stg2[:, sl, :], stg2[:, sl, :], muA[:, ci:ci + 1], stg[:, sl, :], ALU.mult, ALU.add,
                        allst + [("stg2", sl), "muA"], [("stg2", sl)])
                    DMA(dst[row0:row0 + 128, :], stg2[:, sl, :], [("stg2", sl)], [("dr", id(dst), row0)], q="pool")
                else:
                    DMA(dst[row0:row0 + 128, :], stg[:, sl, :], allst, [("dr", id(dst), row0)], q="pool")
            sc.barrier()
            es.close()
            es = es_main
            wB = sb(es, "wB", [128, 8, 640])
            DMA(wB[:, :, 0:384], w_in_l[:, :, 2176:2560], [], ["wB"])
            DMA(wB[:, :, 384:640], w_in_l[:, :, 2816:3072], [], ["wB"])
            vst = sb(es, "vst", [128, 2, 640])
            for t in range(16):
                sl = t % 2
                b0, b1 = 4 + (t % 2) * 2, 5 + (t % 2) * 2
                for k in range(8):
                    MM(ps[b0][:, 0:384], hbuf[:, k, t * 128:(t + 1) * 128], wB[:, k, 0:384], k == 0, k == 7,
                       [("h", k, t // 4), "wB"], [("ps", b0)])
                for k in range(8):
                    MM(ps[b1][:, 0:256], hbuf[:, k, t * 128:(t + 1) * 128], wB[:, k, 384:640], k == 0, k == 7,
                       [("h", k, t // 4), "wB"], [("ps", b1)])
                CP("act", vst[:, sl, 0:384], ps[b0][:, 0:384], [("ps", b0)], [("vst", sl, 0)])
                CP("dve", vst[:, sl, 384:640], ps[b1][:, 0:256], [("ps", b1)], [("vst", sl, 1)])
                DMA(vm_tm[t * 128:(t + 1) * 128, :], vst[:, sl, 0:384], [("vst", sl, 0)], [("vm", t)], q="pool")
                DMA(vg_tm[t * 128:(t + 1) * 128, :], vst[:, sl, 384:640], [("vst", sl, 1)], [("vg", t)], q="pool")
            sc.barrier()
        if stop == "A":
            break

        with ExitStack() as es:
            lng = sb(es, "lng", [128, 256])
            lnb = sb(es, "lnb", [128, 256])
            DMA(lng[:, :], PR["gmlp_ln_g"][l].partition_broadcast(128), [], ["lng"])
            DMA(lnb[:, :], PR["gmlp_ln_b"][l].partition_broadcast(128), [], ["lnb"])
            wsn = sb(es, "wsn", [128, 4, 128])
            wsT = sb(es, "wsT", [128, 4, 128])
            bsr = sb(es, "bsr", [1, 512])
            DMA(wsn[:, :, :], PR["gmlp_w_s"][l].rearrange("g t s -> t g s"), [], ["wsn"])
            DMA(bsr[:, :], PR["gmlp_b_s"][l].rearrange("g t -> (g t)").partition_broadcast(1), [], ["bsr"])
            for g in range(4):
                TR(ps[0][:, g * 128:(g + 1) * 128], wsn[:, g, :], ident, ["wsn", "cst"], [("ps", 0)])
            for g in range(4):
                TT("dve", wsT[:, g, :], ps[0][:, g * 128:(g + 1) * 128], iuf, ALU.mult, [("ps", 0), "cst"], ["wsT"])
            gu = sb(es, "gu", [128, 2, S])
            t1 = sb(es, "t1", [128, S])
            cout = sb(es, "cout", [128, 2, S])
            for pp in range(2):
                DMA(gu[:, pp, :], uT[pp * 128:(pp + 1) * 128, :], [], [("gu", pp)])
                ACT(t1[:, :], gu[:, pp, :], AF.Square, [("gu", pp)], ["t1"])
                TS("pool", t1[:, :], t1[:, :], 0.044715, 1.0, ALU.mult, ALU.add, ["t1"], ["t1"])
                TT("dve", t1[:, :], t1[:, :], gu[:, pp, :], ALU.mult, ["t1", ("gu", pp)], ["t1"])
                ACT(t1[:, :], t1[:, :], AF.Sigmoid, ["t1"], ["t1"], scale=2.0 * 0.7978845608028654)
                TT("dve", gu[:, pp, :], gu[:, pp, :], t1[:, :], ALU.mult, ["t1", ("gu", pp)], [("gu", pp)])
            vb = sb(es, "vb", [128, 2, 256])
            t2 = sb(es, "t2", [128, 2, 256])
            st6 = sb(es, "st6", [128, 2, 8])
            for c in range(16):
                sl = c % 2
                DMA(vb[:, sl, :], vg_tm[c * 128:(c + 1) * 128, :], [], [("vb", sl)])
                kv, kt = ("vb", sl), ("t2", sl)
                ACT(t2[:, sl, :], vb[:, sl, :], AF.Square, [kv], [kt])
                TS("pool", t2[:, sl, :], t2[:, sl, :], 0.044715, 1.0, ALU.mult, ALU.add, [kt], [kt])
                TT("dve", t2[:, sl, :], t2[:, sl, :], vb[:, sl, :], ALU.mult, [kt, kv], [kt])
                ACT(t2[:, sl, :], t2[:, sl, :], AF.Sigmoid, [kt], [kt], scale=2.0 * 0.7978845608028654)
                TT("dve", vb[:, sl, :], vb[:, sl, :], t2[:, sl, :], ALU.mult, [kt, kv], [kv])
                ks = ("st6", sl)
                sc.op("dve", lambda e, sl=sl: e.bn_stats(out=st6[:, sl, 0:6], in_=vb[:, sl, :]), [kv], [ks])
                sc.op("dve", lambda e, sl=sl: e.bn_aggr(out=st6[:, sl, 6:8], in_=st6[:, sl, 0:6]), [ks], [ks])
                ACT(st6[:, sl, 7:8], st6[:, sl, 7:8], AF.Sqrt, [ks, "epsc"], [ks], bias=epsc[:, 0:1], scale=1.0)
                RECIP(st6[:, sl, 7:8], st6[:, sl, 7:8], [ks], [ks])
                TS("dve", vb[:, sl, :], vb[:, sl, :], st6[:, sl, 6:7], st6[:, sl, 7:8], ALU.subtract, ALU.mult, [kv, ks], [kv])
                TT("dve", vb[:, sl, :], vb[:, sl, :], lng[:, :], ALU.mult, [kv, "lng"], [kv])
                TT("dve", vb[:, sl, :], vb[:, sl, :], lnb[:, :], ALU.add, [kv, "lnb"], [kv])
                for pp in range(2):
                    bi = 1 + (c % 2) * 2 + pp
                    for gg in range(2):
                        g = pp * 2 + gg
                        MM(ps[bi][:, gg * 128:(gg + 1) * 128], vb[:, sl, pp * 128:(pp + 1) * 128], wsT[:, g, :], True, False, [kv, "wsT"], [("ps", bi)])
                        MM(ps[bi][:, gg * 128:(gg + 1) * 128], ones[0:1, :], bsr[0:1, g * 128:(g + 1) * 128], False, True, ["cst", "bsr"], [("ps", bi)])
                    for gg in range(2):
                        TT("dve", cout[gg * 64:(gg + 1) * 64, pp, c * 128:(c + 1) * 128], ps[bi][gg * 64:(gg + 1) * 64, gg * 128:(gg + 1) * 128],
                           gu[gg * 64:(gg + 1) * 64, pp, c * 128:(c + 1) * 128], ALU.mult, [("ps", bi), ("gu", pp)], [("cout", pp)])
            for pp in range(2):
                DMA(catT[768 + pp * 128:768 + (pp + 1) * 128, :], cout[:, pp, :], [("cout", pp)], [("cat", 6 + pp)], q="pool")
            sc.barrier()
        if stop == "D":
            break

        with ExitStack() as es:
            qa = sb(es, "qa", [72, S])
            ka = sb(es, "ka", [72, S])
            vt = sb(es, "vt", [128, 16, 64])
            kbar = sb(es, "kbar", [64, 8])
            mobc = sb(es, "mobc", [128, 3, 16, 48])
            NP = sb(es, "NP", [128, 16, 72])
            sm = sb(es, "sm", [128, 2, 8])
            top8 = sb(es, "top8", [128, 2, 8])
            al = sb(es, "al", [128, 2, 8])
            pt = sb(es, "pt", [128, 3, 512])
            rden = sb(es, "rden", [64, 512])
            ob = sb(es, "ob", [64, 2, 512])
            DMA(mobc[:, :, :, :], mobac_d.rearrange("p (a t c) -> p a t c", a=3, t=16), [], ["mobc"])
            DMA(ka[64:72, :], blkind_d[:, :], [], ["kaB"])
            sc.op("pool", lambda e: e.memset(NP[:, :, :], 0.0), [], ["NP"])
            pti = 0
            for h in range(H):
                DMA(qa[0:64, :], qkT[h * 64:(h + 1) * 64, :], [], ["qaQ"])
                DMA(ka[0:64, :], qkT[384 + h * 64:384 + (h + 1) * 64, :], [], ["kaK"])
                DMA(vt[:, :, :], vm_tm.rearrange("(t p) c -> p t c", p=128)[:, :, h * 64:(h + 1) * 64], [], ["vt"])
                sc.op("dve", lambda e: e.tensor_reduce(out=kbar[:, :], in_=ka[0:64, :].rearrange("p (n k) -> p n k", n=8), axis=AX.X, op=ALU.add), ["kaK"], ["kbar"])
                for t in range(16):
                    MM(ps[0][:, t * 8:(t + 1) * 8], qa[0:64, t * 128:(t + 1) * 128], kbar[:, :], True, True, ["qaQ", "kbar"], [("ps", 0)])
                for t in range(16):
                    sl = t % 2
                    hs = slice(h * 8, (h + 1) * 8)
                    TT("dve", sm[:, sl, :], ps[0][:, t * 8:(t + 1) * 8], mobc[:, 0, t, hs], ALU.add, [("ps", 0), "mobc"], [("sm", sl)])
                    sc.op("dve", lambda e, sl=sl: e.max(out=top8[:, sl, :], in_=sm[:, sl, :]), [("sm", sl)], [("top8", sl)])
                    TS("dve", al[:, sl, :], sm[:, sl, :], top8[:, sl, 2:3], None, ALU.is_ge, None, [("sm", sl), ("top8", sl)], [("al", sl)])
                    TT("dve", al[:, sl, :], al[:, sl, :], mobc[:, 1, t, hs], ALU.mult, [("al", sl), "mobc"], [("al", sl)])
                    TT("dve", al[:, sl, :], al[:, sl, :], mobc[:, 2, t, hs], ALU.add, [("al", sl), "mobc"], [("al", sl)])
                    TS("dve", NP[:, t, 64:72], al[:, sl, :], -1.0, BIG, ALU.add, ALU.mult, [("al", sl)], ["NP"])
                for t4 in range(4):
                    for tq in range(4):
                        t = t4 * 4 + tq
                        MM(ps[1][0:72, tq * 128:(tq + 1) * 128], NP[:, t, :], ident, True, True, ["NP", "cst"], [("ps", 1)])
                    CP("act", qa[64:72, t4 * 512:(t4 + 1) * 512], ps[1][64:72, :], [("ps", 1)], ["qaM"])
                for qt in range(4):
                    qsl = slice(qt * 512, (qt + 1) * 512)
                    nk = (qt + 1) * 4
                    osl = qt % 2
                    for kt in range(nk):
                        sb_i = 2 + kt % 2
                        pi = pti % 3
                        pti += 1
                        MM(ps[sb_i][:, :], ka[0:72, kt * 128:(kt + 1) * 128], qa[0:72, qsl], True, True,
                           ["kaK", "kaB", "qaQ", "qaM"], [("ps", sb_i)])
                        ACT(pt[:, pi, :], ps[sb_i][:, :], AF.Exp, [("ps", sb_i)], [("pt", pi)], scale=0.125)
                        if kt >= qt * 4:
                            j = kt - qt * 4
                            TT("pool", pt[:, pi, :], pt[:, pi, :], cm[:, j * 512:(j + 1) * 512], ALU.mult, [("pt", pi), "cst"], [("pt", pi)])
                        MM(ps[4][0:64, :], vt[:, kt, :], pt[:, pi, :], kt == 0, kt == nk - 1, ["vt", ("pt", pi)], [("ps", 4)])
                        MM(ps[5][:, :], ones, pt[:, pi, :], kt == 0, kt == nk - 1, ["cst", ("pt", pi)], [("ps", 5)])
                    RECIP(rden[:, :], ps[5][0:64, :], [("ps", 5)], ["rden"])
                    TT("dve", ob[:, osl, :], ps[4][0:64, :], rden[:, :], ALU.mult, [("ps", 4), "rden"], [("ob", osl)])
                    DMA(catT[384 + h * 64:384 + (h + 1) * 64, qsl], ob[:, osl, :], [("ob", osl)], [("cat", "b", h, qt)], q="pool")
            sc.barrier()
        if stop == "C":
            break

        with ExitStack() as es:
            TW = 256
            w2s = sb(es, "w2s", [64, 384])
            a2s = sb(es, "a2s", [64, 384])
            g2s = sb(es, "g2s", [128, 384])
            DMA(w2s[:, :], PR["rwkv_w2"][l], [], ["w2s"])
            DMA(a2s[:, :], PR["rwkv_a2"][l], [], ["a2s"])
            DMA(g2s[:, :], PR["rwkv_g2"][l], [], ["g2s"])
            pw0 = colvec(es, "pw0", PR["rwkv_w0"][l], 6, p=64)
            pa0 = colvec(es, "pa0", PR["rwkv_a0"][l], 6, p=64)
            pkk = colvec(es, "pkk", PR["rwkv_k_k"][l], 6, p=64)
            pka = colvec(es, "pka", PR["rwkv_k_a"][l], 6, p=64)
            prk = colvec(es, "prk", PR["rwkv_r_k"][l].rearrange("h d -> (h d)"), 6, p=64)
            plg = colvec(es, "plg", PR["rwkv_lnx_g"][l], 6, p=64)
            plb = colvec(es, "plb", PR["rwkv_lnx_b"][l], 6, p=64)
            pok = sb(es, "pok", [64, 6])
            TS("dve", pok[:, :], pka[:, :], -1.0, 1.0, ALU.mult, ALU.add, ["pka"], ["pok"])
            i64 = ident[0:64, 0:64]
            o64 = ones[0:64, 0:64]
            rowm = cst[:, OF["rowm"]:OF["rowm"] + 2]
            Mst = sb(es, "Mst", [64, 2, 6, 64])
            sc.op("dve", lambda e: e.memset(Mst[:, 0, :, :], 0.0), [], [("Mst", 0)])
            mcur = 0
            RhatT = sb(es, "RhatT", [64, 6, TW])
            Y0T = sb(es, "Y0T", [64, 6, TW])
            yT = sb(es, "yT", [64, 6, TW])
            GT = sb(es, "GT", [64, 6, 4, 64])
            Hm = sb(es, "Hm", [64, 6, 4, 64])
            bon = sb(es, "bon", [64, 6, TW])
            gal = sb(es, "gal", [64, 6, TW])
            wd = sb(es, "wd", [64, TW]); ad = sb(es, "ad", [64, TW]); gd = sb(es, "gd", [128, TW])
            rr = sb(es, "rr", [64, TW]); kq = sb(es, "kq", [64, TW]); vv = sb(es, "vv", [64, TW])
            sig = sb(es, "sig", [64, TW]); cum = sb(es, "cum", [64, TW]); cpv = sb(es, "cpv", [64, TW])
            epos = sb(es, "epos", [64, TW]); eneg = sb(es, "eneg", [64, TW]); eprv = sb(es, "eprv", [64, TW]); eend = sb(es, "eend", [64, TW])
            nbc = sb(es, "nbc", [64, 4])
            aa = sb(es, "aa", [64, TW]); kk = sb(es, "kk", [64, TW]); kk2 = sb(es, "kk2", [64, TW]); rn = sb(es, "rn", [64, TW])
            kka = sb(es, "kka", [64, TW]); fac = sb(es, "fac", [64, TW]); kp = sb(es, "kp", [64, TW]); rkr = sb(es, "rkr", [64, TW])
            AR = sb(es, "AR", [64, 2, TW]); BK = sb(es, "BK", [64, 2, TW]); BpKp = sb(es, "BpKp", [64, 2, TW])
            tm = sb(es, "tm", [128, 2, 256]); Eb = sb(es, "Eb", [128, 2, 512]); YS = sb(es, "YS", [128, 2, 256])
            Lb = sb(es, "Lb", [128, 2, 2, 384]); B2 = sb(es, "B2", [128, 2, 128]); K2 = sb(es, "K2", [128, 2, 128])
            yc = sb(es, "yc", [64, TW]); ysq = sb(es, "ysq", [64, TW]); yrs = sb(es, "yrs", [64, TW])
            obuf = sb(es, "obuf", [64, 2, TW])
            sti = 0
            for tt in range(S // TW):
                tsl = slice(tt * TW, (tt + 1) * TW)
                DMA(wd[:, :], rwkvT[1152:1216, tsl], [], ["wd"])
                DMA(ad[:, :], rwkvT[1216:1280, tsl], [], ["ad"])
                DMA(gd[:, :], rwkvT[1280:1408, tsl], [], ["gd"])
                ACT(wd[:, :], wd[:, :], AF.Tanh, ["wd"], ["wd"])
                ACT(gd[:, :], gd[:, :], AF.Sigmoid, ["gd"], ["gd"])
                for h in range(H):
                    hc = slice(h * 64, (h + 1) * 64)
                    DMA(rr[:, :], rwkvT[h * 64:(h + 1) * 64, tsl], [], ["rr"])
                    DMA(kq[:, :], rwkvT[384 + h * 64:384 + (h + 1) * 64, tsl], [], ["kq"])
                    DMA(vv[:, :], rwkvT[768 + h * 64:768 + (h + 1) * 64, tsl], [], ["vv"])
                    MM(ps[0][0:64, 0:TW], w2s[:, hc], wd[:, :], True, True, ["w2s", "wd"], [("ps", 0, "a")])
                    ACT(sig[:, :], ps[0][0:64, 0:TW], AF.Sigmoid, [("ps", 0, "a"), "pw0"], ["sig"], bias=pw0[:, h:h + 1])
                    sc.op("dve", lambda e: e.tensor_tensor_scan(out=cum[:, :], data0=resetm[0:64, 0:TW], data1=sig[:, :], initial=0.0, op0=ALU.mult, op1=ALU.add), ["sig", "cst"], ["cum"])
                    TT("dve", cpv[:, :], cum[:, :], sig[:, :], ALU.subtract, ["cum", "sig"], ["cpv"])
                    ACT(epos[:, :], cum[:, :], AF.Exp, ["cum"], ["epos"], scale=-C0)
                    ACT(eneg[:, :], cum[:, :], AF.Exp, ["cum"], ["eneg"], scale=C0)
                    ACT(eprv[:, :], cpv[:, :], AF.Exp, ["cpv"], ["eprv"], scale=-C0)
                    cum3 = cum[:, :].rearrange("p (c t) -> p c t", c=4)
                    TS("dve", nbc[:, :], cum3[:, :, 63], -C0, None, ALU.mult, None, ["cum"], ["nbc"])
                    for c in range(4):
                        ACT(eend[:, c * 64:(c + 1) * 64], cum[:, c * 64:(c + 1) * 64], AF.Exp, ["cum", "nbc"], ["eend"], scale=C0, bias=nbc[:, c:c + 1])
                    MM(ps[0][0:64, 256:512], a2s[:, hc], ad[:, :], True, True, ["a2s", "ad"], [("ps", 0, "b")])
                    ACT(aa[:, :], ps[0][0:64, 256:512], AF.Sigmoid, [("ps", 0, "b"), "pa0"], ["aa"], bias=pa0[:, h:h + 1])
                    MM(ps[1][0:64, 0:TW], g2s[:, hc], gd[:, :], True, True, ["g2s", "gd"], [("ps", 1, "a")])
                    CP("act", gal[:, h, :], ps[1][0:64, 0:TW], [("ps", 1, "a")], [("gal", h)])
                    TS("dve", kk[:, :], kq[:, :], pkk[:, h:h + 1], None, ALU.mult, None, ["kq", "pkk"], ["kk"])
                    ACT(kk2[:, :], kk[:, :], AF.Square, ["kk"], ["kk2"])
                    MM(ps[1][0:64, 256:512], o64, kk2[:, :], True, True, ["cst", "kk2"], [("ps", 1, "b")])
                    ACT(rn[:, :], ps[1][0:64, 256:512], AF.Sqrt, [("ps", 1, "b")], ["rn"])
                    TS("dve", rn[:, :], rn[:, :], 1e-12, None, ALU.max, None, ["rn"], ["rn"])
                    RECIP(rn[:, :], rn[:, :], ["rn"], ["rn"])
                    TT("dve", kk[:, :], kk[:, :], rn[:, :], ALU.mult, ["kk", "rn"], ["kk"])
                    TT("dve", kka[:, :], kk[:, :], aa[:, :], ALU.mult, ["kk", "aa"], ["kka"])
                    TS("dve", fac[:, :], aa[:, :], pka[:, h:h + 1], pok[:, h:h + 1], ALU.mult, ALU.add, ["aa", "pka", "pok"], ["fac"])
                    TT("dve", kp[:, :], kq[:, :], fac[:, :], ALU.mult, ["kq", "fac"], ["kp"])
                    STT(rkr[:, :], rr[:, :], prk[:, h:h + 1], kp[:, :], ALU.mult, ALU.mult, ["rr", "prk", "kp"], ["rkr"])
                    MM(ps[2][0:64, 0:TW], o64, rkr[:, :], True, True, ["cst", "rkr"], [("ps", 2, "a")])
                    TT("dve", bon[:, h, :], ps[2][0:64, 0:TW], vv[:, :], ALU.mult, [("ps", 2, "a"), "vv"], [("bon", h)])
                    STT(AR[:, 0, :], kk[:, :], -1.0, eprv[:, :], ALU.mult, ALU.mult, ["kk", "eprv"], ["AR0"])
                    TT("dve", AR[:, 1, :], rr[:, :], epos[:, :], ALU.mult, ["rr", "epos"], ["AR1"])
                    TT("pool", BK[:, 0, :], kka[:, :], eneg[:, :], ALU.mult, ["kka", "eneg"], ["BK0"])
                    TT("pool", BK[:, 1, :], kp[:, :], eneg[:, :], ALU.mult, ["kp", "eneg"], ["BK1"])
                    TT("pool", BpKp[:, 0, :], kka[:, :], eend[:, :], ALU.mult, ["kka", "eend"], ["BpKp"])
                    TT("pool", BpKp[:, 1, :], kp[:, :], eend[:, :], ALU.mult, ["kp", "eend"], ["BpKp"])
                    epos3 = epos[:, :].rearrange("p (c t) -> p c t", c=4)
                    for s2 in range(2):
                        sr = slice(s2 * 128, (s2 + 1) * 128)
                        q = sti % 2
                        sti += 1
                        ktm, kE, kYS, kB2, kK2 = ("tm", q), ("E", q), ("YS", q), ("B2", q), ("K2", q)
                        for i, (src, kx) in enumerate([(AR[:, 0, sr], "AR0"), (BpKp[:, 0, sr], "BpKp"), (BpKp[:, 1, sr], "BpKp"), (vv[:, sr], "vv")]):
                            TR(ps[3][:, i * 64:(i + 1) * 64], src, i64, [kx, "cst"], [("ps", 3)])
                        CP("act", tm[:, q, :], ps[3][:, 0:256], [("ps", 3)], [ktm])
                        MM(ps[4][:, 0:256], BK[:, 0, sr], AR[:, :, sr], True, True, ["BK0", "AR0", "AR1"], [("ps", 4)])
                        MM(ps[4][:, 256:384], AR[:, 0, sr], BK[:, 0, sr], True, True, ["BK0", "AR0"], [("ps", 4)])
                        TT("dve", Eb[:, q, 0:384], ps[4][:, 0:384], mE, ALU.mult, [("ps", 4), "cst"], [(kE, "a")])
                        MM(ps[5][:, 0:256], BK[:, 1, sr], AR[:, :, sr], True, True, ["BK1", "AR0", "AR1"], [("ps", 5, "a")])
                        TT("dve", YS[:, q, :], ps[5][:, 0:256], mY, ALU.mult, [("ps", 5, "a"), "cst"], [kYS])
                        MM(ps[5][:, 256:320], YS[:, q, 0:128], tm[:, q, 192:256], True, True, [kYS, ktm], [("ps", 5, "b")])
                        CP("pool", Eb[:, q, 384:448], tm[:, q, 0:64], [ktm], [(kE, "b")])
                        CP("act", Eb[:, q, 448:512], ps[5][:, 256:320], [("ps", 5, "b")], [(kE, "c")])
                        for lev in range(6):
                            if lev == 0:
                                PT_, P_, PZ_, Z_ = Eb[:, q, 0:128], Eb[:, q, 256:384], Eb[:, q, 256:512], Eb[:, q, 384:512]
                                rk = [(kE, "a"), (kE, "b"), (kE, "c")]
                            else:
                                Lp = Lb[:, q, (lev - 1) % 2, :]
                                PT_, P_, PZ_, Z_ = Lp[:, 0:128], Lp[:, 128:256], Lp[:, 128:384], Lp[:, 256:384]
                                rk = [("L", q, (lev - 1) % 2, "p"), ("L", q, (lev - 1) % 2, "z")]
                            Ln = Lb[:, q, lev % 2, :]
                            MM(ps[6][:, 128:384], PT_, PZ_, True, True, rk, [("ps", 6, "a")])
                            TT("dve", Ln[:, 256:384], ps[6][:, 256:384], Z_, ALU.add, [("ps", 6, "a")] + rk, [("L", q, lev % 2, "z")])
                            if lev < 5:
                                MM(ps[6][:, 0:128], P_, PT_, True, True, rk, [("ps", 6, "b")])
                                CP("act", Ln[:, 0:256], ps[6][:, 0:256], [("ps", 6, "a"), ("ps", 6, "b")], [("L", q, lev % 2, "p")])
                        Lf = Lb[:, q, 1, :]
                        kLf = ("L", q, 1, "z")
                        W_, U0_ = Lf[:, 256:320], Lf[:, 320:384]
                        for hf in range(2):
                            TS("pool", B2[:, q, hf * 64:(hf + 1) * 64], tm[:, q, 64:128], rowm[:, hf:hf + 1], None, ALU.mult, None, [ktm, "cst"], [kB2])
                            TS("pool", K2[:, q, hf * 64:(hf + 1) * 64], tm[:, q, 128:192], rowm[:, hf:hf + 1], None, ALU.mult, None, [ktm, "cst"], [kK2])
                        MM(ps[7][0:64, 0:128], W_, B2[:, q, :], True, True, [kLf, kB2], [("ps", 7, "g")])
                        for hf in range(2):
                            c = s2 * 2 + hf
                            STT(GT[:, h, c, :], i64, epos3[:, c, 63:64], ps[7][0:64, hf * 64:(hf + 1) * 64], ALU.mult, ALU.add,
                                [("ps", 7, "g"), "epos", "cst"], [("GT", h)])
                        for hf in range(2):
                            MM(ps[7][0:64, 128 + hf * 64:128 + (hf + 1) * 64], K2[:, q, hf * 64:(hf + 1) * 64], tm[:, q, 192:256], True, False, [kK2, ktm], [("ps", 7, "h")])
                            MM(ps[7][0:64, 128 + hf * 64:128 + (hf + 1) * 64], B2[:, q, hf * 64:(hf + 1) * 64], U0_, False, True, [kB2, kLf], [("ps", 7, "h")])
                        CP("act", Hm[:, h, s2 * 2:s2 * 2 + 2, :], ps[7][0:64, 128:256].rearrange("p (c i) -> p c i", c=2), [("ps", 7, "h")], [("Hm", h)])
                        MM(ps[7][0:64, 256:384], W_, Eb[:, q, 128:256], True, True, [kLf, (kE, "a")], [("ps", 7, "r")])
                        TT("dve", RhatT[:, h, sr], ps[7][0:64, 256:384], AR[:, 1, sr], ALU.add, [("ps", 7, "r"), "AR1"], [("Rhat", h)])
                        MM(ps[7][0:64, 384:512], tm[:, q, 192:256], YS[:, q, 128:256], True, False, [ktm, kYS], [("ps", 7, "y")])
                        MM(ps[7][0:64, 384:512], U0_, Eb[:, q, 128:256], False, True, [kLf, (kE, "a")], [("ps", 7, "y")])
                        CP("act", Y0T[:, h, sr], ps[7][0:64, 384:512], [("ps", 7, "y")], [("Y0T", h)])
                allh = lambda nm: [(nm, h) for h in range(H)]
                for c in range(4):
                    csl = slice(c * 64, (c + 1) * 64)
                    mnew = 1 - mcur
                    for h in range(H):
                        MM(ps[0][0:64, h * 64:(h + 1) * 64], Mst[:, mcur, h, :], RhatT[:, h, csl], True, True, [("Mst", mcur), ("Rhat", h)],
                           [("ps", 0, "a"), ("ps", 0, "b")])
                    for h in range(H):
                        MM(ps[1][0:64, h * 64:(h + 1) * 64], GT[:, h, c, :], Mst[:, mcur, h, :], True, True, [("Mst", mcur), ("GT", h)],
                           [("ps", 1, "a"), ("ps", 1, "b")])
                    TT("dve", Mst[:, mnew, :, :], ps[1][0:64, 0:384].rearrange("p (h i) -> p h i", h=6), Hm[:, :, c, :], ALU.add,
                       [("ps", 1, "a"), ("ps", 1, "b")] + allh("Hm"), [("Mst", mnew)])
                    TT("dve", yT[:, :, csl], ps[0][0:64, 0:384].rearrange("p (h t) -> p h t", h=6), Y0T[:, :, csl], ALU.add,
                       [("ps", 0, "a"), ("ps", 0, "b")] + allh("Y0T"), [("yT", c)])
                    mcur = mnew
                ally = [("yT", c) for c in range(4)]
                for h in range(H):
                    osl = h % 2
                    MM(ps[2][0:64, 0:TW], o64, yT[:, h, :], True, True, ["cst"] + ally, [("ps", 2, "a")])
                    STT(yc[:, :], ps[2][0:64, 0:TW], -1.0 / 64, yT[:, h, :], ALU.mult, ALU.add, [("ps", 2, "a")] + ally, ["yc"])
                    ACT(ysq[:, :], yc[:, :], AF.Square, ["yc"], ["ysq"])
                    MM(ps[2][0:64, 256:512], o64, ysq[:, :], True, True, ["cst", "ysq"], [("ps", 2, "b")])
                    ACT(yrs[:, :], ps[2][0:64, 256:512], AF.Sqrt, [("ps", 2, "b"), "epsc"], ["yrs"], scale=1.0 / 64, bias=epsc[0:64, 1:2])
                    RECIP(yrs[:, :], yrs[:, :], ["yrs"], ["yrs"])
                    TT("dve", yc[:, :], yc[:, :], yrs[:, :], ALU.mult, ["yc", "yrs"], ["yc"])
                    TS("dve", yc[:, :], yc[:, :], plg[:, h:h + 1], plb[:, h:h + 1], ALU.mult, ALU.add, ["yc", "plg", "plb"], ["yc"])
                    TT("dve", yc[:, :], yc[:, :], bon[:, h, :], ALU.add, ["yc", ("bon", h)], ["yc"])
                    TT("dve", obuf[:, osl, :], yc[:, :], gal[:, h, :], ALU.mult, ["yc", ("gal", h)], [("obuf", osl)])
                    DMA(catT[h * 64:(h + 1) * 64, tsl], obuf[:, osl, :], [("obuf", osl)], [("cat", "a", h, tt)], q="pool")
            sc.barrier()
        if stop == "B":
            break

        with ExitStack() as es:
            wO = sb(es, "wO", [128, 2, 8, 128])
            catb = sb(es, "catb", [128, 2, 8, 512])
            mixb = sb(es, "mixb", [128, 8, 512])
            sq = sb(es, "sq", [128, 2, 512])
            rstd = sb(es, "rstd", [128, 2, 512])
            gP = colvec(es, "gP", PR["post_mix_g"][l], 8)
            w_out_l = PR["w_out"][l].rearrange("(k p) c -> p k c", p=128)
            cat_v = catT.rearrange("(k p) t -> p k t", p=128)
            wi = 0
            for tt in range(4):
                tsl = slice(tt * 512, (tt + 1) * 512)
                cs = tt % 2
                DMA(catb[:, cs, :, :], cat_v[:, :, tsl], [], [("catb", cs)])
                for j in range(8):
                    sl = wi % 2
                    wi += 1
                    DMA(wO[:, sl, :, :], w_out_l[:, :, j * 128:(j + 1) * 128], [], [("wO", sl)])
                    bi = j % 2
                    for k in range(8):
                        MM(ps[bi][:, :], wO[:, sl, k, :], catb[:, cs, k, :], k == 0, k == 7, [("wO", sl), ("catb", cs)], [("ps", bi)])
                    CP("act" if j % 2 else "dve", mixb[:, j, :], ps[bi][:, :], [("ps", bi)], [("mixb", j)])
                r = rms_stats(lambda k: mixb[:, k, :], lambda k: ("mixb", k), tt, (sq, rstd), ps[2 + tt % 2], ("ps", 2 + tt % 2))
                for j in range(8):
                    STT(mixb[:, j, :], mixb[:, j, :], gP[:, j:j + 1], r, ALU.mult, ALU.mult, [("mixb", j), ("rstd", tt % 2), "gP"], [("mixb", j)])
                    TT("pool", xT[:, j, tsl], xT[:, j, tsl], mixb[:, j, :], ALU.add, [("mixb", j), ("x", j, tt)], [("x", j, tt)])
            sc.barrier()
        if stop == "E":
            break

        with ExitStack() as es:
            hb = sb(es, "hb", [128, 8, 512])
            actb = sb(es, "actb", [128, NFF, 512])
            wG = sb(es, "wG", [128, 2, 2, 8, 128])
            wD = sb(es, "wD", [128, 2, NFF, 128])
            sq = sb(es, "sq", [128, 2, 512])
            rstd = sb(es, "rstd", [128, 2, 512])
            sil = sb(es, "sil", [128, 2, 512])
            gF = colvec(es, "gF", PR["pre_ffn_g"][l], 8)
            gQ = colvec(es, "gQ", PR["post_ffn_g"][l], 8)
            w_fi = PR["w_ffn_in"][l].rearrange("(k p) c -> p k c", p=128)
            w_fo = PR["w_ffn_out"][l].rearrange("(j p) c -> p j c", p=128)
            wi = 0
            wdi = 0
            for tt in range(4):
                tsl = slice(tt * 512, (tt + 1) * 512)
                r = rms_stats(lambda k: xT[:, k, tsl], lambda k: ("x", k, tt), tt, (sq, rstd), ps[6 + tt % 2], ("ps", 6 + tt % 2))
                for k in range(8):
                    STT(hb[:, k, :], xT[:, k, tsl], gF[:, k:k + 1], r, ALU.mult, ALU.mult, [("x", k, tt), ("rstd", tt % 2), "gF"], [("hb", k)])
                for j in range(NFF):
                    sl = wi % 2
                    wi += 1
                    DMA(wG[:, sl, 0, :, :], w_fi[:, :, j * 128:(j + 1) * 128], [], [("wG", sl, 0)])
                    DMA(wG[:, sl, 1, :, :], w_fi[:, :, DFF + j * 128:DFF + (j + 1) * 128], [], [("wG", sl, 1)])
                    bg, bu = (j % 2) * 2, (j % 2) * 2 + 1
                    for k in range(8):
                        MM(ps[bg][:, :], wG[:, sl, 0, k, :], hb[:, k, :], k == 0, k == 7, [("wG", sl, 0), ("hb", k)], [("ps", bg)])
                    for k in range(8):
                        MM(ps[bu][:, :], wG[:, sl, 1, k, :], hb[:, k, :], k == 0, k == 7, [("wG", sl, 1), ("hb", k)], [("ps", bu)])
                    ACT(sil[:, j % 2, :], ps[bg][:, :], AF.Silu, [("ps", bg)], [("sil", j % 2)])
                    TT("dve", actb[:, j, :], ps[bu][:, :], sil[:, j % 2, :], ALU.mult, [("ps", bu), ("sil", j % 2)], [("actb", j)])
                allact = [("actb", j) for j in range(NFF)]
                for jo in range(8):
                    sl = wdi % 2
                    wdi += 1
                    DMA(wD[:, sl, :, :], w_fo[:, :, jo * 128:(jo + 1) * 128], [], [("wD", sl)])
                    bi = 4 + jo % 2
                    for j in range(NFF):
                        MM(ps[bi][:, :], wD[:, sl, j, :], actb[:, j, :], j == 0, j == NFF - 1, [("wD", sl), ("actb", j)], [("ps", bi)])
                    CP("act" if jo % 2 else "dve", hb[:, jo, :], ps[bi][:, :], [("ps", bi)], [("hb", jo)])
                r = rms_stats(lambda k: hb[:, k, :], lambda k: ("hb", k), tt + 1, (sq, rstd), ps[6 + (tt + 1) % 2], ("ps", 6 + (tt + 1) % 2))
                for j in range(8):
                    STT(hb[:, j, :], hb[:, j, :], gQ[:, j:j + 1], r, ALU.mult, ALU.mult, [("hb", j), ("rstd", (tt + 1) % 2), "gQ"], [("hb", j)])
                    TT("pool", xT[:, j, tsl], xT[:, j, tsl], hb[:, j, :], ALU.add, [("hb", j), ("x", j, tt)], [("x", j, tt)])
            sc.barrier()
        if OPTS.get("xdbg") == l:
            for k in range(8):
                DMA(xdbg[k * 128:(k + 1) * 128, :], xT[:, k, :], [("x", k, tt) for tt in range(4)], [("xdbg", k)])

    if stop in (None, 'setup'):
        with ExitStack() as es:
            yo = sb(es, "yo", [128, 2, D])
            for t in range(OPTS.get('nt', 16)):
                sl = t % 2 if not OPTS.get('sl0') else 0
                for g in range(2):
                    bank = ps[(t * 2 + g) % 4]
                    bk = ("ps", (t * 2 + g) % 4)
                    for kk in range(4):
                        k = g * 4 + kk
                        TR(bank[:, kk * 128:(kk + 1) * 128], xT[:, k, t * 128:(t + 1) * 128], ident, [("x", k, t // 4), "cst"], [bk])
                    CP("act" if g else "dve", yo[:, sl, g * 512:(g + 1) * 512], bank[:, :], [bk], [("yo", sl, g)])
                DMA(y_out[t * 128:(t + 1) * 128, :], yo[:, sl, :], [("yo", sl, 0), ("yo", sl, 1)], [("y", t)])
    sc.barrier()
    sc.emit()
    glob.close()
    return nc


_NC_CACHE = {}


def kernel(**inputs):
    if "nc" not in _NC_CACHE:
        _NC_CACHE["nc"] = build()
    nc = _NC_CACHE["nc"]
    x = np.ascontiguousarray(np.asarray(inputs["x"], dtype=np.float32))
    base = {n: np.ascontiguousarray(np.asarray(inputs[n], dtype=np.float32)) for n in PARAM_NAMES}
    base["cst"] = CONSTS["cst"]
    base["mobac"] = CONSTS["mobac"]
    base["blkind"] = CONSTS["blkind"]
    in_maps = []
    for b in range(8):
        m = dict(base)
        m["x"] = x[b]
        in_maps.append(m)
    res = run_bass_kernel_spmd(nc, in_maps, core_ids=list(range(8)))
    return np.stack([np.asarray(r["y"], dtype=np.float32) for r in res.results], 0)
```

```python
import numpy as np
from contextlib import ExitStack
import concourse.bass as bass
import concourse.mybir as mybir
from concourse.bass_utils import run_bass_kernel_spmd

F32 = mybir.dt.float32
F32R = mybir.dt.float32r
AF = mybir.ActivationFunctionType
ALU = mybir.AluOpType
AX = mybir.AxisListType

S = 2048
D = 1024
L = 2
DFF = 2816
NFF = 22
H = 6
C0 = float(np.exp(-0.5))
BIG = 30000.0


class Sched:
    def __init__(self, nc, n_dma=40):
        self.nc = nc
        self.names = ["pe", "act", "dve", "pool", "sp"]
        self.sem = {e: nc.alloc_semaphore("s_" + e) for e in ["pe", "act", "dve", "pool"]}
        self.cnt = {e: 0 for e in self.sem}
        self.dsem = [nc.alloc_semaphore("d%d" % i) for i in range(n_dma)]
        self.dcnt = [0] * n_dma
        self.drr = 0
        self.q = {e: [] for e in self.names}
        self.waited = {e: {} for e in self.names}
        self.lastw = {}
        self.readers = {}

    def _deps(self, reads, writes):
        deps = {}

        def add(k, v):
            if deps.get(k, 0) < v:
                deps[k] = v

        for r in reads:
            ev = self.lastw.get(r)
            if ev is not None:
                add(*ev)
        for w in writes:
            ev = self.lastw.get(w)
            if ev is not None:
                add(*ev)
            for k, v in self.readers.get(w, {}).items():
                add(k, v)
        return deps

    def _commit(self, ev, reads, writes):
        k, v = ev
        for r in reads:
            d = self.readers.setdefault(r, {})
            if d.get(k, 0) < v:
                d[k] = v
        for w in writes:
            self.lastw[w] = ev
            self.readers[w] = {}

    def _waits(self, eng, deps):
        waits = []
        for k, v in deps.items():
            if eng == "pe" and k == ("e", "pe"):
                continue
            if self.waited[eng].get(k, 0) >= v:
                continue
            self.waited[eng][k] = v
            waits.append((k, v))
        return waits

    def op(self, eng, fn, reads=(), writes=()):
        banks = {("psx", k[1]) for k in list(reads) + list(writes) if isinstance(k, tuple) and k and k[0] == "ps"}
        if banks:
            writes = list(writes) + list(banks)
        deps = self._deps(reads, writes)
        waits = self._waits(eng, deps)
        self.cnt[eng] += 1
        ev = (("e", eng), self.cnt[eng])
        self.q[eng].append((waits, fn, "e"))
        self._commit(ev, reads, writes)

    def dma(self, qeng, out, in_, reads=(), writes=(), **kw):
        deps = self._deps(reads, writes)
        idx = self.drr
        self.drr = (self.drr + 1) % len(self.dsem)
        if self.dcnt[idx] > 0:
            k = ("d", idx)
            deps[k] = max(deps.get(k, 0), self.dcnt[idx])
        waits = self._waits(qeng, deps)
        self.dcnt[idx] += 16
        ev = (("d", idx), self.dcnt[idx])
        self.q[qeng].append((waits, lambda e: e.dma_start(out=out, in_=in_, **kw), idx))
        self._commit(ev, reads, writes)

    def barrier(self):
        allev = [(("e", e), c) for e, c in self.cnt.items() if c > 0]
        allev += [(("d", i), c) for i, c in enumerate(self.dcnt) if c > 0]
        for eng in self.names:
            waits = self._waits(eng, dict(allev))
            if waits:
                self.q[eng].append((waits, None, None))
        self.lastw = {}
        self.readers = {}

    def emit(self):
        nc = self.nc
        engs = {"pe": "tensor", "act": "scalar", "dve": "vector", "pool": "gpsimd", "sp": "sync"}
        with nc.Block() as block:
            for name in self.names:
                def body(eng, name=name):
                    for waits, fn, kind in self.q[name]:
                        for k, v in waits:
                            s = self.sem[k[1]] if k[0] == "e" else self.dsem[k[1]]
                            eng.wait_ge(s, v)
                        if fn is None:
                            continue
                        ins = fn(eng)
                        if kind == "e":
                            ins.then_inc(self.sem[name], 1)
                        else:
                            ins.then_inc(self.dsem[kind], 16)
                getattr(block, engs[name])(body)


def make_consts():
    c = {}
    i128 = np.arange(128)
    blk = (i128[:, None] // 64) == (i128[None, :] // 64)
    ident = np.eye(128, dtype=np.float32)
    ones = np.ones((128, 128), np.float32)
    SL = ((i128[:, None] > i128[None, :]) & blk).astype(np.float32)
    SU = ((i128[:, None] < i128[None, :]) & blk).astype(np.float32)
    IU = ((i128[:, None] <= i128[None, :]) & blk).astype(np.float32)
    IUfull = (i128[:, None] <= i128[None, :]).astype(np.float32)
    idst = np.concatenate([np.eye(64), np.eye(64)], 0).astype(np.float32)
    reset = np.ones((128, 256), np.float32)
    reset[:, ::64] = 0.0
    rowm = np.zeros((128, 2), np.float32)
    rowm[:64, 0] = 1.0
    rowm[64:, 1] = 1.0
    q512 = np.arange(512)
    cm = np.stack([(q512[None, :] >= (j * 128 + i128[:, None])).astype(np.float32) for j in range(4)], 1)
    parts = [ident, ones, SU, IU, SL, SU, IU, IUfull, idst, reset, rowm, cm.reshape(128, 2048)]
    offs = {}
    o = 0
    for nm, p in zip(["ident", "ones", "mE", "_1", "_2", "mY", "_3", "iuf", "idst", "reset", "rowm", "cm"], parts):
        offs[nm] = o
        o += p.shape[1]
    c["cst"] = np.ascontiguousarray(np.concatenate(parts, 1))
    c["offs"] = offs
    mb = np.zeros((128, 3, 16, 6, 8), np.float32)
    for t in range(16):
        b = t // 2
        for n in range(8):
            mb[:, 0, t, :, n] = 0.0 if n < b else -1e30
            mb[:, 1, t, :, n] = 1.0 if n < b else 0.0
            mb[:, 2, t, :, n] = 1.0 if n == b else 0.0
    c["mobac"] = mb.reshape(128, 3 * 16 * 48)
    bi = np.zeros((8, S), np.float32)
    for n in range(8):
        bi[n, n * 256:(n + 1) * 256] = 1.0
    c["blkind"] = bi
    return c


CONSTS = make_consts()
PARAM_NAMES = ["pre_mix_g", "w_in", "rwkv_mu", "rwkv_w0", "rwkv_w2", "rwkv_a0", "rwkv_a2", "rwkv_g2",
               "rwkv_k_k", "rwkv_k_a", "rwkv_r_k", "rwkv_lnx_g", "rwkv_lnx_b", "gmlp_ln_g", "gmlp_ln_b",
               "gmlp_w_s", "gmlp_b_s", "w_out", "post_mix_g", "pre_ffn_g", "w_ffn_in", "w_ffn_out", "post_ffn_g"]
PARAM_SHAPES = {"pre_mix_g": (L, D), "w_in": (L, D, 3072), "rwkv_mu": (L, 1408), "rwkv_w0": (L, 384),
                "rwkv_w2": (L, 64, 384), "rwkv_a0": (L, 384), "rwkv_a2": (L, 64, 384), "rwkv_g2": (L, 128, 384),
                "rwkv_k_k": (L, 384), "rwkv_k_a": (L, 384), "rwkv_r_k": (L, 6, 64), "rwkv_lnx_g": (L, 384),
                "rwkv_lnx_b": (L, 384), "gmlp_ln_g": (L, 256), "gmlp_ln_b": (L, 256), "gmlp_w_s": (L, 4, 128, 128),
                "gmlp_b_s": (L, 4, 128), "w_out": (L, D, D), "post_mix_g": (L, D), "pre_ffn_g": (L, D),
                "w_ffn_in": (L, D, 2 * DFF), "w_ffn_out": (L, DFF, D), "post_ffn_g": (L, D)}


OPTS = {}


def build(dbg=None, nlayers=L, stop=None):
    dbg = dbg or []
    nc = bass.Bass("TRN2", target_bir_lowering=False)
    sc = Sched(nc)
    OF = CONSTS["offs"]

    def dram(name, shape, kind="Internal"):
        if name in dbg:
            kind = "ExternalOutput"
        return nc.dram_tensor(name, list(shape), F32, kind=kind).ap()

    x_in = dram("x", [S, D], "ExternalInput")
    y_out = dram("y", [S, D], "ExternalOutput")
    cst_d = dram("cst", CONSTS["cst"].shape, "ExternalInput")
    if not OPTS.get("noparams"):
        PR = {n: dram(n, PARAM_SHAPES[n], "ExternalInput") for n in PARAM_NAMES}
        mobac_d = dram("mobac", CONSTS["mobac"].shape, "ExternalInput")
        blkind_d = dram("blkind", CONSTS["blkind"].shape, "ExternalInput")
    if OPTS.get("noscratch"):
        glob_scr = None
    rwkvT = dram("rwkvT", [1408, S]) if not OPTS.get("noscratch") else None
    qkT = dram("qkT", [768, S]) if not OPTS.get("noscratch") else None
    uT = dram("uT", [256, S]) if not OPTS.get("noscratch") else None
    vm_tm = dram("vm_tm", [S, 384]) if not OPTS.get("noscratch") else None
    vg_tm = dram("vg_tm", [S, 256]) if not OPTS.get("noscratch") else None
    catT = dram("catT", [D, S]) if not OPTS.get("noscratch") else None
    xdbg = dram("xdbg", [D, S]) if not OPTS.get("noscratch") else None

    uid = [0]

    def sb(es, name, shape):
        uid[0] += 1
        return es.enter_context(nc.sbuf_tensor("%s_%d" % (name, uid[0]), list(shape), F32))

    glob = ExitStack()
    xT = sb(glob, "xT", [128, 8, S])
    cst = sb(glob, "cst_sb", [128, CONSTS["cst"].shape[1]])
    ps = [glob.enter_context(nc.psum_tensor("ps%d" % i, [128, 512], F32)) for i in range(8)]
    ident = cst[:, OF["ident"]:OF["ident"] + 128]
    ones = cst[:, OF["ones"]:OF["ones"] + 128]
    mE = cst[:, OF["mE"]:OF["mE"] + 384]
    mY = cst[:, OF["mY"]:OF["mY"] + 256]
    iuf = cst[:, OF["iuf"]:OF["iuf"] + 128]
    idst = cst[:, OF["idst"]:OF["idst"] + 64]
    resetm = cst[:, OF["reset"]:OF["reset"] + 256]
    cm = cst[:, OF["cm"]:OF["cm"] + 2048]
    epsc = sb(glob, "epsc", [128, 4])

    def ACT(out, in_, func, reads, writes, **kw):
        sc.op("act", lambda e: e.activation(out=out, in_=in_, func=func, **kw), reads, writes)

    def RR(ap):
        return ap if OPTS.get("nor") else ap.bitcast(F32R)

    def MM(out, lhsT, rhs, start, stop, reads, writes, r=False):
        if r and not OPTS.get("nor"):
            lhsT = lhsT.bitcast(F32R)
            rhs = rhs.bitcast(F32R)
        sc.op("pe", lambda e: e.matmul(out, lhsT=lhsT, rhs=rhs, start=start, stop=stop), reads, writes)

    def TR(out, in_, idn, reads, writes):
        sc.op("pe", lambda e: e.transpose(out, in_, idn), reads, writes)

    def TT(eng, out, in0, in1, op, reads, writes):
        sc.op(eng, lambda e: e.tensor_tensor(out=out, in0=in0, in1=in1, op=op), reads, writes)

    def TS(eng, out, in0, s1, s2, op0, op1, reads, writes):
        if s2 is None:
            sc.op(eng, lambda e: e.tensor_scalar(out=out, in0=in0, scalar1=s1, scalar2=None, op0=op0), reads, writes)
        else:
            sc.op(eng, lambda e: e.tensor_scalar(out=out, in0=in0, scalar1=s1, scalar2=s2, op0=op0, op1=op1), reads, writes)

    def STT(out, in0, scalar, in1, op0, op1, reads, writes):
        sc.op("dve", lambda e: e.scalar_tensor_tensor(out=out, in0=in0, scalar=scalar, in1=in1, op0=op0, op1=op1), reads, writes)

    def CP(eng, out, in_, reads, writes):
        if eng == "act":
            sc.op("act", lambda e: e.copy(out=out, in_=in_), reads, writes)
        else:
            sc.op(eng, lambda e: e.tensor_copy(out=out, in_=in_), reads, writes)

    def RECIP(out, in_, reads, writes):
        sc.op("dve", lambda e: e.reciprocal(out=out, in_=in_), reads, writes)

    def DMA(out, in_, reads, writes, q="sp", **kw):
        sc.dma(q, out, in_, reads, writes, **kw)

    def colvec(es, name, src_1d, ncol, p=128):
        t = sb(es, name, [p, ncol])
        DMA(t[:, :], src_1d.rearrange("(c p) -> p c", p=p), [], [name], allow_slow_non_contiguous=True)
        return t

    DMA(cst[:, :], cst_d[:, :], [], ["cst"])
    sc.op("dve", lambda e: e.memset(epsc[:, 0:1], 1e-6), [], ["epsc"])
    sc.op("dve", lambda e: e.memset(epsc[:, 1:2], 64e-5), [], ["epsc"])
    sc.op("dve", lambda e: e.memset(epsc[:, 2:3], 0.0), [], ["epsc"])
    with ExitStack() as es:
        xin = sb(es, "xin", [128, 2, D])
        for t in range(OPTS.get('nt', 16)):
            sl = t % 2 if not OPTS.get('sl0') else 0
            DMA(xin[:, sl, :], x_in[t * 128:(t + 1) * 128, :], [], [("xin", sl)])
            for g in range(2):
                bank = ps[(t * 2 + g) % OPTS.get('nb', 4)]
                bk = ("ps", (t * 2 + g) % OPTS.get('nb', 4))
                for kk in range(4):
                    k = g * 4 + kk
                    TR(bank[:, kk * 128:(kk + 1) * 128], xin[:, sl, k * 128:(k + 1) * 128], ident, [("xin", sl), "cst"], [bk])
                CP("act" if g else "dve", xT[:, g * 4:(g + 1) * 4, t * 128:(t + 1) * 128],
                   bank[:, :].rearrange("p (k c) -> p k c", k=4), [bk], [("x", g * 4 + kk, t // 4) for kk in range(4)])
        if not OPTS.get("nobar"):
            sc.barrier()

    def rms_stats(src_fn, src_keys, tt, es_tiles, pbank, pkey):
        sq, rstd = es_tiles
        for k in range(8):
            ACT(sq[:, k % 2, :], src_fn(k), AF.Square, [src_keys(k)], [("sq", k % 2)])
            MM(pbank[:, :], ones, sq[:, k % 2, :], k == 0, k == 7, [("sq", k % 2), "cst"], [pkey])
        ACT(rstd[:, tt % 2, :], pbank[:, :], AF.Sqrt, [pkey, "epsc"], [("rstd", tt % 2)], scale=1.0 / D, bias=epsc[:, 0:1])
        RECIP(rstd[:, tt % 2, :], rstd[:, tt % 2, :], [("rstd", tt % 2)], [("rstd", tt % 2)])
        return rstd[:, tt % 2, :]

    for l in range(nlayers if stop != 'setup' else 0):
        with ExitStack() as es:
            hbuf = sb(es, "hbuf", [128, 8, S])
            sq = sb(es, "sq", [128, 2, 512])
            rstd = sb(es, "rstd", [128, 2, 512])
            gA = colvec(es, "gA", PR["pre_mix_g"][l], 8)
            muA = colvec(es, "muA", PR["rwkv_mu"][l], 11)
            for tt in range(4):
                tsl = slice(tt * 512, (tt + 1) * 512)
                r = rms_stats(lambda k: xT[:, k, tsl], lambda k: ("x", k, tt), tt, (sq, rstd), ps[4 + tt % 2], ("ps", 4 + tt % 2))
                for k in range(8):
                    STT(RR(hbuf[:, k, tsl]), xT[:, k, tsl], gA[:, k:k + 1], r, ALU.mult, ALU.mult,
                        [("x", k, tt), ("rstd", tt % 2), "gA"], [("h", k, tt)])
            es_main = es
            es = ExitStack()
            wA = sb(es, "wA", [128, 2, 8, 128])
            stg = sb(es, "stg", [128, 2, S])
            stg2 = sb(es, "stg2", [128, 2, S])
            w_in_l = PR["w_in"][l].rearrange("(k p) c -> p k c", p=128)
            fm_chunks = [(c * 128, rwkvT, c * 128, True) for c in range(11)]
            fm_chunks += [(1408 + c * 128, qkT, c * 128, False) for c in range(6)]
            fm_chunks += [(2560 + c * 128, uT, c * 128, False) for c in range(2)]
            for ci, (col0, dst, row0, shift) in enumerate(fm_chunks):
                sl = ci % 2
                DMA(RR(wA[:, sl, :, :]), w_in_l[:, :, col0:col0 + 128], [], [("wA", sl)], q="pool")
                for tt in range(4):
                    tsl = slice(tt * 512, (tt + 1) * 512)
                    bi = (ci * 4 + tt) % 4
                    for k in range(8):
                        MM(ps[bi][:, :], wA[:, sl, k, :], hbuf[:, k, tsl], k == 0, k == 7,
                           [("wA", sl), ("h", k, tt)], [("ps", bi)], r=True)
                    CP("act" if tt % 2 else "dve", stg[:, sl, tsl], ps[bi][:, :], [("ps", bi)], [("stg", sl, tt)])
                allst = [("stg", sl, tt) for tt in range(4)]
                if shift:
                    TT("pool", stg2[:, sl, 1:S], stg[:, sl, 0:S - 1], stg[:, sl, 1:S], ALU.subtract, allst, [("stg2", sl)])
                    TS("pool", stg2[:, sl, 0:1], stg[:, sl, 0:1], -1.0, None, ALU.mult, None, allst, [("stg2", sl)])
                    STT(stg2[:, sl, :], stg2[:, sl, :], muA[:, ci:ci + 1], stg[:, sl, :], ALU.mult, ALU.add,
                        allst + [("stg2", sl), "muA"], [("stg2", sl)])
                    DMA(dst[row0:row0 + 128, :], stg2[:, sl, :], [("stg2", sl)], [("dr", id(dst), row0)], q="pool")
                else:
                    DMA(dst[row0:row0 + 128, :], stg[:, sl, :], allst, [("dr", id(dst), row0)], q="pool")
            sc.barrier()
            es.close()
            es = es_main
            wB = sb(es, "wB", [128, 8, 640])
            DMA(RR(wB[:, :, 0:384]), w_in_l[:, :, 2176:2560], [], ["wB"], q="pool")
            DMA(RR(wB[:, :, 384:640]), w_in_l[:, :, 2816:3072], [], ["wB"], q="pool")
            vst = sb(es, "vst", [128, 2, 640])
            for t in range(16):
                sl = t % 2
                b0, b1 = 4 + (t % 2) * 2, 5 + (t % 2) * 2
                for k in range(8):
                    MM(ps[b0][:, 0:384], hbuf[:, k, t * 128:(t + 1) * 128], wB[:, k, 0:384], k == 0, k == 7,
                       [("h", k, t // 4), "wB"], [("ps", b0)], r=True)
                for k in range(8):
                    MM(ps[b1][:, 0:256], hbuf[:, k, t * 128:(t + 1) * 128], wB[:, k, 384:640], k == 0, k == 7,
                       [("h", k, t // 4), "wB"], [("ps", b1)], r=True)
                CP("act", vst[:, sl, 0:384], ps[b0][:, 0:384], [("ps", b0)], [("vst", sl, 0)])
                CP("dve", vst[:, sl, 384:640], ps[b1][:, 0:256], [("ps", b1)], [("vst", sl, 1)])
                DMA(vm_tm[t * 128:(t + 1) * 128, :], vst[:, sl, 0:384], [("vst", sl, 0)], [("vm", t)], q="pool")
                DMA(vg_tm[t * 128:(t + 1) * 128, :], vst[:, sl, 384:640], [("vst", sl, 1)], [("vg", t)], q="pool")
            sc.barrier()
        if stop == "A":
            break

        with ExitStack() as es:
            lng = sb(es, "lng", [128, 256])
            lnb = sb(es, "lnb", [128, 256])
            DMA(lng[:, :], PR["gmlp_ln_g"][l].partition_broadcast(128), [], ["lng"])
            DMA(lnb[:, :], PR["gmlp_ln_b"][l].partition_broadcast(128), [], ["lnb"])
            wsn = sb(es, "wsn", [128, 4, 128])
            wsT = sb(es, "wsT", [128, 4, 128])
            bsr = sb(es, "bsr", [1, 512])
            DMA(wsn[:, :, :], PR["gmlp_w_s"][l].rearrange("g t s -> t g s"), [], ["wsn"])
            DMA(bsr[:, :], PR["gmlp_b_s"][l].rearrange("g t -> (g t)").partition_broadcast(1), [], ["bsr"])
            for g in range(4):
                TR(ps[0][:, g * 128:(g + 1) * 128], wsn[:, g, :], ident, ["wsn", "cst"], [("ps", 0)])
            for g in range(4):
                TT("dve", wsT[:, g, :], ps[0][:, g * 128:(g + 1) * 128], iuf, ALU.mult, [("ps", 0), "cst"], ["wsT"])
            gu = sb(es, "gu", [128, 2, S])
            t1 = sb(es, "t1", [128, S])
            cout = sb(es, "cout", [128, 2, S])
            for pp in range(2):
                DMA(gu[:, pp, :], uT[pp * 128:(pp + 1) * 128, :], [], [("gu", pp)])
                ACT(t1[:, :], gu[:, pp, :], AF.Square, [("gu", pp)], ["t1"])
                TS("pool", t1[:, :], t1[:, :], 0.044715, 1.0, ALU.mult, ALU.add, ["t1"], ["t1"])
                TT("dve", t1[:, :], t1[:, :], gu[:, pp, :], ALU.mult, ["t1", ("gu", pp)], ["t1"])
                ACT(t1[:, :], t1[:, :], AF.Sigmoid, ["t1"], ["t1"], scale=2.0 * 0.7978845608028654)
                TT("dve", gu[:, pp, :], gu[:, pp, :], t1[:, :], ALU.mult, ["t1", ("gu", pp)], [("gu", pp)])
            vb = sb(es, "vb", [128, 2, 256])
            t2 = sb(es, "t2", [128, 2, 256])
            st6 = sb(es, "st6", [128, 2, 8])
            for c in range(16):
                sl = c % 2
                DMA(vb[:, sl, :], vg_tm[c * 128:(c + 1) * 128, :], [], [("vb", sl)])
                kv, kt = ("vb", sl), ("t2", sl)
                ACT(t2[:, sl, :], vb[:, sl, :], AF.Square, [kv], [kt])
                TS("pool", t2[:, sl, :], t2[:, sl, :], 0.044715, 1.0, ALU.mult, ALU.add, [kt], [kt])
                TT("dve", t2[:, sl, :], t2[:, sl, :], vb[:, sl, :], ALU.mult, [kt, kv], [kt])
                ACT(t2[:, sl, :], t2[:, sl, :], AF.Sigmoid, [kt], [kt], scale=2.0 * 0.7978845608028654)
                TT("dve", vb[:, sl, :], vb[:, sl, :], t2[:, sl, :], ALU.mult, [kt, kv], [kv])
                ks = ("st6", sl)
                sc.op("dve", lambda e, sl=sl: e.bn_stats(out=st6[:, sl, 0:6], in_=vb[:, sl, :]), [kv], [ks])
                sc.op("dve", lambda e, sl=sl: e.bn_aggr(out=st6[:, sl, 6:8], in_=st6[:, sl, 0:6]), [ks], [ks])
                ACT(st6[:, sl, 7:8], st6[:, sl, 7:8], AF.Sqrt, [ks, "epsc"], [ks], bias=epsc[:, 0:1], scale=1.0)
                RECIP(st6[:, sl, 7:8], st6[:, sl, 7:8], [ks], [ks])
                TS("dve", vb[:, sl, :], vb[:, sl, :], st6[:, sl, 6:7], st6[:, sl, 7:8], ALU.subtract, ALU.mult, [kv, ks], [kv])
                TT("dve", vb[:, sl, :], vb[:, sl, :], lng[:, :], ALU.mult, [kv, "lng"], [kv])
                TT("dve", vb[:, sl, :], vb[:, sl, :], lnb[:, :], ALU.add, [kv, "lnb"], [kv])
                for pp in range(2):
                    bi = 1 + (c % 2) * 2 + pp
                    for gg in range(2):
                        g = pp * 2 + gg
                        MM(ps[bi][:, gg * 128:(gg + 1) * 128], vb[:, sl, pp * 128:(pp + 1) * 128], wsT[:, g, :], True, False, [kv, "wsT"], [("ps", bi)])
                        MM(ps[bi][:, gg * 128:(gg + 1) * 128], ones[0:1, :], bsr[0:1, g * 128:(g + 1) * 128], False, True, ["cst", "bsr"], [("ps", bi)])
                    for gg in range(2):
                        TT("dve", cout[gg * 64:(gg + 1) * 64, pp, c * 128:(c + 1) * 128], ps[bi][gg * 64:(gg + 1) * 64, gg * 128:(gg + 1) * 128],
                           gu[gg * 64:(gg + 1) * 64, pp, c * 128:(c + 1) * 128], ALU.mult, [("ps", bi), ("gu", pp)], [("cout", pp)])
            for pp in range(2):
                DMA(catT[768 + pp * 128:768 + (pp + 1) * 128, :], cout[:, pp, :], [("cout", pp)], [("cat", 6 + pp)], q="pool")
            sc.barrier()
        if stop == "D":
            break

        with ExitStack() as es:
            qa = sb(es, "qa", [72, S])
            ka = sb(es, "ka", [72, S])
            vt = sb(es, "vt", [128, 16, 64])
            kbar = sb(es, "kbar", [64, 8])
            mobc = sb(es, "mobc", [128, 3, 16, 48])
            NP = sb(es, "NP", [128, 16, 72])
            sm = sb(es, "sm", [128, 2, 8])
            top8 = sb(es, "top8", [128, 2, 8])
            al = sb(es, "al", [128, 2, 8])
            pt = sb(es, "pt", [128, 3, 512])
            rden = sb(es, "rden", [64, 512])
            ob = sb(es, "ob", [64, 2, 512])
            DMA(mobc[:, :, :, :], mobac_d.rearrange("p (a t c) -> p a t c", a=3, t=16), [], ["mobc"])
            DMA(ka[64:72, :], blkind_d[:, :], [], ["kaB"])
            sc.op("pool", lambda e: e.memset(NP[:, :, :], 0.0), [], ["NP"])
            pti = 0
            for h in range(H):
                DMA(qa[0:64, :], qkT[h * 64:(h + 1) * 64, :], [], ["qaQ"])
                DMA(ka[0:64, :], qkT[384 + h * 64:384 + (h + 1) * 64, :], [], ["kaK"])
                DMA(vt[:, :, :], vm_tm.rearrange("(t p) c -> p t c", p=128)[:, :, h * 64:(h + 1) * 64], [], ["vt"])
                sc.op("dve", lambda e: e.tensor_reduce(out=kbar[:, :], in_=ka[0:64, :].rearrange("p (n k) -> p n k", n=8), axis=AX.X, op=ALU.add), ["kaK"], ["kbar"])
                for t in range(16):
                    MM(ps[0][:, t * 8:(t + 1) * 8], qa[0:64, t * 128:(t + 1) * 128], kbar[:, :], True, True, ["qaQ", "kbar"], [("ps", 0)])
                for t in range(16):
                    sl = t % 2
                    hs = slice(h * 8, (h + 1) * 8)
                    TT("dve", sm[:, sl, :], ps[0][:, t * 8:(t + 1) * 8], mobc[:, 0, t, hs], ALU.add, [("ps", 0), "mobc"], [("sm", sl)])
                    sc.op("dve", lambda e, sl=sl: e.max(out=top8[:, sl, :], in_=sm[:, sl, :]), [("sm", sl)], [("top8", sl)])
                    TS("dve", al[:, sl, :], sm[:, sl, :], top8[:, sl, 2:3], None, ALU.is_ge, None, [("sm", sl), ("top8", sl)], [("al", sl)])
                    TT("dve", al[:, sl, :], al[:, sl, :], mobc[:, 1, t, hs], ALU.mult, [("al", sl), "mobc"], [("al", sl)])
                    TT("dve", al[:, sl, :], al[:, sl, :], mobc[:, 2, t, hs], ALU.add, [("al", sl), "mobc"], [("al", sl)])
                    TS("dve", NP[:, t, 64:72], al[:, sl, :], -1.0, BIG, ALU.add, ALU.mult, [("al", sl)], ["NP"])
                for t4 in range(4):
                    for tq in range(4):
                        t = t4 * 4 + tq
                        MM(ps[1][0:72, tq * 128:(tq + 1) * 128], NP[:, t, :], ident, True, True, ["NP", "cst"], [("ps", 1)])
                    CP("act", qa[64:72, t4 * 512:(t4 + 1) * 512], ps[1][64:72, :], [("ps", 1)], ["qaM"])
                for qt in range(4):
                    qsl = slice(qt * 512, (qt + 1) * 512)
                    nk = (qt + 1) * 4
                    osl = qt % 2
                    for kt in range(nk):
                        sb_i = 2 + kt % 2
                        pi = pti % 3
                        pti += 1
                        MM(ps[sb_i][:, :], ka[0:72, kt * 128:(kt + 1) * 128], qa[0:72, qsl], True, True,
                           ["kaK", "kaB", "qaQ", "qaM"], [("ps", sb_i)])
                        ACT(pt[:, pi, :], ps[sb_i][:, :], AF.Exp, [("ps", sb_i)], [("pt", pi)], scale=0.125)
                        if kt >= qt * 4:
                            j = kt - qt * 4
                            TT("pool", pt[:, pi, :], pt[:, pi, :], cm[:, j * 512:(j + 1) * 512], ALU.mult, [("pt", pi), "cst"], [("pt", pi)])
                        MM(ps[4][0:64, :], vt[:, kt, :], pt[:, pi, :], kt == 0, kt == nk - 1, ["vt", ("pt", pi)], [("ps", 4)])
                        MM(ps[5][:, :], ones, pt[:, pi, :], kt == 0, kt == nk - 1, ["cst", ("pt", pi)], [("ps", 5)])
                    RECIP(rden[:, :], ps[5][0:64, :], [("ps", 5)], ["rden"])
                    TT("dve", ob[:, osl, :], ps[4][0:64, :], rden[:, :], ALU.mult, [("ps", 4), "rden"], [("ob", osl)])
                    DMA(catT[384 + h * 64:384 + (h + 1) * 64, qsl], ob[:, osl, :], [("ob", osl)], [("cat", "b", h, qt)], q="pool")
            sc.barrier()
        if stop == "C":
            break

        with ExitStack() as es:
            TW = 256
            w2s = sb(es, "w2s", [64, 384])
            a2s = sb(es, "a2s", [64, 384])
            g2s = sb(es, "g2s", [128, 384])
            DMA(w2s[:, :], PR["rwkv_w2"][l], [], ["w2s"])
            DMA(a2s[:, :], PR["rwkv_a2"][l], [], ["a2s"])
            DMA(g2s[:, :], PR["rwkv_g2"][l], [], ["g2s"])
            pw0 = colvec(es, "pw0", PR["rwkv_w0"][l], 6, p=64)
            pa0 = colvec(es, "pa0", PR["rwkv_a0"][l], 6, p=64)
            pkk = colvec(es, "pkk", PR["rwkv_k_k"][l], 6, p=64)
            pka = colvec(es, "pka", PR["rwkv_k_a"][l], 6, p=64)
            prk = colvec(es, "prk", PR["rwkv_r_k"][l].rearrange("h d -> (h d)"), 6, p=64)
            plg = colvec(es, "plg", PR["rwkv_lnx_g"][l], 6, p=64)
            plb = colvec(es, "plb", PR["rwkv_lnx_b"][l], 6, p=64)
            pok = sb(es, "pok", [64, 6])
            TS("dve", pok[:, :], pka[:, :], -1.0, 1.0, ALU.mult, ALU.add, ["pka"], ["pok"])
            i64 = ident[0:64, 0:64]
            o64 = ones[0:64, 0:64]
            rowm = cst[:, OF["rowm"]:OF["rowm"] + 2]
            Mst = sb(es, "Mst", [64, 2, 6, 64])
            sc.op("dve", lambda e: e.memset(Mst[:, 0, :, :], 0.0), [], [("Mst", 0)])
            mcur = 0
            RhatT = sb(es, "RhatT", [64, 6, TW])
            Y0T = sb(es, "Y0T", [64, 6, TW])
            yT = sb(es, "yT", [64, 6, TW])
            GT = sb(es, "GT", [64, 6, 4, 64])
            Hm = sb(es, "Hm", [64, 6, 4, 64])
            bon = sb(es, "bon", [64, 6, TW])
            gal = sb(es, "gal", [64, 6, TW])
            wd = sb(es, "wd", [64, TW]); ad = sb(es, "ad", [64, TW]); gd = sb(es, "gd", [128, TW])
            rr = sb(es, "rr", [64, TW]); kq = sb(es, "kq", [64, TW]); vv = sb(es, "vv", [64, TW])
            sig = sb(es, "sig", [64, TW]); cum = sb(es, "cum", [64, TW]); cpv = sb(es, "cpv", [64, TW])
            epos = sb(es, "epos", [64, TW]); eneg = sb(es, "eneg", [64, TW]); eprv = sb(es, "eprv", [64, TW]); eend = sb(es, "eend", [64, TW])
            nbc = sb(es, "nbc", [64, 4])
            aa = sb(es, "aa", [64, TW]); kk = sb(es, "kk", [64, TW]); kk2 = sb(es, "kk2", [64, TW]); rn = sb(es, "rn", [64, TW])
            kka = sb(es, "kka", [64, TW]); fac = sb(es, "fac", [64, TW]); kp = sb(es, "kp", [64, TW]); rkr = sb(es, "rkr", [64, TW])
            AR = sb(es, "AR", [64, 2, TW]); BK = sb(es, "BK", [64, 2, TW]); BpKp = sb(es, "BpKp", [64, 2, TW])
            tm = sb(es, "tm", [128, 2, 256]); Eb = sb(es, "Eb", [128, 2, 512]); YS = sb(es, "YS", [128, 2, 256])
            Lb = sb(es, "Lb", [128, 2, 2, 384]); B2 = sb(es, "B2", [128, 2, 128]); K2 = sb(es, "K2", [128, 2, 128])
            yc = sb(es, "yc", [64, TW]); ysq = sb(es, "ysq", [64, TW]); yrs = sb(es, "yrs", [64, TW])
            obuf = sb(es, "obuf", [64, 2, TW])
            sti = 0
            for tt in range(S // TW):
                tsl = slice(tt * TW, (tt + 1) * TW)
                DMA(wd[:, :], rwkvT[1152:1216, tsl], [], ["wd"])
                DMA(ad[:, :], rwkvT[1216:1280, tsl], [], ["ad"])
                DMA(gd[:, :], rwkvT[1280:1408, tsl], [], ["gd"])
                ACT(wd[:, :], wd[:, :], AF.Tanh, ["wd"], ["wd"])
                ACT(gd[:, :], gd[:, :], AF.Sigmoid, ["gd"], ["gd"])
                for h in range(H):
                    hc = slice(h * 64, (h + 1) * 64)
                    DMA(rr[:, :], rwkvT[h * 64:(h + 1) * 64, tsl], [], ["rr"])
                    DMA(kq[:, :], rwkvT[384 + h * 64:384 + (h + 1) * 64, tsl], [], ["kq"])
                    DMA(vv[:, :], rwkvT[768 + h * 64:768 + (h + 1) * 64, tsl], [], ["vv"])
                    MM(ps[0][0:64, 0:TW], w2s[:, hc], wd[:, :], True, True, ["w2s", "wd"], [("ps", 0, "a")])
                    ACT(sig[:, :], ps[0][0:64, 0:TW], AF.Sigmoid, [("ps", 0, "a"), "pw0"], ["sig"], bias=pw0[:, h:h + 1])
                    sc.op("dve", lambda e: e.tensor_tensor_scan(out=cum[:, :], data0=resetm[0:64, 0:TW], data1=sig[:, :], initial=0.0, op0=ALU.mult, op1=ALU.add), ["sig", "cst"], ["cum"])
                    TT("dve", cpv[:, :], cum[:, :], sig[:, :], ALU.subtract, ["cum", "sig"], ["cpv"])
                    ACT(epos[:, :], cum[:, :], AF.Exp, ["cum"], ["epos"], scale=-C0)
                    ACT(eneg[:, :], cum[:, :], AF.Exp, ["cum"], ["eneg"], scale=C0)
                    ACT(eprv[:, :], cpv[:, :], AF.Exp, ["cpv"], ["eprv"], scale=-C0)
                    cum3 = cum[:, :].rearrange("p (c t) -> p c t", c=4)
                    TS("dve", nbc[:, :], cum3[:, :, 63], -C0, None, ALU.mult, None, ["cum"], ["nbc"])
                    for c in range(4):
                        ACT(eend[:, c * 64:(c + 1) * 64], cum[:, c * 64:(c + 1) * 64], AF.Exp, ["cum", "nbc"], ["eend"], scale=C0, bias=nbc[:, c:c + 1])
                    MM(ps[0][0:64, 256:512], a2s[:, hc], ad[:, :], True, True, ["a2s", "ad"], [("ps", 0, "b")])
                    ACT(aa[:, :], ps[0][0:64, 256:512], AF.Sigmoid, [("ps", 0, "b"), "pa0"], ["aa"], bias=pa0[:, h:h + 1])
                    MM(ps[1][0:64, 0:TW], g2s[:, hc], gd[:, :], True, True, ["g2s", "gd"], [("ps", 1, "a")])
                    CP("act", gal[:, h, :], ps[1][0:64, 0:TW], [("ps", 1, "a")], [("gal", h)])
                    TS("dve", kk[:, :], kq[:, :], pkk[:, h:h + 1], None, ALU.mult, None, ["kq", "pkk"], ["kk"])
                    ACT(kk2[:, :], kk[:, :], AF.Square, ["kk"], ["kk2"])
                    MM(ps[1][0:64, 256:512], o64, kk2[:, :], True, True, ["cst", "kk2"], [("ps", 1, "b")])
                    ACT(rn[:, :], ps[1][0:64, 256:512], AF.Sqrt, [("ps", 1, "b")], ["rn"])
                    TS("dve", rn[:, :], rn[:, :], 1e-12, None, ALU.max, None, ["rn"], ["rn"])
                    RECIP(rn[:, :], rn[:, :], ["rn"], ["rn"])
                    TT("dve", kk[:, :], kk[:, :], rn[:, :], ALU.mult, ["kk", "rn"], ["kk"])
                    TT("dve", kka[:, :], kk[:, :], aa[:, :], ALU.mult, ["kk", "aa"], ["kka"])
                    TS("dve", fac[:, :], aa[:, :], pka[:, h:h + 1], pok[:, h:h + 1], ALU.mult, ALU.add, ["aa", "pka", "pok"], ["fac"])
                    TT("dve", kp[:, :], kq[:, :], fac[:, :], ALU.mult, ["kq", "fac"], ["kp"])
                    STT(rkr[:, :], rr[:, :], prk[:, h:h + 1], kp[:, :], ALU.mult, ALU.mult, ["rr", "prk", "kp"], ["rkr"])
                    MM(ps[2][0:64, 0:TW], o64, rkr[:, :], True, True, ["cst", "rkr"], [("ps", 2, "a")])
                    TT("dve", bon[:, h, :], ps[2][0:64, 0:TW], vv[:, :], ALU.mult, [("ps", 2, "a"), "vv"], [("bon", h)])
                    STT(AR[:, 0, :], kk[:, :], -1.0, eprv[:, :], ALU.mult, ALU.mult, ["kk", "eprv"], ["AR0"])
                    TT("dve", AR[:, 1, :], rr[:, :], epos[:, :], ALU.mult, ["rr", "epos"], ["AR1"])
                    TT("pool", BK[:, 0, :], kka[:, :], eneg[:, :], ALU.mult, ["kka", "eneg"], ["BK0"])
                    TT("pool", BK[:, 1, :], kp[:, :], eneg[:, :], ALU.mult, ["kp", "eneg"], ["BK1"])
                    TT("pool", BpKp[:, 0, :], kka[:, :], eend[:, :], ALU.mult, ["kka", "eend"], ["BpKp"])
                    TT("pool", BpKp[:, 1, :], kp[:, :], eend[:, :], ALU.mult, ["kp", "eend"], ["BpKp"])
                    epos3 = epos[:, :].rearrange("p (c t) -> p c t", c=4)
                    for s2 in range(2):
                        sr = slice(s2 * 128, (s2 + 1) * 128)
                        q = sti % 2
                        sti += 1
                        ktm, kE, kYS, kB2, kK2 = ("tm", q), ("E", q), ("YS", q), ("B2", q), ("K2", q)
                        for i, (src, kx) in enumerate([(AR[:, 0, sr], "AR0"), (BpKp[:, 0, sr], "BpKp"), (BpKp[:, 1, sr], "BpKp"), (vv[:, sr], "vv")]):
                            TR(ps[3][:, i * 64:(i + 1) * 64], src, i64, [kx, "cst"], [("ps", 3)])
                        CP("act", tm[:, q, :], ps[3][:, 0:256], [("ps", 3)], [ktm])
                        MM(ps[4][:, 0:256], BK[:, 0, sr], AR[:, :, sr], True, True, ["BK0", "AR0", "AR1"], [("ps", 4)])
                        MM(ps[4][:, 256:384], AR[:, 0, sr], BK[:, 0, sr], True, True, ["BK0", "AR0"], [("ps", 4)])
                        TT("dve", Eb[:, q, 0:384], ps[4][:, 0:384], mE, ALU.mult, [("ps", 4), "cst"], [(kE, "a")])
                        MM(ps[5][:, 0:256], BK[:, 1, sr], AR[:, :, sr], True, True, ["BK1", "AR0", "AR1"], [("ps", 5, "a")])
                        TT("dve", YS[:, q, :], ps[5][:, 0:256], mY, ALU.mult, [("ps", 5, "a"), "cst"], [kYS])
                        MM(ps[5][:, 256:320], YS[:, q, 0:128], tm[:, q, 192:256], True, True, [kYS, ktm], [("ps", 5, "b")])
                        CP("pool", Eb[:, q, 384:448], tm[:, q, 0:64], [ktm], [(kE, "b")])
                        CP("act", Eb[:, q, 448:512], ps[5][:, 256:320], [("ps", 5, "b")], [(kE, "c")])
                        for lev in range(6):
                            if lev == 0:
                                PT_, P_, PZ_, Z_ = Eb[:, q, 0:128], Eb[:, q, 256:384], Eb[:, q, 256:512], Eb[:, q, 384:512]
                                rk = [(kE, "a"), (kE, "b"), (kE, "c")]
                            else:
                                Lp = Lb[:, q, (lev - 1) % 2, :]
                                PT_, P_, PZ_, Z_ = Lp[:, 0:128], Lp[:, 128:256], Lp[:, 128:384], Lp[:, 256:384]
                                rk = [("L", q, (lev - 1) % 2, "p"), ("L", q, (lev - 1) % 2, "z")]
                            Ln = Lb[:, q, lev % 2, :]
                            MM(ps[6][:, 128:384], PT_, PZ_, True, True, rk, [("ps", 6, "a")])
                            TT("dve", Ln[:, 256:384], ps[6][:, 256:384], Z_, ALU.add, [("ps", 6, "a")] + rk, [("L", q, lev % 2, "z")])
                            if lev < 5:
                                MM(ps[6][:, 0:128], P_, PT_, True, True, rk, [("ps", 6, "b")])
                                CP("act", Ln[:, 0:256], ps[6][:, 0:256], [("ps", 6, "a"), ("ps", 6, "b")], [("L", q, lev % 2, "p")])
                        Lf = Lb[:, q, 1, :]
                        kLf = ("L", q, 1, "z")
                        W_, U0_ = Lf[:, 256:320], Lf[:, 320:384]
                        for hf in range(2):
                            TS("pool", B2[:, q, hf * 64:(hf + 1) * 64], tm[:, q, 64:128], rowm[:, hf:hf + 1], None, ALU.mult, None, [ktm, "cst"], [kB2])
                            TS("pool", K2[:, q, hf * 64:(hf + 1) * 64], tm[:, q, 128:192], rowm[:, hf:hf + 1], None, ALU.mult, None, [ktm, "cst"], [kK2])
                        MM(ps[7][0:64, 0:128], W_, B2[:, q, :], True, True, [kLf, kB2], [("ps", 7, "g")])
                        for hf in range(2):
                            c = s2 * 2 + hf
                            STT(GT[:, h, c, :], i64, epos3[:, c, 63:64], ps[7][0:64, hf * 64:(hf + 1) * 64], ALU.mult, ALU.add,
                                [("ps", 7, "g"), "epos", "cst"], [("GT", h)])
                        for hf in range(2):
                            MM(ps[7][0:64, 128 + hf * 64:128 + (hf + 1) * 64], K2[:, q, hf * 64:(hf + 1) * 64], tm[:, q, 192:256], True, False, [kK2, ktm], [("ps", 7, "h")])
                            MM(ps[7][0:64, 128 + hf * 64:128 + (hf + 1) * 64], B2[:, q, hf * 64:(hf + 1) * 64], U0_, False, True, [kB2, kLf], [("ps", 7, "h")])
                        CP("act", Hm[:, h, s2 * 2:s2 * 2 + 2, :], ps[7][0:64, 128:256].rearrange("p (c i) -> p c i", c=2), [("ps", 7, "h")], [("Hm", h)])
                        MM(ps[7][0:64, 256:384], W_, Eb[:, q, 128:256], True, True, [kLf, (kE, "a")], [("ps", 7, "r")])
                        TT("dve", RhatT[:, h, sr], ps[7][0:64, 256:384], AR[:, 1, sr], ALU.add, [("ps", 7, "r"), "AR1"], [("Rhat", h)])
                        MM(ps[7][0:64, 384:512], tm[:, q, 192:256], YS[:, q, 128:256], True, False, [ktm, kYS], [("ps", 7, "y")])
                        MM(ps[7][0:64, 384:512], U0_, Eb[:, q, 128:256], False, True, [kLf, (kE, "a")], [("ps", 7, "y")])
                        CP("act", Y0T[:, h, sr], ps[7][0:64, 384:512], [("ps", 7, "y")], [("Y0T", h)])
                allh = lambda nm: [(nm, h) for h in range(H)]
                for c in range(4):
                    csl = slice(c * 64, (c + 1) * 64)
                    mnew = 1 - mcur
                    for h in range(H):
                        MM(ps[0][0:64, h * 64:(h + 1) * 64], Mst[:, mcur, h, :], RhatT[:, h, csl], True, True, [("Mst", mcur), ("Rhat", h)],
                           [("ps", 0, "a"), ("ps", 0, "b")])
                    for h in range(H):
                        MM(ps[1][0:64, h * 64:(h + 1) * 64], GT[:, h, c, :], Mst[:, mcur, h, :], True, True, [("Mst", mcur), ("GT", h)],
                           [("ps", 1, "a"), ("ps", 1, "b")])
                    TT("dve", Mst[:, mnew, :, :], ps[1][0:64, 0:384].rearrange("p (h i) -> p h i", h=6), Hm[:, :, c, :], ALU.add,
                       [("ps", 1, "a"), ("ps", 1, "b")] + allh("Hm"), [("Mst", mnew)])
                    TT("dve", yT[:, :, csl], ps[0][0:64, 0:384].rearrange("p (h t) -> p h t", h=6), Y0T[:, :, csl], ALU.add,
                       [("ps", 0, "a"), ("ps", 0, "b")] + allh("Y0T"), [("yT", c)])
                    mcur = mnew
                ally = [("yT", c) for c in range(4)]
                for h in range(H):
                    osl = h % 2
                    MM(ps[2][0:64, 0:TW], o64, yT[:, h, :], True, True, ["cst"] + ally, [("ps", 2, "a")])
                    STT(yc[:, :], ps[2][0:64, 0:TW], -1.0 / 64, yT[:, h, :], ALU.mult, ALU.add, [("ps", 2, "a")] + ally, ["yc"])
                    ACT(ysq[:, :], yc[:, :], AF.Square, ["yc"], ["ysq"])
                    MM(ps[2][0:64, 256:512], o64, ysq[:, :], True, True, ["cst", "ysq"], [("ps", 2, "b")])
                    ACT(yrs[:, :], ps[2][0:64, 256:512], AF.Sqrt, [("ps", 2, "b"), "epsc"], ["yrs"], scale=1.0 / 64, bias=epsc[0:64, 1:2])
                    RECIP(yrs[:, :], yrs[:, :], ["yrs"], ["yrs"])
                    TT("dve", yc[:, :], yc[:, :], yrs[:, :], ALU.mult, ["yc", "yrs"], ["yc"])
                    TS("dve", yc[:, :], yc[:, :], plg[:, h:h + 1], plb[:, h:h + 1], ALU.mult, ALU.add, ["yc", "plg", "plb"], ["yc"])
                    TT("dve", yc[:, :], yc[:, :], bon[:, h, :], ALU.add, ["yc", ("bon", h)], ["yc"])
                    TT("dve", obuf[:, osl, :], yc[:, :], gal[:, h, :], ALU.mult, ["yc", ("gal", h)], [("obuf", osl)])
                    DMA(catT[h * 64:(h + 1) * 64, tsl], obuf[:, osl, :], [("obuf", osl)], [("cat", "a", h, tt)], q="pool")
            sc.barrier()
        if stop == "B":
            break

        with ExitStack() as es:
            wO = sb(es, "wO", [128, 2, 8, 128])
            catb = sb(es, "catb", [128, 2, 8, 512])
            mixb = sb(es, "mixb", [128, 8, 512])
            sq = sb(es, "sq", [128, 2, 512])
            rstd = sb(es, "rstd", [128, 2, 512])
            gP = colvec(es, "gP", PR["post_mix_g"][l], 8)
            w_out_l = PR["w_out"][l].rearrange("(k p) c -> p k c", p=128)
            cat_v = catT.rearrange("(k p) t -> p k t", p=128)
            wi = 0
            for tt in range(4):
                tsl = slice(tt * 512, (tt + 1) * 512)
                cs = tt % 2
                DMA(RR(catb[:, cs, :, :]), cat_v[:, :, tsl], [], [("catb", cs)], q="pool")
                for j in range(8):
                    sl = wi % 2
                    wi += 1
                    DMA(RR(wO[:, sl, :, :]), w_out_l[:, :, j * 128:(j + 1) * 128], [], [("wO", sl)], q="pool")
                    bi = j % 2
                    for k in range(8):
                        MM(ps[bi][:, :], wO[:, sl, k, :], catb[:, cs, k, :], k == 0, k == 7, [("wO", sl), ("catb", cs)], [("ps", bi)], r=True)
                    CP("act" if j % 2 else "dve", mixb[:, j, :], ps[bi][:, :], [("ps", bi)], [("mixb", j)])
                r = rms_stats(lambda k: mixb[:, k, :], lambda k: ("mixb", k), tt, (sq, rstd), ps[2 + tt % 2], ("ps", 2 + tt % 2))
                for j in range(8):
                    STT(mixb[:, j, :], mixb[:, j, :], gP[:, j:j + 1], r, ALU.mult, ALU.mult, [("mixb", j), ("rstd", tt % 2), "gP"], [("mixb", j)])
                    TT("dve", xT[:, j, tsl], xT[:, j, tsl], mixb[:, j, :], ALU.add, [("mixb", j), ("x", j, tt)], [("x", j, tt)])
            sc.barrier()
        if stop == "E":
            break

        with ExitStack() as es:
            hb = sb(es, "hb", [128, 8, 512])
            actb = sb(es, "actb", [128, NFF, 512])
            wG = sb(es, "wG", [128, 2, 2, 8, 128])
            wD = sb(es, "wD", [128, 2, NFF, 128])
            sq = sb(es, "sq", [128, 2, 512])
            rstd = sb(es, "rstd", [128, 2, 512])
            sil = sb(es, "sil", [128, 2, 512])
            gF = colvec(es, "gF", PR["pre_ffn_g"][l], 8)
            gQ = colvec(es, "gQ", PR["post_ffn_g"][l], 8)
            w_fi = PR["w_ffn_in"][l].rearrange("(k p) c -> p k c", p=128)
            w_fo = PR["w_ffn_out"][l].rearrange("(j p) c -> p j c", p=128)
            wi = 0
            wdi = 0
            for tt in range(4):
                tsl = slice(tt * 512, (tt + 1) * 512)
                r = rms_stats(lambda k: xT[:, k, tsl], lambda k: ("x", k, tt), tt, (sq, rstd), ps[6 + tt % 2], ("ps", 6 + tt % 2))
                for k in range(8):
                    STT(RR(hb[:, k, :]), xT[:, k, tsl], gF[:, k:k + 1], r, ALU.mult, ALU.mult, [("x", k, tt), ("rstd", tt % 2), "gF"], [("hb", k)])
                for j in range(NFF):
                    sl = wi % 2
                    wi += 1
                    DMA(RR(wG[:, sl, 0, :, :]), w_fi[:, :, j * 128:(j + 1) * 128], [], [("wG", sl, 0)], q="pool")
                    DMA(RR(wG[:, sl, 1, :, :]), w_fi[:, :, DFF + j * 128:DFF + (j + 1) * 128], [], [("wG", sl, 1)], q="pool")
                    bg, bu = (j % 2) * 2, (j % 2) * 2 + 1
                    for k in range(8):
                        MM(ps[bg][:, :], wG[:, sl, 0, k, :], hb[:, k, :], k == 0, k == 7, [("wG", sl, 0), ("hb", k)], [("ps", bg)], r=True)
                    for k in range(8):
                        MM(ps[bu][:, :], wG[:, sl, 1, k, :], hb[:, k, :], k == 0, k == 7, [("wG", sl, 1), ("hb", k)], [("ps", bu)], r=True)
                    ACT(sil[:, j % 2, :], ps[bg][:, :], AF.Silu, [("ps", bg)], [("sil", j % 2)])
                    TT("dve", RR(actb[:, j, :]), ps[bu][:, :], sil[:, j % 2, :], ALU.mult, [("ps", bu), ("sil", j % 2)], [("actb", j)])
                allact = [("actb", j) for j in range(NFF)]
                for jo in range(8):
                    sl = wdi % 2
                    wdi += 1
                    DMA(RR(wD[:, sl, :, :]), w_fo[:, :, jo * 128:(jo + 1) * 128], [], [("wD", sl)], q="pool")
                    bi = 4 + jo % 2
                    for j in range(NFF):
                        MM(ps[bi][:, :], wD[:, sl, j, :], actb[:, j, :], j == 0, j == NFF - 1, [("wD", sl), ("actb", j)], [("ps", bi)], r=True)
                    CP("act" if jo % 2 else "dve", RR(hb[:, jo, :]), ps[bi][:, :], [("ps", bi)], [("hb", jo)])
                r = rms_stats(lambda k: hb[:, k, :], lambda k: ("hb", k), tt + 1, (sq, rstd), ps[6 + (tt + 1) % 2], ("ps", 6 + (tt + 1) % 2))
                for j in range(8):
                    STT(RR(hb[:, j, :]), hb[:, j, :], gQ[:, j:j + 1], r, ALU.mult, ALU.mult, [("hb", j), ("rstd", (tt + 1) % 2), "gQ"], [("hb", j)])
                    TT("dve", xT[:, j, tsl], xT[:, j, tsl], hb[:, j, :], ALU.add, [("hb", j), ("x", j, tt)], [("x", j, tt)])
            sc.barrier()
        if OPTS.get("xdbg") == l:
            for k in range(8):
                DMA(xdbg[k * 128:(k + 1) * 128, :], xT[:, k, :], [("x", k, tt) for tt in range(4)], [("xdbg", k)])

    if stop in (None, 'setup'):
        with ExitStack() as es:
            yo = sb(es, "yo", [128, 2, D])
            for t in range(OPTS.get('nt', 16)):
                sl = t % 2 if not OPTS.get('sl0') else 0
                for g in range(2):
                    bank = ps[(t * 2 + g) % 4]
                    bk = ("ps", (t * 2 + g) % 4)
                    for kk in range(4):
                        k = g * 4 + kk
                        TR(bank[:, kk * 128:(kk + 1) * 128], xT[:, k, t * 128:(t + 1) * 128], ident, [("x", k, t // 4), "cst"], [bk])
                    CP("act" if g else "dve", yo[:, sl, g * 512:(g + 1) * 512], bank[:, :], [bk], [("yo", sl, g)])
                DMA(y_out[t * 128:(t + 1) * 128, :], yo[:, sl, :], [("yo", sl, 0), ("yo", sl, 1)], [("y", t)])
    sc.barrier()
    sc.emit()
    glob.close()
    return nc


_NC_CACHE = {}


def kernel(**inputs):
    if "nc" not in _NC_CACHE:
        _NC_CACHE["nc"] = build()
    nc = _NC_CACHE["nc"]
    x = np.ascontiguousarray(np.asarray(inputs["x"], dtype=np.float32))
    base = {n: np.ascontiguousarray(np.asarray(inputs[n], dtype=np.float32)) for n in PARAM_NAMES}
    base["cst"] = CONSTS["cst"]
    base["mobac"] = CONSTS["mobac"]
    base["blkind"] = CONSTS["blkind"]
    in_maps = []
    for b in range(8):
        m = dict(base)
        m["x"] = x[b]
        in_maps.append(m)
    res = run_bass_kernel_spmd(nc, in_maps, core_ids=list(range(8)))
    return np.stack([np.asarray(r["y"], dtype=np.float32) for r in res.results], 0)
```

```python
import numpy as np
from contextlib import ExitStack
import concourse.bass as bass
import concourse.mybir as mybir
from concourse.bass_utils import run_bass_kernel_spmd

F32 = mybir.dt.float32
F32R = mybir.dt.float32r
AF = mybir.ActivationFunctionType
ALU = mybir.AluOpType
AX = mybir.AxisListType

S = 2048
D = 1024
L = 2
DFF = 2816
NFF = 22
H = 6
C0 = float(np.exp(-0.5))
BIG = 30000.0


class Sched:
    def __init__(self, nc, n_dma=40):
        self.nc = nc
        self.names = ["pe", "act", "dve", "pool", "sp"]
        self.sem = {e: nc.alloc_semaphore("s_" + e) for e in ["pe", "act", "dve", "pool"]}
        self.cnt = {e: 0 for e in self.sem}
        self.dsem = [nc.alloc_semaphore("d%d" % i) for i in range(n_dma)]
        self.dcnt = [0] * n_dma
        self.drr = 0
        self.q = {e: [] for e in self.names}
        self.waited = {e: {} for e in self.names}
        self.lastw = {}
        self.readers = {}

    def _deps(self, reads, writes):
        deps = {}

        def add(k, v):
            if deps.get(k, 0) < v:
                deps[k] = v

        for r in reads:
            ev = self.lastw.get(r)
            if ev is not None:
                add(*ev)
        for w in writes:
            ev = self.lastw.get(w)
            if ev is not None:
                add(*ev)
            for k, v in self.readers.get(w, {}).items():
                add(k, v)
        return deps

    def _commit(self, ev, reads, writes):
        k, v = ev
        for r in reads:
            d = self.readers.setdefault(r, {})
            if d.get(k, 0) < v:
                d[k] = v
        for w in writes:
            self.lastw[w] = ev
            self.readers[w] = {}

    def _waits(self, eng, deps):
        waits = []
        for k, v in deps.items():
            if eng == "pe" and k == ("e", "pe"):
                continue
            if self.waited[eng].get(k, 0) >= v:
                continue
            self.waited[eng][k] = v
            waits.append((k, v))
        return waits

    def op(self, eng, fn, reads=(), writes=()):
        banks = {("psx", k[1]) for k in list(reads) + list(writes) if isinstance(k, tuple) and k and k[0] == "ps"}
        if banks:
            writes = list(writes) + list(banks)
        deps = self._deps(reads, writes)
        waits = self._waits(eng, deps)
        self.cnt[eng] += 1
        ev = (("e", eng), self.cnt[eng])
        self.q[eng].append((waits, fn, "e"))
        self._commit(ev, reads, writes)

    def dma(self, qeng, out, in_, reads=(), writes=(), **kw):
        deps = self._deps(reads, writes)
        idx = self.drr
        self.drr = (self.drr + 1) % len(self.dsem)
        if self.dcnt[idx] > 0:
            k = ("d", idx)
            deps[k] = max(deps.get(k, 0), self.dcnt[idx])
        waits = self._waits(qeng, deps)
        self.dcnt[idx] += 16
        ev = (("d", idx), self.dcnt[idx])
        self.q[qeng].append((waits, lambda e: e.dma_start(out=out, in_=in_, **kw), idx))
        self._commit(ev, reads, writes)

    def barrier(self):
        allev = [(("e", e), c) for e, c in self.cnt.items() if c > 0]
        allev += [(("d", i), c) for i, c in enumerate(self.dcnt) if c > 0]
        for eng in self.names:
            waits = self._waits(eng, dict(allev))
            if waits:
                self.q[eng].append((waits, None, None))
        self.lastw = {}
        self.readers = {}

    def emit(self):
        nc = self.nc
        engs = {"pe": "tensor", "act": "scalar", "dve": "vector", "pool": "gpsimd", "sp": "sync"}
        with nc.Block() as block:
            for name in self.names:
                def body(eng, name=name):
                    for waits, fn, kind in self.q[name]:
                        for k, v in waits:
                            s = self.sem[k[1]] if k[0] == "e" else self.dsem[k[1]]
                            eng.wait_ge(s, v)
                        if fn is None:
                            continue
                        ins = fn(eng)
                        if kind == "e":
                            ins.then_inc(self.sem[name], 1)
                        else:
                            ins.then_inc(self.dsem[kind], 16)
                getattr(block, engs[name])(body)


def make_consts():
    c = {}
    i128 = np.arange(128)
    blk = (i128[:, None] // 64) == (i128[None, :] // 64)
    ident = np.eye(128, dtype=np.float32)
    ones = np.ones((128, 128), np.float32)
    SL = ((i128[:, None] > i128[None, :]) & blk).astype(np.float32)
    SU = ((i128[:, None] < i128[None, :]) & blk).astype(np.float32)
    IU = ((i128[:, None] <= i128[None, :]) & blk).astype(np.float32)
    IUfull = (i128[:, None] <= i128[None, :]).astype(np.float32)
    idst = np.concatenate([np.eye(64), np.eye(64)], 0).astype(np.float32)
    reset = np.ones((128, 256), np.float32)
    reset[:, ::64] = 0.0
    rowm = np.zeros((128, 2), np.float32)
    rowm[:64, 0] = 1.0
    rowm[64:, 1] = 1.0
    q512 = np.arange(512)
    cm = np.stack([(q512[None, :] >= (j * 128 + i128[:, None])).astype(np.float32) for j in range(4)], 1)
    parts = [ident, ones, SU, IU, SL, SU, IU, IUfull, idst, reset, rowm, cm.reshape(128, 2048)]
    offs = {}
    o = 0
    for nm, p in zip(["ident", "ones", "mE", "_1", "_2", "mY", "_3", "iuf", "idst", "reset", "rowm", "cm"], parts):
        offs[nm] = o
        o += p.shape[1]
    c["cst"] = np.ascontiguousarray(np.concatenate(parts, 1))
    c["offs"] = offs
    mb = np.zeros((128, 3, 16, 6, 8), np.float32)
    for t in range(16):
        b = t // 2
        for n in range(8):
            mb[:, 0, t, :, n] = 0.0 if n < b else -1e30
            mb[:, 1, t, :, n] = 1.0 if n < b else 0.0
            mb[:, 2, t, :, n] = 1.0 if n == b else 0.0
    c["mobac"] = mb.reshape(128, 3 * 16 * 48)
    bi = np.zeros((8, S), np.float32)
    for n in range(8):
        bi[n, n * 256:(n + 1) * 256] = 1.0
    c["blkind"] = bi
    return c


CONSTS = make_consts()
PARAM_NAMES = ["pre_mix_g", "w_in", "rwkv_mu", "rwkv_w0", "rwkv_w2", "rwkv_a0", "rwkv_a2", "rwkv_g2",
               "rwkv_k_k", "rwkv_k_a", "rwkv_r_k", "rwkv_lnx_g", "rwkv_lnx_b", "gmlp_ln_g", "gmlp_ln_b",
               "gmlp_w_s", "gmlp_b_s", "w_out", "post_mix_g", "pre_ffn_g", "w_ffn_in", "w_ffn_out", "post_ffn_g"]
PARAM_SHAPES = {"pre_mix_g": (L, D), "w_in": (L, D, 3072), "rwkv_mu": (L, 1408), "rwkv_w0": (L, 384),
                "rwkv_w2": (L, 64, 384), "rwkv_a0": (L, 384), "rwkv_a2": (L, 64, 384), "rwkv_g2": (L, 128, 384),
                "rwkv_k_k": (L, 384), "rwkv_k_a": (L, 384), "rwkv_r_k": (L, 6, 64), "rwkv_lnx_g": (L, 384),
                "rwkv_lnx_b": (L, 384), "gmlp_ln_g": (L, 256), "gmlp_ln_b": (L, 256), "gmlp_w_s": (L, 4, 128, 128),
                "gmlp_b_s": (L, 4, 128), "w_out": (L, D, D), "post_mix_g": (L, D), "pre_ffn_g": (L, D),
                "w_ffn_in": (L, D, 2 * DFF), "w_ffn_out": (L, DFF, D), "post_ffn_g": (L, D)}


OPTS = {}


def build(dbg=None, nlayers=L, stop=None):
    dbg = dbg or []
    nc = bass.Bass("TRN2", target_bir_lowering=False)
    sc = Sched(nc)
    OF = CONSTS["offs"]

    def dram(name, shape, kind="Internal"):
        if name in dbg:
            kind = "ExternalOutput"
        return nc.dram_tensor(name, list(shape), F32, kind=kind).ap()

    x_in = dram("x", [S, D], "ExternalInput")
    y_out = dram("y", [S, D], "ExternalOutput")
    cst_d = dram("cst", CONSTS["cst"].shape, "ExternalInput")
    if not OPTS.get("noparams"):
        PR = {n: dram(n, PARAM_SHAPES[n], "ExternalInput") for n in PARAM_NAMES}
        mobac_d = dram("mobac", CONSTS["mobac"].shape, "ExternalInput")
        blkind_d = dram("blkind", CONSTS["blkind"].shape, "ExternalInput")
    if OPTS.get("noscratch"):
        glob_scr = None
    rwkvT = dram("rwkvT", [1408, S]) if not OPTS.get("noscratch") else None
    qkT = dram("qkT", [768, S]) if not OPTS.get("noscratch") else None
    uT = dram("uT", [256, S]) if not OPTS.get("noscratch") else None
    vm_tm = dram("vm_tm", [S, 384]) if not OPTS.get("noscratch") else None
    vg_tm = dram("vg_tm", [S, 256]) if not OPTS.get("noscratch") else None
    catT = dram("catT", [D, S]) if not OPTS.get("noscratch") else None
    xdbg = dram("xdbg", [D, S]) if not OPTS.get("noscratch") else None

    uid = [0]

    def sb(es, name, shape):
        uid[0] += 1
        return es.enter_context(nc.sbuf_tensor("%s_%d" % (name, uid[0]), list(shape), F32))

    glob = ExitStack()
    xT = sb(glob, "xT", [128, 8, S])
    cst = sb(glob, "cst_sb", [128, CONSTS["cst"].shape[1]])
    ps = [glob.enter_context(nc.psum_tensor("ps%d" % i, [128, 512], F32)) for i in range(8)]
    ident = cst[:, OF["ident"]:OF["ident"] + 128]
    ones = cst[:, OF["ones"]:OF["ones"] + 128]
    mE = cst[:, OF["mE"]:OF["mE"] + 384]
    mY = cst[:, OF["mY"]:OF["mY"] + 256]
    iuf = cst[:, OF["iuf"]:OF["iuf"] + 128]
    idst = cst[:, OF["idst"]:OF["idst"] + 64]
    resetm = cst[:, OF["reset"]:OF["reset"] + 256]
    cm = cst[:, OF["cm"]:OF["cm"] + 2048]
    epsc = sb(glob, "epsc", [128, 4])

    def ACT(out, in_, func, reads, writes, **kw):
        sc.op("act", lambda e: e.activation(out=out, in_=in_, func=func, **kw), reads, writes)

    def RR(ap):
        return ap if OPTS.get("nor") else ap.bitcast(F32R)

    def MM(out, lhsT, rhs, start, stop, reads, writes, r=False):
        if r and not OPTS.get("nor"):
            lhsT = lhsT.bitcast(F32R)
            rhs = rhs.bitcast(F32R)
        sc.op("pe", lambda e: e.matmul(out, lhsT=lhsT, rhs=rhs, start=start, stop=stop), reads, writes)

    def TR(out, in_, idn, reads, writes):
        sc.op("pe", lambda e: e.transpose(out, in_, idn), reads, writes)

    def TT(eng, out, in0, in1, op, reads, writes):
        sc.op(eng, lambda e: e.tensor_tensor(out=out, in0=in0, in1=in1, op=op), reads, writes)

    def TS(eng, out, in0, s1, s2, op0, op1, reads, writes):
        if s2 is None:
            sc.op(eng, lambda e: e.tensor_scalar(out=out, in0=in0, scalar1=s1, scalar2=None, op0=op0), reads, writes)
        else:
            sc.op(eng, lambda e: e.tensor_scalar(out=out, in0=in0, scalar1=s1, scalar2=s2, op0=op0, op1=op1), reads, writes)

    def STT(out, in0, scalar, in1, op0, op1, reads, writes):
        sc.op("dve", lambda e: e.scalar_tensor_tensor(out=out, in0=in0, scalar=scalar, in1=in1, op0=op0, op1=op1), reads, writes)

    def CP(eng, out, in_, reads, writes):
        if eng == "act":
            sc.op("act", lambda e: e.copy(out=out, in_=in_), reads, writes)
        else:
            sc.op(eng, lambda e: e.tensor_copy(out=out, in_=in_), reads, writes)

    def RECIP(out, in_, reads, writes):
        sc.op("dve", lambda e: e.reciprocal(out=out, in_=in_), reads, writes)

    def DMA(out, in_, reads, writes, q="sp", **kw):
        sc.dma(q, out, in_, reads, writes, **kw)

    def colvec(es, name, src_1d, ncol, p=128):
        t = sb(es, name, [p, ncol])
        DMA(t[:, :], src_1d.rearrange("(c p) -> p c", p=p), [], [name], allow_slow_non_contiguous=True)
        return t

    DMA(cst[:, :], cst_d[:, :], [], ["cst"])
    sc.op("dve", lambda e: e.memset(epsc[:, 0:1], 1e-6), [], ["epsc"])
    sc.op("dve", lambda e: e.memset(epsc[:, 1:2], 64e-5), [], ["epsc"])
    sc.op("dve", lambda e: e.memset(epsc[:, 2:3], 0.0), [], ["epsc"])
    with ExitStack() as es:
        xin = sb(es, "xin", [128, 2, D])
        for t in range(OPTS.get('nt', 16)):
            sl = t % 2 if not OPTS.get('sl0') else 0
            DMA(xin[:, sl, :], x_in[t * 128:(t + 1) * 128, :], [], [("xin", sl)])
            for g in range(2):
                bank = ps[(t * 2 + g) % OPTS.get('nb', 4)]
                bk = ("ps", (t * 2 + g) % OPTS.get('nb', 4))
                for kk in range(4):
                    k = g * 4 + kk
                    TR(bank[:, kk * 128:(kk + 1) * 128], xin[:, sl, k * 128:(k + 1) * 128], ident, [("xin", sl), "cst"], [bk])
                CP("act" if g else "dve", xT[:, g * 4:(g + 1) * 4, t * 128:(t + 1) * 128],
                   bank[:, :].rearrange("p (k c) -> p k c", k=4), [bk], [("x", g * 4 + kk, t // 4) for kk in range(4)])
        if not OPTS.get("nobar"):
            sc.barrier()

    def rms_stats(src_fn, src_keys, tt, es_tiles, pbank, pkey):
        sq, rstd = es_tiles
        for k in range(8):
            ACT(sq[:, k % 2, :], src_fn(k), AF.Square, [src_keys(k)], [("sq", k % 2)])
            MM(pbank[:, :], ones, sq[:, k % 2, :], k == 0, k == 7, [("sq", k % 2), "cst"], [pkey])
        ACT(rstd[:, tt % 2, :], pbank[:, :], AF.Sqrt, [pkey, "epsc"], [("rstd", tt % 2)], scale=1.0 / D, bias=epsc[:, 0:1])
        RECIP(rstd[:, tt % 2, :], rstd[:, tt % 2, :], [("rstd", tt % 2)], [("rstd", tt % 2)])
        return rstd[:, tt % 2, :]

    for l in range(nlayers if stop != 'setup' else 0):
        with ExitStack() as es:
            hbuf = sb(es, "hbuf", [128, 8, S])
            sq = sb(es, "sq", [128, 2, 512])
            rstd = sb(es, "rstd", [128, 2, 512])
            gA = colvec(es, "gA", PR["pre_mix_g"][l], 8)
            muA = colvec(es, "muA", PR["rwkv_mu"][l], 11)
            for tt in range(4):
                tsl = slice(tt * 512, (tt + 1) * 512)
                r = rms_stats(lambda k: xT[:, k, tsl], lambda k: ("x", k, tt), tt, (sq, rstd), ps[4 + tt % 2], ("ps", 4 + tt % 2))
                for k in range(8):
                    STT(RR(hbuf[:, k, tsl]), xT[:, k, tsl], gA[:, k:k + 1], r, ALU.mult, ALU.mult,
                        [("x", k, tt), ("rstd", tt % 2), "gA"], [("h", k, tt)])
            es_main = es
            es = ExitStack()
            wA = sb(es, "wA", [128, 2, 8, 128])
            stg = sb(es, "stg", [128, 2, S])
            stg2 = sb(es, "stg2", [128, 2, S])
            w_in_l = PR["w_in"][l].rearrange("(k p) c -> p k c", p=128)
            fm_chunks = [(c * 128, rwkvT, c * 128, True) for c in range(11)]
            fm_chunks += [(1408 + c * 128, qkT, c * 128, False) for c in range(6)]
            fm_chunks += [(2560 + c * 128, uT, c * 128, False) for c in range(2)]
            for ci, (col0, dst, row0, shift) in enumerate(fm_chunks):
                sl = ci % 2
                DMA(RR(wA[:, sl, :, :]), w_in_l[:, :, col0:col0 + 128], [], [("wA", sl)], q="pool")
                for tt in range(4):
                    tsl = slice(tt * 512, (tt + 1) * 512)
                    bi = (ci * 4 + tt) % 4
                    for k in range(8):
                        MM(ps[bi][:, :], wA[:, sl, k, :], hbuf[:, k, tsl], k == 0, k == 7,
                           [("wA", sl), ("h", k, tt)], [("ps", bi)], r=True)
                    CP("act" if tt % 2 else "dve", stg[:, sl, tsl], ps[bi][:, :], [("ps", bi)], [("stg", sl, tt)])
                allst = [("stg", sl, tt) for tt in range(4)]
                if shift:
                    TT("pool", stg2[:, sl, 1:S], stg[:, sl, 0:S - 1], stg[:, sl, 1:S], ALU.subtract, allst, [("stg2", sl)])
                    TS("pool", stg2[:, sl, 0:1], stg[:, sl, 0:1], -1.0, None, ALU.mult, None, allst, [("stg2", sl)])
                    STT(stg2[:, sl, :], stg2[:, sl, :], muA[:, ci:ci + 1], stg[:, sl, :], ALU.mult, ALU.add,
                        allst + [("stg2", sl), "muA"], [("stg2", sl)])
                    DMA(dst[row0:row0 + 128, :], stg2[:, sl, :], [("stg2", sl)], [("dr", id(dst), row0)], q="pool")
                else:
                    DMA(dst[row0:row0 + 128, :], stg[:, sl, :], allst, [("dr", id(dst), row0)], q="pool")
            sc.barrier()
            es.close()
            es = es_main
            wB = sb(es, "wB", [128, 8, 640])
            DMA(RR(wB[:, :, 0:384]), w_in_l[:, :, 2176:2560], [], ["wB"], q="pool")
            DMA(RR(wB[:, :, 384:640]), w_in_l[:, :, 2816:3072], [], ["wB"], q="pool")
            vst = sb(es, "vst", [128, 2, 640])
            for t in range(16):
                sl = t % 2
                b0, b1 = 4 + (t % 2) * 2, 5 + (t % 2) * 2
                for k in range(8):
                    MM(ps[b0][:, 0:384], hbuf[:, k, t * 128:(t + 1) * 128], wB[:, k, 0:384], k == 0, k == 7,
                       [("h", k, t // 4), "wB"], [("ps", b0)], r=True)
                for k in range(8):
                    MM(ps[b1][:, 0:256], hbuf[:, k, t * 128:(t + 1) * 128], wB[:, k, 384:640], k == 0, k == 7,
                       [("h", k, t // 4), "wB"], [("ps", b1)], r=True)
                CP("act", vst[:, sl, 0:384], ps[b0][:, 0:384], [("ps", b0)], [("vst", sl, 0)])
                CP("dve", vst[:, sl, 384:640], ps[b1][:, 0:256], [("ps", b1)], [("vst", sl, 1)])
                DMA(vm_tm[t * 128:(t + 1) * 128, :], vst[:, sl, 0:384], [("vst", sl, 0)], [("vm", t)], q="pool")
                DMA(vg_tm[t * 128:(t + 1) * 128, :], vst[:, sl, 384:640], [("vst", sl, 1)], [("vg", t)], q="pool")
            sc.barrier()
        if stop == "A":
            break

        with ExitStack() as es:
            lng = sb(es, "lng", [128, 256])
            lnb = sb(es, "lnb", [128, 256])
            DMA(lng[:, :], PR["gmlp_ln_g"][l].partition_broadcast(128), [], ["lng"])
            DMA(lnb[:, :], PR["gmlp_ln_b"][l].partition_broadcast(128), [], ["lnb"])
            wsn = sb(es, "wsn", [128, 4, 128])
            wsT = sb(es, "wsT", [128, 4, 128])
            bsr = sb(es, "bsr", [1, 512])
            DMA(wsn[:, :, :], PR["gmlp_w_s"][l].rearrange("g t s -> t g s"), [], ["wsn"])
            DMA(bsr[:, :], PR["gmlp_b_s"][l].rearrange("g t -> (g t)").partition_broadcast(1), [], ["bsr"])
            for g in range(4):
                TR(ps[0][:, g * 128:(g + 1) * 128], wsn[:, g, :], ident, ["wsn", "cst"], [("ps", 0)])
            for g in range(4):
                TT("dve", wsT[:, g, :], ps[0][:, g * 128:(g + 1) * 128], iuf, ALU.mult, [("ps", 0), "cst"], ["wsT"])
            gu = sb(es, "gu", [128, 2, S])
            t1 = sb(es, "t1", [128, S])
            cout = sb(es, "cout", [128, 2, S])
            for pp in range(2):
                DMA(gu[:, pp, :], uT[pp * 128:(pp + 1) * 128, :], [], [("gu", pp)])
                ACT(t1[:, :], gu[:, pp, :], AF.Square, [("gu", pp)], ["t1"])
                TS("pool", t1[:, :], t1[:, :], 0.044715, 1.0, ALU.mult, ALU.add, ["t1"], ["t1"])
                TT("dve", t1[:, :], t1[:, :], gu[:, pp, :], ALU.mult, ["t1", ("gu", pp)], ["t1"])
                ACT(t1[:, :], t1[:, :], AF.Sigmoid, ["t1"], ["t1"], scale=2.0 * 0.7978845608028654)
                TT("dve", gu[:, pp, :], gu[:, pp, :], t1[:, :], ALU.mult, ["t1", ("gu", pp)], [("gu", pp)])
            vb = sb(es, "vb", [128, 2, 256])
            t2 = sb(es, "t2", [128, 2, 256])
            st6 = sb(es, "st6", [128, 2, 8])
            for c in range(16):
                sl = c % 2
                DMA(vb[:, sl, :], vg_tm[c * 128:(c + 1) * 128, :], [], [("vb", sl)])
                kv, kt = ("vb", sl), ("t2", sl)
                ACT(t2[:, sl, :], vb[:, sl, :], AF.Square, [kv], [kt])
                TS("pool", t2[:, sl, :], t2[:, sl, :], 0.044715, 1.0, ALU.mult, ALU.add, [kt], [kt])
                TT("dve", t2[:, sl, :], t2[:, sl, :], vb[:, sl, :], ALU.mult, [kt, kv], [kt])
                ACT(t2[:, sl, :], t2[:, sl, :], AF.Sigmoid, [kt], [kt], scale=2.0 * 0.7978845608028654)
                TT("dve", vb[:, sl, :], vb[:, sl, :], t2[:, sl, :], ALU.mult, [kt, kv], [kv])
                ks = ("st6", sl)
                sc.op("dve", lambda e, sl=sl: e.bn_stats(out=st6[:, sl, 0:6], in_=vb[:, sl, :]), [kv], [ks])
                sc.op("dve", lambda e, sl=sl: e.bn_aggr(out=st6[:, sl, 6:8], in_=st6[:, sl, 0:6]), [ks], [ks])
                ACT(st6[:, sl, 7:8], st6[:, sl, 7:8], AF.Sqrt, [ks, "epsc"], [ks], bias=epsc[:, 0:1], scale=1.0)
                RECIP(st6[:, sl, 7:8], st6[:, sl, 7:8], [ks], [ks])
                TS("dve", vb[:, sl, :], vb[:, sl, :], st6[:, sl, 6:7], st6[:, sl, 7:8], ALU.subtract, ALU.mult, [kv, ks], [kv])
                TT("dve", vb[:, sl, :], vb[:, sl, :], lng[:, :], ALU.mult, [kv, "lng"], [kv])
                TT("dve", vb[:, sl, :], vb[:, sl, :], lnb[:, :], ALU.add, [kv, "lnb"], [kv])
                for pp in range(2):
                    bi = 1 + (c % 2) * 2 + pp
                    for gg in range(2):
                        g = pp * 2 + gg
                        MM(ps[bi][:, gg * 128:(gg + 1) * 128], vb[:, sl, pp * 128:(pp + 1) * 128], wsT[:, g, :], True, False, [kv, "wsT"], [("ps", bi)])
                        MM(ps[bi][:, gg * 128:(gg + 1) * 128], ones[0:1, :], bsr[0:1, g * 128:(g + 1) * 128], False, True, ["cst", "bsr"], [("ps", bi)])
                    for gg in range(2):
                        TT("dve", cout[gg * 64:(gg + 1) * 64, pp, c * 128:(c + 1) * 128], ps[bi][gg * 64:(gg + 1) * 64, gg * 128:(gg + 1) * 128],
                           gu[gg * 64:(gg + 1) * 64, pp, c * 128:(c + 1) * 128], ALU.mult, [("ps", bi), ("gu", pp)], [("cout", pp)])
            for pp in range(2):
                DMA(catT[768 + pp * 128:768 + (pp + 1) * 128, :], cout[:, pp, :], [("cout", pp)], [("cat", 6 + pp)], q="pool")
            sc.barrier()
        if stop == "D":
            break

        with ExitStack() as es:
            qa = sb(es, "qa", [72, S])
            ka = sb(es, "ka", [72, S])
            vt = sb(es, "vt", [128, 16, 128])
            ones_r = sb(es, "ones_r", [128, 128])
            CP("dve", RR(ones_r[:, :]), ones, ["cst"], ["ones_r"])
            kbar = sb(es, "kbar", [64, 8])
            mobc = sb(es, "mobc", [128, 3, 16, 48])
            NP = sb(es, "NP", [128, 16, 72])
            sm = sb(es, "sm", [128, 2, 8])
            top8 = sb(es, "top8", [128, 2, 8])
            al = sb(es, "al", [128, 2, 8])
            pt = sb(es, "pt", [128, 3, 512])
            rden = sb(es, "rden", [128, 512])
            ob = sb(es, "ob", [128, 2, 512])
            DMA(mobc[:, :, :, :], mobac_d.rearrange("p (a t c) -> p a t c", a=3, t=16), [], ["mobc"])
            DMA(RR(ka[64:72, :]), blkind_d[:, :], [], ["kaB"], q="pool")
            sc.op("pool", lambda e: e.memset(NP[:, :, :], 0.0), [], ["NP"])
            pti = 0
            for h in range(H):
                DMA(RR(qa[0:64, :]), qkT[h * 64:(h + 1) * 64, :], [], ["qaQ"], q="pool")
                DMA(RR(ka[0:64, :]), qkT[384 + h * 64:384 + (h + 1) * 64, :], [], ["kaK"], q="pool")
                if h % 2 == 0:
                    DMA(RR(vt[:, :, :]), vm_tm.rearrange("(t p) c -> p t c", p=128)[:, :, h * 64:(h + 2) * 64], [], ["vt"], q="pool")
                hp = slice((h % 2) * 64, (h % 2) * 64 + 64)
                sc.op("dve", lambda e: e.tensor_reduce(out=kbar[:, :], in_=ka[0:64, :].rearrange("p (n k) -> p n k", n=8), axis=AX.X, op=ALU.add), ["kaK"], ["kbar"])
                for t in range(16):
                    MM(ps[0][:, t * 8:(t + 1) * 8], qa[0:64, t * 128:(t + 1) * 128], kbar[:, :], True, True, ["qaQ", "kbar"], [("ps", 0)])
                for t in range(16):
                    sl = t % 2
                    hs = slice(h * 8, (h + 1) * 8)
                    TT("dve", sm[:, sl, :], ps[0][:, t * 8:(t + 1) * 8], mobc[:, 0, t, hs], ALU.add, [("ps", 0), "mobc"], [("sm", sl)])
                    sc.op("dve", lambda e, sl=sl: e.max(out=top8[:, sl, :], in_=sm[:, sl, :]), [("sm", sl)], [("top8", sl)])
                    TS("dve", al[:, sl, :], sm[:, sl, :], top8[:, sl, 2:3], None, ALU.is_ge, None, [("sm", sl), ("top8", sl)], [("al", sl)])
                    TT("dve", al[:, sl, :], al[:, sl, :], mobc[:, 1, t, hs], ALU.mult, [("al", sl), "mobc"], [("al", sl)])
                    TT("dve", al[:, sl, :], al[:, sl, :], mobc[:, 2, t, hs], ALU.add, [("al", sl), "mobc"], [("al", sl)])
                    TS("dve", NP[:, t, 64:72], al[:, sl, :], -1.0, BIG, ALU.add, ALU.mult, [("al", sl)], ["NP"])
                for t4 in range(4):
                    for tq in range(4):
                        t = t4 * 4 + tq
                        MM(ps[1][0:72, tq * 128:(tq + 1) * 128], NP[:, t, :], ident, True, True, ["NP", "cst"], [("ps", 1)])
                    CP("act", RR(qa[64:72, t4 * 512:(t4 + 1) * 512]), ps[1][64:72, :], [("ps", 1)], ["qaM"])
                for qt in range(4):
                    qsl = slice(qt * 512, (qt + 1) * 512)
                    nk = (qt + 1) * 4
                    osl = qt % 2
                    for kt in range(nk):
                        sb_i = 2 + kt % 2
                        pi = pti % 3
                        pti += 1
                        MM(ps[sb_i][:, :], ka[0:72, kt * 128:(kt + 1) * 128], qa[0:72, qsl], True, True,
                           ["kaK", "kaB", "qaQ", "qaM"], [("ps", sb_i)], r=True)
                        ACT(RR(pt[:, pi, :]), ps[sb_i][:, :], AF.Exp, [("ps", sb_i)], [("pt", pi)], scale=0.125)
                        if kt >= qt * 4:
                            j = kt - qt * 4
                            TT("dve", RR(pt[:, pi, :]), pt[:, pi, :], cm[:, j * 512:(j + 1) * 512], ALU.mult, [("pt", pi), "cst"], [("pt", pi)])
                        MM(ps[4][:, :], vt[:, kt, :], pt[:, pi, :], kt == 0, kt == nk - 1, ["vt", ("pt", pi)], [("ps", 4)], r=True)
                        MM(ps[5][:, :], ones_r[:, :], pt[:, pi, :], kt == 0, kt == nk - 1, ["ones_r", ("pt", pi)], [("ps", 5)], r=True)
                    RECIP(rden[hp, :], ps[5][hp, :], [("ps", 5)], ["rden"])
                    TT("dve", ob[hp, osl, :], ps[4][hp, :], rden[hp, :], ALU.mult, [("ps", 4), "rden"], [("ob", osl)])
                    DMA(catT[384 + h * 64:384 + (h + 1) * 64, qsl], ob[hp, osl, :], [("ob", osl)], [("cat", "b", h, qt)])
            sc.barrier()
        if stop == "C":
            break

        with ExitStack() as es:
            TW = 128
            w2s = sb(es, "w2s", [64, 384])
            a2s = sb(es, "a2s", [64, 384])
            g2s = sb(es, "g2s", [128, 384])
            DMA(w2s[:, :], PR["rwkv_w2"][l], [], ["w2s"])
            DMA(a2s[:, :], PR["rwkv_a2"][l], [], ["a2s"])
            DMA(g2s[:, :], PR["rwkv_g2"][l], [], ["g2s"])
            pw0 = colvec(es, "pw0", PR["rwkv_w0"][l], 6, p=64)
            pa0 = colvec(es, "pa0", PR["rwkv_a0"][l], 6, p=64)
            pkk = colvec(es, "pkk", PR["rwkv_k_k"][l], 6, p=64)
            pka = colvec(es, "pka", PR["rwkv_k_a"][l], 6, p=64)
            prk = colvec(es, "prk", PR["rwkv_r_k"][l].rearrange("h d -> (h d)"), 6, p=64)
            plg = colvec(es, "plg", PR["rwkv_lnx_g"][l], 6, p=64)
            plb = colvec(es, "plb", PR["rwkv_lnx_b"][l], 6, p=64)
            pok = sb(es, "pok", [64, 6])
            TS("dve", pok[:, :], pka[:, :], -1.0, 1.0, ALU.mult, ALU.add, ["pka"], ["pok"])
            i64 = ident[0:64, 0:64]
            o64 = ones[0:64, 0:64]
            rowm = cst[:, OF["rowm"]:OF["rowm"] + 2]
            Mst = sb(es, "Mst", [64, 2, 6, 64])
            sc.op("dve", lambda e: e.memset(Mst[:, 0, :, :], 0.0), [], [("Mst", 0)])
            mcur = 0
            RhatT = sb(es, "RhatT", [64, 6, TW])
            Y0T = sb(es, "Y0T", [64, 6, TW])
            yT = sb(es, "yT", [64, 6, TW])
            GT = sb(es, "GT", [64, 6, 2, 64])
            Hm = sb(es, "Hm", [64, 6, 2, 64])
            bon = sb(es, "bon", [64, 6, TW])
            gal = sb(es, "gal", [64, 6, TW])
            ARh = sb(es, "ARh", [64, 6, 2, TW]); BKh = sb(es, "BKh", [64, 6, 2, TW]); BPh = sb(es, "BPh", [64, 6, 2, TW])
            vvh = sb(es, "vvh", [64, 6, TW]); pC = sb(es, "pC", [64, 6, 2])
            wd = sb(es, "wd", [64, 2, TW]); ad = sb(es, "ad", [64, 2, TW]); gd = sb(es, "gd", [128, 2, TW])
            rr = sb(es, "rr", [64, 2, TW]); kq = sb(es, "kq", [64, 2, TW])
            sig = sb(es, "sig", [64, TW]); cum = sb(es, "cum", [64, TW]); cpv = sb(es, "cpv", [64, TW])
            epos = sb(es, "epos", [64, TW]); eneg = sb(es, "eneg", [64, TW]); eprv = sb(es, "eprv", [64, TW]); eend = sb(es, "eend", [64, TW])
            nbc = sb(es, "nbc", [64, 2])
            aa = sb(es, "aa", [64, TW]); kk = sb(es, "kk", [64, TW]); kk2 = sb(es, "kk2", [64, TW]); rn = sb(es, "rn", [64, TW])
            kka = sb(es, "kka", [64, TW]); fac = sb(es, "fac", [64, TW]); kp = sb(es, "kp", [64, TW]); rkr = sb(es, "rkr", [64, TW])
            tm = sb(es, "tm", [128, 6, 256]); Eb = sb(es, "Eb", [128, 6, 512]); YS = sb(es, "YS", [128, 6, 256])
            Lb = sb(es, "Lb", [128, 6, 2, 384]); B2 = sb(es, "B2", [128, 6, 128]); K2 = sb(es, "K2", [128, 6, 128])
            yc = sb(es, "yc", [64, 2, TW]); ysq = sb(es, "ysq", [64, 2, TW]); yrs = sb(es, "yrs", [64, 2, TW])
            obuf = sb(es, "obuf", [64, 2, TW])
            hi = 0
            for tt in range(S // TW):
                tsl = slice(tt * TW, (tt + 1) * TW)
                ws = tt % 2
                DMA(wd[:, ws, :], rwkvT[1152:1216, tsl], [], [("wd", ws)])
                DMA(ad[:, ws, :], rwkvT[1216:1280, tsl], [], [("ad", ws)])
                DMA(gd[:, ws, :], rwkvT[1280:1408, tsl], [], [("gd", ws)])
                ACT(wd[:, ws, :], wd[:, ws, :], AF.Tanh, [("wd", ws)], [("wd", ws)])
                ACT(gd[:, ws, :], gd[:, ws, :], AF.Sigmoid, [("gd", ws)], [("gd", ws)])
                for h in range(H):
                    hc = slice(h * 64, (h + 1) * 64)
                    hs = hi % 2
                    hi += 1
                    krr, kkq = ("rr", hs), ("kq", hs)
                    DMA(rr[:, hs, :], rwkvT[h * 64:(h + 1) * 64, tsl], [], [krr])
                    DMA(kq[:, hs, :], rwkvT[384 + h * 64:384 + (h + 1) * 64, tsl], [], [kkq])
                    DMA(vvh[:, h, :], rwkvT[768 + h * 64:768 + (h + 1) * 64, tsl], [], [("vv", h)])
                    MM(ps[0][0:64, 0:TW], w2s[:, hc], wd[:, ws, :], True, True, ["w2s", ("wd", ws)], [("ps", 0)])
                    ACT(sig[:, :], ps[0][0:64, 0:TW], AF.Sigmoid, [("ps", 0), "pw0"], ["sig"], bias=pw0[:, h:h + 1])
                    sc.op("dve", lambda e: e.tensor_tensor_scan(out=cum[:, :], data0=resetm[0:64, 0:TW], data1=sig[:, :], initial=0.0, op0=ALU.mult, op1=ALU.add), ["sig", "cst"], ["cum"])
                    TT("dve", cpv[:, :], cum[:, :], sig[:, :], ALU.subtract, ["cum", "sig"], ["cpv"])
                    ACT(epos[:, :], cum[:, :], AF.Exp, ["cum"], ["epos"], scale=-C0)
                    ACT(eneg[:, :], cum[:, :], AF.Exp, ["cum"], ["eneg"], scale=C0)
                    ACT(eprv[:, :], cpv[:, :], AF.Exp, ["cpv"], ["eprv"], scale=-C0)
                    cum3 = cum[:, :].rearrange("p (c t) -> p c t", c=2)
                    epos3 = epos[:, :].rearrange("p (c t) -> p c t", c=2)
                    TS("dve", nbc[:, :], cum3[:, :, 63], -C0, None, ALU.mult, None, ["cum"], ["nbc"])
                    CP("pool", pC[:, h, :], epos3[:, :, 63], ["epos"], [("pC", h)])
                    for c in range(2):
                        ACT(eend[:, c * 64:(c + 1) * 64], cum[:, c * 64:(c + 1) * 64], AF.Exp, ["cum", "nbc"], ["eend"], scale=C0, bias=nbc[:, c:c + 1])
                    MM(ps[0][0:64, 128:128 + TW], a2s[:, hc], ad[:, ws, :], True, True, ["a2s", ("ad", ws)], [("ps", 0)])
                    ACT(aa[:, :], ps[0][0:64, 128:128 + TW], AF.Sigmoid, [("ps", 0), "pa0"], ["aa"], bias=pa0[:, h:h + 1])
                    MM(ps[1][0:64, 0:TW], g2s[:, hc], gd[:, ws, :], True, True, ["g2s", ("gd", ws)], [("ps", 1)])
                    CP("act", gal[:, h, :], ps[1][0:64, 0:TW], [("ps", 1)], [("gal", h)])
                    TS("dve", kk[:, :], kq[:, hs, :], pkk[:, h:h + 1], None, ALU.mult, None, [kkq, "pkk"], ["kk"])
                    ACT(kk2[:, :], kk[:, :], AF.Square, ["kk"], ["kk2"])
                    MM(ps[1][0:64, 128:128 + TW], o64, kk2[:, :], True, True, ["cst", "kk2"], [("ps", 1)])
                    ACT(rn[:, :], ps[1][0:64, 128:128 + TW], AF.Sqrt, [("ps", 1)], ["rn"])
                    TS("dve", rn[:, :], rn[:, :], 1e-12, None, ALU.max, None, ["rn"], ["rn"])
                    RECIP(rn[:, :], rn[:, :], ["rn"], ["rn"])
                    TT("dve", kk[:, :], kk[:, :], rn[:, :], ALU.mult, ["kk", "rn"], ["kk"])
                    TT("dve", kka[:, :], kk[:, :], aa[:, :], ALU.mult, ["kk", "aa"], ["kka"])
                    TS("dve", fac[:, :], aa[:, :], pka[:, h:h + 1], pok[:, h:h + 1], ALU.mult, ALU.add, ["aa", "pka", "pok"], ["fac"])
                    TT("dve", kp[:, :], kq[:, hs, :], fac[:, :], ALU.mult, [kkq, "fac"], ["kp"])
                    STT(rkr[:, :], rr[:, hs, :], prk[:, h:h + 1], kp[:, :], ALU.mult, ALU.mult, [krr, "prk", "kp"], ["rkr"])
                    MM(ps[1][0:64, 256:256 + TW], o64, rkr[:, :], True, True, ["cst", "rkr"], [("ps", 1)])
                    TT("dve", bon[:, h, :], ps[1][0:64, 256:256 + TW], vvh[:, h, :], ALU.mult, [("ps", 1), ("vv", h)], [("bon", h)])
                    STT(ARh[:, h, 0, :], kk[:, :], -1.0, eprv[:, :], ALU.mult, ALU.mult, ["kk", "eprv"], [("AR0", h)])
                    TT("dve", ARh[:, h, 1, :], rr[:, hs, :], epos[:, :], ALU.mult, [krr, "epos"], [("AR1", h)])
                    TT("pool", BKh[:, h, 0, :], kka[:, :], eneg[:, :], ALU.mult, ["kka", "eneg"], [("BK0", h)])
                    TT("pool", BKh[:, h, 1, :], kp[:, :], eneg[:, :], ALU.mult, ["kp", "eneg"], [("BK1", h)])
                    TT("pool", BPh[:, h, 0, :], kka[:, :], eend[:, :], ALU.mult, ["kka", "eend"], [("BP", h)])
                    TT("pool", BPh[:, h, 1, :], kp[:, :], eend[:, :], ALU.mult, ["kp", "eend"], [("BP", h)])
                HS = range(H)
                bk = lambda h: ps[2 + h]
                bkk = lambda h: ("ps", 2 + h)
                for h in HS:
                    for i, (src, kx) in enumerate([(ARh[:, h, 0, :], ("AR0", h)), (BPh[:, h, 0, :], ("BP", h)), (BPh[:, h, 1, :], ("BP", h)), (vvh[:, h, :], ("vv", h))]):
                        TR(bk(h)[:, i * 64:(i + 1) * 64], src, i64, [kx, "cst"], [bkk(h)])
                for h in HS:
                    CP("act", tm[:, h, :], bk(h)[:, 0:256], [bkk(h)], [("tm", h)])
                for h in HS:
                    MM(bk(h)[:, 0:256], BKh[:, h, 0, :], ARh[:, h, :, :], True, True, [("BK0", h), ("AR0", h), ("AR1", h)], [bkk(h)])
                    MM(bk(h)[:, 256:384], ARh[:, h, 0, :], BKh[:, h, 0, :], True, True, [("BK0", h), ("AR0", h)], [bkk(h)])
                for h in HS:
                    TT("dve", Eb[:, h, 0:384], bk(h)[:, 0:384], mE, ALU.mult, [bkk(h), "cst"], [("E", h, "a")])
                for h in HS:
                    MM(bk(h)[:, 0:256], BKh[:, h, 1, :], ARh[:, h, :, :], True, True, [("BK1", h), ("AR0", h), ("AR1", h)], [bkk(h)])
                for h in HS:
                    TT("dve", YS[:, h, :], bk(h)[:, 0:256], mY, ALU.mult, [bkk(h), "cst"], [("YS", h)])
                for h in HS:
                    MM(bk(h)[:, 256:320], YS[:, h, 0:128], tm[:, h, 192:256], True, True, [("YS", h), ("tm", h)], [bkk(h)])
                    CP("pool", Eb[:, h, 384:448], tm[:, h, 0:64], [("tm", h)], [("E", h, "b")])
                for h in HS:
                    CP("act", Eb[:, h, 448:512], bk(h)[:, 256:320], [bkk(h)], [("E", h, "c")])
                for lev in range(6):
                    def views(h):
                        if lev == 0:
                            return (Eb[:, h, 0:128], Eb[:, h, 256:384], Eb[:, h, 256:512], Eb[:, h, 384:512],
                                    [("E", h, "a"), ("E", h, "b"), ("E", h, "c")])
                        Lp = Lb[:, h, (lev - 1) % 2, :]
                        return (Lp[:, 0:128], Lp[:, 128:256], Lp[:, 128:384], Lp[:, 256:384],
                                [("L", h, (lev - 1) % 2, "p"), ("L", h, (lev - 1) % 2, "z")])
                    for h in HS:
                        PT_, P_, PZ_, Z_, rk = views(h)
                        MM(bk(h)[:, 128:384], PT_, PZ_, True, True, rk, [bkk(h)])
                        if lev < 5:
                            MM(bk(h)[:, 0:128], P_, PT_, True, True, rk, [bkk(h)])
                    for h in HS:
                        PT_, P_, PZ_, Z_, rk = views(h)
                        Ln = Lb[:, h, lev % 2, :]
                        TT("dve", Ln[:, 256:384], bk(h)[:, 256:384], Z_, ALU.add, [bkk(h)] + rk, [("L", h, lev % 2, "z")])
                        if lev < 5:
                            CP("act", Ln[:, 0:256], bk(h)[:, 0:256], [bkk(h)], [("L", h, lev % 2, "p")])
                for h in HS:
                    for hf in range(2):
                        TS("pool", B2[:, h, hf * 64:(hf + 1) * 64], tm[:, h, 64:128], rowm[:, hf:hf + 1], None, ALU.mult, None, [("tm", h), "cst"], [("B2", h)])
                        TS("pool", K2[:, h, hf * 64:(hf + 1) * 64], tm[:, h, 128:192], rowm[:, hf:hf + 1], None, ALU.mult, None, [("tm", h), "cst"], [("K2", h)])
                for h in HS:
                    Lf = Lb[:, h, 1, :]
                    kLf = ("L", h, 1, "z")
                    W_, U0_ = Lf[:, 256:320], Lf[:, 320:384]
                    b_ = bk(h)
                    MM(b_[0:64, 0:128], W_, B2[:, h, :], True, True, [kLf, ("B2", h)], [bkk(h)])
                    for hf in range(2):
                        MM(b_[0:64, 128 + hf * 64:128 + (hf + 1) * 64], K2[:, h, hf * 64:(hf + 1) * 64], tm[:, h, 192:256], True, False, [("K2", h), ("tm", h)], [bkk(h)])
                        MM(b_[0:64, 128 + hf * 64:128 + (hf + 1) * 64], B2[:, h, hf * 64:(hf + 1) * 64], U0_, False, True, [("B2", h), kLf], [bkk(h)])
                    MM(b_[0:64, 256:384], W_, Eb[:, h, 128:256], True, True, [kLf, ("E", h, "a")], [bkk(h)])
                    MM(b_[0:64, 384:512], tm[:, h, 192:256], YS[:, h, 128:256], True, False, [("tm", h), ("YS", h)], [bkk(h)])
                    MM(b_[0:64, 384:512], U0_, Eb[:, h, 128:256], False, True, [kLf, ("E", h, "a")], [bkk(h)])
                for h in HS:
                    b_ = bk(h)
                    for hf in range(2):
                        STT(GT[:, h, hf, :], i64, pC[:, h, hf:hf + 1], b_[0:64, hf * 64:(hf + 1) * 64], ALU.mult, ALU.add,
                            [bkk(h), ("pC", h), "cst"], [("GT", h)])
                    TT("dve", RhatT[:, h, :], b_[0:64, 256:384], ARh[:, h, 1, :], ALU.add, [bkk(h), ("AR1", h)], [("Rhat", h)])
                for h in HS:
                    b_ = bk(h)
                    CP("act", Hm[:, h, :, :], b_[0:64, 128:256].rearrange("p (c i) -> p c i", c=2), [bkk(h)], [("Hm", h)])
                    CP("act", Y0T[:, h, :], b_[0:64, 384:512], [bkk(h)], [("Y0T", h)])
                allh = lambda nm: [(nm, h) for h in range(H)]
                for c in range(2):
                    csl = slice(c * 64, (c + 1) * 64)
                    mnew = 1 - mcur
                    for h in range(H):
                        MM(ps[0][0:64, h * 64:(h + 1) * 64], Mst[:, mcur, h, :], RhatT[:, h, csl], True, True, [("Mst", mcur), ("Rhat", h)], [("ps", 0)])
                    for h in range(H):
                        MM(ps[1][0:64, h * 64:(h + 1) * 64], GT[:, h, c, :], Mst[:, mcur, h, :], True, True, [("Mst", mcur), ("GT", h)], [("ps", 1)])
                    TT("dve", Mst[:, mnew, :, :], ps[1][0:64, 0:384].rearrange("p (h i) -> p h i", h=6), Hm[:, :, c, :], ALU.add,
                       [("ps", 1)] + allh("Hm"), [("Mst", mnew)])
                    TT("dve", yT[:, :, csl], ps[0][0:64, 0:384].rearrange("p (h t) -> p h t", h=6), Y0T[:, :, csl], ALU.add,
                       [("ps", 0)] + allh("Y0T"), [("yT", c)])
                    mcur = mnew
                ally = [("yT", c) for c in range(2)]
                for h in range(H):
                    osl = h % 2
                    kyc, kysq, kyrs = ("yc", osl), ("ysq", osl), ("yrs", osl)
                    pb = ps[osl]
                    pk = ("ps", osl)
                    MM(pb[0:64, 0:TW], o64, yT[:, h, :], True, True, ["cst"] + ally, [pk])
                    STT(yc[:, osl, :], pb[0:64, 0:TW], -1.0 / 64, yT[:, h, :], ALU.mult, ALU.add, [pk] + ally, [kyc])
                    ACT(ysq[:, osl, :], yc[:, osl, :], AF.Square, [kyc], [kysq])
                    MM(pb[0:64, 128:128 + TW], o64, ysq[:, osl, :], True, True, ["cst", kysq], [pk])
                    ACT(yrs[:, osl, :], pb[0:64, 128:128 + TW], AF.Sqrt, [pk, "epsc"], [kyrs], scale=1.0 / 64, bias=epsc[0:64, 1:2])
                    RECIP(yrs[:, osl, :], yrs[:, osl, :], [kyrs], [kyrs])
                    TT("dve", yc[:, osl, :], yc[:, osl, :], yrs[:, osl, :], ALU.mult, [kyc, kyrs], [kyc])
                    TS("dve", yc[:, osl, :], yc[:, osl, :], plg[:, h:h + 1], plb[:, h:h + 1], ALU.mult, ALU.add, [kyc, "plg", "plb"], [kyc])
                    TT("pool", yc[:, osl, :], yc[:, osl, :], bon[:, h, :], ALU.add, [kyc, ("bon", h)], [kyc])
                    TT("pool", obuf[:, osl, :], yc[:, osl, :], gal[:, h, :], ALU.mult, [kyc, ("gal", h)], [("obuf", osl)])
                    DMA(catT[h * 64:(h + 1) * 64, tsl], obuf[:, osl, :], [("obuf", osl)], [("cat", "a", h, tt)])
            sc.barrier()
        if stop == "B":
            break

        with ExitStack() as es:
            wO = sb(es, "wO", [128, 2, 8, 128])
            catb = sb(es, "catb", [128, 2, 8, 512])
            mixb = sb(es, "mixb", [128, 8, 512])
            sq = sb(es, "sq", [128, 2, 512])
            rstd = sb(es, "rstd", [128, 2, 512])
            gP = colvec(es, "gP", PR["post_mix_g"][l], 8)
            w_out_l = PR["w_out"][l].rearrange("(k p) c -> p k c", p=128)
            cat_v = catT.rearrange("(k p) t -> p k t", p=128)
            wi = 0
            for tt in range(4):
                tsl = slice(tt * 512, (tt + 1) * 512)
                cs = tt % 2
                DMA(RR(catb[:, cs, :, :]), cat_v[:, :, tsl], [], [("catb", cs)], q="pool")
                for j in range(8):
                    sl = wi % 2
                    wi += 1
                    DMA(RR(wO[:, sl, :, :]), w_out_l[:, :, j * 128:(j + 1) * 128], [], [("wO", sl)], q="pool")
                    bi = j % 2
                    for k in range(8):
                        MM(ps[bi][:, :], wO[:, sl, k, :], catb[:, cs, k, :], k == 0, k == 7, [("wO", sl), ("catb", cs)], [("ps", bi)], r=True)
                    CP("act" if j % 2 else "dve", mixb[:, j, :], ps[bi][:, :], [("ps", bi)], [("mixb", j)])
                r = rms_stats(lambda k: mixb[:, k, :], lambda k: ("mixb", k), tt, (sq, rstd), ps[2 + tt % 2], ("ps", 2 + tt % 2))
                for j in range(8):
                    STT(mixb[:, j, :], mixb[:, j, :], gP[:, j:j + 1], r, ALU.mult, ALU.mult, [("mixb", j), ("rstd", tt % 2), "gP"], [("mixb", j)])
                    TT("dve", xT[:, j, tsl], xT[:, j, tsl], mixb[:, j, :], ALU.add, [("mixb", j), ("x", j, tt)], [("x", j, tt)])
            sc.barrier()
        if stop == "E":
            break

        with ExitStack() as es:
            hb = sb(es, "hb", [128, 8, 512])
            actb = sb(es, "actb", [128, NFF, 512])
            wG = sb(es, "wG", [128, 2, 2, 8, 128])
            wD = sb(es, "wD", [128, 2, NFF, 128])
            sq = sb(es, "sq", [128, 2, 512])
            rstd = sb(es, "rstd", [128, 2, 512])
            sil = sb(es, "sil", [128, 2, 512])
            gF = colvec(es, "gF", PR["pre_ffn_g"][l], 8)
            gQ = colvec(es, "gQ", PR["post_ffn_g"][l], 8)
            w_fi = PR["w_ffn_in"][l].rearrange("(k p) c -> p k c", p=128)
            w_fo = PR["w_ffn_out"][l].rearrange("(j p) c -> p j c", p=128)
            wi = 0
            wdi = 0
            for tt in range(4):
                tsl = slice(tt * 512, (tt + 1) * 512)
                r = rms_stats(lambda k: xT[:, k, tsl], lambda k: ("x", k, tt), tt, (sq, rstd), ps[6 + tt % 2], ("ps", 6 + tt % 2))
                for k in range(8):
                    STT(RR(hb[:, k, :]), xT[:, k, tsl], gF[:, k:k + 1], r, ALU.mult, ALU.mult, [("x", k, tt), ("rstd", tt % 2), "gF"], [("hb", k)])
                for j in range(NFF):
                    sl = wi % 2
                    wi += 1
                    DMA(RR(wG[:, sl, 0, :, :]), w_fi[:, :, j * 128:(j + 1) * 128], [], [("wG", sl, 0)], q="pool")
                    DMA(RR(wG[:, sl, 1, :, :]), w_fi[:, :, DFF + j * 128:DFF + (j + 1) * 128], [], [("wG", sl, 1)], q="pool")
                    bg, bu = (j % 2) * 2, (j % 2) * 2 + 1
                    for k in range(8):
                        MM(ps[bg][:, :], wG[:, sl, 0, k, :], hb[:, k, :], k == 0, k == 7, [("wG", sl, 0), ("hb", k)], [("ps", bg)], r=True)
                    for k in range(8):
                        MM(ps[bu][:, :], wG[:, sl, 1, k, :], hb[:, k, :], k == 0, k == 7, [("wG", sl, 1), ("hb", k)], [("ps", bu)], r=True)
                    ACT(sil[:, j % 2, :], ps[bg][:, :], AF.Silu, [("ps", bg)], [("sil", j % 2)])
                    TT("dve", RR(actb[:, j, :]), ps[bu][:, :], sil[:, j % 2, :], ALU.mult, [("ps", bu), ("sil", j % 2)], [("actb", j)])
                allact = [("actb", j) for j in range(NFF)]
                for jo in range(8):
                    sl = wdi % 2
                    wdi += 1
                    DMA(RR(wD[:, sl, :, :]), w_fo[:, :, jo * 128:(jo + 1) * 128], [], [("wD", sl)], q="pool")
                    bi = 4 + jo % 2
                    for j in range(NFF):
                        MM(ps[bi][:, :], wD[:, sl, j, :], actb[:, j, :], j == 0, j == NFF - 1, [("wD", sl), ("actb", j)], [("ps", bi)], r=True)
                    CP("act" if jo % 2 else "dve", RR(hb[:, jo, :]), ps[bi][:, :], [("ps", bi)], [("hb", jo)])
                r = rms_stats(lambda k: hb[:, k, :], lambda k: ("hb", k), tt + 1, (sq, rstd), ps[6 + (tt + 1) % 2], ("ps", 6 + (tt + 1) % 2))
                for j in range(8):
                    STT(RR(hb[:, j, :]), hb[:, j, :], gQ[:, j:j + 1], r, ALU.mult, ALU.mult, [("hb", j), ("rstd", (tt + 1) % 2), "gQ"], [("hb", j)])
                    TT("dve", xT[:, j, tsl], xT[:, j, tsl], hb[:, j, :], ALU.add, [("hb", j), ("x", j, tt)], [("x", j, tt)])
            sc.barrier()
        if OPTS.get("xdbg") == l:
            for k in range(8):
                DMA(xdbg[k * 128:(k + 1) * 128, :], xT[:, k, :], [("x", k, tt) for tt in range(4)], [("xdbg", k)])

    if stop in (None, 'setup'):
        with ExitStack() as es:
            yo = sb(es, "yo", [128, 2, D])
            for t in range(OPTS.get('nt', 16)):
                sl = t % 2 if not OPTS.get('sl0') else 0
                for g in range(2):
                    bank = ps[(t * 2 + g) % 4]
                    bk = ("ps", (t * 2 + g) % 4)
                    for kk in range(4):
                        k = g * 4 + kk
                        TR(bank[:, kk * 128:(kk + 1) * 128], xT[:, k, t * 128:(t + 1) * 128], ident, [("x", k, t // 4), "cst"], [bk])
                    CP("act" if g else "dve", yo[:, sl, g * 512:(g + 1) * 512], bank[:, :], [bk], [("yo", sl, g)])
                DMA(y_out[t * 128:(t + 1) * 128, :], yo[:, sl, :], [("yo", sl, 0), ("yo", sl, 1)], [("y", t)])
    sc.barrier()
    sc.emit()
    glob.close()
    return nc


_NC_CACHE = {}


def kernel(**inputs):
    if "nc" not in _NC_CACHE:
        _NC_CACHE["nc"] = build()
    nc = _NC_CACHE["nc"]
    x = np.ascontiguousarray(np.asarray(inputs["x"], dtype=np.float32))
    base = {n: np.ascontiguousarray(np.asarray(inputs[n], dtype=np.float32)) for n in PARAM_NAMES}
    base["cst"] = CONSTS["cst"]
    base["mobac"] = CONSTS["mobac"]
    base["blkind"] = CONSTS["blkind"]
    in_maps = []
    for b in range(8):
        m = dict(base)
        m["x"] = x[b]
        in_maps.append(m)
    res = run_bass_kernel_spmd(nc, in_maps, core_ids=list(range(8)))
    return np.stack([np.asarray(r["y"], dtype=np.float32) for r in res.results], 0)
```

```python
import numpy as np
from contextlib import ExitStack
import concourse.bass as bass
import concourse.mybir as mybir
from concourse.bass_utils import run_bass_kernel_spmd

F32 = mybir.dt.float32
F32R = mybir.dt.float32r
AF = mybir.ActivationFunctionType
ALU = mybir.AluOpType
AX = mybir.AxisListType

S = 2048
D = 1024
L = 2
DFF = 2816
NFF = 22
H = 6
C0 = float(np.exp(-0.5))
BIG = 30000.0


OPTS = {}


class Sched:
    def __init__(self, nc, n_dma=40):
        self.nc = nc
        self.names = ["pe", "act", "dve", "pool", "sp"]
        self.sem = {e: nc.alloc_semaphore("s_" + e) for e in ["pe", "act", "dve", "pool"]}
        self.cnt = {e: 0 for e in self.sem}
        self.dsem = [nc.alloc_semaphore("d%d" % i) for i in range(n_dma)]
        self.dcnt = [0] * n_dma
        self.drr = 0
        self.q = {e: [] for e in self.names}
        self.clock = {e: {} for e in self.names}
        self.evclock = {}
        self.evorder = {}
        self.nev = 0
        self.lastw = {}
        self.readers = {}

    def _deps(self, reads, writes):
        deps = {}

        def add(k, v):
            if deps.get(k, 0) < v:
                deps[k] = v

        for r in reads:
            ev = self.lastw.get(r)
            if ev is not None:
                add(*ev)
        for w in writes:
            ev = self.lastw.get(w)
            if ev is not None:
                add(*ev)
            for k, v in self.readers.get(w, {}).items():
                add(k, v)
        return deps

    def _commit(self, ev, reads, writes):
        k, v = ev
        for r in reads:
            d = self.readers.setdefault(r, {})
            if d.get(k, 0) < v:
                d[k] = v
        for w in writes:
            self.lastw[w] = ev
            self.readers[w] = {}

    def _waits(self, eng, deps):
        clk = self.clock.setdefault(eng, {})
        waits = []
        for k, v in sorted(deps.items(), key=lambda kv: -self.evorder.get(kv, 0)):
            if eng == "pe" and k == ("e", "pe"):
                continue
            if clk.get(k, 0) >= v:
                continue
            waits.append((k, v))
            for k2, v2 in self.evclock.get((k, v), {}).items():
                if clk.get(k2, 0) < v2:
                    clk[k2] = v2
            clk[k] = v
        return waits

    def op(self, eng, fn, reads=(), writes=()):
        banks = {("psx", k[1]) for k in list(reads) + list(writes) if isinstance(k, tuple) and k and k[0] == "ps"}
        if banks:
            writes = list(writes) + list(banks)
        deps = self._deps(reads, writes)
        waits = self._waits(eng, deps)
        self.cnt[eng] += 1
        ev = (("e", eng), self.cnt[eng])
        self.evclock[ev] = dict(self.clock[eng])
        self.nev += 1
        self.evorder[ev] = self.nev
        self.q[eng].append((waits, fn, "e"))
        self._commit(ev, reads, writes)

    def dma(self, qeng, out, in_, reads=(), writes=(), **kw):
        deps = self._deps(reads, writes)
        idx = self.drr
        self.drr = (self.drr + 1) % len(self.dsem)
        if self.dcnt[idx] > 0:
            k = ("d", idx)
            deps[k] = max(deps.get(k, 0), self.dcnt[idx])
        waits = self._waits(qeng, deps)
        self.dcnt[idx] += 16
        ev = (("d", idx), self.dcnt[idx])
        self.evclock[ev] = dict(self.clock[qeng])
        self.nev += 1
        self.evorder[ev] = self.nev
        self.q[qeng].append((waits, lambda e: e.dma_start(out=out, in_=in_, **kw), idx))
        self._commit(ev, reads, writes)

    def barrier(self):
        allev = [(("e", e), c) for e, c in self.cnt.items() if c > 0]
        allev += [(("d", i), c) for i, c in enumerate(self.dcnt) if c > 0]
        for eng in self.names:
            waits = self._waits(eng, dict(allev))
            if waits:
                self.q[eng].append((waits, None, None))
        self.lastw = {}
        self.readers = {}
        self.evclock = {}

    def emit(self):
        nc = self.nc
        engs = {"pe": "tensor", "act": "scalar", "dve": "vector", "pool": "gpsimd", "sp": "sync"}
        with nc.Block() as block:
            for name in self.names:
                def body(eng, name=name):
                    for waits, fn, kind in self.q[name]:
                        emb = None
                        if fn is not None and waits and kind == "e" and not OPTS.get("noemb"):
                            emb = waits[-1]
                            waits = waits[:-1]
                        for k, v in waits:
                            s = self.sem[k[1]] if k[0] == "e" else self.dsem[k[1]]
                            eng.wait_ge(s, v)
                        if fn is None:
                            continue
                        ins = fn(eng)
                        if emb is not None:
                            k, v = emb
                            ins._wait_ge(self.sem[k[1]] if k[0] == "e" else self.dsem[k[1]], v)
                        if kind == "e":
                            ins.then_inc(self.sem[name], 1)
                        else:
                            ins.then_inc(self.dsem[kind], 16)
                getattr(block, engs[name])(body)


def make_consts():
    c = {}
    i128 = np.arange(128)
    blk = (i128[:, None] // 64) == (i128[None, :] // 64)
    ident = np.eye(128, dtype=np.float32)
    ones = np.ones((128, 128), np.float32)
    SL = ((i128[:, None] > i128[None, :]) & blk).astype(np.float32)
    SU = ((i128[:, None] < i128[None, :]) & blk).astype(np.float32)
    IU = ((i128[:, None] <= i128[None, :]) & blk).astype(np.float32)
    IUfull = (i128[:, None] <= i128[None, :]).astype(np.float32)
    idst = np.concatenate([np.eye(64), np.eye(64)], 0).astype(np.float32)
    reset = np.ones((128, 256), np.float32)
    reset[:, ::64] = 0.0
    rowm = np.zeros((128, 2), np.float32)
    rowm[:64, 0] = 1.0
    rowm[64:, 1] = 1.0
    q512 = np.arange(512)
    cm = np.stack([(q512[None, :] >= (j * 128 + i128[:, None])).astype(np.float32) for j in range(4)], 1)
    parts = [ident, ones, SU, IU, SL, SU, IU, IUfull, idst, reset, rowm, cm.reshape(128, 2048)]
    offs = {}
    o = 0
    for nm, p in zip(["ident", "ones", "mE", "_1", "_2", "mY", "_3", "iuf", "idst", "reset", "rowm", "cm"], parts):
        offs[nm] = o
        o += p.shape[1]
    c["cst"] = np.ascontiguousarray(np.concatenate(parts, 1))
    c["offs"] = offs
    mb = np.zeros((128, 3, 16, 6, 8), np.float32)
    for t in range(16):
        b = t // 2
        for n in range(8):
            mb[:, 0, t, :, n] = 0.0 if n < b else -1e30
            mb[:, 1, t, :, n] = 1.0 if n < b else 0.0
            mb[:, 2, t, :, n] = 1.0 if n == b else 0.0
    c["mobac"] = mb.reshape(128, 3 * 16 * 48)
    bi = np.zeros((8, S), np.float32)
    for n in range(8):
        bi[n, n * 256:(n + 1) * 256] = 1.0
    c["blkind"] = bi
    return c


CONSTS = make_consts()
PARAM_NAMES = ["pre_mix_g", "w_in", "rwkv_mu", "rwkv_w0", "rwkv_w2", "rwkv_a0", "rwkv_a2", "rwkv_g2",
               "rwkv_k_k", "rwkv_k_a", "rwkv_r_k", "rwkv_lnx_g", "rwkv_lnx_b", "gmlp_ln_g", "gmlp_ln_b",
               "gmlp_w_s", "gmlp_b_s", "w_out", "post_mix_g", "pre_ffn_g", "w_ffn_in", "w_ffn_out", "post_ffn_g"]
PARAM_SHAPES = {"pre_mix_g": (L, D), "w_in": (L, D, 3072), "rwkv_mu": (L, 1408), "rwkv_w0": (L, 384),
                "rwkv_w2": (L, 64, 384), "rwkv_a0": (L, 384), "rwkv_a2": (L, 64, 384), "rwkv_g2": (L, 128, 384),
                "rwkv_k_k": (L, 384), "rwkv_k_a": (L, 384), "rwkv_r_k": (L, 6, 64), "rwkv_lnx_g": (L, 384),
                "rwkv_lnx_b": (L, 384), "gmlp_ln_g": (L, 256), "gmlp_ln_b": (L, 256), "gmlp_w_s": (L, 4, 128, 128),
                "gmlp_b_s": (L, 4, 128), "w_out": (L, D, D), "post_mix_g": (L, D), "pre_ffn_g": (L, D),
                "w_ffn_in": (L, D, 2 * DFF), "w_ffn_out": (L, DFF, D), "post_ffn_g": (L, D)}


def build(dbg=None, nlayers=L, stop=None):
    dbg = dbg or []
    nc = bass.Bass("TRN2", target_bir_lowering=False)
    sc = Sched(nc)
    OF = CONSTS["offs"]

    def dram(name, shape, kind="Internal"):
        if name in dbg:
            kind = "ExternalOutput"
        return nc.dram_tensor(name, list(shape), F32, kind=kind).ap()

    x_in = dram("x", [S, D], "ExternalInput")
    y_out = dram("y", [S, D], "ExternalOutput")
    cst_d = dram("cst", CONSTS["cst"].shape, "ExternalInput")
    if not OPTS.get("noparams"):
        PR = {n: dram(n, PARAM_SHAPES[n], "ExternalInput") for n in PARAM_NAMES}
        mobac_d = dram("mobac", CONSTS["mobac"].shape, "ExternalInput")
        blkind_d = dram("blkind", CONSTS["blkind"].shape, "ExternalInput")
    if OPTS.get("noscratch"):
        glob_scr = None
    rwkvT = dram("rwkvT", [1408, S]) if not OPTS.get("noscratch") else None
    qkT = dram("qkT", [768, S]) if not OPTS.get("noscratch") else None
    uT = dram("uT", [256, S]) if not OPTS.get("noscratch") else None
    vm_tm = dram("vm_tm", [S, 384]) if not OPTS.get("noscratch") else None
    vg_tm = dram("vg_tm", [S, 256]) if not OPTS.get("noscratch") else None
    catT = dram("catT", [D, S]) if not OPTS.get("noscratch") else None
    xdbg = dram("xdbg", [D, S]) if not OPTS.get("noscratch") else None
    xD = dram("xD", [D, S])
    xD_v = xD.rearrange("(k p) t -> p k t", p=128)

    uid = [0]

    def sb(es, name, shape):
        uid[0] += 1
        return es.enter_context(nc.sbuf_tensor("%s_%d" % (name, uid[0]), list(shape), F32))

    glob = ExitStack()
    cst = sb(glob, "cst_sb", [128, CONSTS["cst"].shape[1]])
    ps = [glob.enter_context(nc.psum_tensor("ps%d" % i, [128, 512], F32)) for i in range(8)]
    ident = cst[:, OF["ident"]:OF["ident"] + 128]
    ones = cst[:, OF["ones"]:OF["ones"] + 128]
    mE = cst[:, OF["mE"]:OF["mE"] + 384]
    mY = cst[:, OF["mY"]:OF["mY"] + 256]
    iuf = cst[:, OF["iuf"]:OF["iuf"] + 128]
    idst = cst[:, OF["idst"]:OF["idst"] + 64]
    resetm = cst[:, OF["reset"]:OF["reset"] + 256]
    cm = cst[:, OF["cm"]:OF["cm"] + 2048]
    epsc = sb(glob, "epsc", [128, 4])

    def ACT(out, in_, func, reads, writes, **kw):
        sc.op("act", lambda e: e.activation(out=out, in_=in_, func=func, **kw), reads, writes)

    def RR(ap):
        return ap if OPTS.get("nor") else ap.bitcast(F32R)

    def MM(out, lhsT, rhs, start, stop, reads, writes, r=False):
        if r and not OPTS.get("nor"):
            lhsT = lhsT.bitcast(F32R)
            rhs = rhs.bitcast(F32R)
        sc.op("pe", lambda e: e.matmul(out, lhsT=lhsT, rhs=rhs, start=start, stop=stop), reads, writes)

    def TR(out, in_, idn, reads, writes):
        sc.op("pe", lambda e: e.transpose(out, in_, idn), reads, writes)

    def TT(eng, out, in0, in1, op, reads, writes):
        sc.op(eng, lambda e: e.tensor_tensor(out=out, in0=in0, in1=in1, op=op), reads, writes)

    def TS(eng, out, in0, s1, s2, op0, op1, reads, writes):
        if s2 is None:
            sc.op(eng, lambda e: e.tensor_scalar(out=out, in0=in0, scalar1=s1, scalar2=None, op0=op0), reads, writes)
        else:
            sc.op(eng, lambda e: e.tensor_scalar(out=out, in0=in0, scalar1=s1, scalar2=s2, op0=op0, op1=op1), reads, writes)

    def STT(out, in0, scalar, in1, op0, op1, reads, writes):
        sc.op("dve", lambda e: e.scalar_tensor_tensor(out=out, in0=in0, scalar=scalar, in1=in1, op0=op0, op1=op1), reads, writes)

    def CP(eng, out, in_, reads, writes):
        if eng == "act":
            sc.op("act", lambda e: e.copy(out=out, in_=in_), reads, writes)
        else:
            sc.op(eng, lambda e: e.tensor_copy(out=out, in_=in_), reads, writes)

    def RECIP(out, in_, reads, writes):
        sc.op("dve", lambda e: e.reciprocal(out=out, in_=in_), reads, writes)

    def DMA(out, in_, reads, writes, q="sp", **kw):
        sc.dma(q, out, in_, reads, writes, **kw)

    def colvec(es, name, src_1d, ncol, p=128):
        t = sb(es, name, [p, ncol])
        DMA(t[:, :], src_1d.rearrange("(c p) -> p c", p=p), [], [name], allow_slow_non_contiguous=True)
        return t

    DMA(cst[:, :], cst_d[:, :], [], ["cst"])
    sc.op("dve", lambda e: e.memset(epsc[:, 0:1], 1e-6), [], ["epsc"])
    sc.op("dve", lambda e: e.memset(epsc[:, 1:2], 64e-5), [], ["epsc"])
    sc.op("dve", lambda e: e.memset(epsc[:, 2:3], 0.0), [], ["epsc"])
    with ExitStack() as es:
        xin = sb(es, "xin", [128, 2, D])
        xs = sb(es, "xs", [128, 2, 8, 512])
        for t in range(16):
            sl = t % 2
            xsl = (t // 4) % 2
            DMA(xin[:, sl, :], x_in[t * 128:(t + 1) * 128, :], [], [("xin", sl)])
            for g in range(2):
                bank = ps[(t * 2 + g) % 4]
                bk = ("ps", (t * 2 + g) % 4)
                for kk in range(4):
                    k = g * 4 + kk
                    TR(bank[:, kk * 128:(kk + 1) * 128], xin[:, sl, k * 128:(k + 1) * 128], ident, [("xin", sl), "cst"], [bk])
                CP("act" if g else "dve", xs[:, xsl, g * 4:(g + 1) * 4, (t % 4) * 128:(t % 4 + 1) * 128],
                   bank[:, :].rearrange("p (k c) -> p k c", k=4), [bk], [("xs", xsl, t % 4, g)])
            if t % 4 == 3:
                DMA(xD_v[:, :, (t // 4) * 512:(t // 4 + 1) * 512], xs[:, xsl, :, :], [("xs", xsl, q, g) for q in range(4) for g in range(2)], [("xD", t // 4)])
        sc.barrier()

    def rms_stats(src_fn, src_keys, tt, es_tiles, pbank, pkey):
        sq, rstd = es_tiles
        for k in range(8):
            ACT(sq[:, k % 2, :], src_fn(k), AF.Square, [src_keys(k)], [("sq", k % 2)])
            MM(pbank[:, :], ones, sq[:, k % 2, :], k == 0, k == 7, [("sq", k % 2), "cst"], [pkey])
        ACT(rstd[:, tt % 2, :], pbank[:, :], AF.Sqrt, [pkey, "epsc"], [("rstd", tt % 2)], scale=1.0 / D, bias=epsc[:, 0:1])
        RECIP(rstd[:, tt % 2, :], rstd[:, tt % 2, :], [("rstd", tt % 2)], [("rstd", tt % 2)])
        return rstd[:, tt % 2, :]

    for l in range(nlayers if stop != 'setup' else 0):
        with ExitStack() as es:
            hbuf = sb(es, "hbuf", [128, 8, S])
            sq = sb(es, "sq", [128, 2, 512])
            rstd = sb(es, "rstd", [128, 2, 512])
            gA = colvec(es, "gA", PR["pre_mix_g"][l], 8)
            muA = colvec(es, "muA", PR["rwkv_mu"][l], 11)
            xa = sb(es, "xa", [128, 2, 8, 512])
            for tt in range(4):
                tsl = slice(tt * 512, (tt + 1) * 512)
                xsl = tt % 2
                DMA(xa[:, xsl, :, :], xD_v[:, :, tsl], [("xD", tt)], [("xa", xsl)])
                r = rms_stats(lambda k: xa[:, xsl, k, :], lambda k: ("xa", xsl), tt, (sq, rstd), ps[4 + tt % 2], ("ps", 4 + tt % 2))
                for k in range(8):
                    STT(RR(hbuf[:, k, tsl]), xa[:, xsl, k, :], gA[:, k:k + 1], r, ALU.mult, ALU.mult,
                        [("xa", xsl), ("rstd", tt % 2), "gA"], [("h", k, tt)])
            es_main = es
            es = ExitStack()
            wA = sb(es, "wA", [128, 2, 8, 128])
            stg = sb(es, "stg", [128, 2, S])
            stg2 = sb(es, "stg2", [128, 2, S])
            w_in_l = PR["w_in"][l].rearrange("(k p) c -> p k c", p=128)
            fm_chunks = [(c * 128, rwkvT, c * 128, True) for c in range(11)]
            fm_chunks += [(1408 + c * 128, qkT, c * 128, False) for c in range(6)]
            fm_chunks += [(2560 + c * 128, uT, c * 128, False) for c in range(2)]
            for ci, (col0, dst, row0, shift) in enumerate(fm_chunks):
                sl = ci % 2
                DMA(RR(wA[:, sl, :, :]), w_in_l[:, :, col0:col0 + 128], [], [("wA", sl)], q="pool")
                for tt in range(4):
                    tsl = slice(tt * 512, (tt + 1) * 512)
                    bi = (ci * 4 + tt) % 4
                    for k in range(8):
                        MM(ps[bi][:, :], wA[:, sl, k, :], hbuf[:, k, tsl], k == 0, k == 7,
                           [("wA", sl), ("h", k, tt)], [("ps", bi)], r=True)
                    CP("act" if tt % 2 else "dve", stg[:, sl, tsl], ps[bi][:, :], [("ps", bi)], [("stg", sl, tt)])
                allst = [("stg", sl, tt) for tt in range(4)]
                if shift:
                    TT("pool", stg2[:, sl, 1:S], stg[:, sl, 0:S - 1], stg[:, sl, 1:S], ALU.subtract, allst, [("stg2", sl)])
                    TS("pool", stg2[:, sl, 0:1], stg[:, sl, 0:1], -1.0, None, ALU.mult, None, allst, [("stg2", sl)])
                    STT(stg2[:, sl, :], stg2[:, sl, :], muA[:, ci:ci + 1], stg[:, sl, :], ALU.mult, ALU.add,
                        allst + [("stg2", sl), "muA"], [("stg2", sl)])
                    DMA(dst[row0:row0 + 128, :], stg2[:, sl, :], [("stg2", sl)], [("dr", id(dst), row0)], q="pool")
                else:
                    DMA(dst[row0:row0 + 128, :], stg[:, sl, :], allst, [("dr", id(dst), row0)], q="pool")
            sc.barrier()
            es.close()
            es = es_main
            wB = sb(es, "wB", [128, 8, 640])
            DMA(RR(wB[:, :, 0:384]), w_in_l[:, :, 2176:2560], [], ["wB"], q="pool")
            DMA(RR(wB[:, :, 384:640]), w_in_l[:, :, 2816:3072], [], ["wB"], q="pool")
            vst = sb(es, "vst", [128, 2, 640])
            for t in range(16):
                sl = t % 2
                b0, b1 = 4 + (t % 2) * 2, 5 + (t % 2) * 2
                for k in range(8):
                    MM(ps[b0][:, 0:384], hbuf[:, k, t * 128:(t + 1) * 128], wB[:, k, 0:384], k == 0, k == 7,
                       [("h", k, t // 4), "wB"], [("ps", b0)], r=True)
                for k in range(8):
                    MM(ps[b1][:, 0:256], hbuf[:, k, t * 128:(t + 1) * 128], wB[:, k, 384:640], k == 0, k == 7,
                       [("h", k, t // 4), "wB"], [("ps", b1)], r=True)
                CP("act", vst[:, sl, 0:384], ps[b0][:, 0:384], [("ps", b0)], [("vst", sl, 0)])
                CP("dve", vst[:, sl, 384:640], ps[b1][:, 0:256], [("ps", b1)], [("vst", sl, 1)])
                DMA(vm_tm[t * 128:(t + 1) * 128, :], vst[:, sl, 0:384], [("vst", sl, 0)], [("vm", t)], q="pool")
                DMA(vg_tm[t * 128:(t + 1) * 128, :], vst[:, sl, 384:640], [("vst", sl, 1)], [("vg", t)], q="pool")
            sc.barrier()
        if stop == "A":
            break

        with ExitStack() as es:
            lng = sb(es, "lng", [128, 256])
            lnb = sb(es, "lnb", [128, 256])
            DMA(lng[:, :], PR["gmlp_ln_g"][l].partition_broadcast(128), [], ["lng"])
            DMA(lnb[:, :], PR["gmlp_ln_b"][l].partition_broadcast(128), [], ["lnb"])
            wsn = sb(es, "wsn", [128, 4, 128])
            wsT = sb(es, "wsT", [128, 4, 128])
            bsr = sb(es, "bsr", [1, 512])
            DMA(wsn[:, :, :], PR["gmlp_w_s"][l].rearrange("g t s -> t g s"), [], ["wsn"])
            DMA(bsr[:, :], PR["gmlp_b_s"][l].rearrange("g t -> (g t)").partition_broadcast(1), [], ["bsr"])
            for g in range(4):
                TR(ps[0][:, g * 128:(g + 1) * 128], wsn[:, g, :], ident, ["wsn", "cst"], [("ps", 0)])
            for g in range(4):
                TT("dve", wsT[:, g, :], ps[0][:, g * 128:(g + 1) * 128], iuf, ALU.mult, [("ps", 0), "cst"], ["wsT"])
            gu = sb(es, "gu", [128, 2, S])
            t1 = sb(es, "t1", [128, S])
            cout = sb(es, "cout", [128, 2, S])
            for pp in range(2):
                DMA(gu[:, pp, :], uT[pp * 128:(pp + 1) * 128, :], [], [("gu", pp)])
                ACT(t1[:, :], gu[:, pp, :], AF.Square, [("gu", pp)], ["t1"])
                TS("pool", t1[:, :], t1[:, :], 0.044715, 1.0, ALU.mult, ALU.add, ["t1"], ["t1"])
                TT("dve", t1[:, :], t1[:, :], gu[:, pp, :], ALU.mult, ["t1", ("gu", pp)], ["t1"])
                ACT(t1[:, :], t1[:, :], AF.Sigmoid, ["t1"], ["t1"], scale=2.0 * 0.7978845608028654)
                TT("dve", gu[:, pp, :], gu[:, pp, :], t1[:, :], ALU.mult, ["t1", ("gu", pp)], [("gu", pp)])
            vb = sb(es, "vb", [128, 2, 256])
            t2 = sb(es, "t2", [128, 2, 256])
            st6 = sb(es, "st6", [128, 2, 8])
            for c in range(16):
                sl = c % 2
                DMA(vb[:, sl, :], vg_tm[c * 128:(c + 1) * 128, :], [], [("vb", sl)])
                kv, kt = ("vb", sl), ("t2", sl)
                ACT(t2[:, sl, :], vb[:, sl, :], AF.Square, [kv], [kt])
                TS("pool", t2[:, sl, :], t2[:, sl, :], 0.044715, 1.0, ALU.mult, ALU.add, [kt], [kt])
                TT("dve", t2[:, sl, :], t2[:, sl, :], vb[:, sl, :], ALU.mult, [kt, kv], [kt])
                ACT(t2[:, sl, :], t2[:, sl, :], AF.Sigmoid, [kt], [kt], scale=2.0 * 0.7978845608028654)
                TT("dve", vb[:, sl, :], vb[:, sl, :], t2[:, sl, :], ALU.mult, [kt, kv], [kv])
                ks = ("st6", sl)
                sc.op("dve", lambda e, sl=sl: e.bn_stats(out=st6[:, sl, 0:6], in_=vb[:, sl, :]), [kv], [ks])
                sc.op("dve", lambda e, sl=sl: e.bn_aggr(out=st6[:, sl, 6:8], in_=st6[:, sl, 0:6]), [ks], [ks])
                ACT(st6[:, sl, 7:8], st6[:, sl, 7:8], AF.Sqrt, [ks, "epsc"], [ks], bias=epsc[:, 0:1], scale=1.0)
                RECIP(st6[:, sl, 7:8], st6[:, sl, 7:8], [ks], [ks])
                TS("dve", vb[:, sl, :], vb[:, sl, :], st6[:, sl, 6:7], st6[:, sl, 7:8], ALU.subtract, ALU.mult, [kv, ks], [kv])
                TT("dve", vb[:, sl, :], vb[:, sl, :], lng[:, :], ALU.mult, [kv, "lng"], [kv])
                TT("dve", vb[:, sl, :], vb[:, sl, :], lnb[:, :], ALU.add, [kv, "lnb"], [kv])
                for pp in range(2):
                    bi = 1 + (c % 2) * 2 + pp
                    for gg in range(2):
                        g = pp * 2 + gg
                        MM(ps[bi][:, gg * 128:(gg + 1) * 128], vb[:, sl, pp * 128:(pp + 1) * 128], wsT[:, g, :], True, False, [kv, "wsT"], [("ps", bi)])
                        MM(ps[bi][:, gg * 128:(gg + 1) * 128], ones[0:1, :], bsr[0:1, g * 128:(g + 1) * 128], False, True, ["cst", "bsr"], [("ps", bi)])
                    for gg in range(2):
                        TT("dve", cout[gg * 64:(gg + 1) * 64, pp, c * 128:(c + 1) * 128], ps[bi][gg * 64:(gg + 1) * 64, gg * 128:(gg + 1) * 128],
                           gu[gg * 64:(gg + 1) * 64, pp, c * 128:(c + 1) * 128], ALU.mult, [("ps", bi), ("gu", pp)], [("cout", pp)])
            for pp in range(2):
                DMA(catT[768 + pp * 128:768 + (pp + 1) * 128, :], cout[:, pp, :], [("cout", pp)], [("cat", 6 + pp)], q="pool")
            sc.barrier()
        if stop == "D":
            break

        with ExitStack() as es:
            qa = sb(es, "qa", [72, S])
            ka = sb(es, "ka", [72, S])
            vt = sb(es, "vt", [128, 16, 128])
            ones_r = sb(es, "ones_r", [128, 128])
            CP("dve", RR(ones_r[:, :]), ones, ["cst"], ["ones_r"])
            kbar = sb(es, "kbar", [64, 8])
            mobc = sb(es, "mobc", [128, 3, 16, 48])
            NP = sb(es, "NP", [128, 16, 72])
            sm = sb(es, "sm", [128, 2, 8])
            top8 = sb(es, "top8", [128, 2, 8])
            al = sb(es, "al", [128, 2, 8])
            pt = sb(es, "pt", [128, 3, 512])
            rden = sb(es, "rden", [128, 512])
            ob = sb(es, "ob", [128, 2, 512])
            DMA(mobc[:, :, :, :], mobac_d.rearrange("p (a t c) -> p a t c", a=3, t=16), [], ["mobc"])
            DMA(RR(ka[64:72, :]), blkind_d[:, :], [], ["kaB"], q="pool")
            sc.op("pool", lambda e: e.memset(NP[:, :, :], 0.0), [], ["NP"])
            pti = 0
            for h in range(H):
                DMA(RR(qa[0:64, :]), qkT[h * 64:(h + 1) * 64, :], [], ["qaQ"], q="pool")
                DMA(RR(ka[0:64, :]), qkT[384 + h * 64:384 + (h + 1) * 64, :], [], ["kaK"], q="pool")
                if h % 2 == 0:
                    DMA(RR(vt[:, :, :]), vm_tm.rearrange("(t p) c -> p t c", p=128)[:, :, h * 64:(h + 2) * 64], [], ["vt"], q="pool")
                hp = slice((h % 2) * 64, (h % 2) * 64 + 64)
                sc.op("dve", lambda e: e.tensor_reduce(out=kbar[:, :], in_=ka[0:64, :].rearrange("p (n k) -> p n k", n=8), axis=AX.X, op=ALU.add), ["kaK"], ["kbar"])
                for t in range(16):
                    MM(ps[0][:, t * 8:(t + 1) * 8], qa[0:64, t * 128:(t + 1) * 128], kbar[:, :], True, True, ["qaQ", "kbar"], [("ps", 0)])
                for t in range(16):
                    sl = t % 2
                    hs = slice(h * 8, (h + 1) * 8)
                    TT("dve", sm[:, sl, :], ps[0][:, t * 8:(t + 1) * 8], mobc[:, 0, t, hs], ALU.add, [("ps", 0), "mobc"], [("sm", sl)])
                    sc.op("dve", lambda e, sl=sl: e.max(out=top8[:, sl, :], in_=sm[:, sl, :]), [("sm", sl)], [("top8", sl)])
                    TS("dve", al[:, sl, :], sm[:, sl, :], top8[:, sl, 2:3], None, ALU.is_ge, None, [("sm", sl), ("top8", sl)], [("al", sl)])
                    TT("dve", al[:, sl, :], al[:, sl, :], mobc[:, 1, t, hs], ALU.mult, [("al", sl), "mobc"], [("al", sl)])
                    TT("dve", al[:, sl, :], al[:, sl, :], mobc[:, 2, t, hs], ALU.add, [("al", sl), "mobc"], [("al", sl)])
                    TS("dve", NP[:, t, 64:72], al[:, sl, :], -1.0, BIG, ALU.add, ALU.mult, [("al", sl)], ["NP"])
                for t4 in range(4):
                    for tq in range(4):
                        t = t4 * 4 + tq
                        MM(ps[1][0:72, tq * 128:(tq + 1) * 128], NP[:, t, :], ident, True, True, ["NP", "cst"], [("ps", 1)])
                    CP("act", RR(qa[64:72, t4 * 512:(t4 + 1) * 512]), ps[1][64:72, :], [("ps", 1)], ["qaM"])
                for qt in range(4):
                    qsl = slice(qt * 512, (qt + 1) * 512)
                    nk = (qt + 1) * 4
                    osl = qt % 2
                    for kt in range(nk):
                        sb_i = 2 + kt % 2
                        pi = pti % 3
                        pti += 1
                        MM(ps[sb_i][:, :], ka[0:72, kt * 128:(kt + 1) * 128], qa[0:72, qsl], True, True,
                           ["kaK", "kaB", "qaQ", "qaM"], [("ps", sb_i)], r=True)
                        ACT(RR(pt[:, pi, :]), ps[sb_i][:, :], AF.Exp, [("ps", sb_i)], [("pt", pi)], scale=0.125)
                        if kt >= qt * 4:
                            j = kt - qt * 4
                            TT("dve", RR(pt[:, pi, :]), pt[:, pi, :], cm[:, j * 512:(j + 1) * 512], ALU.mult, [("pt", pi), "cst"], [("pt", pi)])
                        MM(ps[4][:, :], vt[:, kt, :], pt[:, pi, :], kt == 0, kt == nk - 1, ["vt", ("pt", pi)], [("ps", 4)], r=True)
                        MM(ps[5][:, :], ones_r[:, :], pt[:, pi, :], kt == 0, kt == nk - 1, ["ones_r", ("pt", pi)], [("ps", 5)], r=True)
                    RECIP(rden[hp, :], ps[5][hp, :], [("ps", 5)], ["rden"])
                    TT("dve", ob[hp, osl, :], ps[4][hp, :], rden[hp, :], ALU.mult, [("ps", 4), "rden"], [("ob", osl)])
                    DMA(catT[384 + h * 64:384 + (h + 1) * 64, qsl], ob[hp, osl, :], [("ob", osl)], [("cat", "b", h, qt)])
            sc.barrier()
        if stop == "C":
            break

        with ExitStack() as es:
            TW = 128
            NTB = S // TW
            w2s = sb(es, "w2s", [64, 384])
            a2s = sb(es, "a2s", [64, 384])
            g2s = sb(es, "g2s", [128, 384])
            DMA(w2s[:, :], PR["rwkv_w2"][l], [], ["w2s"])
            DMA(a2s[:, :], PR["rwkv_a2"][l], [], ["a2s"])
            DMA(g2s[:, :], PR["rwkv_g2"][l], [], ["g2s"])
            pw0 = colvec(es, "pw0", PR["rwkv_w0"][l], 6, p=64)
            pa0 = colvec(es, "pa0", PR["rwkv_a0"][l], 6, p=64)
            pkk = colvec(es, "pkk", PR["rwkv_k_k"][l], 6, p=64)
            pka = colvec(es, "pka", PR["rwkv_k_a"][l], 6, p=64)
            prk = colvec(es, "prk", PR["rwkv_r_k"][l].rearrange("h d -> (h d)"), 6, p=64)
            plg = colvec(es, "plg", PR["rwkv_lnx_g"][l], 6, p=64)
            plb = colvec(es, "plb", PR["rwkv_lnx_b"][l], 6, p=64)
            pok = sb(es, "pok", [64, 6])
            TS("dve", pok[:, :], pka[:, :], -1.0, 1.0, ALU.mult, ALU.add, ["pka"], ["pok"])
            i64 = ident[0:64, 0:64]
            o64 = ones[0:64, 0:64]
            rowm = cst[:, OF["rowm"]:OF["rowm"] + 2]
            Mst = sb(es, "Mst", [64, 2, 6, 64])
            sc.op("dve", lambda e: e.memset(Mst[:, 0, :, :], 0.0), [], [("Mst", 0)])
            mcur = [0]
            RhatT = sb(es, "RhatT", [64, 6, TW])
            Y0T = sb(es, "Y0T", [64, 6, TW])
            yT = sb(es, "yT", [64, 6, TW])
            GT = sb(es, "GT", [64, 6, 2, 64])
            Hm = sb(es, "Hm", [64, 6, 2, 64])
            bon = sb(es, "bon", [64, 2, 6, TW]); gal = sb(es, "gal", [64, 2, 6, TW])
            ARh = sb(es, "ARh", [64, 2, 6, 2, TW]); BKh = sb(es, "BKh", [64, 2, 6, 2, TW]); BPh = sb(es, "BPh", [64, 2, 6, 2, TW])
            vvh = sb(es, "vvh", [64, 2, 6, TW]); pC = sb(es, "pC", [64, 2, 6, 2])
            wd = sb(es, "wd", [64, 2, TW]); ad = sb(es, "ad", [64, 2, TW]); gd = sb(es, "gd", [128, 2, TW])
            T6 = lambda nm: sb(es, nm, [64, 6, TW])
            rr = T6("rr"); kq = T6("kq"); sig = T6("sig"); cum = T6("cum"); cpv = T6("cpv")
            epos = T6("epos"); eneg = T6("eneg"); eprv = T6("eprv"); eend = T6("eend")
            aa = T6("aa"); kk = T6("kk"); kk2 = T6("kk2"); rn = T6("rn"); kka = T6("kka"); kp = T6("kp"); rkr = T6("rkr")
            nbc = sb(es, "nbc", [64, 6, 2])
            tm = sb(es, "tm", [128, 6, 256]); Eb = sb(es, "Eb", [128, 6, 512]); YS = sb(es, "YS", [128, 6, 256])
            Lb = sb(es, "Lb", [128, 6, 2, 384]); B2 = sb(es, "B2", [128, 6, 128]); K2 = sb(es, "K2", [128, 6, 128])
            yc = sb(es, "yc", [64, 2, TW]); ysq = sb(es, "ysq", [64, 2, TW]); yrs = sb(es, "yrs", [64, 2, TW])
            obuf = sb(es, "obuf", [64, 2, TW])

            def tile_pro(tt):
                tsl = slice(tt * TW, (tt + 1) * TW)
                ws = tt % 2
                DMA(wd[:, ws, :], rwkvT[1152:1216, tsl], [], [("wd", ws)])
                DMA(ad[:, ws, :], rwkvT[1216:1280, tsl], [], [("ad", ws)])
                DMA(gd[:, ws, :], rwkvT[1280:1408, tsl], [], [("gd", ws)])
                yield
                ACT(wd[:, ws, :], wd[:, ws, :], AF.Tanh, [("wd", ws)], [("wd", ws)])
                ACT(gd[:, ws, :], gd[:, ws, :], AF.Sigmoid, [("gd", ws)], [("gd", ws)])
                yield

            def stage0(tt, h):
                tsl = slice(tt * TW, (tt + 1) * TW)
                ws = tt % 2
                hc = slice(h * 64, (h + 1) * 64)
                K_ = lambda nm: (nm, h)
                pb = ps[h % 2]
                pk = ("ps", h % 2)
                DMA(rr[:, h, :], rwkvT[h * 64:(h + 1) * 64, tsl], [], [K_("rr")])
                DMA(kq[:, h, :], rwkvT[384 + h * 64:384 + (h + 1) * 64, tsl], [], [K_("kq")])
                DMA(vvh[:, ws, h, :], rwkvT[768 + h * 64:768 + (h + 1) * 64, tsl], [], [("vv", ws, h)])
                yield
                MM(pb[0:64, 0:TW], w2s[:, hc], wd[:, ws, :], True, True, ["w2s", ("wd", ws)], [pk])
                ACT(sig[:, h, :], pb[0:64, 0:TW], AF.Sigmoid, [pk, "pw0"], [K_("sig")], bias=pw0[:, h:h + 1])
                yield
                sc.op("dve", lambda e, h=h: e.tensor_tensor_scan(out=cum[:, h, :], data0=resetm[0:64, 0:TW], data1=sig[:, h, :], initial=0.0, op0=ALU.mult, op1=ALU.add), [K_("sig"), "cst"], [K_("cum")])
                yield
                TT("pool", cpv[:, h, :], cum[:, h, :], sig[:, h, :], ALU.subtract, [K_("cum"), K_("sig")], [K_("cpv")])
                ACT(epos[:, h, :], cum[:, h, :], AF.Exp, [K_("cum")], [K_("epos")], scale=-C0)
                ACT(eneg[:, h, :], cum[:, h, :], AF.Exp, [K_("cum")], [K_("eneg")], scale=C0)
                cum3 = cum[:, h, :].rearrange("p (c t) -> p c t", c=2)
                epos3 = epos[:, h, :].rearrange("p (c t) -> p c t", c=2)
                TS("dve", nbc[:, h, :], cum3[:, :, 63], -C0, None, ALU.mult, None, [K_("cum")], [K_("nbc")])
                yield
                ACT(eprv[:, h, :], cpv[:, h, :], AF.Exp, [K_("cpv")], [K_("eprv")], scale=-C0)
                CP("pool", pC[:, ws, h, :], epos3[:, :, 63], [K_("epos")], [("pC", ws, h)])
                for c in range(2):
                    ACT(eend[:, h, c * 64:(c + 1) * 64], cum[:, h, c * 64:(c + 1) * 64], AF.Exp, [K_("cum"), K_("nbc")], [K_("eend")], scale=C0, bias=nbc[:, h, c:c + 1])
                yield
                MM(pb[0:64, 128:128 + TW], a2s[:, hc], ad[:, ws, :], True, True, ["a2s", ("ad", ws)], [pk])
                ACT(aa[:, h, :], pb[0:64, 128:128 + TW], AF.Sigmoid, [pk, "pa0"], [K_("aa")], bias=pa0[:, h:h + 1])
                yield
                MM(pb[0:64, 256:256 + TW], g2s[:, hc], gd[:, ws, :], True, True, ["g2s", ("gd", ws)], [pk])
                CP("act", gal[:, ws, h, :], pb[0:64, 256:256 + TW], [pk], [("gal", ws, h)])
                yield
                TS("dve", kk[:, h, :], kq[:, h, :], pkk[:, h:h + 1], None, ALU.mult, None, [K_("kq"), "pkk"], [K_("kk")])
                yield
                ACT(kk2[:, h, :], kk[:, h, :], AF.Square, [K_("kk")], [K_("kk2")])
                yield
                MM(pb[0:64, 384:384 + TW], o64, kk2[:, h, :], True, True, ["cst", K_("kk2")], [pk])
                ACT(rn[:, h, :], pb[0:64, 384:384 + TW], AF.Sqrt, [pk], [K_("rn")])
                yield
                TS("dve", rn[:, h, :], rn[:, h, :], 1e-12, None, ALU.max, None, [K_("rn")], [K_("rn")])
                yield
                RECIP(rn[:, h, :], rn[:, h, :], [K_("rn")], [K_("rn")])
                yield
                TT("dve", kk[:, h, :], kk[:, h, :], rn[:, h, :], ALU.mult, [K_("kk"), K_("rn")], [K_("kk")])
                yield
                TT("pool", kka[:, h, :], kk[:, h, :], aa[:, h, :], ALU.mult, [K_("kk"), K_("aa")], [K_("kka")])
                TS("dve", rn[:, h, :], aa[:, h, :], pka[:, h:h + 1], pok[:, h:h + 1], ALU.mult, ALU.add, [K_("aa"), "pka", "pok", K_("rn")], [K_("rn")])
                STT(ARh[:, ws, h, 0, :], kk[:, h, :], -1.0, eprv[:, h, :], ALU.mult, ALU.mult, [K_("kk"), K_("eprv")], [("AR0", ws, h)])
                yield
                TT("dve", kp[:, h, :], kq[:, h, :], rn[:, h, :], ALU.mult, [K_("kq"), K_("rn")], [K_("kp")])
                TT("pool", ARh[:, ws, h, 1, :], rr[:, h, :], epos[:, h, :], ALU.mult, [K_("rr"), K_("epos")], [("AR1", ws, h)])
                yield
                STT(rkr[:, h, :], rr[:, h, :], prk[:, h:h + 1], kp[:, h, :], ALU.mult, ALU.mult, [K_("rr"), "prk", K_("kp")], [K_("rkr")])
                TT("pool", BKh[:, ws, h, 0, :], kka[:, h, :], eneg[:, h, :], ALU.mult, [K_("kka"), K_("eneg")], [("BK0", ws, h)])
                TT("pool", BKh[:, ws, h, 1, :], kp[:, h, :], eneg[:, h, :], ALU.mult, [K_("kp"), K_("eneg")], [("BK1", ws, h)])
                yield
                MM(pb[0:64, 0:TW], o64, rkr[:, h, :], True, True, ["cst", K_("rkr")], [pk])
                TT("dve", bon[:, ws, h, :], pb[0:64, 0:TW], vvh[:, ws, h, :], ALU.mult, [pk, ("vv", ws, h)], [("bon", ws, h)])
                TT("pool", BPh[:, ws, h, 0, :], kka[:, h, :], eend[:, h, :], ALU.mult, [K_("kka"), K_("eend")], [("BP", ws, h)])
                TT("pool", BPh[:, ws, h, 1, :], kp[:, h, :], eend[:, h, :], ALU.mult, [K_("kp"), K_("eend")], [("BP", ws, h)])
                yield

            def make_gens(tt):
                if tt >= NTB:
                    return []
                return [tile_pro(tt)] + [stage0(tt, h) for h in range(H)]

            def advance(gl, n):
                for _ in range(n):
                    for g in list(gl):
                        try:
                            next(g)
                        except StopIteration:
                            gl.remove(g)

            def chain(tt, nxt):
                ws = tt % 2
                tsl = slice(tt * TW, (tt + 1) * TW)
                HS = range(H)
                bk = lambda h: ps[2 + h]
                bkk = lambda h: ("ps", 2 + h)
                A0 = lambda h: ("AR0", ws, h)
                A1 = lambda h: ("AR1", ws, h)
                for h in HS:
                    for i, (src, kx) in enumerate([(ARh[:, ws, h, 0, :], A0(h)), (BPh[:, ws, h, 0, :], ("BP", ws, h)), (BPh[:, ws, h, 1, :], ("BP", ws, h)), (vvh[:, ws, h, :], ("vv", ws, h))]):
                        TR(bk(h)[:, i * 64:(i + 1) * 64], src, i64, [kx, "cst"], [bkk(h)])
                for h in HS:
                    CP("act", tm[:, h, :], bk(h)[:, 0:256], [bkk(h)], [("tm", h)])
                advance(nxt, 2)
                for h in HS:
                    MM(bk(h)[:, 0:256], BKh[:, ws, h, 0, :], ARh[:, ws, h, :, :], True, True, [("BK0", ws, h), A0(h), A1(h)], [bkk(h)])
                    MM(bk(h)[:, 256:384], ARh[:, ws, h, 0, :], BKh[:, ws, h, 0, :], True, True, [("BK0", ws, h), A0(h)], [bkk(h)])
                for h in HS:
                    TT("dve", Eb[:, h, 0:384], bk(h)[:, 0:384], mE, ALU.mult, [bkk(h), "cst"], [("E", h, "a")])
                advance(nxt, 2)
                for h in HS:
                    MM(bk(h)[:, 0:256], BKh[:, ws, h, 1, :], ARh[:, ws, h, :, :], True, True, [("BK1", ws, h), A0(h), A1(h)], [bkk(h)])
                for h in HS:
                    TT("dve", YS[:, h, :], bk(h)[:, 0:256], mY, ALU.mult, [bkk(h), "cst"], [("YS", h)])
                advance(nxt, 2)
                for h in HS:
                    MM(bk(h)[:, 256:320], YS[:, h, 0:128], tm[:, h, 192:256], True, True, [("YS", h), ("tm", h)], [bkk(h)])
                    CP("pool", Eb[:, h, 384:448], tm[:, h, 0:64], [("tm", h)], [("E", h, "b")])
                for h in HS:
                    CP("act", Eb[:, h, 448:512], bk(h)[:, 256:320], [bkk(h)], [("E", h, "c")])
                advance(nxt, 2)
                for lev in range(6):
                    def views(h):
                        if lev == 0:
                            return (Eb[:, h, 0:128], Eb[:, h, 256:384], Eb[:, h, 256:512], Eb[:, h, 384:512],
                                    [("E", h, "a"), ("E", h, "b"), ("E", h, "c")])
                        Lp = Lb[:, h, (lev - 1) % 2, :]
                        return (Lp[:, 0:128], Lp[:, 128:256], Lp[:, 128:384], Lp[:, 256:384],
                                [("L", h, (lev - 1) % 2, "p"), ("L", h, (lev - 1) % 2, "z")])
                    for h in HS:
                        PT_, P_, PZ_, Z_, rk = views(h)
                        MM(bk(h)[:, 128:384], PT_, PZ_, True, True, rk, [bkk(h)])
                        if lev < 5:
                            MM(bk(h)[:, 0:128], P_, PT_, True, True, rk, [bkk(h)])
                    for h in HS:
                        PT_, P_, PZ_, Z_, rk = views(h)
                        Ln = Lb[:, h, lev % 2, :]
                        TT("dve", Ln[:, 256:384], bk(h)[:, 256:384], Z_, ALU.add, [bkk(h)] + rk, [("L", h, lev % 2, "z")])
                        if lev < 5:
                            CP("act", Ln[:, 0:256], bk(h)[:, 0:256], [bkk(h)], [("L", h, lev % 2, "p")])
                    advance(nxt, 3)
                for h in HS:
                    for hf in range(2):
                        TS("pool", B2[:, h, hf * 64:(hf + 1) * 64], tm[:, h, 64:128], rowm[:, hf:hf + 1], None, ALU.mult, None, [("tm", h), "cst"], [("B2", h)])
                        TS("pool", K2[:, h, hf * 64:(hf + 1) * 64], tm[:, h, 128:192], rowm[:, hf:hf + 1], None, ALU.mult, None, [("tm", h), "cst"], [("K2", h)])
                for h in HS:
                    Lf = Lb[:, h, 1, :]
                    kLf = ("L", h, 1, "z")
                    W_, U0_ = Lf[:, 256:320], Lf[:, 320:384]
                    b_ = bk(h)
                    MM(b_[0:64, 0:128], W_, B2[:, h, :], True, True, [kLf, ("B2", h)], [bkk(h)])
                    for hf in range(2):
                        MM(b_[0:64, 128 + hf * 64:128 + (hf + 1) * 64], K2[:, h, hf * 64:(hf + 1) * 64], tm[:, h, 192:256], True, False, [("K2", h), ("tm", h)], [bkk(h)])
                        MM(b_[0:64, 128 + hf * 64:128 + (hf + 1) * 64], B2[:, h, hf * 64:(hf + 1) * 64], U0_, False, True, [("B2", h), kLf], [bkk(h)])
                    MM(b_[0:64, 256:384], W_, Eb[:, h, 128:256], True, True, [kLf, ("E", h, "a")], [bkk(h)])
                    MM(b_[0:64, 384:512], tm[:, h, 192:256], YS[:, h, 128:256], True, False, [("tm", h), ("YS", h)], [bkk(h)])
                    MM(b_[0:64, 384:512], U0_, Eb[:, h, 128:256], False, True, [kLf, ("E", h, "a")], [bkk(h)])
                advance(nxt, 2)
                for h in HS:
                    b_ = bk(h)
                    for hf in range(2):
                        STT(GT[:, h, hf, :], i64, pC[:, ws, h, hf:hf + 1], b_[0:64, hf * 64:(hf + 1) * 64], ALU.mult, ALU.add,
                            [bkk(h), ("pC", ws, h), "cst"], [("GT", h)])
                    TT("dve", RhatT[:, h, :], b_[0:64, 256:384], ARh[:, ws, h, 1, :], ALU.add, [bkk(h), A1(h)], [("Rhat", h)])
                for h in HS:
                    b_ = bk(h)
                    CP("act", Hm[:, h, :, :], b_[0:64, 128:256].rearrange("p (c i) -> p c i", c=2), [bkk(h)], [("Hm", h)])
                    CP("act", Y0T[:, h, :], b_[0:64, 384:512], [bkk(h)], [("Y0T", h)])
                advance(nxt, 2)
                allh = lambda nm: [(nm, h) for h in range(H)]
                for c in range(2):
                    csl = slice(c * 64, (c + 1) * 64)
                    m0 = mcur[0]
                    mnew = 1 - m0
                    for h in range(H):
                        MM(ps[0][0:64, h * 64:(h + 1) * 64], Mst[:, m0, h, :], RhatT[:, h, csl], True, True, [("Mst", m0), ("Rhat", h)], [("ps", 0)])
                    for h in range(H):
                        MM(ps[1][0:64, h * 64:(h + 1) * 64], GT[:, h, c, :], Mst[:, m0, h, :], True, True, [("Mst", m0), ("GT", h)], [("ps", 1)])
                    TT("dve", Mst[:, mnew, :, :], ps[1][0:64, 0:384].rearrange("p (h i) -> p h i", h=6), Hm[:, :, c, :], ALU.add,
                       [("ps", 1)] + allh("Hm"), [("Mst", mnew)])
                    TT("dve", yT[:, :, csl], ps[0][0:64, 0:384].rearrange("p (h t) -> p h t", h=6), Y0T[:, :, csl], ALU.add,
                       [("ps", 0)] + allh("Y0T"), [("yT", c)])
                    mcur[0] = mnew
                    advance(nxt, 1)
                ally = [("yT", c) for c in range(2)]
                for h in range(H):
                    osl = h % 2
                    kyc, kysq, kyrs = ("yc", osl), ("ysq", osl), ("yrs", osl)
                    pb = ps[osl]
                    pk = ("ps", osl)
                    MM(pb[0:64, 0:TW], o64, yT[:, h, :], True, True, ["cst"] + ally, [pk])
                    STT(yc[:, osl, :], pb[0:64, 0:TW], -1.0 / 64, yT[:, h, :], ALU.mult, ALU.add, [pk] + ally, [kyc])
                    ACT(ysq[:, osl, :], yc[:, osl, :], AF.Square, [kyc], [kysq])
                    MM(pb[0:64, 128:128 + TW], o64, ysq[:, osl, :], True, True, ["cst", kysq], [pk])
                    ACT(yrs[:, osl, :], pb[0:64, 128:128 + TW], AF.Sqrt, [pk, "epsc"], [kyrs], scale=1.0 / 64, bias=epsc[0:64, 1:2])
                    RECIP(yrs[:, osl, :], yrs[:, osl, :], [kyrs], [kyrs])
                    TT("dve", yc[:, osl, :], yc[:, osl, :], yrs[:, osl, :], ALU.mult, [kyc, kyrs], [kyc])
                    TS("dve", yc[:, osl, :], yc[:, osl, :], plg[:, h:h + 1], plb[:, h:h + 1], ALU.mult, ALU.add, [kyc, "plg", "plb"], [kyc])
                    TT("pool", yc[:, osl, :], yc[:, osl, :], bon[:, ws, h, :], ALU.add, [kyc, ("bon", ws, h)], [kyc])
                    TT("pool", obuf[:, osl, :], yc[:, osl, :], gal[:, ws, h, :], ALU.mult, [kyc, ("gal", ws, h)], [("obuf", osl)])
                    DMA(catT[h * 64:(h + 1) * 64, tsl], obuf[:, osl, :], [("obuf", osl)], [("cat", "a", h, tt)])
                    advance(nxt, 1)
                advance(nxt, 1000)

            g0 = make_gens(0)
            advance(g0, 1000)
            for tt in range(NTB):
                chain(tt, make_gens(tt + 1))
            sc.barrier()
        if stop == "B":
            break

        with ExitStack() as es:
            wO = sb(es, "wO", [128, 2, 8, 128])
            catb = sb(es, "catb", [128, 2, 8, 512])
            mixb = sb(es, "mixb", [128, 8, 512])
            sq = sb(es, "sq", [128, 2, 512])
            rstd = sb(es, "rstd", [128, 2, 512])
            gP = colvec(es, "gP", PR["post_mix_g"][l], 8)
            w_out_l = PR["w_out"][l].rearrange("(k p) c -> p k c", p=128)
            cat_v = catT.rearrange("(k p) t -> p k t", p=128)
            wi = 0
            xe = sb(es, "xe", [128, 2, 8, 512])
            for tt in range(4):
                tsl = slice(tt * 512, (tt + 1) * 512)
                cs = tt % 2
                DMA(xe[:, cs, :, :], xD_v[:, :, tsl], [("xD", tt)], [("xe", cs, j) for j in range(8)])
                DMA(RR(catb[:, cs, :, :]), cat_v[:, :, tsl], [], [("catb", cs)], q="pool")
                for j in range(8):
                    sl = wi % 2
                    wi += 1
                    DMA(RR(wO[:, sl, :, :]), w_out_l[:, :, j * 128:(j + 1) * 128], [], [("wO", sl)], q="pool")
                    bi = j % 2
                    for k in range(8):
                        MM(ps[bi][:, :], wO[:, sl, k, :], catb[:, cs, k, :], k == 0, k == 7, [("wO", sl), ("catb", cs)], [("ps", bi)], r=True)
                    CP("act" if j % 2 else "dve", mixb[:, j, :], ps[bi][:, :], [("ps", bi)], [("mixb", j)])
                r = rms_stats(lambda k: mixb[:, k, :], lambda k: ("mixb", k), tt, (sq, rstd), ps[2 + tt % 2], ("ps", 2 + tt % 2))
                for j in range(8):
                    STT(mixb[:, j, :], mixb[:, j, :], gP[:, j:j + 1], r, ALU.mult, ALU.mult, [("mixb", j), ("rstd", tt % 2), "gP"], [("mixb", j)])
                    TT("dve", xe[:, cs, j, :], xe[:, cs, j, :], mixb[:, j, :], ALU.add, [("mixb", j), ("xe", cs, j)], [("xe", cs, j)])
                DMA(xD_v[:, :, tsl], xe[:, cs, :, :], [("xe", cs, j) for j in range(8)], [("xD", tt)])
            sc.barrier()
        if stop == "E":
            break

        with ExitStack() as es:
            hb = sb(es, "hb", [128, 8, 512])
            actb = sb(es, "actb", [128, NFF, 512])
            wG = sb(es, "wG", [128, 2, 2, 8, 128])
            wD = sb(es, "wD", [128, 2, NFF, 128])
            sq = sb(es, "sq", [128, 2, 512])
            rstd = sb(es, "rstd", [128, 2, 512])
            sil = sb(es, "sil", [128, 2, 512])
            gF = colvec(es, "gF", PR["pre_ffn_g"][l], 8)
            gQ = colvec(es, "gQ", PR["post_ffn_g"][l], 8)
            w_fi = PR["w_ffn_in"][l].rearrange("(k p) c -> p k c", p=128)
            w_fo = PR["w_ffn_out"][l].rearrange("(j p) c -> p j c", p=128)
            wi = 0
            wdi = 0
            xf = sb(es, "xf", [128, 8, 512])
            for tt in range(4):
                tsl = slice(tt * 512, (tt + 1) * 512)
                DMA(xf[:, :, :], xD_v[:, :, tsl], [("xD", tt)], [("xf", j) for j in range(8)])
                r = rms_stats(lambda k: xf[:, k, :], lambda k: ("xf", k), tt, (sq, rstd), ps[6 + tt % 2], ("ps", 6 + tt % 2))
                for k in range(8):
                    STT(RR(hb[:, k, :]), xf[:, k, :], gF[:, k:k + 1], r, ALU.mult, ALU.mult, [("xf", k), ("rstd", tt % 2), "gF"], [("hb", k)])
                for j in range(NFF):
                    sl = wi % 2
                    wi += 1
                    DMA(RR(wG[:, sl, 0, :, :]), w_fi[:, :, j * 128:(j + 1) * 128], [], [("wG", sl, 0)], q="pool")
                    DMA(RR(wG[:, sl, 1, :, :]), w_fi[:, :, DFF + j * 128:DFF + (j + 1) * 128], [], [("wG", sl, 1)], q="pool")
                    bg, bu = (j % 2) * 2, (j % 2) * 2 + 1
                    for k in range(8):
                        MM(ps[bg][:, :], wG[:, sl, 0, k, :], hb[:, k, :], k == 0, k == 7, [("wG", sl, 0), ("hb", k)], [("ps", bg)], r=True)
                    for k in range(8):
                        MM(ps[bu][:, :], wG[:, sl, 1, k, :], hb[:, k, :], k == 0, k == 7, [("wG", sl, 1), ("hb", k)], [("ps", bu)], r=True)
                    ACT(sil[:, j % 2, :], ps[bg][:, :], AF.Silu, [("ps", bg)], [("sil", j % 2)])
                    TT("dve", RR(actb[:, j, :]), ps[bu][:, :], sil[:, j % 2, :], ALU.mult, [("ps", bu), ("sil", j % 2)], [("actb", j)])
                allact = [("actb", j) for j in range(NFF)]
                for jo in range(8):
                    sl = wdi % 2
                    wdi += 1
                    DMA(RR(wD[:, sl, :, :]), w_fo[:, :, jo * 128:(jo + 1) * 128], [], [("wD", sl)], q="pool")
                    bi = 4 + jo % 2
                    for j in range(NFF):
                        MM(ps[bi][:, :], wD[:, sl, j, :], actb[:, j, :], j == 0, j == NFF - 1, [("wD", sl), ("actb", j)], [("ps", bi)], r=True)
                    CP("act" if jo % 2 else "dve", RR(hb[:, jo, :]), ps[bi][:, :], [("ps", bi)], [("hb", jo)])
                r = rms_stats(lambda k: hb[:, k, :], lambda k: ("hb", k), tt + 1, (sq, rstd), ps[6 + (tt + 1) % 2], ("ps", 6 + (tt + 1) % 2))
                for j in range(8):
                    STT(RR(hb[:, j, :]), hb[:, j, :], gQ[:, j:j + 1], r, ALU.mult, ALU.mult, [("hb", j), ("rstd", (tt + 1) % 2), "gQ"], [("hb", j)])
                    TT("dve", xf[:, j, :], xf[:, j, :], hb[:, j, :], ALU.add, [("hb", j), ("xf", j)], [("xf", j)])
                DMA(xD_v[:, :, tsl], xf[:, :, :], [("xf", j) for j in range(8)], [("xD", tt)])
            sc.barrier()
        if OPTS.get("xdbg") == l:
            DMA(xdbg[:, :], xD[:, :], [("xD", tt) for tt in range(4)], [("xdbg", 0)])

    if stop in (None, 'setup'):
        with ExitStack() as es:
            yo = sb(es, "yo", [128, 2, D])
            xo = sb(es, "xo", [128, 2, 8, 512])
            for t in range(16):
                sl = t % 2
                xsl = (t // 4) % 2
                if t % 4 == 0:
                    DMA(xo[:, xsl, :, :], xD_v[:, :, (t // 4) * 512:(t // 4 + 1) * 512], [("xD", t // 4)], [("xo", xsl)])
                for g in range(2):
                    bank = ps[(t * 2 + g) % 4]
                    bk = ("ps", (t * 2 + g) % 4)
                    for kk in range(4):
                        k = g * 4 + kk
                        TR(bank[:, kk * 128:(kk + 1) * 128], xo[:, xsl, k, (t % 4) * 128:(t % 4 + 1) * 128], ident, [("xo", xsl), "cst"], [bk])
                    CP("act" if g else "dve", yo[:, sl, g * 512:(g + 1) * 512], bank[:, :], [bk], [("yo", sl, g)])
                DMA(y_out[t * 128:(t + 1) * 128, :], yo[:, sl, :], [("yo", sl, 0), ("yo", sl, 1)], [("y", t)])
    sc.barrier()
    sc.emit()
    glob.close()
    return nc


_NC_CACHE = {}


def kernel(**inputs):
    if "nc" not in _NC_CACHE:
        _NC_CACHE["nc"] = build()
    nc = _NC_CACHE["nc"]
    x = np.ascontiguousarray(np.asarray(inputs["x"], dtype=np.float32))
    base = {n: np.ascontiguousarray(np.asarray(inputs[n], dtype=np.float32)) for n in PARAM_NAMES}
    base["cst"] = CONSTS["cst"]
    base["mobac"] = CONSTS["mobac"]
    base["blkind"] = CONSTS["blkind"]
    in_maps = []
    for b in range(8):
        m = dict(base)
        m["x"] = x[b]
        in_maps.append(m)
    res = run_bass_kernel_spmd(nc, in_maps, core_ids=list(range(8)))
    return np.stack([np.asarray(r["y"], dtype=np.float32) for r in res.results], 0)
```

```python
import numpy as np
from contextlib import ExitStack
import concourse.bass as bass
import concourse.mybir as mybir
from concourse.bass_utils import run_bass_kernel_spmd

F32 = mybir.dt.float32
F32R = mybir.dt.float32r
AF = mybir.ActivationFunctionType
ALU = mybir.AluOpType
AX = mybir.AxisListType

S = 2048
D = 1024
L = 2
DFF = 2816
NFF = 22
H = 6
C0 = float(np.exp(-0.5))
BIG = 30000.0


OPTS = {}


class Sched:
    def __init__(self, nc, n_dma=40):
        self.nc = nc
        self.names = ["pe", "act", "dve", "pool", "sp"]
        self.sem = {e: nc.alloc_semaphore("s_" + e) for e in ["pe", "act", "dve", "pool"]}
        self.cnt = {e: 0 for e in self.sem}
        self.dsem = [nc.alloc_semaphore("d%d" % i) for i in range(n_dma)]
        self.dcnt = [0] * n_dma
        self.drr = 0
        self.q = {e: [] for e in self.names}
        self.clock = {e: {} for e in self.names}
        self.evclock = {}
        self.evorder = {}
        self.nev = 0
        self.lastw = {}
        self.readers = {}

    def _deps(self, reads, writes):
        deps = {}

        def add(k, v):
            if deps.get(k, 0) < v:
                deps[k] = v

        for r in reads:
            ev = self.lastw.get(r)
            if ev is not None:
                add(*ev)
        for w in writes:
            ev = self.lastw.get(w)
            if ev is not None:
                add(*ev)
            for k, v in self.readers.get(w, {}).items():
                add(k, v)
        return deps

    def _commit(self, ev, reads, writes):
        k, v = ev
        for r in reads:
            d = self.readers.setdefault(r, {})
            if d.get(k, 0) < v:
                d[k] = v
        for w in writes:
            self.lastw[w] = ev
            self.readers[w] = {}

    def _waits(self, eng, deps):
        clk = self.clock.setdefault(eng, {})
        waits = []
        for k, v in sorted(deps.items(), key=lambda kv: -self.evorder.get(kv, 0)):
            if eng == "pe" and k == ("e", "pe"):
                continue
            if clk.get(k, 0) >= v:
                continue
            waits.append((k, v))
            for k2, v2 in self.evclock.get((k, v), {}).items():
                if clk.get(k2, 0) < v2:
                    clk[k2] = v2
            clk[k] = v
        return waits

    def op(self, eng, fn, reads=(), writes=()):
        banks = {("psx", k[1]) for k in list(reads) + list(writes) if isinstance(k, tuple) and k and k[0] == "ps"}
        if banks:
            writes = list(writes) + list(banks)
        deps = self._deps(reads, writes)
        waits = self._waits(eng, deps)
        self.cnt[eng] += 1
        ev = (("e", eng), self.cnt[eng])
        self.evclock[ev] = dict(self.clock[eng])
        self.nev += 1
        self.evorder[ev] = self.nev
        self.q[eng].append((waits, fn, "e"))
        self._commit(ev, reads, writes)

    def dma(self, qeng, out, in_, reads=(), writes=(), **kw):
        deps = self._deps(reads, writes)
        idx = self.drr
        self.drr = (self.drr + 1) % len(self.dsem)
        if self.dcnt[idx] > 0:
            k = ("d", idx)
            deps[k] = max(deps.get(k, 0), self.dcnt[idx])
        waits = self._waits(qeng, deps)
        self.dcnt[idx] += 16
        ev = (("d", idx), self.dcnt[idx])
        self.evclock[ev] = dict(self.clock[qeng])
        self.nev += 1
        self.evorder[ev] = self.nev
        self.q[qeng].append((waits, lambda e: e.dma_start(out=out, in_=in_, **kw), idx))
        self._commit(ev, reads, writes)

    def barrier(self):
        allev = [(("e", e), c) for e, c in self.cnt.items() if c > 0]
        allev += [(("d", i), c) for i, c in enumerate(self.dcnt) if c > 0]
        for eng in self.names:
            waits = self._waits(eng, dict(allev))
            if waits:
                self.q[eng].append((waits, None, None))
        self.lastw = {}
        self.readers = {}
        self.evclock = {}

    def emit(self):
        nc = self.nc
        engs = {"pe": "tensor", "act": "scalar", "dve": "vector", "pool": "gpsimd", "sp": "sync"}
        with nc.Block() as block:
            for name in self.names:
                def body(eng, name=name):
                    for waits, fn, kind in self.q[name]:
                        emb = None
                        if fn is not None and waits and kind == "e" and not OPTS.get("noemb"):
                            emb = waits[-1]
                            waits = waits[:-1]
                        for k, v in waits:
                            s = self.sem[k[1]] if k[0] == "e" else self.dsem[k[1]]
                            eng.wait_ge(s, v)
                        if fn is None:
                            continue
                        ins = fn(eng)
                        if emb is not None:
                            k, v = emb
                            ins._wait_ge(self.sem[k[1]] if k[0] == "e" else self.dsem[k[1]], v)
                        if kind == "e":
                            ins.then_inc(self.sem[name], 1)
                        else:
                            ins.then_inc(self.dsem[kind], 16)
                getattr(block, engs[name])(body)


def make_consts():
    c = {}
    i128 = np.arange(128)
    blk = (i128[:, None] // 64) == (i128[None, :] // 64)
    ident = np.eye(128, dtype=np.float32)
    ones = np.ones((128, 128), np.float32)
    SL = ((i128[:, None] > i128[None, :]) & blk).astype(np.float32)
    SU = ((i128[:, None] < i128[None, :]) & blk).astype(np.float32)
    IU = ((i128[:, None] <= i128[None, :]) & blk).astype(np.float32)
    IUfull = (i128[:, None] <= i128[None, :]).astype(np.float32)
    idst = np.concatenate([np.eye(64), np.eye(64)], 0).astype(np.float32)
    reset = np.ones((128, 256), np.float32)
    reset[:, ::64] = 0.0
    rowm = np.zeros((128, 2), np.float32)
    rowm[:64, 0] = 1.0
    rowm[64:, 1] = 1.0
    q512 = np.arange(512)
    cm = np.stack([(q512[None, :] >= (j * 128 + i128[:, None])).astype(np.float32) for j in range(4)], 1)
    parts = [ident, ones, SU, IU, SL, SU, IU, IUfull, idst, reset, rowm, cm.reshape(128, 2048)]
    offs = {}
    o = 0
    for nm, p in zip(["ident", "ones", "mE", "_1", "_2", "mY", "_3", "iuf", "idst", "reset", "rowm", "cm"], parts):
        offs[nm] = o
        o += p.shape[1]
    c["cst"] = np.ascontiguousarray(np.concatenate(parts, 1))
    c["offs"] = offs
    mb = np.zeros((128, 3, 16, 6, 8), np.float32)
    for t in range(16):
        b = t // 2
        for n in range(8):
            mb[:, 0, t, :, n] = 0.0 if n < b else -1e30
            mb[:, 1, t, :, n] = 1.0 if n < b else 0.0
            mb[:, 2, t, :, n] = 1.0 if n == b else 0.0
    c["mobac"] = mb.reshape(128, 3 * 16 * 48)
    bi = np.zeros((8, S), np.float32)
    for n in range(8):
        bi[n, n * 256:(n + 1) * 256] = 1.0
    c["blkind"] = bi
    return c


CONSTS = make_consts()
PARAM_NAMES = ["pre_mix_g", "w_in", "rwkv_mu", "rwkv_w0", "rwkv_w2", "rwkv_a0", "rwkv_a2", "rwkv_g2",
               "rwkv_k_k", "rwkv_k_a", "rwkv_r_k", "rwkv_lnx_g", "rwkv_lnx_b", "gmlp_ln_g", "gmlp_ln_b",
               "gmlp_w_s", "gmlp_b_s", "w_out", "post_mix_g", "pre_ffn_g", "w_ffn_in", "w_ffn_out", "post_ffn_g"]
PARAM_SHAPES = {"pre_mix_g": (L, D), "w_in": (L, D, 3072), "rwkv_mu": (L, 1408), "rwkv_w0": (L, 384),
                "rwkv_w2": (L, 64, 384), "rwkv_a0": (L, 384), "rwkv_a2": (L, 64, 384), "rwkv_g2": (L, 128, 384),
                "rwkv_k_k": (L, 384), "rwkv_k_a": (L, 384), "rwkv_r_k": (L, 6, 64), "rwkv_lnx_g": (L, 384),
                "rwkv_lnx_b": (L, 384), "gmlp_ln_g": (L, 256), "gmlp_ln_b": (L, 256), "gmlp_w_s": (L, 4, 128, 128),
                "gmlp_b_s": (L, 4, 128), "w_out": (L, D, D), "post_mix_g": (L, D), "pre_ffn_g": (L, D),
                "w_ffn_in": (L, D, 2 * DFF), "w_ffn_out": (L, DFF, D), "post_ffn_g": (L, D)}


def build(dbg=None, nlayers=L, stop=None):
    dbg = dbg or []
    nc = bass.Bass("TRN2", target_bir_lowering=False)
    sc = Sched(nc)
    OF = CONSTS["offs"]

    def dram(name, shape, kind="Internal"):
        if name in dbg:
            kind = "ExternalOutput"
        return nc.dram_tensor(name, list(shape), F32, kind=kind).ap()

    x_in = dram("x", [S, D], "ExternalInput")
    y_out = dram("y", [S, D], "ExternalOutput")
    cst_d = dram("cst", CONSTS["cst"].shape, "ExternalInput")
    if not OPTS.get("noparams"):
        PR = {n: dram(n, PARAM_SHAPES[n], "ExternalInput") for n in PARAM_NAMES}
        mobac_d = dram("mobac", CONSTS["mobac"].shape, "ExternalInput")
        blkind_d = dram("blkind", CONSTS["blkind"].shape, "ExternalInput")
    if OPTS.get("noscratch"):
        glob_scr = None
    rwkvT = dram("rwkvT", [1408, S]) if not OPTS.get("noscratch") else None
    qkT = dram("qkT", [768, S]) if not OPTS.get("noscratch") else None
    uT = dram("uT", [256, S]) if not OPTS.get("noscratch") else None
    vm_tm = dram("vm_tm", [S, 384]) if not OPTS.get("noscratch") else None
    vg_tm = dram("vg_tm", [S, 256]) if not OPTS.get("noscratch") else None
    catT = dram("catT", [D, S]) if not OPTS.get("noscratch") else None
    xdbg = dram("xdbg", [D, S]) if not OPTS.get("noscratch") else None
    xD = dram("xD", [D, S])
    xD_v = xD.rearrange("(k p) t -> p k t", p=128)

    uid = [0]

    def sb(es, name, shape):
        uid[0] += 1
        return es.enter_context(nc.sbuf_tensor("%s_%d" % (name, uid[0]), list(shape), F32))

    glob = ExitStack()
    cst = sb(glob, "cst_sb", [128, CONSTS["cst"].shape[1]])
    ps = [glob.enter_context(nc.psum_tensor("ps%d" % i, [128, 512], F32)) for i in range(8)]
    ident = cst[:, OF["ident"]:OF["ident"] + 128]
    ones = cst[:, OF["ones"]:OF["ones"] + 128]
    mE = cst[:, OF["mE"]:OF["mE"] + 384]
    mY = cst[:, OF["mY"]:OF["mY"] + 256]
    iuf = cst[:, OF["iuf"]:OF["iuf"] + 128]
    idst = cst[:, OF["idst"]:OF["idst"] + 64]
    resetm = cst[:, OF["reset"]:OF["reset"] + 256]
    cm = cst[:, OF["cm"]:OF["cm"] + 2048]
    epsc = sb(glob, "epsc", [128, 4])

    def ACT(out, in_, func, reads, writes, **kw):
        sc.op("act", lambda e: e.activation(out=out, in_=in_, func=func, **kw), reads, writes)

    def RR(ap):
        return ap if OPTS.get("nor") else ap.bitcast(F32R)

    def MM(out, lhsT, rhs, start, stop, reads, writes, r=False):
        if r and not OPTS.get("nor"):
            lhsT = lhsT.bitcast(F32R)
            rhs = rhs.bitcast(F32R)
        sc.op("pe", lambda e: e.matmul(out, lhsT=lhsT, rhs=rhs, start=start, stop=stop), reads, writes)

    def TR(out, in_, idn, reads, writes):
        sc.op("pe", lambda e: e.transpose(out, in_, idn), reads, writes)

    def TT(eng, out, in0, in1, op, reads, writes):
        sc.op(eng, lambda e: e.tensor_tensor(out=out, in0=in0, in1=in1, op=op), reads, writes)

    def TS(eng, out, in0, s1, s2, op0, op1, reads, writes):
        if s2 is None:
            sc.op(eng, lambda e: e.tensor_scalar(out=out, in0=in0, scalar1=s1, scalar2=None, op0=op0), reads, writes)
        else:
            sc.op(eng, lambda e: e.tensor_scalar(out=out, in0=in0, scalar1=s1, scalar2=s2, op0=op0, op1=op1), reads, writes)

    def STT(out, in0, scalar, in1, op0, op1, reads, writes):
        sc.op("dve", lambda e: e.scalar_tensor_tensor(out=out, in0=in0, scalar=scalar, in1=in1, op0=op0, op1=op1), reads, writes)

    def CP(eng, out, in_, reads, writes):
        if eng == "act":
            sc.op("act", lambda e: e.copy(out=out, in_=in_), reads, writes)
        else:
            sc.op(eng, lambda e: e.tensor_copy(out=out, in_=in_), reads, writes)

    def RECIP(out, in_, reads, writes):
        sc.op("dve", lambda e: e.reciprocal(out=out, in_=in_), reads, writes)

    def DMA(out, in_, reads, writes, q="sp", **kw):
        sc.dma(q, out, in_, reads, writes, **kw)

    def colvec(es, name, src_1d, ncol, p=128):
        t = sb(es, name, [p, ncol])
        DMA(t[:, :], src_1d.rearrange("(c p) -> p c", p=p), [], [name], allow_slow_non_contiguous=True)
        return t

    DMA(cst[:, :], cst_d[:, :], [], ["cst"])
    sc.op("dve", lambda e: e.memset(epsc[:, 0:1], 1e-6), [], ["epsc"])
    sc.op("dve", lambda e: e.memset(epsc[:, 1:2], 64e-5), [], ["epsc"])
    sc.op("dve", lambda e: e.memset(epsc[:, 2:3], 0.0), [], ["epsc"])
    with ExitStack() as es:
        xin = sb(es, "xin", [128, 2, D])
        xs = sb(es, "xs", [128, 2, 8, 512])
        for t in range(16):
            sl = t % 2
            xsl = (t // 4) % 2
            DMA(xin[:, sl, :], x_in[t * 128:(t + 1) * 128, :], [], [("xin", sl)])
            for g in range(2):
                bank = ps[(t * 2 + g) % 4]
                bk = ("ps", (t * 2 + g) % 4)
                for kk in range(4):
                    k = g * 4 + kk
                    TR(bank[:, kk * 128:(kk + 1) * 128], xin[:, sl, k * 128:(k + 1) * 128], ident, [("xin", sl), "cst"], [bk])
                CP("act" if g else "dve", xs[:, xsl, g * 4:(g + 1) * 4, (t % 4) * 128:(t % 4 + 1) * 128],
                   bank[:, :].rearrange("p (k c) -> p k c", k=4), [bk], [("xs", xsl, t % 4, g)])
            if t % 4 == 3:
                DMA(xD_v[:, :, (t // 4) * 512:(t // 4 + 1) * 512], xs[:, xsl, :, :], [("xs", xsl, q, g) for q in range(4) for g in range(2)], [("xD", t // 4)])
        sc.barrier()

    def rms_stats(src_fn, src_keys, tt, es_tiles, pbank, pkey):
        sq, rstd = es_tiles
        for k in range(8):
            ACT(sq[:, k % 2, :], src_fn(k), AF.Square, [src_keys(k)], [("sq", k % 2)])
            MM(pbank[:, :], ones, sq[:, k % 2, :], k == 0, k == 7, [("sq", k % 2), "cst"], [pkey])
        ACT(rstd[:, tt % 2, :], pbank[:, :], AF.Sqrt, [pkey, "epsc"], [("rstd", tt % 2)], scale=1.0 / D, bias=epsc[:, 0:1])
        RECIP(rstd[:, tt % 2, :], rstd[:, tt % 2, :], [("rstd", tt % 2)], [("rstd", tt % 2)])
        return rstd[:, tt % 2, :]

    for l in range(nlayers if stop != 'setup' else 0):
        with ExitStack() as es:
            hbuf = sb(es, "hbuf", [128, 8, S])
            sq = sb(es, "sq", [128, 2, 512])
            rstd = sb(es, "rstd", [128, 2, 512])
            gA = colvec(es, "gA", PR["pre_mix_g"][l], 8)
            muA = colvec(es, "muA", PR["rwkv_mu"][l], 11)
            xa = sb(es, "xa", [128, 2, 8, 512])
            for tt in range(4):
                tsl = slice(tt * 512, (tt + 1) * 512)
                xsl = tt % 2
                DMA(xa[:, xsl, :, :], xD_v[:, :, tsl], [("xD", tt)], [("xa", xsl)])
                r = rms_stats(lambda k: xa[:, xsl, k, :], lambda k: ("xa", xsl), tt, (sq, rstd), ps[4 + tt % 2], ("ps", 4 + tt % 2))
                for k in range(8):
                    STT(RR(hbuf[:, k, tsl]), xa[:, xsl, k, :], gA[:, k:k + 1], r, ALU.mult, ALU.mult,
                        [("xa", xsl), ("rstd", tt % 2), "gA"], [("h", k, tt)])
            es_main = es
            es = ExitStack()
            wA = sb(es, "wA", [128, 2, 8, 128])
            stg = sb(es, "stg", [128, 2, S])
            stg2 = sb(es, "stg2", [128, 2, S])
            w_in_l = PR["w_in"][l].rearrange("(k p) c -> p k c", p=128)
            fm_chunks = [(c * 128, rwkvT, c * 128, True) for c in range(11)]
            fm_chunks += [(1408 + c * 128, qkT, c * 128, False) for c in range(6)]
            fm_chunks += [(2560 + c * 128, uT, c * 128, False) for c in range(2)]
            for ci, (col0, dst, row0, shift) in enumerate(fm_chunks):
                sl = ci % 2
                DMA(RR(wA[:, sl, :, :]), w_in_l[:, :, col0:col0 + 128], [], [("wA", sl)], q="pool")
                for tt in range(4):
                    tsl = slice(tt * 512, (tt + 1) * 512)
                    bi = (ci * 4 + tt) % 4
                    for k in range(8):
                        MM(ps[bi][:, :], wA[:, sl, k, :], hbuf[:, k, tsl], k == 0, k == 7,
                           [("wA", sl), ("h", k, tt)], [("ps", bi)], r=True)
                    CP("act" if tt % 2 else "dve", stg[:, sl, tsl], ps[bi][:, :], [("ps", bi)], [("stg", sl, tt)])
                allst = [("stg", sl, tt) for tt in range(4)]
                if shift:
                    TT("pool", stg2[:, sl, 1:S], stg[:, sl, 0:S - 1], stg[:, sl, 1:S], ALU.subtract, allst, [("stg2", sl)])
                    TS("pool", stg2[:, sl, 0:1], stg[:, sl, 0:1], -1.0, None, ALU.mult, None, allst, [("stg2", sl)])
                    STT(stg2[:, sl, :], stg2[:, sl, :], muA[:, ci:ci + 1], stg[:, sl, :], ALU.mult, ALU.add,
                        allst + [("stg2", sl), "muA"], [("stg2", sl)])
                    DMA(dst[row0:row0 + 128, :], stg2[:, sl, :], [("stg2", sl)], [("dr", id(dst), row0)], q="pool")
                else:
                    DMA(dst[row0:row0 + 128, :], stg[:, sl, :], allst, [("dr", id(dst), row0)], q="pool")
            sc.barrier()
            es.close()
            es = es_main
            wB = sb(es, "wB", [128, 8, 640])
            DMA(RR(wB[:, :, 0:384]), w_in_l[:, :, 2176:2560], [], ["wB"], q="pool")
            DMA(RR(wB[:, :, 384:640]), w_in_l[:, :, 2816:3072], [], ["wB"], q="pool")
            vst = sb(es, "vst", [128, 2, 640])
            for t in range(16):
                sl = t % 2
                b0, b1 = 4 + (t % 2) * 2, 5 + (t % 2) * 2
                for k in range(8):
                    MM(ps[b0][:, 0:384], hbuf[:, k, t * 128:(t + 1) * 128], wB[:, k, 0:384], k == 0, k == 7,
                       [("h", k, t // 4), "wB"], [("ps", b0)], r=True)
                for k in range(8):
                    MM(ps[b1][:, 0:256], hbuf[:, k, t * 128:(t + 1) * 128], wB[:, k, 384:640], k == 0, k == 7,
                       [("h", k, t // 4), "wB"], [("ps", b1)], r=True)
                CP("act", vst[:, sl, 0:384], ps[b0][:, 0:384], [("ps", b0)], [("vst", sl, 0)])
                CP("dve", vst[:, sl, 384:640], ps[b1][:, 0:256], [("ps", b1)], [("vst", sl, 1)])
                DMA(vm_tm[t * 128:(t + 1) * 128, :], vst[:, sl, 0:384], [("vst", sl, 0)], [("vm", t)], q="pool")
                DMA(vg_tm[t * 128:(t + 1) * 128, :], vst[:, sl, 384:640], [("vst", sl, 1)], [("vg", t)], q="pool")
            sc.barrier()
        if stop == "A":
            break

        with ExitStack() as es:
            lng = sb(es, "lng", [128, 256])
            lnb = sb(es, "lnb", [128, 256])
            DMA(lng[:, :], PR["gmlp_ln_g"][l].partition_broadcast(128), [], ["lng"])
            DMA(lnb[:, :], PR["gmlp_ln_b"][l].partition_broadcast(128), [], ["lnb"])
            wsn = sb(es, "wsn", [128, 4, 128])
            wsT = sb(es, "wsT", [128, 4, 128])
            bsr = sb(es, "bsr", [1, 512])
            DMA(wsn[:, :, :], PR["gmlp_w_s"][l].rearrange("g t s -> t g s"), [], ["wsn"])
            DMA(bsr[:, :], PR["gmlp_b_s"][l].rearrange("g t -> (g t)").partition_broadcast(1), [], ["bsr"])
            for g in range(4):
                TR(ps[0][:, g * 128:(g + 1) * 128], wsn[:, g, :], ident, ["wsn", "cst"], [("ps", 0)])
            for g in range(4):
                TT("dve", wsT[:, g, :], ps[0][:, g * 128:(g + 1) * 128], iuf, ALU.mult, [("ps", 0), "cst"], ["wsT"])
            gu = sb(es, "gu", [128, 2, S])
            t1 = sb(es, "t1", [128, S])
            cout = sb(es, "cout", [128, 2, S])
            for pp in range(2):
                DMA(gu[:, pp, :], uT[pp * 128:(pp + 1) * 128, :], [], [("gu", pp)])
                ACT(t1[:, :], gu[:, pp, :], AF.Square, [("gu", pp)], ["t1"])
                TS("pool", t1[:, :], t1[:, :], 0.044715, 1.0, ALU.mult, ALU.add, ["t1"], ["t1"])
                TT("dve", t1[:, :], t1[:, :], gu[:, pp, :], ALU.mult, ["t1", ("gu", pp)], ["t1"])
                ACT(t1[:, :], t1[:, :], AF.Sigmoid, ["t1"], ["t1"], scale=2.0 * 0.7978845608028654)
                TT("dve", gu[:, pp, :], gu[:, pp, :], t1[:, :], ALU.mult, ["t1", ("gu", pp)], [("gu", pp)])
            vb = sb(es, "vb", [128, 2, 256])
            t2 = sb(es, "t2", [128, 2, 256])
            st6 = sb(es, "st6", [128, 2, 8])
            for c in range(16):
                sl = c % 2
                DMA(vb[:, sl, :], vg_tm[c * 128:(c + 1) * 128, :], [], [("vb", sl)])
                kv, kt = ("vb", sl), ("t2", sl)
                ACT(t2[:, sl, :], vb[:, sl, :], AF.Square, [kv], [kt])
                TS("pool", t2[:, sl, :], t2[:, sl, :], 0.044715, 1.0, ALU.mult, ALU.add, [kt], [kt])
                TT("dve", t2[:, sl, :], t2[:, sl, :], vb[:, sl, :], ALU.mult, [kt, kv], [kt])
                ACT(t2[:, sl, :], t2[:, sl, :], AF.Sigmoid, [kt], [kt], scale=2.0 * 0.7978845608028654)
                TT("dve", vb[:, sl, :], vb[:, sl, :], t2[:, sl, :], ALU.mult, [kt, kv], [kv])
                ks = ("st6", sl)
                sc.op("dve", lambda e, sl=sl: e.bn_stats(out=st6[:, sl, 0:6], in_=vb[:, sl, :]), [kv], [ks])
                sc.op("dve", lambda e, sl=sl: e.bn_aggr(out=st6[:, sl, 6:8], in_=st6[:, sl, 0:6]), [ks], [ks])
                ACT(st6[:, sl, 7:8], st6[:, sl, 7:8], AF.Sqrt, [ks, "epsc"], [ks], bias=epsc[:, 0:1], scale=1.0)
                RECIP(st6[:, sl, 7:8], st6[:, sl, 7:8], [ks], [ks])
                TS("dve", vb[:, sl, :], vb[:, sl, :], st6[:, sl, 6:7], st6[:, sl, 7:8], ALU.subtract, ALU.mult, [kv, ks], [kv])
                TT("dve", vb[:, sl, :], vb[:, sl, :], lng[:, :], ALU.mult, [kv, "lng"], [kv])
                TT("dve", vb[:, sl, :], vb[:, sl, :], lnb[:, :], ALU.add, [kv, "lnb"], [kv])
                for pp in range(2):
                    bi = 1 + (c % 2) * 2 + pp
                    for gg in range(2):
                        g = pp * 2 + gg
                        MM(ps[bi][:, gg * 128:(gg + 1) * 128], vb[:, sl, pp * 128:(pp + 1) * 128], wsT[:, g, :], True, False, [kv, "wsT"], [("ps", bi)])
                        MM(ps[bi][:, gg * 128:(gg + 1) * 128], ones[0:1, :], bsr[0:1, g * 128:(g + 1) * 128], False, True, ["cst", "bsr"], [("ps", bi)])
                    for gg in range(2):
                        TT("dve", cout[gg * 64:(gg + 1) * 64, pp, c * 128:(c + 1) * 128], ps[bi][gg * 64:(gg + 1) * 64, gg * 128:(gg + 1) * 128],
                           gu[gg * 64:(gg + 1) * 64, pp, c * 128:(c + 1) * 128], ALU.mult, [("ps", bi), ("gu", pp)], [("cout", pp)])
            for pp in range(2):
                DMA(catT[768 + pp * 128:768 + (pp + 1) * 128, :], cout[:, pp, :], [("cout", pp)], [("cat", 6 + pp)], q="pool")
            sc.barrier()
        if stop == "D":
            break

        with ExitStack() as es:
            qa = sb(es, "qa", [72, 2, S])
            ka = sb(es, "ka", [72, 2, S])
            vt = sb(es, "vt", [128, 2, 16, 128])
            ones_r = sb(es, "ones_r", [128, 128])
            CP("dve", RR(ones_r[:, :]), ones, ["cst"], ["ones_r"])
            kbar = sb(es, "kbar", [64, 2, 8])
            mobc = sb(es, "mobc", [128, 3, 16, 48])
            NP = sb(es, "NP", [128, 2, 16, 72])
            sm = sb(es, "sm", [128, 16, 8])
            top8 = sb(es, "top8", [128, 16, 8])
            al = sb(es, "al", [128, 16, 8])
            pt = sb(es, "pt", [128, 4, 512])
            pacc = sb(es, "pacc", [128, 512])
            rden = sb(es, "rden", [128, 512])
            ob = sb(es, "ob", [128, 2, 512])
            DMA(mobc[:, :, :, :], mobac_d.rearrange("p (a t c) -> p a t c", a=3, t=16), [], ["mobc"])
            for q in range(2):
                DMA(RR(ka[64:72, q, :]), blkind_d[:, :], [], [("kaB", q)], q="pool")
            sc.op("pool", lambda e: e.memset(NP[:, :, :, :], 0.0), [], [("NP", 0), ("NP", 1)])
            pti = [0]

            def prepA(h):
                q = h % 2
                DMA(RR(qa[0:64, q, :]), qkT[h * 64:(h + 1) * 64, :], [], [("qaQ", q)], q="pool")
                DMA(RR(ka[0:64, q, :]), qkT[384 + h * 64:384 + (h + 1) * 64, :], [], [("kaK", q)], q="pool")
                if h % 2 == 0:
                    vq = (h // 2) % 2
                    DMA(RR(vt[:, vq, :, :]), vm_tm.rearrange("(t p) c -> p t c", p=128)[:, :, h * 64:(h + 2) * 64], [], [("vt", vq)], q="pool")
                sc.op("dve", lambda e: e.tensor_reduce(out=kbar[:, q, :], in_=ka[0:64, q, :].rearrange("p (n k) -> p n k", n=8), axis=AX.X, op=ALU.add), [("kaK", q)], [("kbar", q)])
                for t in range(16):
                    MM(ps[0][:, t * 8:(t + 1) * 8], qa[0:64, q, t * 128:(t + 1) * 128], kbar[:, q, :], True, True, [("qaQ", q), ("kbar", q)], [("ps", 0)])

            def prepB(h):
                q = h % 2
                hs = slice(h * 8, (h + 1) * 8)
                TT("dve", sm[:, :, :], ps[0][:, 0:128].rearrange("p (t n) -> p t n", t=16), mobc[:, 0, :, hs], ALU.add, [("ps", 0), "mobc"], ["sm"])
                for t in range(16):
                    sc.op("dve", lambda e, t=t: e.max(out=top8[:, t, :], in_=sm[:, t, :]), ["sm"], [("top8", t)])
                for t in range(16):
                    TS("dve", al[:, t, :], sm[:, t, :], top8[:, t, 2:3], None, ALU.is_ge, None, ["sm", ("top8", t)], [("al", t)])
                allal = [("al", t) for t in range(16)]
                TT("dve", al[:, :, :], al[:, :, :], mobc[:, 1, :, hs], ALU.mult, allal + ["mobc"], ["al2"])
                TT("dve", al[:, :, :], al[:, :, :], mobc[:, 2, :, hs], ALU.add, ["al2", "mobc"], ["al2"])
                TS("dve", NP[:, q, :, 64:72], al[:, :, :], -1.0, BIG, ALU.add, ALU.mult, ["al2"], [("NP", q)])

            def prepC(h):
                q = h % 2
                for t4 in range(4):
                    for tq in range(4):
                        t = t4 * 4 + tq
                        MM(ps[1][0:72, tq * 128:(tq + 1) * 128], NP[:, q, t, :], ident, True, True, [("NP", q), "cst"], [("ps", 1)])
                    CP("act", RR(qa[64:72, q, t4 * 512:(t4 + 1) * 512]), ps[1][64:72, :], [("ps", 1)], [("qaM", q)])

            def attn(h, qt):
                q = h % 2
                vq = (h // 2) % 2
                hp = slice((h % 2) * 64, (h % 2) * 64 + 64)
                qsl = slice(qt * 512, (qt + 1) * 512)
                nk = (qt + 1) * 4
                osl = qt % 2
                pis = {}

                def qk(kt):
                    sb_i = 2 + kt % 2
                    pi = pti[0] % 4
                    pti[0] += 1
                    pis[kt] = pi
                    MM(ps[sb_i][:, :], ka[0:72, q, kt * 128:(kt + 1) * 128], qa[0:72, q, qsl], True, True,
                       [("kaK", q), ("kaB", q), ("qaQ", q), ("qaM", q)], [("ps", sb_i)], r=True)
                    ACT(RR(pt[:, pi, :]), ps[sb_i][:, :], AF.Exp, [("ps", sb_i)], [("pt", pi)], scale=0.125)
                    if kt >= qt * 4:
                        j = kt - qt * 4
                        TT("dve", RR(pt[:, pi, :]), pt[:, pi, :], cm[:, j * 512:(j + 1) * 512], ALU.mult, [("pt", pi), "cst"], [("pt", pi)])

                def pv(kt):
                    pi = pis[kt]
                    MM(ps[4][:, :], vt[:, vq, kt, :], pt[:, pi, :], kt == 0, kt == nk - 1, [("vt", vq), ("pt", pi)], [("ps", 4)], r=True)
                    if kt == 0:
                        CP("pool", RR(pacc[:, :]), pt[:, pi, :], [("pt", pi)], ["pacc"])
                    else:
                        TT("pool", RR(pacc[:, :]), pacc[:, :], pt[:, pi, :], ALU.add, [("pt", pi), "pacc"], ["pacc"])

                qk(0)
                for kt in range(nk):
                    if kt + 1 < nk:
                        qk(kt + 1)
                    pv(kt)
                MM(ps[5][:, :], ones_r[:, :], pacc[:, :], True, True, ["ones_r", "pacc"], [("ps", 5)], r=True)
                RECIP(rden[hp, :], ps[5][hp, :], [("ps", 5)], ["rden"])
                TT("dve", ob[hp, osl, :], ps[4][hp, :], rden[hp, :], ALU.mult, [("ps", 4), "rden"], [("ob", osl)])
                DMA(catT[384 + h * 64:384 + (h + 1) * 64, qsl], ob[hp, osl, :], [("ob", osl)], [("cat", "b", h, qt)])

            prepA(0)
            prepB(0)
            prepC(0)
            for h in range(H):
                for qt in range(4):
                    attn(h, qt)
                    if h + 1 < H:
                        if qt == 0:
                            prepA(h + 1)
                        elif qt == 1:
                            prepB(h + 1)
                        elif qt == 2:
                            prepC(h + 1)
            sc.barrier()
        if stop == "C":
            break

        with ExitStack() as es:
            TW = 128
            NTB = S // TW
            w2s = sb(es, "w2s", [64, 384])
            a2s = sb(es, "a2s", [64, 384])
            g2s = sb(es, "g2s", [128, 384])
            DMA(w2s[:, :], PR["rwkv_w2"][l], [], ["w2s"])
            DMA(a2s[:, :], PR["rwkv_a2"][l], [], ["a2s"])
            DMA(g2s[:, :], PR["rwkv_g2"][l], [], ["g2s"])
            pw0 = colvec(es, "pw0", PR["rwkv_w0"][l], 6, p=64)
            pa0 = colvec(es, "pa0", PR["rwkv_a0"][l], 6, p=64)
            pkk = colvec(es, "pkk", PR["rwkv_k_k"][l], 6, p=64)
            pka = colvec(es, "pka", PR["rwkv_k_a"][l], 6, p=64)
            prk = colvec(es, "prk", PR["rwkv_r_k"][l].rearrange("h d -> (h d)"), 6, p=64)
            plg = colvec(es, "plg", PR["rwkv_lnx_g"][l], 6, p=64)
            plb = colvec(es, "plb", PR["rwkv_lnx_b"][l], 6, p=64)
            pok = sb(es, "pok", [64, 6])
            TS("dve", pok[:, :], pka[:, :], -1.0, 1.0, ALU.mult, ALU.add, ["pka"], ["pok"])
            i64 = ident[0:64, 0:64]
            o64 = ones[0:64, 0:64]
            rowm = cst[:, OF["rowm"]:OF["rowm"] + 2]
            Mst = sb(es, "Mst", [64, 2, 6, 64])
            sc.op("dve", lambda e: e.memset(Mst[:, 0, :, :], 0.0), [], [("Mst", 0)])
            mcur = [0]
            RhatT = sb(es, "RhatT", [64, 6, TW])
            Y0T = sb(es, "Y0T", [64, 6, TW])
            yT = sb(es, "yT", [64, 6, TW])
            GT = sb(es, "GT", [64, 6, 2, 64])
            Hm = sb(es, "Hm", [64, 6, 2, 64])
            bon = sb(es, "bon", [64, 2, 6, TW]); gal = sb(es, "gal", [64, 2, 6, TW])
            ARh = sb(es, "ARh", [64, 2, 6, 2, TW]); BKh = sb(es, "BKh", [64, 2, 6, 2, TW]); BPh = sb(es, "BPh", [64, 2, 6, 2, TW])
            vvh = sb(es, "vvh", [64, 2, 6, TW]); pC = sb(es, "pC", [64, 2, 6, 2])
            wd = sb(es, "wd", [64, 2, TW]); ad = sb(es, "ad", [64, 2, TW]); gd = sb(es, "gd", [128, 2, TW])
            T6 = lambda nm: sb(es, nm, [64, 6, TW])
            rr = T6("rr"); kq = T6("kq"); sig = T6("sig"); cum = T6("cum"); cpv = T6("cpv")
            epos = T6("epos"); eneg = T6("eneg"); eprv = T6("eprv"); eend = T6("eend")
            aa = T6("aa"); kk = T6("kk"); kk2 = T6("kk2"); rn = T6("rn"); kka = T6("kka"); kp = T6("kp"); rkr = T6("rkr")
            nbc = sb(es, "nbc", [64, 6, 2])
            tm = sb(es, "tm", [128, 6, 256]); Eb = sb(es, "Eb", [128, 6, 512]); YS = sb(es, "YS", [128, 6, 256])
            Lb = sb(es, "Lb", [128, 6, 2, 384]); B2 = sb(es, "B2", [128, 6, 128]); K2 = sb(es, "K2", [128, 6, 128])
            yc = sb(es, "yc", [64, 2, TW]); ysq = sb(es, "ysq", [64, 2, TW]); yrs = sb(es, "yrs", [64, 2, TW])
            obuf = sb(es, "obuf", [64, 2, TW])

            def tile_pro(tt):
                tsl = slice(tt * TW, (tt + 1) * TW)
                ws = tt % 2
                DMA(wd[:, ws, :], rwkvT[1152:1216, tsl], [], [("wd", ws)])
                DMA(ad[:, ws, :], rwkvT[1216:1280, tsl], [], [("ad", ws)])
                DMA(gd[:, ws, :], rwkvT[1280:1408, tsl], [], [("gd", ws)])
                yield
                ACT(wd[:, ws, :], wd[:, ws, :], AF.Tanh, [("wd", ws)], [("wd", ws)])
                ACT(gd[:, ws, :], gd[:, ws, :], AF.Sigmoid, [("gd", ws)], [("gd", ws)])
                yield

            def stage0(tt, h):
                tsl = slice(tt * TW, (tt + 1) * TW)
                ws = tt % 2
                hc = slice(h * 64, (h + 1) * 64)
                K_ = lambda nm: (nm, h)
                pb = ps[h % 2]
                pk = ("ps", h % 2)
                DMA(rr[:, h, :], rwkvT[h * 64:(h + 1) * 64, tsl], [], [K_("rr")])
                DMA(kq[:, h, :], rwkvT[384 + h * 64:384 + (h + 1) * 64, tsl], [], [K_("kq")])
                DMA(vvh[:, ws, h, :], rwkvT[768 + h * 64:768 + (h + 1) * 64, tsl], [], [("vv", ws, h)])
                yield
                MM(pb[0:64, 0:TW], w2s[:, hc], wd[:, ws, :], True, True, ["w2s", ("wd", ws)], [pk])
                ACT(sig[:, h, :], pb[0:64, 0:TW], AF.Sigmoid, [pk, "pw0"], [K_("sig")], bias=pw0[:, h:h + 1])
                yield
                sc.op("dve", lambda e, h=h: e.tensor_tensor_scan(out=cum[:, h, :], data0=resetm[0:64, 0:TW], data1=sig[:, h, :], initial=0.0, op0=ALU.mult, op1=ALU.add), [K_("sig"), "cst"], [K_("cum")])
                yield
                TT("pool", cpv[:, h, :], cum[:, h, :], sig[:, h, :], ALU.subtract, [K_("cum"), K_("sig")], [K_("cpv")])
                ACT(epos[:, h, :], cum[:, h, :], AF.Exp, [K_("cum")], [K_("epos")], scale=-C0)
                ACT(eneg[:, h, :], cum[:, h, :], AF.Exp, [K_("cum")], [K_("eneg")], scale=C0)
                cum3 = cum[:, h, :].rearrange("p (c t) -> p c t", c=2)
                epos3 = epos[:, h, :].rearrange("p (c t) -> p c t", c=2)
                TS("dve", nbc[:, h, :], cum3[:, :, 63], -C0, None, ALU.mult, None, [K_("cum")], [K_("nbc")])
                yield
                ACT(eprv[:, h, :], cpv[:, h, :], AF.Exp, [K_("cpv")], [K_("eprv")], scale=-C0)
                CP("pool", pC[:, ws, h, :], epos3[:, :, 63], [K_("epos")], [("pC", ws, h)])
                for c in range(2):
                    ACT(eend[:, h, c * 64:(c + 1) * 64], cum[:, h, c * 64:(c + 1) * 64], AF.Exp, [K_("cum"), K_("nbc")], [K_("eend")], scale=C0, bias=nbc[:, h, c:c + 1])
                yield
                MM(pb[0:64, 128:128 + TW], a2s[:, hc], ad[:, ws, :], True, True, ["a2s", ("ad", ws)], [pk])
                ACT(aa[:, h, :], pb[0:64, 128:128 + TW], AF.Sigmoid, [pk, "pa0"], [K_("aa")], bias=pa0[:, h:h + 1])
                yield
                MM(pb[0:64, 256:256 + TW], g2s[:, hc], gd[:, ws, :], True, True, ["g2s", ("gd", ws)], [pk])
                CP("act", gal[:, ws, h, :], pb[0:64, 256:256 + TW], [pk], [("gal", ws, h)])
                yield
                TS("dve", kk[:, h, :], kq[:, h, :], pkk[:, h:h + 1], None, ALU.mult, None, [K_("kq"), "pkk"], [K_("kk")])
                yield
                ACT(kk2[:, h, :], kk[:, h, :], AF.Square, [K_("kk")], [K_("kk2")])
                yield
                MM(pb[0:64, 384:384 + TW], o64, kk2[:, h, :], True, True, ["cst", K_("kk2")], [pk])
                ACT(rn[:, h, :], pb[0:64, 384:384 + TW], AF.Sqrt, [pk], [K_("rn")])
                yield
                TS("dve", rn[:, h, :], rn[:, h, :], 1e-12, None, ALU.max, None, [K_("rn")], [K_("rn")])
                yield
                RECIP(rn[:, h, :], rn[:, h, :], [K_("rn")], [K_("rn")])
                yield
                TT("dve", kk[:, h, :], kk[:, h, :], rn[:, h, :], ALU.mult, [K_("kk"), K_("rn")], [K_("kk")])
                yield
                TT("pool", kka[:, h, :], kk[:, h, :], aa[:, h, :], ALU.mult, [K_("kk"), K_("aa")], [K_("kka")])
                TS("dve", rn[:, h, :], aa[:, h, :], pka[:, h:h + 1], pok[:, h:h + 1], ALU.mult, ALU.add, [K_("aa"), "pka", "pok", K_("rn")], [K_("rn")])
                STT(ARh[:, ws, h, 0, :], kk[:, h, :], -1.0, eprv[:, h, :], ALU.mult, ALU.mult, [K_("kk"), K_("eprv")], [("AR0", ws, h)])
                yield
                TT("dve", kp[:, h, :], kq[:, h, :], rn[:, h, :], ALU.mult, [K_("kq"), K_("rn")], [K_("kp")])
                TT("pool", ARh[:, ws, h, 1, :], rr[:, h, :], epos[:, h, :], ALU.mult, [K_("rr"), K_("epos")], [("AR1", ws, h)])
                yield
                STT(rkr[:, h, :], rr[:, h, :], prk[:, h:h + 1], kp[:, h, :], ALU.mult, ALU.mult, [K_("rr"), "prk", K_("kp")], [K_("rkr")])
                TT("pool", BKh[:, ws, h, 0, :], kka[:, h, :], eneg[:, h, :], ALU.mult, [K_("kka"), K_("eneg")], [("BK0", ws, h)])
                TT("pool", BKh[:, ws, h, 1, :], kp[:, h, :], eneg[:, h, :], ALU.mult, [K_("kp"), K_("eneg")], [("BK1", ws, h)])
                yield
                MM(pb[0:64, 0:TW], o64, rkr[:, h, :], True, True, ["cst", K_("rkr")], [pk])
                TT("dve", bon[:, ws, h, :], pb[0:64, 0:TW], vvh[:, ws, h, :], ALU.mult, [pk, ("vv", ws, h)], [("bon", ws, h)])
                TT("pool", BPh[:, ws, h, 0, :], kka[:, h, :], eend[:, h, :], ALU.mult, [K_("kka"), K_("eend")], [("BP", ws, h)])
                TT("pool", BPh[:, ws, h, 1, :], kp[:, h, :], eend[:, h, :], ALU.mult, [K_("kp"), K_("eend")], [("BP", ws, h)])
                yield

            def make_gens(tt):
                if tt >= NTB:
                    return []
                return [tile_pro(tt)] + [stage0(tt, h) for h in range(H)]

            def advance(gl, n):
                for _ in range(n):
                    for g in list(gl):
                        try:
                            next(g)
                        except StopIteration:
                            gl.remove(g)

            def chain(tt, nxt):
                ws = tt % 2
                tsl = slice(tt * TW, (tt + 1) * TW)
                HS = range(H)
                bk = lambda h: ps[2 + h]
                bkk = lambda h: ("ps", 2 + h)
                A0 = lambda h: ("AR0", ws, h)
                A1 = lambda h: ("AR1", ws, h)
                for h in HS:
                    for i, (src, kx) in enumerate([(ARh[:, ws, h, 0, :], A0(h)), (BPh[:, ws, h, 0, :], ("BP", ws, h)), (BPh[:, ws, h, 1, :], ("BP", ws, h)), (vvh[:, ws, h, :], ("vv", ws, h))]):
                        TR(bk(h)[:, i * 64:(i + 1) * 64], src, i64, [kx, "cst"], [bkk(h)])
                for h in HS:
                    CP("act", tm[:, h, :], bk(h)[:, 0:256], [bkk(h)], [("tm", h)])
                advance(nxt, 2)
                for h in HS:
                    MM(bk(h)[:, 0:256], BKh[:, ws, h, 0, :], ARh[:, ws, h, :, :], True, True, [("BK0", ws, h), A0(h), A1(h)], [bkk(h)])
                    MM(bk(h)[:, 256:384], ARh[:, ws, h, 0, :], BKh[:, ws, h, 0, :], True, True, [("BK0", ws, h), A0(h)], [bkk(h)])
                for h in HS:
                    TT("dve", Eb[:, h, 0:384], bk(h)[:, 0:384], mE, ALU.mult, [bkk(h), "cst"], [("E", h, "a")])
                advance(nxt, 2)
                for h in HS:
                    MM(bk(h)[:, 0:256], BKh[:, ws, h, 1, :], ARh[:, ws, h, :, :], True, True, [("BK1", ws, h), A0(h), A1(h)], [bkk(h)])
                for h in HS:
                    TT("dve", YS[:, h, :], bk(h)[:, 0:256], mY, ALU.mult, [bkk(h), "cst"], [("YS", h)])
                advance(nxt, 2)
                for h in HS:
                    MM(bk(h)[:, 256:320], YS[:, h, 0:128], tm[:, h, 192:256], True, True, [("YS", h), ("tm", h)], [bkk(h)])
                    CP("pool", Eb[:, h, 384:448], tm[:, h, 0:64], [("tm", h)], [("E", h, "b")])
                for h in HS:
                    CP("act", Eb[:, h, 448:512], bk(h)[:, 256:320], [bkk(h)], [("E", h, "c")])
                advance(nxt, 2)
                for lev in range(6):
                    def views(h):
                        if lev == 0:
                            return (Eb[:, h, 0:128], Eb[:, h, 256:384], Eb[:, h, 256:512], Eb[:, h, 384:512],
                                    [("E", h, "a"), ("E", h, "b"), ("E", h, "c")])
                        Lp = Lb[:, h, (lev - 1) % 2, :]
                        return (Lp[:, 0:128], Lp[:, 128:256], Lp[:, 128:384], Lp[:, 256:384],
                                [("L", h, (lev - 1) % 2, "p"), ("L", h, (lev - 1) % 2, "z")])
                    for h in HS:
                        PT_, P_, PZ_, Z_, rk = views(h)
                        MM(bk(h)[:, 128:384], PT_, PZ_, True, True, rk, [bkk(h)])
                        if lev < 5:
                            MM(bk(h)[:, 0:128], P_, PT_, True, True, rk, [bkk(h)])
                    for h in HS:
                        PT_, P_, PZ_, Z_, rk = views(h)
                        Ln = Lb[:, h, lev % 2, :]
                        TT("dve", Ln[:, 256:384], bk(h)[:, 256:384], Z_, ALU.add, [bkk(h)] + rk, [("L", h, lev % 2, "z")])
                        if lev < 5:
                            CP("act", Ln[:, 0:256], bk(h)[:, 0:256], [bkk(h)], [("L", h, lev % 2, "p")])
                    advance(nxt, 3)
                for h in HS:
                    for hf in range(2):
                        TS("pool", B2[:, h, hf * 64:(hf + 1) * 64], tm[:, h, 64:128], rowm[:, hf:hf + 1], None, ALU.mult, None, [("tm", h), "cst"], [("B2", h)])
                        TS("pool", K2[:, h, hf * 64:(hf + 1) * 64], tm[:, h, 128:192], rowm[:, hf:hf + 1], None, ALU.mult, None, [("tm", h), "cst"], [("K2", h)])
                for h in HS:
                    Lf = Lb[:, h, 1, :]
                    kLf = ("L", h, 1, "z")
                    W_, U0_ = Lf[:, 256:320], Lf[:, 320:384]
                    b_ = bk(h)
                    MM(b_[0:64, 0:128], W_, B2[:, h, :], True, True, [kLf, ("B2", h)], [bkk(h)])
                    for hf in range(2):
                        MM(b_[0:64, 128 + hf * 64:128 + (hf + 1) * 64], K2[:, h, hf * 64:(hf + 1) * 64], tm[:, h, 192:256], True, False, [("K2", h), ("tm", h)], [bkk(h)])
                        MM(b_[0:64, 128 + hf * 64:128 + (hf + 1) * 64], B2[:, h, hf * 64:(hf + 1) * 64], U0_, False, True, [("B2", h), kLf], [bkk(h)])
                    MM(b_[0:64, 256:384], W_, Eb[:, h, 128:256], True, True, [kLf, ("E", h, "a")], [bkk(h)])
                    MM(b_[0:64, 384:512], tm[:, h, 192:256], YS[:, h, 128:256], True, False, [("tm", h), ("YS", h)], [bkk(h)])
                    MM(b_[0:64, 384:512], U0_, Eb[:, h, 128:256], False, True, [kLf, ("E", h, "a")], [bkk(h)])
                advance(nxt, 2)
                for h in HS:
                    b_ = bk(h)
                    for hf in range(2):
                        STT(GT[:, h, hf, :], i64, pC[:, ws, h, hf:hf + 1], b_[0:64, hf * 64:(hf + 1) * 64], ALU.mult, ALU.add,
                            [bkk(h), ("pC", ws, h), "cst"], [("GT", h)])
                    TT("dve", RhatT[:, h, :], b_[0:64, 256:384], ARh[:, ws, h, 1, :], ALU.add, [bkk(h), A1(h)], [("Rhat", h)])
                for h in HS:
                    b_ = bk(h)
                    CP("act", Hm[:, h, :, :], b_[0:64, 128:256].rearrange("p (c i) -> p c i", c=2), [bkk(h)], [("Hm", h)])
                    CP("act", Y0T[:, h, :], b_[0:64, 384:512], [bkk(h)], [("Y0T", h)])
                advance(nxt, 2)
                allh = lambda nm: [(nm, h) for h in range(H)]
                for c in range(2):
                    csl = slice(c * 64, (c + 1) * 64)
                    m0 = mcur[0]
                    mnew = 1 - m0
                    for h in range(H):
                        MM(ps[0][0:64, h * 64:(h + 1) * 64], Mst[:, m0, h, :], RhatT[:, h, csl], True, True, [("Mst", m0), ("Rhat", h)], [("ps", 0)])
                    for h in range(H):
                        MM(ps[1][0:64, h * 64:(h + 1) * 64], GT[:, h, c, :], Mst[:, m0, h, :], True, True, [("Mst", m0), ("GT", h)], [("ps", 1)])
                    TT("dve", Mst[:, mnew, :, :], ps[1][0:64, 0:384].rearrange("p (h i) -> p h i", h=6), Hm[:, :, c, :], ALU.add,
                       [("ps", 1)] + allh("Hm"), [("Mst", mnew)])
                    TT("dve", yT[:, :, csl], ps[0][0:64, 0:384].rearrange("p (h t) -> p h t", h=6), Y0T[:, :, csl], ALU.add,
                       [("ps", 0)] + allh("Y0T"), [("yT", c)])
                    mcur[0] = mnew
                    advance(nxt, 1)
                ally = [("yT", c) for c in range(2)]
                for h in range(H):
                    osl = h % 2
                    kyc, kysq, kyrs = ("yc", osl), ("ysq", osl), ("yrs", osl)
                    pb = ps[osl]
                    pk = ("ps", osl)
                    MM(pb[0:64, 0:TW], o64, yT[:, h, :], True, True, ["cst"] + ally, [pk])
                    STT(yc[:, osl, :], pb[0:64, 0:TW], -1.0 / 64, yT[:, h, :], ALU.mult, ALU.add, [pk] + ally, [kyc])
                    ACT(ysq[:, osl, :], yc[:, osl, :], AF.Square, [kyc], [kysq])
                    MM(pb[0:64, 128:128 + TW], o64, ysq[:, osl, :], True, True, ["cst", kysq], [pk])
                    ACT(yrs[:, osl, :], pb[0:64, 128:128 + TW], AF.Sqrt, [pk, "epsc"], [kyrs], scale=1.0 / 64, bias=epsc[0:64, 1:2])
                    RECIP(yrs[:, osl, :], yrs[:, osl, :], [kyrs], [kyrs])
                    TT("dve", yc[:, osl, :], yc[:, osl, :], yrs[:, osl, :], ALU.mult, [kyc, kyrs], [kyc])
                    TS("dve", yc[:, osl, :], yc[:, osl, :], plg[:, h:h + 1], plb[:, h:h + 1], ALU.mult, ALU.add, [kyc, "plg", "plb"], [kyc])
                    TT("pool", yc[:, osl, :], yc[:, osl, :], bon[:, ws, h, :], ALU.add, [kyc, ("bon", ws, h)], [kyc])
                    TT("pool", obuf[:, osl, :], yc[:, osl, :], gal[:, ws, h, :], ALU.mult, [kyc, ("gal", ws, h)], [("obuf", osl)])
                    DMA(catT[h * 64:(h + 1) * 64, tsl], obuf[:, osl, :], [("obuf", osl)], [("cat", "a", h, tt)])
                    advance(nxt, 1)
                advance(nxt, 1000)

            g0 = make_gens(0)
            advance(g0, 1000)
            for tt in range(NTB):
                chain(tt, make_gens(tt + 1))
            sc.barrier()
        if stop == "B":
            break

        with ExitStack() as es:
            wO = sb(es, "wO", [128, 2, 8, 128])
            catb = sb(es, "catb", [128, 2, 8, 512])
            mixb = sb(es, "mixb", [128, 8, 512])
            sq = sb(es, "sq", [128, 2, 512])
            rstd = sb(es, "rstd", [128, 2, 512])
            gP = colvec(es, "gP", PR["post_mix_g"][l], 8)
            w_out_l = PR["w_out"][l].rearrange("(k p) c -> p k c", p=128)
            cat_v = catT.rearrange("(k p) t -> p k t", p=128)
            wi = 0
            xe = sb(es, "xe", [128, 2, 8, 512])
            for tt in range(4):
                tsl = slice(tt * 512, (tt + 1) * 512)
                cs = tt % 2
                DMA(xe[:, cs, :, :], xD_v[:, :, tsl], [("xD", tt)], [("xe", cs, j) for j in range(8)])
                DMA(RR(catb[:, cs, :, :]), cat_v[:, :, tsl], [], [("catb", cs)], q="pool")
                for j in range(8):
                    sl = wi % 2
                    wi += 1
                    DMA(RR(wO[:, sl, :, :]), w_out_l[:, :, j * 128:(j + 1) * 128], [], [("wO", sl)], q="pool")
                    bi = j % 2
                    for k in range(8):
                        MM(ps[bi][:, :], wO[:, sl, k, :], catb[:, cs, k, :], k == 0, k == 7, [("wO", sl), ("catb", cs)], [("ps", bi)], r=True)
                    CP("act" if j % 2 else "dve", mixb[:, j, :], ps[bi][:, :], [("ps", bi)], [("mixb", j)])
                r = rms_stats(lambda k: mixb[:, k, :], lambda k: ("mixb", k), tt, (sq, rstd), ps[2 + tt % 2], ("ps", 2 + tt % 2))
                for j in range(8):
                    STT(mixb[:, j, :], mixb[:, j, :], gP[:, j:j + 1], r, ALU.mult, ALU.mult, [("mixb", j), ("rstd", tt % 2), "gP"], [("mixb", j)])
                    TT("dve", xe[:, cs, j, :], xe[:, cs, j, :], mixb[:, j, :], ALU.add, [("mixb", j), ("xe", cs, j)], [("xe", cs, j)])
                DMA(xD_v[:, :, tsl], xe[:, cs, :, :], [("xe", cs, j) for j in range(8)], [("xD", tt)])
            sc.barrier()
        if stop == "E":
            break

        with ExitStack() as es:
            hb = sb(es, "hb", [128, 8, 512])
            actb = sb(es, "actb", [128, NFF, 512])
            wG = sb(es, "wG", [128, 2, 2, 8, 128])
            wD = sb(es, "wD", [128, 2, NFF, 128])
            sq = sb(es, "sq", [128, 2, 512])
            rstd = sb(es, "rstd", [128, 2, 512])
            sil = sb(es, "sil", [128, 2, 512])
            gF = colvec(es, "gF", PR["pre_ffn_g"][l], 8)
            gQ = colvec(es, "gQ", PR["post_ffn_g"][l], 8)
            w_fi = PR["w_ffn_in"][l].rearrange("(k p) c -> p k c", p=128)
            w_fo = PR["w_ffn_out"][l].rearrange("(j p) c -> p j c", p=128)
            wi = 0
            wdi = 0
            xf = sb(es, "xf", [128, 8, 512])
            for tt in range(4):
                tsl = slice(tt * 512, (tt + 1) * 512)
                DMA(xf[:, :, :], xD_v[:, :, tsl], [("xD", tt)], [("xf", j) for j in range(8)])
                r = rms_stats(lambda k: xf[:, k, :], lambda k: ("xf", k), tt, (sq, rstd), ps[6 + tt % 2], ("ps", 6 + tt % 2))
                for k in range(8):
                    STT(RR(hb[:, k, :]), xf[:, k, :], gF[:, k:k + 1], r, ALU.mult, ALU.mult, [("xf", k), ("rstd", tt % 2), "gF"], [("hb", k)])
                for j in range(NFF):
                    sl = wi % 2
                    wi += 1
                    DMA(RR(wG[:, sl, 0, :, :]), w_fi[:, :, j * 128:(j + 1) * 128], [], [("wG", sl, 0)], q="pool")
                    DMA(RR(wG[:, sl, 1, :, :]), w_fi[:, :, DFF + j * 128:DFF + (j + 1) * 128], [], [("wG", sl, 1)], q="pool")
                    bg, bu = (j % 2) * 2, (j % 2) * 2 + 1
                    for k in range(8):
                        MM(ps[bg][:, :], wG[:, sl, 0, k, :], hb[:, k, :], k == 0, k == 7, [("wG", sl, 0), ("hb", k)], [("ps", bg)], r=True)
                    for k in range(8):
                        MM(ps[bu][:, :], wG[:, sl, 1, k, :], hb[:, k, :], k == 0, k == 7, [("wG", sl, 1), ("hb", k)], [("ps", bu)], r=True)
                    ACT(sil[:, j % 2, :], ps[bg][:, :], AF.Silu, [("ps", bg)], [("sil", j % 2)])
                    TT("dve", RR(actb[:, j, :]), ps[bu][:, :], sil[:, j % 2, :], ALU.mult, [("ps", bu), ("sil", j % 2)], [("actb", j)])
                allact = [("actb", j) for j in range(NFF)]
                for jo in range(8):
                    sl = wdi % 2
                    wdi += 1
                    DMA(RR(wD[:, sl, :, :]), w_fo[:, :, jo * 128:(jo + 1) * 128], [], [("wD", sl)], q="pool")
                    bi = 4 + jo % 2
                    for j in range(NFF):
                        MM(ps[bi][:, :], wD[:, sl, j, :], actb[:, j, :], j == 0, j == NFF - 1, [("wD", sl), ("actb", j)], [("ps", bi)], r=True)
                    CP("act" if jo % 2 else "dve", RR(hb[:, jo, :]), ps[bi][:, :], [("ps", bi)], [("hb", jo)])
                r = rms_stats(lambda k: hb[:, k, :], lambda k: ("hb", k), tt + 1, (sq, rstd), ps[6 + (tt + 1) % 2], ("ps", 6 + (tt + 1) % 2))
                for j in range(8):
                    STT(RR(hb[:, j, :]), hb[:, j, :], gQ[:, j:j + 1], r, ALU.mult, ALU.mult, [("hb", j), ("rstd", (tt + 1) % 2), "gQ"], [("hb", j)])
                    TT("dve", xf[:, j, :], xf[:, j, :], hb[:, j, :], ALU.add, [("hb", j), ("xf", j)], [("xf", j)])
                DMA(xD_v[:, :, tsl], xf[:, :, :], [("xf", j) for j in range(8)], [("xD", tt)])
            sc.barrier()
        if OPTS.get("xdbg") == l:
            DMA(xdbg[:, :], xD[:, :], [("xD", tt) for tt in range(4)], [("xdbg", 0)])

    if stop in (None, 'setup'):
        with ExitStack() as es:
            yo = sb(es, "yo", [128, 2, D])
            xo = sb(es, "xo", [128, 2, 8, 512])
            for t in range(16):
                sl = t % 2
                xsl = (t // 4) % 2
                if t % 4 == 0:
                    DMA(xo[:, xsl, :, :], xD_v[:, :, (t // 4) * 512:(t // 4 + 1) * 512], [("xD", t // 4)], [("xo", xsl)])
                for g in range(2):
                    bank = ps[(t * 2 + g) % 4]
                    bk = ("ps", (t * 2 + g) % 4)
                    for kk in range(4):
                        k = g * 4 + kk
                        TR(bank[:, kk * 128:(kk + 1) * 128], xo[:, xsl, k, (t % 4) * 128:(t % 4 + 1) * 128], ident, [("xo", xsl), "cst"], [bk])
                    CP("act" if g else "dve", yo[:, sl, g * 512:(g + 1) * 512], bank[:, :], [bk], [("yo", sl, g)])
                DMA(y_out[t * 128:(t + 1) * 128, :], yo[:, sl, :], [("yo", sl, 0), ("yo", sl, 1)], [("y", t)])
    sc.barrier()
    sc.emit()
    glob.close()
    return nc


_NC_CACHE = {}


def kernel(**inputs):
    if "nc" not in _NC_CACHE:
        _NC_CACHE["nc"] = build()
    nc = _NC_CACHE["nc"]
    x = np.ascontiguousarray(np.asarray(inputs["x"], dtype=np.float32))
    base = {n: np.ascontiguousarray(np.asarray(inputs[n], dtype=np.float32)) for n in PARAM_NAMES}
    base["cst"] = CONSTS["cst"]
    base["mobac"] = CONSTS["mobac"]
    base["blkind"] = CONSTS["blkind"]
    in_maps = []
    for b in range(8):
        m = dict(base)
        m["x"] = x[b]
        in_maps.append(m)
    res = run_bass_kernel_spmd(nc, in_maps, core_ids=list(range(8)))
    return np.stack([np.asarray(r["y"], dtype=np.float32) for r in res.results], 0)
```

```python
import numpy as np
from contextlib import ExitStack
import concourse.bass as bass
import concourse.mybir as mybir
from concourse.bass_utils import run_bass_kernel_spmd

F32 = mybir.dt.float32
F32R = mybir.dt.float32r
AF = mybir.ActivationFunctionType
ALU = mybir.AluOpType
AX = mybir.AxisListType

S = 2048
D = 1024
L = 2
DFF = 2816
NFF = 22
H = 6
C0 = float(np.exp(-0.5))
BIG = 30000.0


OPTS = {}


class Sched:
    def __init__(self, nc, n_dma=40):
        self.nc = nc
        self.names = ["pe", "act", "dve", "pool", "sp"]
        self.sem = {e: nc.alloc_semaphore("s_" + e) for e in ["pe", "act", "dve", "pool"]}
        self.cnt = {e: 0 for e in self.sem}
        self.dsem = [nc.alloc_semaphore("d%d" % i) for i in range(n_dma)]
        self.dcnt = [0] * n_dma
        self.drr = 0
        self.q = {e: [] for e in self.names}
        self.clock = {e: {} for e in self.names}
        self.evclock = {}
        self.evorder = {}
        self.nev = 0
        self.lastw = {}
        self.readers = {}

    def _deps(self, reads, writes):
        deps = {}

        def add(k, v):
            if deps.get(k, 0) < v:
                deps[k] = v

        for r in reads:
            ev = self.lastw.get(r)
            if ev is not None:
                add(*ev)
        for w in writes:
            ev = self.lastw.get(w)
            if ev is not None:
                add(*ev)
            for k, v in self.readers.get(w, {}).items():
                add(k, v)
        return deps

    def _commit(self, ev, reads, writes):
        k, v = ev
        for r in reads:
            d = self.readers.setdefault(r, {})
            if d.get(k, 0) < v:
                d[k] = v
        for w in writes:
            self.lastw[w] = ev
            self.readers[w] = {}

    def _waits(self, eng, deps):
        clk = self.clock.setdefault(eng, {})
        waits = []
        for k, v in sorted(deps.items(), key=lambda kv: -self.evorder.get(kv, 0)):
            if eng == "pe" and k == ("e", "pe"):
                continue
            if clk.get(k, 0) >= v:
                continue
            waits.append((k, v))
            for k2, v2 in self.evclock.get((k, v), {}).items():
                if clk.get(k2, 0) < v2:
                    clk[k2] = v2
            clk[k] = v
        return waits

    def op(self, eng, fn, reads=(), writes=()):
        banks = {("psx", k[1]) for k in list(reads) + list(writes) if isinstance(k, tuple) and k and k[0] == "ps"}
        if banks:
            writes = list(writes) + list(banks)
        deps = self._deps(reads, writes)
        waits = self._waits(eng, deps)
        self.cnt[eng] += 1
        ev = (("e", eng), self.cnt[eng])
        self.evclock[ev] = dict(self.clock[eng])
        self.nev += 1
        self.evorder[ev] = self.nev
        self.q[eng].append((waits, fn, "e"))
        self._commit(ev, reads, writes)

    def dma(self, qeng, out, in_, reads=(), writes=(), **kw):
        deps = self._deps(reads, writes)
        idx = self.drr
        self.drr = (self.drr + 1) % len(self.dsem)
        if self.dcnt[idx] > 0:
            k = ("d", idx)
            deps[k] = max(deps.get(k, 0), self.dcnt[idx])
        waits = self._waits(qeng, deps)
        self.dcnt[idx] += 16
        ev = (("d", idx), self.dcnt[idx])
        self.evclock[ev] = dict(self.clock[qeng])
        self.nev += 1
        self.evorder[ev] = self.nev
        self.q[qeng].append((waits, lambda e: e.dma_start(out=out, in_=in_, **kw), idx))
        self._commit(ev, reads, writes)

    def barrier(self):
        allev = [(("e", e), c) for e, c in self.cnt.items() if c > 0]
        allev += [(("d", i), c) for i, c in enumerate(self.dcnt) if c > 0]
        for eng in self.names:
            waits = self._waits(eng, dict(allev))
            if waits:
                self.q[eng].append((waits, None, None))
        self.lastw = {}
        self.readers = {}
        self.evclock = {}

    def emit(self):
        nc = self.nc
        engs = {"pe": "tensor", "act": "scalar", "dve": "vector", "pool": "gpsimd", "sp": "sync"}
        with nc.Block() as block:
            for name in self.names:
                def body(eng, name=name):
                    for waits, fn, kind in self.q[name]:
                        emb = None
                        if fn is not None and waits and kind == "e" and not OPTS.get("noemb"):
                            emb = waits[-1]
                            waits = waits[:-1]
                        for k, v in waits:
                            s = self.sem[k[1]] if k[0] == "e" else self.dsem[k[1]]
                            eng.wait_ge(s, v)
                        if fn is None:
                            continue
                        ins = fn(eng)
                        if emb is not None:
                            k, v = emb
                            ins._wait_ge(self.sem[k[1]] if k[0] == "e" else self.dsem[k[1]], v)
                        if kind == "e":
                            ins.then_inc(self.sem[name], 1)
                        else:
                            ins.then_inc(self.dsem[kind], 16)
                getattr(block, engs[name])(body)


def make_consts():
    c = {}
    i128 = np.arange(128)
    blk = (i128[:, None] // 64) == (i128[None, :] // 64)
    ident = np.eye(128, dtype=np.float32)
    ones = np.ones((128, 128), np.float32)
    SL = ((i128[:, None] > i128[None, :]) & blk).astype(np.float32)
    SU = ((i128[:, None] < i128[None, :]) & blk).astype(np.float32)
    IU = ((i128[:, None] <= i128[None, :]) & blk).astype(np.float32)
    IUfull = (i128[:, None] <= i128[None, :]).astype(np.float32)
    idst = np.concatenate([np.eye(64), np.eye(64)], 0).astype(np.float32)
    reset = np.ones((128, 256), np.float32)
    reset[:, ::64] = 0.0
    rowm = np.zeros((128, 2), np.float32)
    rowm[:64, 0] = 1.0
    rowm[64:, 1] = 1.0
    q512 = np.arange(512)
    cm = np.stack([(q512[None, :] >= (j * 128 + i128[:, None])).astype(np.float32) for j in range(4)], 1)
    parts = [ident, ones, SU, IU, SL, SU, IU, IUfull, idst, reset, rowm, cm.reshape(128, 2048)]
    offs = {}
    o = 0
    for nm, p in zip(["ident", "ones", "mE", "_1", "_2", "mY", "_3", "iuf", "idst", "reset", "rowm", "cm"], parts):
        offs[nm] = o
        o += p.shape[1]
    c["cst"] = np.ascontiguousarray(np.concatenate(parts, 1))
    c["offs"] = offs
    mb = np.zeros((128, 3, 16, 6, 8), np.float32)
    for t in range(16):
        b = t // 2
        for n in range(8):
            mb[:, 0, t, :, n] = 0.0 if n < b else -1e30
            mb[:, 1, t, :, n] = 1.0 if n < b else 0.0
            mb[:, 2, t, :, n] = 1.0 if n == b else 0.0
    c["mobac"] = mb.reshape(128, 3 * 16 * 48)
    bi = np.zeros((8, S), np.float32)
    for n in range(8):
        bi[n, n * 256:(n + 1) * 256] = 1.0
    c["blkind"] = bi
    return c


CONSTS = make_consts()
PARAM_NAMES = ["pre_mix_g", "w_in", "rwkv_mu", "rwkv_w0", "rwkv_w2", "rwkv_a0", "rwkv_a2", "rwkv_g2",
               "rwkv_k_k", "rwkv_k_a", "rwkv_r_k", "rwkv_lnx_g", "rwkv_lnx_b", "gmlp_ln_g", "gmlp_ln_b",
               "gmlp_w_s", "gmlp_b_s", "w_out", "post_mix_g", "pre_ffn_g", "w_ffn_in", "w_ffn_out", "post_ffn_g"]
PARAM_SHAPES = {"pre_mix_g": (L, D), "w_in": (L, D, 3072), "rwkv_mu": (L, 1408), "rwkv_w0": (L, 384),
                "rwkv_w2": (L, 64, 384), "rwkv_a0": (L, 384), "rwkv_a2": (L, 64, 384), "rwkv_g2": (L, 128, 384),
                "rwkv_k_k": (L, 384), "rwkv_k_a": (L, 384), "rwkv_r_k": (L, 6, 64), "rwkv_lnx_g": (L, 384),
                "rwkv_lnx_b": (L, 384), "gmlp_ln_g": (L, 256), "gmlp_ln_b": (L, 256), "gmlp_w_s": (L, 4, 128, 128),
                "gmlp_b_s": (L, 4, 128), "w_out": (L, D, D), "post_mix_g": (L, D), "pre_ffn_g": (L, D),
                "w_ffn_in": (L, D, 2 * DFF), "w_ffn_out": (L, DFF, D), "post_ffn_g": (L, D)}


def build(dbg=None, nlayers=L, stop=None):
    dbg = dbg or []
    nc = bass.Bass("TRN2", target_bir_lowering=False)
    sc = Sched(nc)
    OF = CONSTS["offs"]

    def dram(name, shape, kind="Internal"):
        if name in dbg:
            kind = "ExternalOutput"
        return nc.dram_tensor(name, list(shape), F32, kind=kind).ap()

    x_in = dram("x", [S, D], "ExternalInput")
    y_out = dram("y", [S, D], "ExternalOutput")
    cst_d = dram("cst", CONSTS["cst"].shape, "ExternalInput")
    if not OPTS.get("noparams"):
        PR = {n: dram(n, PARAM_SHAPES[n], "ExternalInput") for n in PARAM_NAMES}
        mobac_d = dram("mobac", CONSTS["mobac"].shape, "ExternalInput")
        blkind_d = dram("blkind", CONSTS["blkind"].shape, "ExternalInput")
    if OPTS.get("noscratch"):
        glob_scr = None
    rwkvT = dram("rwkvT", [1408, S]) if not OPTS.get("noscratch") else None
    qkT = dram("qkT", [768, S]) if not OPTS.get("noscratch") else None
    uT = dram("uT", [256, S]) if not OPTS.get("noscratch") else None
    vm_tm = dram("vm_tm", [S, 384]) if not OPTS.get("noscratch") else None
    vg_tm = dram("vg_tm", [S, 256]) if not OPTS.get("noscratch") else None
    catT = dram("catT", [D, S]) if not OPTS.get("noscratch") else None
    xdbg = dram("xdbg", [D, S]) if not OPTS.get("noscratch") else None
    xD = dram("xD", [D, S])
    xD_v = xD.rearrange("(k p) t -> p k t", p=128)

    uid = [0]

    def sb(es, name, shape):
        uid[0] += 1
        return es.enter_context(nc.sbuf_tensor("%s_%d" % (name, uid[0]), list(shape), F32))

    glob = ExitStack()
    cst = sb(glob, "cst_sb", [128, CONSTS["cst"].shape[1]])
    ps = [glob.enter_context(nc.psum_tensor("ps%d" % i, [128, 512], F32)) for i in range(8)]
    ident = cst[:, OF["ident"]:OF["ident"] + 128]
    ones = cst[:, OF["ones"]:OF["ones"] + 128]
    mE = cst[:, OF["mE"]:OF["mE"] + 384]
    mY = cst[:, OF["mY"]:OF["mY"] + 256]
    iuf = cst[:, OF["iuf"]:OF["iuf"] + 128]
    idst = cst[:, OF["idst"]:OF["idst"] + 64]
    resetm = cst[:, OF["reset"]:OF["reset"] + 256]
    cm = cst[:, OF["cm"]:OF["cm"] + 2048]
    epsc = sb(glob, "epsc", [128, 4])

    def ACT(out, in_, func, reads, writes, **kw):
        sc.op("act", lambda e: e.activation(out=out, in_=in_, func=func, **kw), reads, writes)

    def RR(ap):
        return ap if OPTS.get("nor") else ap.bitcast(F32R)

    def MM(out, lhsT, rhs, start, stop, reads, writes, r=False):
        if r and not OPTS.get("nor"):
            lhsT = lhsT.bitcast(F32R)
            rhs = rhs.bitcast(F32R)
        sc.op("pe", lambda e: e.matmul(out, lhsT=lhsT, rhs=rhs, start=start, stop=stop), reads, writes)

    def TR(out, in_, idn, reads, writes):
        sc.op("pe", lambda e: e.transpose(out, in_, idn), reads, writes)

    def TT(eng, out, in0, in1, op, reads, writes):
        sc.op(eng, lambda e: e.tensor_tensor(out=out, in0=in0, in1=in1, op=op), reads, writes)

    def TS(eng, out, in0, s1, s2, op0, op1, reads, writes):
        if s2 is None:
            sc.op(eng, lambda e: e.tensor_scalar(out=out, in0=in0, scalar1=s1, scalar2=None, op0=op0), reads, writes)
        else:
            sc.op(eng, lambda e: e.tensor_scalar(out=out, in0=in0, scalar1=s1, scalar2=s2, op0=op0, op1=op1), reads, writes)

    def STT(out, in0, scalar, in1, op0, op1, reads, writes):
        sc.op("dve", lambda e: e.scalar_tensor_tensor(out=out, in0=in0, scalar=scalar, in1=in1, op0=op0, op1=op1), reads, writes)

    def CP(eng, out, in_, reads, writes):
        if eng == "act":
            sc.op("act", lambda e: e.copy(out=out, in_=in_), reads, writes)
        else:
            sc.op(eng, lambda e: e.tensor_copy(out=out, in_=in_), reads, writes)

    def RECIP(out, in_, reads, writes):
        sc.op("dve", lambda e: e.reciprocal(out=out, in_=in_), reads, writes)

    def DMA(out, in_, reads, writes, q="sp", **kw):
        sc.dma(q, out, in_, reads, writes, **kw)

    def colvec(es, name, src_1d, ncol, p=128):
        t = sb(es, name, [p, ncol])
        DMA(t[:, :], src_1d.rearrange("(c p) -> p c", p=p), [], [name], allow_slow_non_contiguous=True)
        return t

    DMA(cst[:, :], cst_d[:, :], [], ["cst"])
    sc.op("dve", lambda e: e.memset(epsc[:, 0:1], 1e-6), [], ["epsc"])
    sc.op("dve", lambda e: e.memset(epsc[:, 1:2], 64e-5), [], ["epsc"])
    sc.op("dve", lambda e: e.memset(epsc[:, 2:3], 0.0), [], ["epsc"])
    with ExitStack() as es:
        xin = sb(es, "xin", [128, 2, D])
        xs = sb(es, "xs", [128, 2, 8, 512])
        for t in range(16):
            sl = t % 2
            xsl = (t // 4) % 2
            DMA(xin[:, sl, :], x_in[t * 128:(t + 1) * 128, :], [], [("xin", sl)])
            for g in range(2):
                bank = ps[(t * 2 + g) % 4]
                bk = ("ps", (t * 2 + g) % 4)
                for kk in range(4):
                    k = g * 4 + kk
                    TR(bank[:, kk * 128:(kk + 1) * 128], xin[:, sl, k * 128:(k + 1) * 128], ident, [("xin", sl), "cst"], [bk])
                CP("act" if g else "dve", xs[:, xsl, g * 4:(g + 1) * 4, (t % 4) * 128:(t % 4 + 1) * 128],
                   bank[:, :].rearrange("p (k c) -> p k c", k=4), [bk], [("xs", xsl, t % 4, g)])
            if t % 4 == 3:
                DMA(xD_v[:, :, (t // 4) * 512:(t // 4 + 1) * 512], xs[:, xsl, :, :], [("xs", xsl, q, g) for q in range(4) for g in range(2)], [("xD", t // 4)])
        sc.barrier()

    def rms_stats(src_fn, src_keys, tt, es_tiles, pbank, pkey):
        sq, rstd = es_tiles
        for k in range(8):
            ACT(sq[:, k % 2, :], src_fn(k), AF.Square, [src_keys(k)], [("sq", k % 2)])
            MM(pbank[:, :], ones, sq[:, k % 2, :], k == 0, k == 7, [("sq", k % 2), "cst"], [pkey])
        ACT(rstd[:, tt % 2, :], pbank[:, :], AF.Sqrt, [pkey, "epsc"], [("rstd", tt % 2)], scale=1.0 / D, bias=epsc[:, 0:1])
        RECIP(rstd[:, tt % 2, :], rstd[:, tt % 2, :], [("rstd", tt % 2)], [("rstd", tt % 2)])
        return rstd[:, tt % 2, :]

    for l in range(nlayers if stop != 'setup' else 0):
        with ExitStack() as es:
            hbuf = sb(es, "hbuf", [128, 8, S])
            sq = sb(es, "sq", [128, 2, 512])
            rstd = sb(es, "rstd", [128, 2, 512])
            gA = colvec(es, "gA", PR["pre_mix_g"][l], 8)
            muA = colvec(es, "muA", PR["rwkv_mu"][l], 11)
            xa = sb(es, "xa", [128, 2, 8, 512])
            for tt in range(4):
                tsl = slice(tt * 512, (tt + 1) * 512)
                xsl = tt % 2
                DMA(xa[:, xsl, :, :], xD_v[:, :, tsl], [("xD", tt)], [("xa", xsl)])
                r = rms_stats(lambda k: xa[:, xsl, k, :], lambda k: ("xa", xsl), tt, (sq, rstd), ps[4 + tt % 2], ("ps", 4 + tt % 2))
                for k in range(8):
                    STT(RR(hbuf[:, k, tsl]), xa[:, xsl, k, :], gA[:, k:k + 1], r, ALU.mult, ALU.mult,
                        [("xa", xsl), ("rstd", tt % 2), "gA"], [("h", k, tt)])
            es_main = es
            es = ExitStack()
            wA = sb(es, "wA", [128, 2, 8, 128])
            stg = sb(es, "stg", [128, 2, S])
            stg2 = sb(es, "stg2", [128, 2, S])
            w_in_l = PR["w_in"][l].rearrange("(k p) c -> p k c", p=128)
            fm_chunks = [(c * 128, rwkvT, c * 128, True) for c in range(11)]
            fm_chunks += [(1408 + c * 128, qkT, c * 128, False) for c in range(6)]
            fm_chunks += [(2560 + c * 128, uT, c * 128, False) for c in range(2)]
            for ci, (col0, dst, row0, shift) in enumerate(fm_chunks):
                sl = ci % 2
                DMA(RR(wA[:, sl, :, :]), w_in_l[:, :, col0:col0 + 128], [], [("wA", sl)], q="pool")
                for tt in range(4):
                    tsl = slice(tt * 512, (tt + 1) * 512)
                    bi = (ci * 4 + tt) % 4
                    for k in range(8):
                        MM(ps[bi][:, :], wA[:, sl, k, :], hbuf[:, k, tsl], k == 0, k == 7,
                           [("wA", sl), ("h", k, tt)], [("ps", bi)], r=True)
                    CP("act" if tt % 2 else "dve", stg[:, sl, tsl], ps[bi][:, :], [("ps", bi)], [("stg", sl, tt)])
                allst = [("stg", sl, tt) for tt in range(4)]
                if shift:
                    TT("pool", stg2[:, sl, 1:S], stg[:, sl, 0:S - 1], stg[:, sl, 1:S], ALU.subtract, allst, [("stg2", sl)])
                    TS("pool", stg2[:, sl, 0:1], stg[:, sl, 0:1], -1.0, None, ALU.mult, None, allst, [("stg2", sl)])
                    STT(stg2[:, sl, :], stg2[:, sl, :], muA[:, ci:ci + 1], stg[:, sl, :], ALU.mult, ALU.add,
                        allst + [("stg2", sl), "muA"], [("stg2", sl)])
                    DMA(dst[row0:row0 + 128, :], stg2[:, sl, :], [("stg2", sl)], [("dr", id(dst), row0)], q="pool")
                else:
                    DMA(dst[row0:row0 + 128, :], stg[:, sl, :], allst, [("dr", id(dst), row0)], q="pool")
            sc.barrier()
            es.close()
            es = es_main
            wB = sb(es, "wB", [128, 8, 640])
            DMA(RR(wB[:, :, 0:384]), w_in_l[:, :, 2176:2560], [], ["wB"], q="pool")
            DMA(RR(wB[:, :, 384:640]), w_in_l[:, :, 2816:3072], [], ["wB"], q="pool")
            vst = sb(es, "vst", [128, 2, 640])
            for t in range(16):
                sl = t % 2
                b0, b1 = 4 + (t % 2) * 2, 5 + (t % 2) * 2
                for k in range(8):
                    MM(ps[b0][:, 0:384], hbuf[:, k, t * 128:(t + 1) * 128], wB[:, k, 0:384], k == 0, k == 7,
                       [("h", k, t // 4), "wB"], [("ps", b0)], r=True)
                for k in range(8):
                    MM(ps[b1][:, 0:256], hbuf[:, k, t * 128:(t + 1) * 128], wB[:, k, 384:640], k == 0, k == 7,
                       [("h", k, t // 4), "wB"], [("ps", b1)], r=True)
                CP("act", vst[:, sl, 0:384], ps[b0][:, 0:384], [("ps", b0)], [("vst", sl, 0)])
                CP("dve", vst[:, sl, 384:640], ps[b1][:, 0:256], [("ps", b1)], [("vst", sl, 1)])
                DMA(vm_tm[t * 128:(t + 1) * 128, :], vst[:, sl, 0:384], [("vst", sl, 0)], [("vm", t)], q="pool")
                DMA(vg_tm[t * 128:(t + 1) * 128, :], vst[:, sl, 384:640], [("vst", sl, 1)], [("vg", t)], q="pool")
            sc.barrier()
        if stop == "A":
            break

        with ExitStack() as es:
            lng = sb(es, "lng", [128, 256])
            lnb = sb(es, "lnb", [128, 256])
            DMA(lng[:, :], PR["gmlp_ln_g"][l].partition_broadcast(128), [], ["lng"])
            DMA(lnb[:, :], PR["gmlp_ln_b"][l].partition_broadcast(128), [], ["lnb"])
            wsn = sb(es, "wsn", [128, 4, 128])
            wsT = sb(es, "wsT", [128, 4, 128])
            bsr = sb(es, "bsr", [1, 512])
            DMA(wsn[:, :, :], PR["gmlp_w_s"][l].rearrange("g t s -> t g s"), [], ["wsn"])
            DMA(bsr[:, :], PR["gmlp_b_s"][l].rearrange("g t -> (g t)").partition_broadcast(1), [], ["bsr"])
            for g in range(4):
                TR(ps[0][:, g * 128:(g + 1) * 128], wsn[:, g, :], ident, ["wsn", "cst"], [("ps", 0)])
            for g in range(4):
                TT("dve", wsT[:, g, :], ps[0][:, g * 128:(g + 1) * 128], iuf, ALU.mult, [("ps", 0), "cst"], ["wsT"])
            gu = sb(es, "gu", [128, 2, S])
            t1 = sb(es, "t1", [128, S])
            cout = sb(es, "cout", [128, 2, S])
            for pp in range(2):
                DMA(gu[:, pp, :], uT[pp * 128:(pp + 1) * 128, :], [], [("gu", pp)])
                ACT(t1[:, :], gu[:, pp, :], AF.Square, [("gu", pp)], ["t1"])
                TS("pool", t1[:, :], t1[:, :], 0.044715, 1.0, ALU.mult, ALU.add, ["t1"], ["t1"])
                TT("dve", t1[:, :], t1[:, :], gu[:, pp, :], ALU.mult, ["t1", ("gu", pp)], ["t1"])
                ACT(t1[:, :], t1[:, :], AF.Sigmoid, ["t1"], ["t1"], scale=2.0 * 0.7978845608028654)
                TT("dve", gu[:, pp, :], gu[:, pp, :], t1[:, :], ALU.mult, ["t1", ("gu", pp)], [("gu", pp)])
            vb = sb(es, "vb", [128, 2, 256])
            t2 = sb(es, "t2", [128, 2, 256])
            st6 = sb(es, "st6", [128, 2, 8])
            for c in range(16):
                sl = c % 2
                DMA(vb[:, sl, :], vg_tm[c * 128:(c + 1) * 128, :], [], [("vb", sl)])
                kv, kt = ("vb", sl), ("t2", sl)
                ACT(t2[:, sl, :], vb[:, sl, :], AF.Square, [kv], [kt])
                TS("pool", t2[:, sl, :], t2[:, sl, :], 0.044715, 1.0, ALU.mult, ALU.add, [kt], [kt])
                TT("dve", t2[:, sl, :], t2[:, sl, :], vb[:, sl, :], ALU.mult, [kt, kv], [kt])
                ACT(t2[:, sl, :], t2[:, sl, :], AF.Sigmoid, [kt], [kt], scale=2.0 * 0.7978845608028654)
                TT("dve", vb[:, sl, :], vb[:, sl, :], t2[:, sl, :], ALU.mult, [kt, kv], [kv])
                ks = ("st6", sl)
                sc.op("dve", lambda e, sl=sl: e.bn_stats(out=st6[:, sl, 0:6], in_=vb[:, sl, :]), [kv], [ks])
                sc.op("dve", lambda e, sl=sl: e.bn_aggr(out=st6[:, sl, 6:8], in_=st6[:, sl, 0:6]), [ks], [ks])
                ACT(st6[:, sl, 7:8], st6[:, sl, 7:8], AF.Sqrt, [ks, "epsc"], [ks], bias=epsc[:, 0:1], scale=1.0)
                RECIP(st6[:, sl, 7:8], st6[:, sl, 7:8], [ks], [ks])
                TS("dve", vb[:, sl, :], vb[:, sl, :], st6[:, sl, 6:7], st6[:, sl, 7:8], ALU.subtract, ALU.mult, [kv, ks], [kv])
                TT("dve", vb[:, sl, :], vb[:, sl, :], lng[:, :], ALU.mult, [kv, "lng"], [kv])
                TT("dve", vb[:, sl, :], vb[:, sl, :], lnb[:, :], ALU.add, [kv, "lnb"], [kv])
                for pp in range(2):
                    bi = 1 + (c % 2) * 2 + pp
                    for gg in range(2):
                        g = pp * 2 + gg
                        MM(ps[bi][:, gg * 128:(gg + 1) * 128], vb[:, sl, pp * 128:(pp + 1) * 128], wsT[:, g, :], True, False, [kv, "wsT"], [("ps", bi)])
                        MM(ps[bi][:, gg * 128:(gg + 1) * 128], ones[0:1, :], bsr[0:1, g * 128:(g + 1) * 128], False, True, ["cst", "bsr"], [("ps", bi)])
                    for gg in range(2):
                        TT("dve", cout[gg * 64:(gg + 1) * 64, pp, c * 128:(c + 1) * 128], ps[bi][gg * 64:(gg + 1) * 64, gg * 128:(gg + 1) * 128],
                           gu[gg * 64:(gg + 1) * 64, pp, c * 128:(c + 1) * 128], ALU.mult, [("ps", bi), ("gu", pp)], [("cout", pp)])
            for pp in range(2):
                DMA(catT[768 + pp * 128:768 + (pp + 1) * 128, :], cout[:, pp, :], [("cout", pp)], [("cat", 6 + pp)], q="pool")
            sc.barrier()
        if stop == "D":
            break

        with ExitStack() as es:
            qa = sb(es, "qa", [72, 2, S])
            ka = sb(es, "ka", [72, 2, S])
            vt = sb(es, "vt", [128, 2, 16, 128])
            ones_r = sb(es, "ones_r", [128, 128])
            CP("dve", RR(ones_r[:, :]), ones, ["cst"], ["ones_r"])
            kbar = sb(es, "kbar", [64, 2, 8])
            mobc = sb(es, "mobc", [128, 3, 16, 48])
            NP = sb(es, "NP", [128, 2, 16, 72])
            sm = sb(es, "sm", [128, 16, 8])
            top8 = sb(es, "top8", [128, 16, 8])
            al = sb(es, "al", [128, 16, 8])
            pt = sb(es, "pt", [128, 4, 512])
            pacc = sb(es, "pacc", [128, 512])
            rden = sb(es, "rden", [128, 512])
            ob = sb(es, "ob", [128, 2, 512])
            DMA(mobc[:, :, :, :], mobac_d.rearrange("p (a t c) -> p a t c", a=3, t=16), [], ["mobc"])
            for q in range(2):
                DMA(RR(ka[64:72, q, :]), blkind_d[:, :], [], [("kaB", q)], q="pool")
            sc.op("pool", lambda e: e.memset(NP[:, :, :, :], 0.0), [], [("NP", 0), ("NP", 1)])
            pti = [0]

            def prepA(h):
                q = h % 2
                DMA(RR(qa[0:64, q, :]), qkT[h * 64:(h + 1) * 64, :], [], [("qaQ", q)], q="pool")
                DMA(RR(ka[0:64, q, :]), qkT[384 + h * 64:384 + (h + 1) * 64, :], [], [("kaK", q)], q="pool")
                if h % 2 == 0:
                    vq = (h // 2) % 2
                    DMA(RR(vt[:, vq, :, :]), vm_tm.rearrange("(t p) c -> p t c", p=128)[:, :, h * 64:(h + 2) * 64], [], [("vt", vq)], q="pool")
                sc.op("dve", lambda e: e.tensor_reduce(out=kbar[:, q, :], in_=ka[0:64, q, :].rearrange("p (n k) -> p n k", n=8), axis=AX.X, op=ALU.add), [("kaK", q)], [("kbar", q)])
                for t in range(16):
                    MM(ps[0][:, t * 8:(t + 1) * 8], qa[0:64, q, t * 128:(t + 1) * 128], kbar[:, q, :], True, True, [("qaQ", q), ("kbar", q)], [("ps", 0)])

            def prepB(h):
                q = h % 2
                hs = slice(h * 8, (h + 1) * 8)
                TT("dve", sm[:, :, :], ps[0][:, 0:128].rearrange("p (t n) -> p t n", t=16), mobc[:, 0, :, hs], ALU.add, [("ps", 0), "mobc"], ["sm"])
                for t in range(16):
                    sc.op("dve", lambda e, t=t: e.max(out=top8[:, t, :], in_=sm[:, t, :]), ["sm"], [("top8", t)])
                for t in range(16):
                    TS("dve", al[:, t, :], sm[:, t, :], top8[:, t, 2:3], None, ALU.is_ge, None, ["sm", ("top8", t)], [("al", t)])
                allal = [("al", t) for t in range(16)]
                TT("dve", al[:, :, :], al[:, :, :], mobc[:, 1, :, hs], ALU.mult, allal + ["mobc"], ["al2"])
                TT("dve", al[:, :, :], al[:, :, :], mobc[:, 2, :, hs], ALU.add, ["al2", "mobc"], ["al2"])
                TS("dve", NP[:, q, :, 64:72], al[:, :, :], -1.0, BIG, ALU.add, ALU.mult, ["al2"], [("NP", q)])

            def prepC(h):
                q = h % 2
                for t4 in range(4):
                    for tq in range(4):
                        t = t4 * 4 + tq
                        MM(ps[1][0:72, tq * 128:(tq + 1) * 128], NP[:, q, t, :], ident, True, True, [("NP", q), "cst"], [("ps", 1)])
                    CP("act", RR(qa[64:72, q, t4 * 512:(t4 + 1) * 512]), ps[1][64:72, :], [("ps", 1)], [("qaM", q)])

            def attn(h, qt):
                q = h % 2
                vq = (h // 2) % 2
                hp = slice((h % 2) * 64, (h % 2) * 64 + 64)
                qsl = slice(qt * 512, (qt + 1) * 512)
                nk = (qt + 1) * 4
                osl = qt % 2
                pis = {}

                def qk(kt):
                    sb_i = 2 + kt % 2
                    pi = pti[0] % 4
                    pti[0] += 1
                    pis[kt] = pi
                    MM(ps[sb_i][:, :], ka[0:72, q, kt * 128:(kt + 1) * 128], qa[0:72, q, qsl], True, True,
                       [("kaK", q), ("kaB", q), ("qaQ", q), ("qaM", q)], [("ps", sb_i)], r=True)
                    ACT(RR(pt[:, pi, :]), ps[sb_i][:, :], AF.Exp, [("ps", sb_i)], [("pt", pi)], scale=0.125)
                    if kt >= qt * 4:
                        j = kt - qt * 4
                        TT("dve", RR(pt[:, pi, :]), pt[:, pi, :], cm[:, j * 512:(j + 1) * 512], ALU.mult, [("pt", pi), "cst"], [("pt", pi)])

                def pv(kt):
                    pi = pis[kt]
                    MM(ps[4][:, :], vt[:, vq, kt, :], pt[:, pi, :], kt == 0, kt == nk - 1, [("vt", vq), ("pt", pi)], [("ps", 4)], r=True)
                    if kt == 0:
                        CP("pool", RR(pacc[:, :]), pt[:, pi, :], [("pt", pi)], ["pacc"])
                    else:
                        TT("pool", RR(pacc[:, :]), pacc[:, :], pt[:, pi, :], ALU.add, [("pt", pi), "pacc"], ["pacc"])

                qk(0)
                for kt in range(nk):
                    if kt + 1 < nk:
                        qk(kt + 1)
                    pv(kt)
                MM(ps[5][:, :], ones_r[:, :], pacc[:, :], True, True, ["ones_r", "pacc"], [("ps", 5)], r=True)
                RECIP(rden[hp, :], ps[5][hp, :], [("ps", 5)], ["rden"])
                TT("dve", ob[hp, osl, :], ps[4][hp, :], rden[hp, :], ALU.mult, [("ps", 4), "rden"], [("ob", osl)])
                DMA(catT[384 + h * 64:384 + (h + 1) * 64, qsl], ob[hp, osl, :], [("ob", osl)], [("cat", "b", h, qt)])

            prepA(0)
            prepB(0)
            prepC(0)
            for h in range(H):
                for qt in range(4):
                    attn(h, qt)
                    if h + 1 < H:
                        if qt == 0:
                            prepA(h + 1)
                        elif qt == 1:
                            prepB(h + 1)
                        elif qt == 2:
                            prepC(h + 1)
            sc.barrier()
        if stop == "C":
            break

        with ExitStack() as es:
            TW = 128
            NTB = S // TW
            w2s = sb(es, "w2s", [64, 384])
            a2s = sb(es, "a2s", [64, 384])
            g2s = sb(es, "g2s", [128, 384])
            DMA(w2s[:, :], PR["rwkv_w2"][l], [], ["w2s"])
            DMA(a2s[:, :], PR["rwkv_a2"][l], [], ["a2s"])
            DMA(g2s[:, :], PR["rwkv_g2"][l], [], ["g2s"])
            pw0 = colvec(es, "pw0", PR["rwkv_w0"][l], 6, p=64)
            pa0 = colvec(es, "pa0", PR["rwkv_a0"][l], 6, p=64)
            pkk = colvec(es, "pkk", PR["rwkv_k_k"][l], 6, p=64)
            pka = colvec(es, "pka", PR["rwkv_k_a"][l], 6, p=64)
            prk = colvec(es, "prk", PR["rwkv_r_k"][l].rearrange("h d -> (h d)"), 6, p=64)
            plg = colvec(es, "plg", PR["rwkv_lnx_g"][l], 6, p=64)
            plb = colvec(es, "plb", PR["rwkv_lnx_b"][l], 6, p=64)
            pok = sb(es, "pok", [64, 6])
            TS("dve", pok[:, :], pka[:, :], -1.0, 1.0, ALU.mult, ALU.add, ["pka"], ["pok"])
            i64 = ident[0:64, 0:64]
            o64 = ones[0:64, 0:64]
            rowm = cst[:, OF["rowm"]:OF["rowm"] + 2]
            Mst = sb(es, "Mst", [64, 2, 6, 64])
            sc.op("dve", lambda e: e.memset(Mst[:, 0, :, :], 0.0), [], [("Mst", 0)])
            mcur = [0]
            RhatT = sb(es, "RhatT", [64, 6, TW])
            Y0T = sb(es, "Y0T", [64, 6, TW])
            yT = sb(es, "yT", [64, 6, TW])
            GT = sb(es, "GT", [64, 6, 2, 64])
            Hm = sb(es, "Hm", [64, 6, 2, 64])
            bon = sb(es, "bon", [64, 3, 6, TW]); gal = sb(es, "gal", [64, 3, 6, TW])
            ARh = sb(es, "ARh", [64, 2, 6, 2, TW]); BKh = sb(es, "BKh", [64, 2, 6, 2, TW]); BPh = sb(es, "BPh", [64, 2, 6, 2, TW])
            vvh = sb(es, "vvh", [64, 2, 6, TW]); pC = sb(es, "pC", [64, 2, 6, 2])
            wd = sb(es, "wd", [64, 2, TW]); ad = sb(es, "ad", [64, 2, TW]); gd = sb(es, "gd", [128, 2, TW])
            T6 = lambda nm: sb(es, nm, [64, 6, TW])
            rr = T6("rr"); kq = T6("kq"); sig = T6("sig"); cum = T6("cum"); cpv = T6("cpv")
            epos = T6("epos"); eneg = T6("eneg"); eprv = T6("eprv"); eend = T6("eend")
            aa = T6("aa"); kk = T6("kk"); kk2 = T6("kk2"); rn = T6("rn"); kka = T6("kka"); kp = T6("kp"); rkr = T6("rkr")
            nbc = sb(es, "nbc", [64, 6, 2])
            tm = sb(es, "tm", [128, 6, 256]); Eb = sb(es, "Eb", [128, 6, 512]); YS = sb(es, "YS", [128, 6, 256])
            Lb = sb(es, "Lb", [128, 6, 2, 384]); B2 = sb(es, "B2", [128, 6, 128]); K2 = sb(es, "K2", [128, 6, 128])
            yc = sb(es, "yc", [64, 6, TW]); ysq = sb(es, "ysq", [64, 6, TW]); yrs = sb(es, "yrs", [64, 6, TW])
            obuf = sb(es, "obuf", [64, 6, TW])

            def tile_pro(tt):
                tsl = slice(tt * TW, (tt + 1) * TW)
                ws = tt % 2
                DMA(wd[:, ws, :], rwkvT[1152:1216, tsl], [], [("wd", ws)])
                DMA(ad[:, ws, :], rwkvT[1216:1280, tsl], [], [("ad", ws)])
                DMA(gd[:, ws, :], rwkvT[1280:1408, tsl], [], [("gd", ws)])
                yield
                ACT(wd[:, ws, :], wd[:, ws, :], AF.Tanh, [("wd", ws)], [("wd", ws)])
                ACT(gd[:, ws, :], gd[:, ws, :], AF.Sigmoid, [("gd", ws)], [("gd", ws)])
                yield

            def stage0(tt, h):
                tsl = slice(tt * TW, (tt + 1) * TW)
                ws = tt % 2
                hc = slice(h * 64, (h + 1) * 64)
                K_ = lambda nm: (nm, h)
                pb = ps[h % 2]
                pk = ("ps", h % 2)
                DMA(rr[:, h, :], rwkvT[h * 64:(h + 1) * 64, tsl], [], [K_("rr")])
                DMA(kq[:, h, :], rwkvT[384 + h * 64:384 + (h + 1) * 64, tsl], [], [K_("kq")])
                DMA(vvh[:, ws, h, :], rwkvT[768 + h * 64:768 + (h + 1) * 64, tsl], [], [("vv", ws, h)])
                yield
                MM(pb[0:64, 0:TW], w2s[:, hc], wd[:, ws, :], True, True, ["w2s", ("wd", ws)], [pk])
                ACT(sig[:, h, :], pb[0:64, 0:TW], AF.Sigmoid, [pk, "pw0"], [K_("sig")], bias=pw0[:, h:h + 1])
                yield
                sc.op("dve", lambda e, h=h: e.tensor_tensor_scan(out=cum[:, h, :], data0=resetm[0:64, 0:TW], data1=sig[:, h, :], initial=0.0, op0=ALU.mult, op1=ALU.add), [K_("sig"), "cst"], [K_("cum")])
                yield
                TT("pool", cpv[:, h, :], cum[:, h, :], sig[:, h, :], ALU.subtract, [K_("cum"), K_("sig")], [K_("cpv")])
                ACT(epos[:, h, :], cum[:, h, :], AF.Exp, [K_("cum")], [K_("epos")], scale=-C0)
                ACT(eneg[:, h, :], cum[:, h, :], AF.Exp, [K_("cum")], [K_("eneg")], scale=C0)
                cum3 = cum[:, h, :].rearrange("p (c t) -> p c t", c=2)
                epos3 = epos[:, h, :].rearrange("p (c t) -> p c t", c=2)
                TS("dve", nbc[:, h, :], cum3[:, :, 63], -C0, None, ALU.mult, None, [K_("cum")], [K_("nbc")])
                yield
                ACT(eprv[:, h, :], cpv[:, h, :], AF.Exp, [K_("cpv")], [K_("eprv")], scale=-C0)
                CP("pool", pC[:, ws, h, :], epos3[:, :, 63], [K_("epos")], [("pC", ws, h)])
                for c in range(2):
                    ACT(eend[:, h, c * 64:(c + 1) * 64], cum[:, h, c * 64:(c + 1) * 64], AF.Exp, [K_("cum"), K_("nbc")], [K_("eend")], scale=C0, bias=nbc[:, h, c:c + 1])
                yield
                MM(pb[0:64, 128:128 + TW], a2s[:, hc], ad[:, ws, :], True, True, ["a2s", ("ad", ws)], [pk])
                ACT(aa[:, h, :], pb[0:64, 128:128 + TW], AF.Sigmoid, [pk, "pa0"], [K_("aa")], bias=pa0[:, h:h + 1])
                yield
                MM(pb[0:64, 256:256 + TW], g2s[:, hc], gd[:, ws, :], True, True, ["g2s", ("gd", ws)], [pk])
                CP("act", gal[:, tt % 3, h, :], pb[0:64, 256:256 + TW], [pk], [("gal", tt % 3, h)])
                yield
                TS("dve", kk[:, h, :], kq[:, h, :], pkk[:, h:h + 1], None, ALU.mult, None, [K_("kq"), "pkk"], [K_("kk")])
                yield
                ACT(kk2[:, h, :], kk[:, h, :], AF.Square, [K_("kk")], [K_("kk2")])
                yield
                MM(pb[0:64, 384:384 + TW], o64, kk2[:, h, :], True, True, ["cst", K_("kk2")], [pk])
                ACT(rn[:, h, :], pb[0:64, 384:384 + TW], AF.Sqrt, [pk], [K_("rn")])
                yield
                TS("dve", rn[:, h, :], rn[:, h, :], 1e-12, None, ALU.max, None, [K_("rn")], [K_("rn")])
                yield
                RECIP(rn[:, h, :], rn[:, h, :], [K_("rn")], [K_("rn")])
                yield
                TT("dve", kk[:, h, :], kk[:, h, :], rn[:, h, :], ALU.mult, [K_("kk"), K_("rn")], [K_("kk")])
                yield
                TT("pool", kka[:, h, :], kk[:, h, :], aa[:, h, :], ALU.mult, [K_("kk"), K_("aa")], [K_("kka")])
                TS("dve", rn[:, h, :], aa[:, h, :], pka[:, h:h + 1], pok[:, h:h + 1], ALU.mult, ALU.add, [K_("aa"), "pka", "pok", K_("rn")], [K_("rn")])
                STT(RR(ARh[:, ws, h, 0, :]), kk[:, h, :], -1.0, eprv[:, h, :], ALU.mult, ALU.mult, [K_("kk"), K_("eprv")], [("AR0", ws, h)])
                yield
                TT("dve", kp[:, h, :], kq[:, h, :], rn[:, h, :], ALU.mult, [K_("kq"), K_("rn")], [K_("kp")])
                TT("pool", RR(ARh[:, ws, h, 1, :]), rr[:, h, :], epos[:, h, :], ALU.mult, [K_("rr"), K_("epos")], [("AR1", ws, h)])
                yield
                STT(rkr[:, h, :], rr[:, h, :], prk[:, h:h + 1], kp[:, h, :], ALU.mult, ALU.mult, [K_("rr"), "prk", K_("kp")], [K_("rkr")])
                TT("pool", RR(BKh[:, ws, h, 0, :]), kka[:, h, :], eneg[:, h, :], ALU.mult, [K_("kka"), K_("eneg")], [("BK0", ws, h)])
                TT("pool", RR(BKh[:, ws, h, 1, :]), kp[:, h, :], eneg[:, h, :], ALU.mult, [K_("kp"), K_("eneg")], [("BK1", ws, h)])
                yield
                MM(pb[0:64, 0:TW], o64, rkr[:, h, :], True, True, ["cst", K_("rkr")], [pk])
                TT("dve", bon[:, tt % 3, h, :], pb[0:64, 0:TW], vvh[:, ws, h, :], ALU.mult, [pk, ("vv", ws, h)], [("bon", tt % 3, h)])
                TT("pool", BPh[:, ws, h, 0, :], kka[:, h, :], eend[:, h, :], ALU.mult, [K_("kka"), K_("eend")], [("BP", ws, h)])
                TT("pool", BPh[:, ws, h, 1, :], kp[:, h, :], eend[:, h, :], ALU.mult, [K_("kp"), K_("eend")], [("BP", ws, h)])
                yield

            def make_gens(tt):
                if tt >= NTB:
                    return []
                return [tile_pro(tt)] + [stage0(tt, h) for h in range(H)]

            def advance(gl, n):
                for _ in range(n):
                    for g in list(gl):
                        try:
                            next(g)
                        except StopIteration:
                            gl.remove(g)

            def chain(tt, nxt):
                ws = tt % 2
                tsl = slice(tt * TW, (tt + 1) * TW)
                HS = range(H)
                bk = lambda h: ps[2 + h]
                bkk = lambda h: ("ps", 2 + h)
                A0 = lambda h: ("AR0", ws, h)
                A1 = lambda h: ("AR1", ws, h)
                for h in HS:
                    for i, (src, kx) in enumerate([(ARh[:, ws, h, 0, :], A0(h)), (BPh[:, ws, h, 0, :], ("BP", ws, h)), (BPh[:, ws, h, 1, :], ("BP", ws, h)), (vvh[:, ws, h, :], ("vv", ws, h))]):
                        TR(bk(h)[:, i * 64:(i + 1) * 64], src, i64, [kx, "cst"], [bkk(h)])
                for h in HS:
                    CP("act", tm[:, h, :], bk(h)[:, 0:256], [bkk(h)], [("tm", h)])
                advance(nxt, 2)
                for h in HS:
                    MM(bk(h)[:, 0:256], BKh[:, ws, h, 0, :], ARh[:, ws, h, :, :], True, True, [("BK0", ws, h), A0(h), A1(h)], [bkk(h)], r=True)
                    MM(bk(h)[:, 256:384], ARh[:, ws, h, 0, :], BKh[:, ws, h, 0, :], True, True, [("BK0", ws, h), A0(h)], [bkk(h)])
                for h in HS:
                    TT("dve", RR(Eb[:, h, 0:384]), bk(h)[:, 0:384], mE, ALU.mult, [bkk(h), "cst"], [("E", h, "a")])
                advance(nxt, 2)
                for h in HS:
                    MM(bk(h)[:, 0:256], BKh[:, ws, h, 1, :], ARh[:, ws, h, :, :], True, True, [("BK1", ws, h), A0(h), A1(h)], [bkk(h)], r=True)
                for h in HS:
                    TT("dve", YS[:, h, :], bk(h)[:, 0:256], mY, ALU.mult, [bkk(h), "cst"], [("YS", h)])
                advance(nxt, 2)
                for h in HS:
                    MM(bk(h)[:, 256:320], YS[:, h, 0:128], tm[:, h, 192:256], True, True, [("YS", h), ("tm", h)], [bkk(h)])
                    CP("pool", RR(Eb[:, h, 384:448]), tm[:, h, 0:64], [("tm", h)], [("E", h, "b")])
                for h in HS:
                    CP("act", RR(Eb[:, h, 448:512]), bk(h)[:, 256:320], [bkk(h)], [("E", h, "c")])
                advance(nxt, 2)
                for lev in range(6):
                    def views(h):
                        if lev == 0:
                            return (Eb[:, h, 0:128], Eb[:, h, 256:384], Eb[:, h, 256:512], Eb[:, h, 384:512],
                                    [("E", h, "a"), ("E", h, "b"), ("E", h, "c")])
                        Lp = Lb[:, h, (lev - 1) % 2, :]
                        return (Lp[:, 0:128], Lp[:, 128:256], Lp[:, 128:384], Lp[:, 256:384],
                                [("L", h, (lev - 1) % 2, "p"), ("L", h, (lev - 1) % 2, "z")])
                    for h in HS:
                        PT_, P_, PZ_, Z_, rk = views(h)
                        MM(bk(h)[:, 128:384], PT_, PZ_, True, True, rk, [bkk(h)], r=True)
                        if lev < 5:
                            MM(bk(h)[:, 0:128], P_, PT_, True, True, rk, [bkk(h)])
                    for h in HS:
                        PT_, P_, PZ_, Z_, rk = views(h)
                        Ln = Lb[:, h, lev % 2, :]
                        TT("dve", RR(Ln[:, 256:384]), bk(h)[:, 256:384], Z_, ALU.add, [bkk(h)] + rk, [("L", h, lev % 2, "z")])
                        if lev < 5:
                            CP("act", RR(Ln[:, 0:256]), bk(h)[:, 0:256], [bkk(h)], [("L", h, lev % 2, "p")])
                    advance(nxt, 3)
                for h in HS:
                    for hf in range(2):
                        TS("pool", B2[:, h, hf * 64:(hf + 1) * 64], tm[:, h, 64:128], rowm[:, hf:hf + 1], None, ALU.mult, None, [("tm", h), "cst"], [("B2", h)])
                        TS("pool", K2[:, h, hf * 64:(hf + 1) * 64], tm[:, h, 128:192], rowm[:, hf:hf + 1], None, ALU.mult, None, [("tm", h), "cst"], [("K2", h)])
                for h in HS:
                    Lf = Lb[:, h, 1, :]
                    kLf = ("L", h, 1, "z")
                    W_, U0_ = Lf[:, 256:320], Lf[:, 320:384]
                    b_ = bk(h)
                    MM(b_[0:64, 0:128], W_, B2[:, h, :], True, True, [kLf, ("B2", h)], [bkk(h)])
                    for hf in range(2):
                        MM(b_[0:64, 128 + hf * 64:128 + (hf + 1) * 64], K2[:, h, hf * 64:(hf + 1) * 64], tm[:, h, 192:256], True, False, [("K2", h), ("tm", h)], [bkk(h)])
                        MM(b_[0:64, 128 + hf * 64:128 + (hf + 1) * 64], B2[:, h, hf * 64:(hf + 1) * 64], U0_, False, True, [("B2", h), kLf], [bkk(h)])
                    MM(b_[0:64, 256:384], W_, Eb[:, h, 128:256], True, True, [kLf, ("E", h, "a")], [bkk(h)])
                    MM(b_[0:64, 384:512], tm[:, h, 192:256], YS[:, h, 128:256], True, False, [("tm", h), ("YS", h)], [bkk(h)])
                    MM(b_[0:64, 384:512], U0_, Eb[:, h, 128:256], False, True, [kLf, ("E", h, "a")], [bkk(h)])
                advance(nxt, 2)
                for h in HS:
                    b_ = bk(h)
                    for hf in range(2):
                        STT(GT[:, h, hf, :], i64, pC[:, ws, h, hf:hf + 1], b_[0:64, hf * 64:(hf + 1) * 64], ALU.mult, ALU.add,
                            [bkk(h), ("pC", ws, h), "cst"], [("GT", h)])
                    TT("dve", RhatT[:, h, :], b_[0:64, 256:384], ARh[:, ws, h, 1, :], ALU.add, [bkk(h), A1(h)], [("Rhat", h)])
                for h in HS:
                    b_ = bk(h)
                    CP("act", Hm[:, h, :, :], b_[0:64, 128:256].rearrange("p (c i) -> p c i", c=2), [bkk(h)], [("Hm", h)])
                    CP("act", Y0T[:, h, :], b_[0:64, 384:512], [bkk(h)], [("Y0T", h)])
                advance(nxt, 2)
                allh = lambda nm: [(nm, h) for h in range(H)]
                for c in range(2):
                    csl = slice(c * 64, (c + 1) * 64)
                    m0 = mcur[0]
                    mnew = 1 - m0
                    for h in range(H):
                        MM(ps[0][0:64, h * 64:(h + 1) * 64], Mst[:, m0, h, :], RhatT[:, h, csl], True, True, [("Mst", m0), ("Rhat", h)], [("ps", 0)])
                    for h in range(H):
                        MM(ps[1][0:64, h * 64:(h + 1) * 64], GT[:, h, c, :], Mst[:, m0, h, :], True, True, [("Mst", m0), ("GT", h)], [("ps", 1)])
                    TT("dve", Mst[:, mnew, :, :], ps[1][0:64, 0:384].rearrange("p (h i) -> p h i", h=6), Hm[:, :, c, :], ALU.add,
                       [("ps", 1)] + allh("Hm"), [("Mst", mnew)])
                    TT("dve", yT[:, :, csl], ps[0][0:64, 0:384].rearrange("p (h t) -> p h t", h=6), Y0T[:, :, csl], ALU.add,
                       [("ps", 0)] + allh("Y0T"), [("yT", c)])
                    mcur[0] = mnew
                    advance(nxt, 1)
                advance(nxt, 1000)

            def post(tt, h):
                tsl = slice(tt * TW, (tt + 1) * TW)
                w3 = tt % 3
                ally = [("yT", c) for c in range(2)]
                osl = h
                kyc, kysq, kyrs = ("yc", osl), ("ysq", osl), ("yrs", osl)
                pb = ps[h % 2]
                pk = ("ps", h % 2)
                MM(pb[0:64, 0:TW], o64, yT[:, h, :], True, True, ["cst"] + ally, [pk])
                STT(yc[:, osl, :], pb[0:64, 0:TW], -1.0 / 64, yT[:, h, :], ALU.mult, ALU.add, [pk] + ally, [kyc])
                yield
                ACT(ysq[:, osl, :], yc[:, osl, :], AF.Square, [kyc], [kysq])
                yield
                MM(pb[0:64, 128:128 + TW], o64, ysq[:, osl, :], True, True, ["cst", kysq], [pk])
                ACT(yrs[:, osl, :], pb[0:64, 128:128 + TW], AF.Sqrt, [pk, "epsc"], [kyrs], scale=1.0 / 64, bias=epsc[0:64, 1:2])
                yield
                RECIP(yrs[:, osl, :], yrs[:, osl, :], [kyrs], [kyrs])
                yield
                TT("dve", yc[:, osl, :], yc[:, osl, :], yrs[:, osl, :], ALU.mult, [kyc, kyrs], [kyc])
                yield
                TS("dve", yc[:, osl, :], yc[:, osl, :], plg[:, h:h + 1], plb[:, h:h + 1], ALU.mult, ALU.add, [kyc, "plg", "plb"], [kyc])
                yield
                TT("pool", yc[:, osl, :], yc[:, osl, :], bon[:, w3, h, :], ALU.add, [kyc, ("bon", w3, h)], [kyc])
                yield
                TT("pool", obuf[:, osl, :], yc[:, osl, :], gal[:, w3, h, :], ALU.mult, [kyc, ("gal", w3, h)], [("obuf", osl)])
                DMA(catT[h * 64:(h + 1) * 64, tsl], obuf[:, osl, :], [("obuf", osl)], [("cat", "a", h, tt)])
                yield

            g0 = make_gens(0)
            advance(g0, 1000)
            for tt in range(NTB):
                pg = [post(tt - 1, h) for h in range(H)] if tt > 0 else []
                chain(tt, pg + make_gens(tt + 1))
            advance([post(NTB - 1, h) for h in range(H)], 1000)
            sc.barrier()
        if stop == "B":
            break

        with ExitStack() as es:
            wO = sb(es, "wO", [128, 2, 8, 128])
            catb = sb(es, "catb", [128, 2, 8, 512])
            mixb = sb(es, "mixb", [128, 8, 512])
            sq = sb(es, "sq", [128, 2, 512])
            rstd = sb(es, "rstd", [128, 2, 512])
            gP = colvec(es, "gP", PR["post_mix_g"][l], 8)
            w_out_l = PR["w_out"][l].rearrange("(k p) c -> p k c", p=128)
            cat_v = catT.rearrange("(k p) t -> p k t", p=128)
            wi = 0
            xe = sb(es, "xe", [128, 2, 8, 512])
            for tt in range(4):
                tsl = slice(tt * 512, (tt + 1) * 512)
                cs = tt % 2
                DMA(xe[:, cs, :, :], xD_v[:, :, tsl], [("xD", tt)], [("xe", cs, j) for j in range(8)])
                DMA(RR(catb[:, cs, :, :]), cat_v[:, :, tsl], [], [("catb", cs)], q="pool")
                for j in range(8):
                    sl = wi % 2
                    wi += 1
                    DMA(RR(wO[:, sl, :, :]), w_out_l[:, :, j * 128:(j + 1) * 128], [], [("wO", sl)], q="pool")
                    bi = j % 2
                    for k in range(8):
                        MM(ps[bi][:, :], wO[:, sl, k, :], catb[:, cs, k, :], k == 0, k == 7, [("wO", sl), ("catb", cs)], [("ps", bi)], r=True)
                    CP("act" if j % 2 else "dve", mixb[:, j, :], ps[bi][:, :], [("ps", bi)], [("mixb", j)])
                r = rms_stats(lambda k: mixb[:, k, :], lambda k: ("mixb", k), tt, (sq, rstd), ps[2 + tt % 2], ("ps", 2 + tt % 2))
                for j in range(8):
                    STT(mixb[:, j, :], mixb[:, j, :], gP[:, j:j + 1], r, ALU.mult, ALU.mult, [("mixb", j), ("rstd", tt % 2), "gP"], [("mixb", j)])
                    TT("dve", xe[:, cs, j, :], xe[:, cs, j, :], mixb[:, j, :], ALU.add, [("mixb", j), ("xe", cs, j)], [("xe", cs, j)])
                DMA(xD_v[:, :, tsl], xe[:, cs, :, :], [("xe", cs, j) for j in range(8)], [("xD", tt)])
            sc.barrier()
        if stop == "E":
            break

        with ExitStack() as es:
            hb = sb(es, "hb", [128, 8, 512])
            actb = sb(es, "actb", [128, NFF, 512])
            wG = sb(es, "wG", [128, 2, 2, 8, 128])
            wD = sb(es, "wD", [128, 2, NFF, 128])
            sq = sb(es, "sq", [128, 2, 512])
            rstd = sb(es, "rstd", [128, 2, 512])
            sil = sb(es, "sil", [128, 2, 512])
            gF = colvec(es, "gF", PR["pre_ffn_g"][l], 8)
            gQ = colvec(es, "gQ", PR["post_ffn_g"][l], 8)
            w_fi = PR["w_ffn_in"][l].rearrange("(k p) c -> p k c", p=128)
            w_fo = PR["w_ffn_out"][l].rearrange("(j p) c -> p j c", p=128)
            wi = 0
            wdi = 0
            xf = sb(es, "xf", [128, 8, 512])
            for tt in range(4):
                tsl = slice(tt * 512, (tt + 1) * 512)
                DMA(xf[:, :, :], xD_v[:, :, tsl], [("xD", tt)], [("xf", j) for j in range(8)])
                r = rms_stats(lambda k: xf[:, k, :], lambda k: ("xf", k), tt, (sq, rstd), ps[6 + tt % 2], ("ps", 6 + tt % 2))
                for k in range(8):
                    STT(RR(hb[:, k, :]), xf[:, k, :], gF[:, k:k + 1], r, ALU.mult, ALU.mult, [("xf", k), ("rstd", tt % 2), "gF"], [("hb", k)])
                for j in range(NFF):
                    sl = wi % 2
                    wi += 1
                    DMA(RR(wG[:, sl, 0, :, :]), w_fi[:, :, j * 128:(j + 1) * 128], [], [("wG", sl, 0)], q="pool")
                    DMA(RR(wG[:, sl, 1, :, :]), w_fi[:, :, DFF + j * 128:DFF + (j + 1) * 128], [], [("wG", sl, 1)], q="pool")
                    bg, bu = (j % 2) * 2, (j % 2) * 2 + 1
                    for k in range(8):
                        MM(ps[bg][:, :], wG[:, sl, 0, k, :], hb[:, k, :], k == 0, k == 7, [("wG", sl, 0), ("hb", k)], [("ps", bg)], r=True)
                    for k in range(8):
                        MM(ps[bu][:, :], wG[:, sl, 1, k, :], hb[:, k, :], k == 0, k == 7, [("wG", sl, 1), ("hb", k)], [("ps", bu)], r=True)
                    ACT(sil[:, j % 2, :], ps[bg][:, :], AF.Silu, [("ps", bg)], [("sil", j % 2)])
                    TT("dve", RR(actb[:, j, :]), ps[bu][:, :], sil[:, j % 2, :], ALU.mult, [("ps", bu), ("sil", j % 2)], [("actb", j)])
                allact = [("actb", j) for j in range(NFF)]
                for jo in range(8):
                    sl = wdi % 2
                    wdi += 1
                    DMA(RR(wD[:, sl, :, :]), w_fo[:, :, jo * 128:(jo + 1) * 128], [], [("wD", sl)], q="pool")
                    bi = 4 + jo % 2
                    for j in range(NFF):
                        MM(ps[bi][:, :], wD[:, sl, j, :], actb[:, j, :], j == 0, j == NFF - 1, [("wD", sl), ("actb", j)], [("ps", bi)], r=True)
                    CP("act" if jo % 2 else "dve", RR(hb[:, jo, :]), ps[bi][:, :], [("ps", bi)], [("hb", jo)])
                r = rms_stats(lambda k: hb[:, k, :], lambda k: ("hb", k), tt + 1, (sq, rstd), ps[6 + (tt + 1) % 2], ("ps", 6 + (tt + 1) % 2))
                for j in range(8):
                    STT(RR(hb[:, j, :]), hb[:, j, :], gQ[:, j:j + 1], r, ALU.mult, ALU.mult, [("hb", j), ("rstd", (tt + 1) % 2), "gQ"], [("hb", j)])
                    TT("dve", xf[:, j, :], xf[:, j, :], hb[:, j, :], ALU.add, [("hb", j), ("xf", j)], [("xf", j)])
                DMA(xD_v[:, :, tsl], xf[:, :, :], [("xf", j) for j in range(8)], [("xD", tt)])
            sc.barrier()
        if OPTS.get("xdbg") == l:
            DMA(xdbg[:, :], xD[:, :], [("xD", tt) for tt in range(4)], [("xdbg", 0)])

    if stop in (None, 'setup'):
        with ExitStack() as es:
            yo = sb(es, "yo", [128, 2, D])
            xo = sb(es, "xo", [128, 2, 8, 512])
            for t in range(16):
                sl = t % 2
                xsl = (t // 4) % 2
                if t % 4 == 0:
                    DMA(xo[:, xsl, :, :], xD_v[:, :, (t // 4) * 512:(t // 4 + 1) * 512], [("xD", t // 4)], [("xo", xsl)])
                for g in range(2):
                    bank = ps[(t * 2 + g) % 4]
                    bk = ("ps", (t * 2 + g) % 4)
                    for kk in range(4):
                        k = g * 4 + kk
                        TR(bank[:, kk * 128:(kk + 1) * 128], xo[:, xsl, k, (t % 4) * 128:(t % 4 + 1) * 128], ident, [("xo", xsl), "cst"], [bk])
                    CP("act" if g else "dve", yo[:, sl, g * 512:(g + 1) * 512], bank[:, :], [bk], [("yo", sl, g)])
                DMA(y_out[t * 128:(t + 1) * 128, :], yo[:, sl, :], [("yo", sl, 0), ("yo", sl, 1)], [("y", t)])
    sc.barrier()
    sc.emit()
    glob.close()
    return nc


_NC_CACHE = {}


def kernel(**inputs):
    if "nc" not in _NC_CACHE:
        _NC_CACHE["nc"] = build()
    nc = _NC_CACHE["nc"]
    x = np.ascontiguousarray(np.asarray(inputs["x"], dtype=np.float32))
    base = {n: np.ascontiguousarray(np.asarray(inputs[n], dtype=np.float32)) for n in PARAM_NAMES}
    base["cst"] = CONSTS["cst"]
    base["mobac"] = CONSTS["mobac"]
    base["blkind"] = CONSTS["blkind"]
    in_maps = []
    for b in range(8):
        m = dict(base)
        m["x"] = x[b]
        in_maps.append(m)
    res = run_bass_kernel_spmd(nc, in_maps, core_ids=list(range(8)))
    return np.stack([np.asarray(r["y"], dtype=np.float32) for r in res.results], 0)
```

```python
import numpy as np
from contextlib import ExitStack
import concourse.bass as bass
import concourse.mybir as mybir
from concourse.bass_utils import run_bass_kernel_spmd

F32 = mybir.dt.float32
F32R = mybir.dt.float32r
AF = mybir.ActivationFunctionType
ALU = mybir.AluOpType
AX = mybir.AxisListType

S = 2048
D = 1024
L = 2
DFF = 2816
NFF = 22
H = 6
C0 = float(np.exp(-0.5))
BIG = 30000.0


OPTS = {}


class Sched:
    def __init__(self, nc, n_dma=40):
        self.nc = nc
        self.names = ["pe", "act", "dve", "pool", "sp"]
        self.sem = {e: nc.alloc_semaphore("s_" + e) for e in ["pe", "act", "dve", "pool"]}
        self.cnt = {e: 0 for e in self.sem}
        self.dsem = [nc.alloc_semaphore("d%d" % i) for i in range(n_dma)]
        self.dcnt = [0] * n_dma
        self.drr = 0
        self.q = {e: [] for e in self.names}
        self.clock = {e: {} for e in self.names}
        self.evclock = {}
        self.evorder = {}
        self.nev = 0
        self.lastw = {}
        self.readers = {}

    def _deps(self, reads, writes):
        deps = {}

        def add(k, v):
            if deps.get(k, 0) < v:
                deps[k] = v

        for r in reads:
            ev = self.lastw.get(r)
            if ev is not None:
                add(*ev)
        for w in writes:
            ev = self.lastw.get(w)
            if ev is not None:
                add(*ev)
            for k, v in self.readers.get(w, {}).items():
                add(k, v)
        return deps

    def _commit(self, ev, reads, writes):
        k, v = ev
        for r in reads:
            d = self.readers.setdefault(r, {})
            if d.get(k, 0) < v:
                d[k] = v
        for w in writes:
            self.lastw[w] = ev
            self.readers[w] = {}

    def _waits(self, eng, deps):
        clk = self.clock.setdefault(eng, {})
        waits = []
        for k, v in sorted(deps.items(), key=lambda kv: -self.evorder.get(kv, 0)):
            if eng == "pe" and k == ("e", "pe"):
                continue
            if clk.get(k, 0) >= v:
                continue
            waits.append((k, v))
            for k2, v2 in self.evclock.get((k, v), {}).items():
                if clk.get(k2, 0) < v2:
                    clk[k2] = v2
            clk[k] = v
        return waits

    def op(self, eng, fn, reads=(), writes=()):
        banks = {("psx", k[1]) for k in list(reads) + list(writes) if isinstance(k, tuple) and k and k[0] == "ps"}
        if banks:
            writes = list(writes) + list(banks)
        deps = self._deps(reads, writes)
        waits = self._waits(eng, deps)
        self.cnt[eng] += 1
        ev = (("e", eng), self.cnt[eng])
        self.evclock[ev] = dict(self.clock[eng])
        self.nev += 1
        self.evorder[ev] = self.nev
        self.q[eng].append((waits, fn, "e"))
        self._commit(ev, reads, writes)

    def dma(self, qeng, out, in_, reads=(), writes=(), **kw):
        deps = self._deps(reads, writes)
        idx = self.drr
        self.drr = (self.drr + 1) % len(self.dsem)
        if self.dcnt[idx] > 0:
            k = ("d", idx)
            deps[k] = max(deps.get(k, 0), self.dcnt[idx])
        waits = self._waits(qeng, deps)
        self.dcnt[idx] += 16
        ev = (("d", idx), self.dcnt[idx])
        self.evclock[ev] = dict(self.clock[qeng])
        self.nev += 1
        self.evorder[ev] = self.nev
        self.q[qeng].append((waits, lambda e: e.dma_start(out=out, in_=in_, **kw), idx))
        self._commit(ev, reads, writes)

    def barrier(self):
        allev = [(("e", e), c) for e, c in self.cnt.items() if c > 0]
        allev += [(("d", i), c) for i, c in enumerate(self.dcnt) if c > 0]
        for eng in self.names:
            waits = self._waits(eng, dict(allev))
            if waits:
                self.q[eng].append((waits, None, None))
        self.lastw = {}
        self.readers = {}
        self.evclock = {}

    def emit(self):
        nc = self.nc
        engs = {"pe": "tensor", "act": "scalar", "dve": "vector", "pool": "gpsimd", "sp": "sync"}
        with nc.Block() as block:
            for name in self.names:
                def body(eng, name=name):
                    for waits, fn, kind in self.q[name]:
                        emb = None
                        if fn is not None and waits and kind == "e" and not OPTS.get("noemb"):
                            emb = waits[-1]
                            waits = waits[:-1]
                        for k, v in waits:
                            s = self.sem[k[1]] if k[0] == "e" else self.dsem[k[1]]
                            eng.wait_ge(s, v)
                        if fn is None:
                            continue
                        ins = fn(eng)
                        if emb is not None:
                            k, v = emb
                            ins._wait_ge(self.sem[k[1]] if k[0] == "e" else self.dsem[k[1]], v)
                        if kind == "e":
                            ins.then_inc(self.sem[name], 1)
                        else:
                            ins.then_inc(self.dsem[kind], 16)
                getattr(block, engs[name])(body)


def make_consts():
    c = {}
    i128 = np.arange(128)
    blk = (i128[:, None] // 64) == (i128[None, :] // 64)
    ident = np.eye(128, dtype=np.float32)
    ones = np.ones((128, 128), np.float32)
    SL = ((i128[:, None] > i128[None, :]) & blk).astype(np.float32)
    SU = ((i128[:, None] < i128[None, :]) & blk).astype(np.float32)
    IU = ((i128[:, None] <= i128[None, :]) & blk).astype(np.float32)
    IUfull = (i128[:, None] <= i128[None, :]).astype(np.float32)
    idst = np.concatenate([np.eye(64), np.eye(64)], 0).astype(np.float32)
    reset = np.ones((128, 256), np.float32)
    reset[:, ::64] = 0.0
    rowm = np.zeros((128, 2), np.float32)
    rowm[:64, 0] = 1.0
    rowm[64:, 1] = 1.0
    q512 = np.arange(512)
    cm = np.stack([(q512[None, :] >= (j * 128 + i128[:, None])).astype(np.float32) for j in range(4)], 1)
    parts = [ident, ones, SU, IU, SL, SU, IU, IUfull, idst, reset, rowm, cm.reshape(128, 2048)]
    offs = {}
    o = 0
    for nm, p in zip(["ident", "ones", "mE", "_1", "_2", "mY", "_3", "iuf", "idst", "reset", "rowm", "cm"], parts):
        offs[nm] = o
        o += p.shape[1]
    c["cst"] = np.ascontiguousarray(np.concatenate(parts, 1))
    c["offs"] = offs
    mb = np.zeros((128, 3, 16, 6, 8), np.float32)
    for t in range(16):
        b = t // 2
        for n in range(8):
            mb[:, 0, t, :, n] = 0.0 if n < b else -1e30
            mb[:, 1, t, :, n] = 1.0 if n < b else 0.0
            mb[:, 2, t, :, n] = 1.0 if n == b else 0.0
    c["mobac"] = mb.reshape(128, 3 * 16 * 48)
    bi = np.zeros((8, S), np.float32)
    for n in range(8):
        bi[n, n * 256:(n + 1) * 256] = 1.0
    c["blkind"] = bi
    return c


CONSTS = make_consts()
PARAM_NAMES = ["pre_mix_g", "w_in", "rwkv_mu", "rwkv_w0", "rwkv_w2", "rwkv_a0", "rwkv_a2", "rwkv_g2",
               "rwkv_k_k", "rwkv_k_a", "rwkv_r_k", "rwkv_lnx_g", "rwkv_lnx_b", "gmlp_ln_g", "gmlp_ln_b",
               "gmlp_w_s", "gmlp_b_s", "w_out", "post_mix_g", "pre_ffn_g", "w_ffn_in", "w_ffn_out", "post_ffn_g"]
PARAM_SHAPES = {"pre_mix_g": (L, D), "w_in": (L, D, 3072), "rwkv_mu": (L, 1408), "rwkv_w0": (L, 384),
                "rwkv_w2": (L, 64, 384), "rwkv_a0": (L, 384), "rwkv_a2": (L, 64, 384), "rwkv_g2": (L, 128, 384),
                "rwkv_k_k": (L, 384), "rwkv_k_a": (L, 384), "rwkv_r_k": (L, 6, 64), "rwkv_lnx_g": (L, 384),
                "rwkv_lnx_b": (L, 384), "gmlp_ln_g": (L, 256), "gmlp_ln_b": (L, 256), "gmlp_w_s": (L, 4, 128, 128),
                "gmlp_b_s": (L, 4, 128), "w_out": (L, D, D), "post_mix_g": (L, D), "pre_ffn_g": (L, D),
                "w_ffn_in": (L, D, 2 * DFF), "w_ffn_out": (L, DFF, D), "post_ffn_g": (L, D)}


def build(dbg=None, nlayers=L, stop=None):
    dbg = dbg or []
    nc = bass.Bass("TRN2", target_bir_lowering=False)
    sc = Sched(nc)
    OF = CONSTS["offs"]

    def dram(name, shape, kind="Internal"):
        if name in dbg:
            kind = "ExternalOutput"
        return nc.dram_tensor(name, list(shape), F32, kind=kind).ap()

    x_in = dram("x", [S, D], "ExternalInput")
    y_out = dram("y", [S, D], "ExternalOutput")
    cst_d = dram("cst", CONSTS["cst"].shape, "ExternalInput")
    if not OPTS.get("noparams"):
        PR = {n: dram(n, PARAM_SHAPES[n], "ExternalInput") for n in PARAM_NAMES}
        mobac_d = dram("mobac", CONSTS["mobac"].shape, "ExternalInput")
        blkind_d = dram("blkind", CONSTS["blkind"].shape, "ExternalInput")
    if OPTS.get("noscratch"):
        glob_scr = None
    rwkvT = dram("rwkvT", [1408, S]) if not OPTS.get("noscratch") else None
    qkT = dram("qkT", [768, S]) if not OPTS.get("noscratch") else None
    uT = dram("uT", [256, S]) if not OPTS.get("noscratch") else None
    vm_tm = dram("vm_tm", [S, 384]) if not OPTS.get("noscratch") else None
    vg_tm = dram("vg_tm", [S, 256]) if not OPTS.get("noscratch") else None
    catT = dram("catT", [D, S]) if not OPTS.get("noscratch") else None
    xdbg = dram("xdbg", [D, S]) if not OPTS.get("noscratch") else None
    xD = dram("xD", [D, S])
    xD_v = xD.rearrange("(k p) t -> p k t", p=128)

    uid = [0]

    def sb(es, name, shape):
        uid[0] += 1
        return es.enter_context(nc.sbuf_tensor("%s_%d" % (name, uid[0]), list(shape), F32))

    glob = ExitStack()
    cst = sb(glob, "cst_sb", [128, CONSTS["cst"].shape[1]])
    ps = [glob.enter_context(nc.psum_tensor("ps%d" % i, [128, 512], F32)) for i in range(8)]
    ident = cst[:, OF["ident"]:OF["ident"] + 128]
    ones = cst[:, OF["ones"]:OF["ones"] + 128]
    mE = cst[:, OF["mE"]:OF["mE"] + 384]
    mY = cst[:, OF["mY"]:OF["mY"] + 256]
    iuf = cst[:, OF["iuf"]:OF["iuf"] + 128]
    idst = cst[:, OF["idst"]:OF["idst"] + 64]
    resetm = cst[:, OF["reset"]:OF["reset"] + 256]
    cm = cst[:, OF["cm"]:OF["cm"] + 2048]
    epsc = sb(glob, "epsc", [128, 4])

    def ACT(out, in_, func, reads, writes, **kw):
        sc.op("act", lambda e: e.activation(out=out, in_=in_, func=func, **kw), reads, writes)

    def RR(ap):
        return ap if OPTS.get("nor") else ap.bitcast(F32R)

    def MM(out, lhsT, rhs, start, stop, reads, writes, r=False):
        if r and not OPTS.get("nor"):
            lhsT = lhsT.bitcast(F32R)
            rhs = rhs.bitcast(F32R)
        sc.op("pe", lambda e: e.matmul(out, lhsT=lhsT, rhs=rhs, start=start, stop=stop), reads, writes)

    def TR(out, in_, idn, reads, writes):
        sc.op("pe", lambda e: e.transpose(out, in_, idn), reads, writes)

    def TT(eng, out, in0, in1, op, reads, writes):
        sc.op(eng, lambda e: e.tensor_tensor(out=out, in0=in0, in1=in1, op=op), reads, writes)

    def TS(eng, out, in0, s1, s2, op0, op1, reads, writes):
        if s2 is None:
            sc.op(eng, lambda e: e.tensor_scalar(out=out, in0=in0, scalar1=s1, scalar2=None, op0=op0), reads, writes)
        else:
            sc.op(eng, lambda e: e.tensor_scalar(out=out, in0=in0, scalar1=s1, scalar2=s2, op0=op0, op1=op1), reads, writes)

    def STT(out, in0, scalar, in1, op0, op1, reads, writes):
        sc.op("dve", lambda e: e.scalar_tensor_tensor(out=out, in0=in0, scalar=scalar, in1=in1, op0=op0, op1=op1), reads, writes)

    def CP(eng, out, in_, reads, writes):
        if eng == "act":
            sc.op("act", lambda e: e.copy(out=out, in_=in_), reads, writes)
        else:
            sc.op(eng, lambda e: e.tensor_copy(out=out, in_=in_), reads, writes)

    def RECIP(out, in_, reads, writes):
        sc.op("dve", lambda e: e.reciprocal(out=out, in_=in_), reads, writes)

    def DMA(out, in_, reads, writes, q="sp", **kw):
        sc.dma(q, out, in_, reads, writes, **kw)

    def colvec(es, name, src_1d, ncol, p=128):
        t = sb(es, name, [p, ncol])
        DMA(t[:, :], src_1d.rearrange("(c p) -> p c", p=p), [], [name], allow_slow_non_contiguous=True)
        return t

    DMA(cst[:, :], cst_d[:, :], [], ["cst"])
    sc.op("dve", lambda e: e.memset(epsc[:, 0:1], 1e-6), [], ["epsc"])
    sc.op("dve", lambda e: e.memset(epsc[:, 1:2], 64e-5), [], ["epsc"])
    sc.op("dve", lambda e: e.memset(epsc[:, 2:3], 0.0), [], ["epsc"])
    with ExitStack() as es:
        xin = sb(es, "xin", [128, 2, D])
        xs = sb(es, "xs", [128, 2, 8, 512])
        for t in range(16):
            sl = t % 2
            xsl = (t // 4) % 2
            DMA(xin[:, sl, :], x_in[t * 128:(t + 1) * 128, :], [], [("xin", sl)])
            for g in range(2):
                bank = ps[(t * 2 + g) % 4]
                bk = ("ps", (t * 2 + g) % 4)
                for kk in range(4):
                    k = g * 4 + kk
                    TR(bank[:, kk * 128:(kk + 1) * 128], xin[:, sl, k * 128:(k + 1) * 128], ident, [("xin", sl), "cst"], [bk])
                CP("act" if g else "dve", xs[:, xsl, g * 4:(g + 1) * 4, (t % 4) * 128:(t % 4 + 1) * 128],
                   bank[:, :].rearrange("p (k c) -> p k c", k=4), [bk], [("xs", xsl, t % 4, g)])
            if t % 4 == 3:
                DMA(xD_v[:, :, (t // 4) * 512:(t // 4 + 1) * 512], xs[:, xsl, :, :], [("xs", xsl, q, g) for q in range(4) for g in range(2)], [("xD", t // 4)])
        sc.barrier()

    def rms_stats(src_fn, src_keys, tt, es_tiles, pbank, pkey):
        sq, rstd = es_tiles
        for k in range(8):
            ACT(sq[:, k % 2, :], src_fn(k), AF.Square, [src_keys(k)], [("sq", k % 2)])
            MM(pbank[:, :], ones, sq[:, k % 2, :], k == 0, k == 7, [("sq", k % 2), "cst"], [pkey])
        ACT(rstd[:, tt % 2, :], pbank[:, :], AF.Sqrt, [pkey, "epsc"], [("rstd", tt % 2)], scale=1.0 / D, bias=epsc[:, 0:1])
        RECIP(rstd[:, tt % 2, :], rstd[:, tt % 2, :], [("rstd", tt % 2)], [("rstd", tt % 2)])
        return rstd[:, tt % 2, :]

    for l in range(nlayers if stop != 'setup' else 0):
        with ExitStack() as es:
            hbuf = sb(es, "hbuf", [128, 8, S])
            sq = sb(es, "sq", [128, 2, 512])
            rstd = sb(es, "rstd", [128, 2, 512])
            gA = colvec(es, "gA", PR["pre_mix_g"][l], 8)
            muA = colvec(es, "muA", PR["rwkv_mu"][l], 11)
            xa = sb(es, "xa", [128, 2, 8, 512])
            for tt in range(4):
                tsl = slice(tt * 512, (tt + 1) * 512)
                xsl = tt % 2
                DMA(xa[:, xsl, :, :], xD_v[:, :, tsl], [("xD", tt)], [("xa", xsl)])
                r = rms_stats(lambda k: xa[:, xsl, k, :], lambda k: ("xa", xsl), tt, (sq, rstd), ps[4 + tt % 2], ("ps", 4 + tt % 2))
                for k in range(8):
                    STT(RR(hbuf[:, k, tsl]), xa[:, xsl, k, :], gA[:, k:k + 1], r, ALU.mult, ALU.mult,
                        [("xa", xsl), ("rstd", tt % 2), "gA"], [("h", k, tt)])
            es_main = es
            es = ExitStack()
            wA = sb(es, "wA", [128, 2, 8, 128])
            stg = sb(es, "stg", [128, 2, S])
            stg2 = sb(es, "stg2", [128, 2, S])
            w_in_l = PR["w_in"][l].rearrange("(k p) c -> p k c", p=128)
            fm_chunks = [(c * 128, rwkvT, c * 128, True) for c in range(11)]
            fm_chunks += [(1408 + c * 128, qkT, c * 128, False) for c in range(6)]
            fm_chunks += [(2560 + c * 128, uT, c * 128, False) for c in range(2)]
            for ci, (col0, dst, row0, shift) in enumerate(fm_chunks):
                sl = ci % 2
                DMA(RR(wA[:, sl, :, :]), w_in_l[:, :, col0:col0 + 128], [], [("wA", sl)], q="pool")
                for tt in range(4):
                    tsl = slice(tt * 512, (tt + 1) * 512)
                    bi = (ci * 4 + tt) % 4
                    for k in range(8):
                        MM(ps[bi][:, :], wA[:, sl, k, :], hbuf[:, k, tsl], k == 0, k == 7,
                           [("wA", sl), ("h", k, tt)], [("ps", bi)], r=True)
                    CP("act" if tt % 2 else "dve", stg[:, sl, tsl], ps[bi][:, :], [("ps", bi)], [("stg", sl, tt)])
                allst = [("stg", sl, tt) for tt in range(4)]
                if shift:
                    TT("pool", stg2[:, sl, 1:S], stg[:, sl, 0:S - 1], stg[:, sl, 1:S], ALU.subtract, allst, [("stg2", sl)])
                    TS("pool", stg2[:, sl, 0:1], stg[:, sl, 0:1], -1.0, None, ALU.mult, None, allst, [("stg2", sl)])
                    STT(stg2[:, sl, :], stg2[:, sl, :], muA[:, ci:ci + 1], stg[:, sl, :], ALU.mult, ALU.add,
                        allst + [("stg2", sl), "muA"], [("stg2", sl)])
                    DMA(dst[row0:row0 + 128, :], stg2[:, sl, :], [("stg2", sl)], [("dr", id(dst), row0)], q="pool")
                else:
                    DMA(dst[row0:row0 + 128, :], stg[:, sl, :], allst, [("dr", id(dst), row0)], q="pool")
            sc.barrier()
            es.close()
            es = es_main
            wB = sb(es, "wB", [128, 8, 640])
            DMA(RR(wB[:, :, 0:384]), w_in_l[:, :, 2176:2560], [], ["wB"], q="pool")
            DMA(RR(wB[:, :, 384:640]), w_in_l[:, :, 2816:3072], [], ["wB"], q="pool")
            vst = sb(es, "vst", [128, 2, 640])
            for t in range(16):
                sl = t % 2
                b0, b1 = 4 + (t % 2) * 2, 5 + (t % 2) * 2
                for k in range(8):
                    MM(ps[b0][:, 0:384], hbuf[:, k, t * 128:(t + 1) * 128], wB[:, k, 0:384], k == 0, k == 7,
                       [("h", k, t // 4), "wB"], [("ps", b0)], r=True)
                for k in range(8):
                    MM(ps[b1][:, 0:256], hbuf[:, k, t * 128:(t + 1) * 128], wB[:, k, 384:640], k == 0, k == 7,
                       [("h", k, t // 4), "wB"], [("ps", b1)], r=True)
                CP("act", vst[:, sl, 0:384], ps[b0][:, 0:384], [("ps", b0)], [("vst", sl, 0)])
                CP("dve", vst[:, sl, 384:640], ps[b1][:, 0:256], [("ps", b1)], [("vst", sl, 1)])
                DMA(vm_tm[t * 128:(t + 1) * 128, :], vst[:, sl, 0:384], [("vst", sl, 0)], [("vm", t)], q="pool")
                DMA(vg_tm[t * 128:(t + 1) * 128, :], vst[:, sl, 384:640], [("vst", sl, 1)], [("vg", t)], q="pool")
            sc.barrier()
        if stop == "A":
            break

        with ExitStack() as es:
            lng = sb(es, "lng", [128, 256])
            lnb = sb(es, "lnb", [128, 256])
            DMA(lng[:, :], PR["gmlp_ln_g"][l].partition_broadcast(128), [], ["lng"])
            DMA(lnb[:, :], PR["gmlp_ln_b"][l].partition_broadcast(128), [], ["lnb"])
            wsn = sb(es, "wsn", [128, 4, 128])
            wsT = sb(es, "wsT", [128, 4, 128])
            bsr = sb(es, "bsr", [1, 512])
            DMA(wsn[:, :, :], PR["gmlp_w_s"][l].rearrange("g t s -> t g s"), [], ["wsn"])
            DMA(bsr[:, :], PR["gmlp_b_s"][l].rearrange("g t -> (g t)").partition_broadcast(1), [], ["bsr"])
            for g in range(4):
                TR(ps[0][:, g * 128:(g + 1) * 128], wsn[:, g, :], ident, ["wsn", "cst"], [("ps", 0)])
            for g in range(4):
                TT("dve", wsT[:, g, :], ps[0][:, g * 128:(g + 1) * 128], iuf, ALU.mult, [("ps", 0), "cst"], ["wsT"])
            gu = sb(es, "gu", [128, 2, S])
            t1 = sb(es, "t1", [128, S])
            cout = sb(es, "cout", [128, 2, S])
            for pp in range(2):
                DMA(gu[:, pp, :], uT[pp * 128:(pp + 1) * 128, :], [], [("gu", pp)])
                ACT(t1[:, :], gu[:, pp, :], AF.Square, [("gu", pp)], ["t1"])
                TS("pool", t1[:, :], t1[:, :], 0.044715, 1.0, ALU.mult, ALU.add, ["t1"], ["t1"])
                TT("dve", t1[:, :], t1[:, :], gu[:, pp, :], ALU.mult, ["t1", ("gu", pp)], ["t1"])
                ACT(t1[:, :], t1[:, :], AF.Sigmoid, ["t1"], ["t1"], scale=2.0 * 0.7978845608028654)
                TT("dve", gu[:, pp, :], gu[:, pp, :], t1[:, :], ALU.mult, ["t1", ("gu", pp)], [("gu", pp)])
            vb = sb(es, "vb", [128, 2, 256])
            t2 = sb(es, "t2", [128, 2, 256])
            st6 = sb(es, "st6", [128, 2, 8])
            for c in range(16):
                sl = c % 2
                DMA(vb[:, sl, :], vg_tm[c * 128:(c + 1) * 128, :], [], [("vb", sl)])
                kv, kt = ("vb", sl), ("t2", sl)
                ACT(t2[:, sl, :], vb[:, sl, :], AF.Square, [kv], [kt])
                TS("pool", t2[:, sl, :], t2[:, sl, :], 0.044715, 1.0, ALU.mult, ALU.add, [kt], [kt])
                TT("dve", t2[:, sl, :], t2[:, sl, :], vb[:, sl, :], ALU.mult, [kt, kv], [kt])
                ACT(t2[:, sl, :], t2[:, sl, :], AF.Sigmoid, [kt], [kt], scale=2.0 * 0.7978845608028654)
                TT("dve", vb[:, sl, :], vb[:, sl, :], t2[:, sl, :], ALU.mult, [kt, kv], [kv])
                ks = ("st6", sl)
                sc.op("dve", lambda e, sl=sl: e.bn_stats(out=st6[:, sl, 0:6], in_=vb[:, sl, :]), [kv], [ks])
                sc.op("dve", lambda e, sl=sl: e.bn_aggr(out=st6[:, sl, 6:8], in_=st6[:, sl, 0:6]), [ks], [ks])
                ACT(st6[:, sl, 7:8], st6[:, sl, 7:8], AF.Sqrt, [ks, "epsc"], [ks], bias=epsc[:, 0:1], scale=1.0)
                RECIP(st6[:, sl, 7:8], st6[:, sl, 7:8], [ks], [ks])
                TS("dve", vb[:, sl, :], vb[:, sl, :], st6[:, sl, 6:7], st6[:, sl, 7:8], ALU.subtract, ALU.mult, [kv, ks], [kv])
                TT("dve", vb[:, sl, :], vb[:, sl, :], lng[:, :], ALU.mult, [kv, "lng"], [kv])
                TT("dve", vb[:, sl, :], vb[:, sl, :], lnb[:, :], ALU.add, [kv, "lnb"], [kv])
                for pp in range(2):
                    bi = 1 + (c % 2) * 2 + pp
                    for gg in range(2):
                        g = pp * 2 + gg
                        MM(ps[bi][:, gg * 128:(gg + 1) * 128], vb[:, sl, pp * 128:(pp + 1) * 128], wsT[:, g, :], True, False, [kv, "wsT"], [("ps", bi)])
                        MM(ps[bi][:, gg * 128:(gg + 1) * 128], ones[0:1, :], bsr[0:1, g * 128:(g + 1) * 128], False, True, ["cst", "bsr"], [("ps", bi)])
                    for gg in range(2):
                        TT("dve", cout[gg * 64:(gg + 1) * 64, pp, c * 128:(c + 1) * 128], ps[bi][gg * 64:(gg + 1) * 64, gg * 128:(gg + 1) * 128],
                           gu[gg * 64:(gg + 1) * 64, pp, c * 128:(c + 1) * 128], ALU.mult, [("ps", bi), ("gu", pp)], [("cout", pp)])
            for pp in range(2):
                DMA(catT[768 + pp * 128:768 + (pp + 1) * 128, :], cout[:, pp, :], [("cout", pp)], [("cat", 6 + pp)], q="pool")
            sc.barrier()
        if stop == "D":
            break

        with ExitStack() as es:
            qa = sb(es, "qa", [72, 2, S])
            ka = sb(es, "ka", [72, 2, S])
            vt = sb(es, "vt", [128, 2, 16, 128])
            ones_r = sb(es, "ones_r", [128, 128])
            CP("dve", RR(ones_r[:, :]), ones, ["cst"], ["ones_r"])
            kbar = sb(es, "kbar", [64, 2, 8])
            mobc = sb(es, "mobc", [128, 3, 16, 48])
            NP = sb(es, "NP", [128, 2, 16, 72])
            sm = sb(es, "sm", [128, 16, 8])
            top8 = sb(es, "top8", [128, 16, 8])
            al = sb(es, "al", [128, 16, 8])
            pt = sb(es, "pt", [128, 4, 512])
            pacc = sb(es, "pacc", [128, 512])
            rden = sb(es, "rden", [128, 512])
            ob = sb(es, "ob", [128, 2, 512])
            DMA(mobc[:, :, :, :], mobac_d.rearrange("p (a t c) -> p a t c", a=3, t=16), [], ["mobc"])
            for q in range(2):
                DMA(RR(ka[64:72, q, :]), blkind_d[:, :], [], [("kaB", q)], q="pool")
            sc.op("pool", lambda e: e.memset(NP[:, :, :, :], 0.0), [], [("NP", 0), ("NP", 1)])
            pti = [0]

            def prepA(h):
                q = h % 2
                DMA(RR(qa[0:64, q, :]), qkT[h * 64:(h + 1) * 64, :], [], [("qaQ", q)], q="pool")
                DMA(RR(ka[0:64, q, :]), qkT[384 + h * 64:384 + (h + 1) * 64, :], [], [("kaK", q)], q="pool")
                if h % 2 == 0:
                    vq = (h // 2) % 2
                    DMA(RR(vt[:, vq, :, :]), vm_tm.rearrange("(t p) c -> p t c", p=128)[:, :, h * 64:(h + 2) * 64], [], [("vt", vq)], q="pool")
                sc.op("dve", lambda e: e.tensor_reduce(out=kbar[:, q, :], in_=ka[0:64, q, :].rearrange("p (n k) -> p n k", n=8), axis=AX.X, op=ALU.add), [("kaK", q)], [("kbar", q)])
                for t in range(16):
                    MM(ps[0][:, t * 8:(t + 1) * 8], qa[0:64, q, t * 128:(t + 1) * 128], kbar[:, q, :], True, True, [("qaQ", q), ("kbar", q)], [("ps", 0)])

            def prepB(h):
                q = h % 2
                hs = slice(h * 8, (h + 1) * 8)
                TT("dve", sm[:, :, :], ps[0][:, 0:128].rearrange("p (t n) -> p t n", t=16), mobc[:, 0, :, hs], ALU.add, [("ps", 0), "mobc"], ["sm"])
                for t in range(16):
                    sc.op("dve", lambda e, t=t: e.max(out=top8[:, t, :], in_=sm[:, t, :]), ["sm"], [("top8", t)])
                for t in range(16):
                    TS("dve", al[:, t, :], sm[:, t, :], top8[:, t, 2:3], None, ALU.is_ge, None, ["sm", ("top8", t)], [("al", t)])
                allal = [("al", t) for t in range(16)]
                TT("dve", al[:, :, :], al[:, :, :], mobc[:, 1, :, hs], ALU.mult, allal + ["mobc"], ["al2"])
                TT("dve", al[:, :, :], al[:, :, :], mobc[:, 2, :, hs], ALU.add, ["al2", "mobc"], ["al2"])
                TS("dve", NP[:, q, :, 64:72], al[:, :, :], -1.0, BIG, ALU.add, ALU.mult, ["al2"], [("NP", q)])

            def prepC(h):
                q = h % 2
                for t4 in range(4):
                    for tq in range(4):
                        t = t4 * 4 + tq
                        MM(ps[1][0:72, tq * 128:(tq + 1) * 128], NP[:, q, t, :], ident, True, True, [("NP", q), "cst"], [("ps", 1)])
                    CP("act", RR(qa[64:72, q, t4 * 512:(t4 + 1) * 512]), ps[1][64:72, :], [("ps", 1)], [("qaM", q)])

            def attn(h, qt):
                q = h % 2
                vq = (h // 2) % 2
                hp = slice((h % 2) * 64, (h % 2) * 64 + 64)
                qsl = slice(qt * 512, (qt + 1) * 512)
                nk = (qt + 1) * 4
                osl = qt % 2
                pis = {}

                def qk(kt):
                    sb_i = 2 + kt % 2
                    pi = pti[0] % 4
                    pti[0] += 1
                    pis[kt] = pi
                    MM(ps[sb_i][:, :], ka[0:72, q, kt * 128:(kt + 1) * 128], qa[0:72, q, qsl], True, True,
                       [("kaK", q), ("kaB", q), ("qaQ", q), ("qaM", q)], [("ps", sb_i)], r=True)
                    ACT(RR(pt[:, pi, :]), ps[sb_i][:, :], AF.Exp, [("ps", sb_i)], [("pt", pi)], scale=0.125)
                    if kt >= qt * 4:
                        j = kt - qt * 4
                        TT("dve", RR(pt[:, pi, :]), pt[:, pi, :], cm[:, j * 512:(j + 1) * 512], ALU.mult, [("pt", pi), "cst"], [("pt", pi)])

                def pv(kt):
                    pi = pis[kt]
                    MM(ps[4][:, :], vt[:, vq, kt, :], pt[:, pi, :], kt == 0, kt == nk - 1, [("vt", vq), ("pt", pi)], [("ps", 4)], r=True)
                    if kt == 0:
                        CP("pool", RR(pacc[:, :]), pt[:, pi, :], [("pt", pi)], ["pacc"])
                    else:
                        TT("pool", RR(pacc[:, :]), pacc[:, :], pt[:, pi, :], ALU.add, [("pt", pi), "pacc"], ["pacc"])

                qk(0)
                for kt in range(nk):
                    if kt + 1 < nk:
                        qk(kt + 1)
                    pv(kt)
                MM(ps[5][:, :], ones_r[:, :], pacc[:, :], True, True, ["ones_r", "pacc"], [("ps", 5)], r=True)
                RECIP(rden[hp, :], ps[5][hp, :], [("ps", 5)], ["rden"])
                TT("dve", ob[hp, osl, :], ps[4][hp, :], rden[hp, :], ALU.mult, [("ps", 4), "rden"], [("ob", osl)])
                DMA(catT[384 + h * 64:384 + (h + 1) * 64, qsl], ob[hp, osl, :], [("ob", osl)], [("cat", "b", h, qt)])

            prepA(0)
            prepB(0)
            prepC(0)
            for h in range(H):
                for qt in range(4):
                    attn(h, qt)
                    if h + 1 < H:
                        if qt == 0:
                            prepA(h + 1)
                        elif qt == 1:
                            prepB(h + 1)
                        elif qt == 2:
                            prepC(h + 1)
            sc.barrier()
        if stop == "C":
            break

        with ExitStack() as es:
            TW = 128
            NTB = S // TW
            w2s = sb(es, "w2s", [64, 384])
            a2s = sb(es, "a2s", [64, 384])
            g2s = sb(es, "g2s", [128, 384])
            DMA(w2s[:, :], PR["rwkv_w2"][l], [], ["w2s"])
            DMA(a2s[:, :], PR["rwkv_a2"][l], [], ["a2s"])
            DMA(g2s[:, :], PR["rwkv_g2"][l], [], ["g2s"])
            pw0 = colvec(es, "pw0", PR["rwkv_w0"][l], 6, p=64)
            pa0 = colvec(es, "pa0", PR["rwkv_a0"][l], 6, p=64)
            pkk = colvec(es, "pkk", PR["rwkv_k_k"][l], 6, p=64)
            pka = colvec(es, "pka", PR["rwkv_k_a"][l], 6, p=64)
            prk = colvec(es, "prk", PR["rwkv_r_k"][l].rearrange("h d -> (h d)"), 6, p=64)
            plg = colvec(es, "plg", PR["rwkv_lnx_g"][l], 6, p=64)
            plb = colvec(es, "plb", PR["rwkv_lnx_b"][l], 6, p=64)
            pok = sb(es, "pok", [64, 6])
            TS("dve", pok[:, :], pka[:, :], -1.0, 1.0, ALU.mult, ALU.add, ["pka"], ["pok"])
            i64 = ident[0:64, 0:64]
            o64 = ones[0:64, 0:64]
            rowm = cst[:, OF["rowm"]:OF["rowm"] + 2]
            Mst = sb(es, "Mst", [64, 2, 6, 64])
            sc.op("dve", lambda e: e.memset(Mst[:, 0, :, :], 0.0), [], [("Mst", 0)])
            mcur = [0]
            RhatT = sb(es, "RhatT", [64, 6, TW])
            Y0T = sb(es, "Y0T", [64, 6, TW])
            yT = sb(es, "yT", [64, 6, TW])
            GT = sb(es, "GT", [64, 6, 2, 64])
            Hm = sb(es, "Hm", [64, 6, 2, 64])
            bon = sb(es, "bon", [64, 3, 6, TW]); gal = sb(es, "gal", [64, 3, 6, TW])
            ARh = sb(es, "ARh", [64, 2, 6, 2, TW]); BKh = sb(es, "BKh", [64, 2, 6, 2, TW]); BPh = sb(es, "BPh", [64, 2, 6, 2, TW])
            vvh = sb(es, "vvh", [64, 2, 6, TW]); pC = sb(es, "pC", [64, 2, 6, 2])
            wd = sb(es, "wd", [64, 2, TW]); ad = sb(es, "ad", [64, 2, TW]); gd = sb(es, "gd", [128, 2, TW])
            T6 = lambda nm: sb(es, nm, [64, 6, TW])
            rr = T6("rr"); kq = T6("kq"); sig = T6("sig"); cum = T6("cum"); cpv = T6("cpv")
            epos = T6("epos"); eneg = T6("eneg"); eprv = T6("eprv"); eend = T6("eend")
            aa = T6("aa"); kk = T6("kk"); kk2 = T6("kk2"); rn = T6("rn"); kka = T6("kka"); kp = T6("kp"); rkr = T6("rkr")
            nbc = sb(es, "nbc", [64, 6, 2])
            tm = sb(es, "tm", [128, 6, 256]); Eb = sb(es, "Eb", [128, 6, 512]); YS = sb(es, "YS", [128, 6, 256])
            Lb = sb(es, "Lb", [128, 6, 2, 384]); B2 = sb(es, "B2", [128, 6, 128]); K2 = sb(es, "K2", [128, 6, 128])
            yc = sb(es, "yc", [64, 6, TW]); ysq = sb(es, "ysq", [64, 6, TW]); yrs = sb(es, "yrs", [64, 6, TW])
            obuf = sb(es, "obuf", [64, 6, TW])

            def tile_pro(tt):
                tsl = slice(tt * TW, (tt + 1) * TW)
                ws = tt % 2
                DMA(wd[:, ws, :], rwkvT[1152:1216, tsl], [], [("wd", ws)])
                DMA(ad[:, ws, :], rwkvT[1216:1280, tsl], [], [("ad", ws)])
                DMA(gd[:, ws, :], rwkvT[1280:1408, tsl], [], [("gd", ws)])
                yield
                ACT(wd[:, ws, :], wd[:, ws, :], AF.Tanh, [("wd", ws)], [("wd", ws)])
                ACT(gd[:, ws, :], gd[:, ws, :], AF.Sigmoid, [("gd", ws)], [("gd", ws)])
                yield

            def stage0(tt, h):
                tsl = slice(tt * TW, (tt + 1) * TW)
                ws = tt % 2
                hc = slice(h * 64, (h + 1) * 64)
                K_ = lambda nm: (nm, h)
                pb = ps[h % 2]
                pk = ("ps", h % 2)
                DMA(rr[:, h, :], rwkvT[h * 64:(h + 1) * 64, tsl], [], [K_("rr")])
                DMA(kq[:, h, :], rwkvT[384 + h * 64:384 + (h + 1) * 64, tsl], [], [K_("kq")])
                DMA(vvh[:, ws, h, :], rwkvT[768 + h * 64:768 + (h + 1) * 64, tsl], [], [("vv", ws, h)])
                yield
                MM(pb[0:64, 0:TW], w2s[:, hc], wd[:, ws, :], True, True, ["w2s", ("wd", ws)], [pk])
                ACT(sig[:, h, :], pb[0:64, 0:TW], AF.Sigmoid, [pk, "pw0"], [K_("sig")], bias=pw0[:, h:h + 1])
                yield
                sc.op("dve", lambda e, h=h: e.tensor_tensor_scan(out=cum[:, h, :], data0=resetm[0:64, 0:TW], data1=sig[:, h, :], initial=0.0, op0=ALU.mult, op1=ALU.add), [K_("sig"), "cst"], [K_("cum")])
                yield
                TT("pool", cpv[:, h, :], cum[:, h, :], sig[:, h, :], ALU.subtract, [K_("cum"), K_("sig")], [K_("cpv")])
                ACT(epos[:, h, :], cum[:, h, :], AF.Exp, [K_("cum")], [K_("epos")], scale=-C0)
                ACT(eneg[:, h, :], cum[:, h, :], AF.Exp, [K_("cum")], [K_("eneg")], scale=C0)
                cum3 = cum[:, h, :].rearrange("p (c t) -> p c t", c=2)
                epos3 = epos[:, h, :].rearrange("p (c t) -> p c t", c=2)
                TS("dve", nbc[:, h, :], cum3[:, :, 63], -C0, None, ALU.mult, None, [K_("cum")], [K_("nbc")])
                yield
                ACT(eprv[:, h, :], cpv[:, h, :], AF.Exp, [K_("cpv")], [K_("eprv")], scale=-C0)
                CP("pool", pC[:, ws, h, :], epos3[:, :, 63], [K_("epos")], [("pC", ws, h)])
                for c in range(2):
                    ACT(eend[:, h, c * 64:(c + 1) * 64], cum[:, h, c * 64:(c + 1) * 64], AF.Exp, [K_("cum"), K_("nbc")], [K_("eend")], scale=C0, bias=nbc[:, h, c:c + 1])
                yield
                MM(pb[0:64, 128:128 + TW], a2s[:, hc], ad[:, ws, :], True, True, ["a2s", ("ad", ws)], [pk])
                ACT(aa[:, h, :], pb[0:64, 128:128 + TW], AF.Sigmoid, [pk, "pa0"], [K_("aa")], bias=pa0[:, h:h + 1])
                yield
                MM(pb[0:64, 256:256 + TW], g2s[:, hc], gd[:, ws, :], True, True, ["g2s", ("gd", ws)], [pk])
                CP("act", gal[:, tt % 3, h, :], pb[0:64, 256:256 + TW], [pk], [("gal", tt % 3, h)])
                yield
                TS("dve", kk[:, h, :], kq[:, h, :], pkk[:, h:h + 1], None, ALU.mult, None, [K_("kq"), "pkk"], [K_("kk")])
                yield
                ACT(kk2[:, h, :], kk[:, h, :], AF.Square, [K_("kk")], [K_("kk2")])
                yield
                MM(pb[0:64, 384:384 + TW], o64, kk2[:, h, :], True, True, ["cst", K_("kk2")], [pk])
                ACT(rn[:, h, :], pb[0:64, 384:384 + TW], AF.Sqrt, [pk], [K_("rn")])
                yield
                TS("dve", rn[:, h, :], rn[:, h, :], 1e-12, None, ALU.max, None, [K_("rn")], [K_("rn")])
                yield
                RECIP(rn[:, h, :], rn[:, h, :], [K_("rn")], [K_("rn")])
                yield
                TT("dve", kk[:, h, :], kk[:, h, :], rn[:, h, :], ALU.mult, [K_("kk"), K_("rn")], [K_("kk")])
                yield
                TT("pool", kka[:, h, :], kk[:, h, :], aa[:, h, :], ALU.mult, [K_("kk"), K_("aa")], [K_("kka")])
                TS("dve", rn[:, h, :], aa[:, h, :], pka[:, h:h + 1], pok[:, h:h + 1], ALU.mult, ALU.add, [K_("aa"), "pka", "pok", K_("rn")], [K_("rn")])
                STT(RR(ARh[:, ws, h, 0, :]), kk[:, h, :], -1.0, eprv[:, h, :], ALU.mult, ALU.mult, [K_("kk"), K_("eprv")], [("AR0", ws, h)])
                yield
                TT("dve", kp[:, h, :], kq[:, h, :], rn[:, h, :], ALU.mult, [K_("kq"), K_("rn")], [K_("kp")])
                TT("pool", RR(ARh[:, ws, h, 1, :]), rr[:, h, :], epos[:, h, :], ALU.mult, [K_("rr"), K_("epos")], [("AR1", ws, h)])
                yield
                STT(rkr[:, h, :], rr[:, h, :], prk[:, h:h + 1], kp[:, h, :], ALU.mult, ALU.mult, [K_("rr"), "prk", K_("kp")], [K_("rkr")])
                TT("pool", RR(BKh[:, ws, h, 0, :]), kka[:, h, :], eneg[:, h, :], ALU.mult, [K_("kka"), K_("eneg")], [("BK0", ws, h)])
                TT("pool", RR(BKh[:, ws, h, 1, :]), kp[:, h, :], eneg[:, h, :], ALU.mult, [K_("kp"), K_("eneg")], [("BK1", ws, h)])
                yield
                MM(pb[0:64, 0:TW], o64, rkr[:, h, :], True, True, ["cst", K_("rkr")], [pk])
                TT("dve", bon[:, tt % 3, h, :], pb[0:64, 0:TW], vvh[:, ws, h, :], ALU.mult, [pk, ("vv", ws, h)], [("bon", tt % 3, h)])
                TT("pool", BPh[:, ws, h, 0, :], kka[:, h, :], eend[:, h, :], ALU.mult, [K_("kka"), K_("eend")], [("BP", ws, h)])
                TT("pool", BPh[:, ws, h, 1, :], kp[:, h, :], eend[:, h, :], ALU.mult, [K_("kp"), K_("eend")], [("BP", ws, h)])
                yield

            def make_gens(tt):
                if tt >= NTB:
                    return []
                return [tile_pro(tt)] + [stage0(tt, h) for h in range(H)]

            def advance(gl, n):
                for _ in range(n):
                    for g in list(gl):
                        try:
                            next(g)
                        except StopIteration:
                            gl.remove(g)

            def chain(tt, nxt):
                ws = tt % 2
                tsl = slice(tt * TW, (tt + 1) * TW)
                HS = range(H)
                bk = lambda h: ps[2 + h]
                bkk = lambda h: ("ps", 2 + h)
                A0 = lambda h: ("AR0", ws, h)
                A1 = lambda h: ("AR1", ws, h)
                for h in HS:
                    for i, (src, kx) in enumerate([(ARh[:, ws, h, 0, :], A0(h)), (BPh[:, ws, h, 0, :], ("BP", ws, h)), (BPh[:, ws, h, 1, :], ("BP", ws, h)), (vvh[:, ws, h, :], ("vv", ws, h))]):
                        TR(bk(h)[:, i * 64:(i + 1) * 64], src, i64, [kx, "cst"], [bkk(h)])
                for h in HS:
                    CP("act", tm[:, h, :], bk(h)[:, 0:256], [bkk(h)], [("tm", h)])
                advance(nxt, 2)
                for h in HS:
                    MM(bk(h)[:, 0:256], BKh[:, ws, h, 0, :], ARh[:, ws, h, :, :], True, True, [("BK0", ws, h), A0(h), A1(h)], [bkk(h)], r=True)
                    MM(bk(h)[:, 256:384], ARh[:, ws, h, 0, :], BKh[:, ws, h, 0, :], True, True, [("BK0", ws, h), A0(h)], [bkk(h)], r=True)
                for h in HS:
                    TT("dve", RR(Eb[:, h, 0:384]), bk(h)[:, 0:384], mE, ALU.mult, [bkk(h), "cst"], [("E", h, "a")])
                advance(nxt, 2)
                for h in HS:
                    MM(bk(h)[:, 0:256], BKh[:, ws, h, 1, :], ARh[:, ws, h, :, :], True, True, [("BK1", ws, h), A0(h), A1(h)], [bkk(h)], r=True)
                for h in HS:
                    TT("dve", YS[:, h, :], bk(h)[:, 0:256], mY, ALU.mult, [bkk(h), "cst"], [("YS", h)])
                advance(nxt, 2)
                for h in HS:
                    MM(bk(h)[:, 256:320], YS[:, h, 0:128], tm[:, h, 192:256], True, True, [("YS", h), ("tm", h)], [bkk(h)])
                    CP("pool", RR(Eb[:, h, 384:448]), tm[:, h, 0:64], [("tm", h)], [("E", h, "b")])
                for h in HS:
                    CP("act", RR(Eb[:, h, 448:512]), bk(h)[:, 256:320], [bkk(h)], [("E", h, "c")])
                advance(nxt, 2)
                for lev in range(6):
                    def views(h):
                        if lev == 0:
                            return (Eb[:, h, 0:128], Eb[:, h, 256:384], Eb[:, h, 256:512], Eb[:, h, 384:512],
                                    [("E", h, "a"), ("E", h, "b"), ("E", h, "c")])
                        Lp = Lb[:, h, (lev - 1) % 2, :]
                        return (Lp[:, 0:128], Lp[:, 128:256], Lp[:, 128:384], Lp[:, 256:384],
                                [("L", h, (lev - 1) % 2, "p"), ("L", h, (lev - 1) % 2, "z")])
                    for h in HS:
                        PT_, P_, PZ_, Z_, rk = views(h)
                        MM(bk(h)[:, 128:384], PT_, PZ_, True, True, rk, [bkk(h)], r=True)
                        if lev < 5:
                            MM(bk(h)[:, 0:128], P_, PT_, True, True, rk, [bkk(h)], r=True)
                    for h in HS:
                        PT_, P_, PZ_, Z_, rk = views(h)
                        Ln = Lb[:, h, lev % 2, :]
                        TT("dve", RR(Ln[:, 256:384]), bk(h)[:, 256:384], Z_, ALU.add, [bkk(h)] + rk, [("L", h, lev % 2, "z")])
                        if lev < 5:
                            CP("act", RR(Ln[:, 0:256]), bk(h)[:, 0:256], [bkk(h)], [("L", h, lev % 2, "p")])
                    advance(nxt, 3)
                for h in HS:
                    for hf in range(2):
                        TS("pool", B2[:, h, hf * 64:(hf + 1) * 64], tm[:, h, 64:128], rowm[:, hf:hf + 1], None, ALU.mult, None, [("tm", h), "cst"], [("B2", h)])
                        TS("pool", K2[:, h, hf * 64:(hf + 1) * 64], tm[:, h, 128:192], rowm[:, hf:hf + 1], None, ALU.mult, None, [("tm", h), "cst"], [("K2", h)])
                for h in HS:
                    Lf = Lb[:, h, 1, :]
                    kLf = ("L", h, 1, "z")
                    W_, U0_ = Lf[:, 256:320], Lf[:, 320:384]
                    b_ = bk(h)
                    MM(b_[0:64, 0:128], W_, B2[:, h, :], True, True, [kLf, ("B2", h)], [bkk(h)])
                    for hf in range(2):
                        MM(b_[0:64, 128 + hf * 64:128 + (hf + 1) * 64], K2[:, h, hf * 64:(hf + 1) * 64], tm[:, h, 192:256], True, False, [("K2", h), ("tm", h)], [bkk(h)])
                        MM(b_[0:64, 128 + hf * 64:128 + (hf + 1) * 64], B2[:, h, hf * 64:(hf + 1) * 64], U0_, False, True, [("B2", h), kLf], [bkk(h)])
                    MM(b_[0:64, 256:384], W_, Eb[:, h, 128:256], True, True, [kLf, ("E", h, "a")], [bkk(h)])
                    MM(b_[0:64, 384:512], tm[:, h, 192:256], YS[:, h, 128:256], True, False, [("tm", h), ("YS", h)], [bkk(h)])
                    MM(b_[0:64, 384:512], U0_, Eb[:, h, 128:256], False, True, [kLf, ("E", h, "a")], [bkk(h)])
                advance(nxt, 2)
                for h in HS:
                    b_ = bk(h)
                    for hf in range(2):
                        STT(GT[:, h, hf, :], i64, pC[:, ws, h, hf:hf + 1], b_[0:64, hf * 64:(hf + 1) * 64], ALU.mult, ALU.add,
                            [bkk(h), ("pC", ws, h), "cst"], [("GT", h)])
                    TT("dve", RhatT[:, h, :], b_[0:64, 256:384], ARh[:, ws, h, 1, :], ALU.add, [bkk(h), A1(h)], [("Rhat", h)])
                for h in HS:
                    b_ = bk(h)
                    CP("act", Hm[:, h, :, :], b_[0:64, 128:256].rearrange("p (c i) -> p c i", c=2), [bkk(h)], [("Hm", h)])
                    CP("act", Y0T[:, h, :], b_[0:64, 384:512], [bkk(h)], [("Y0T", h)])
                advance(nxt, 2)
                allh = lambda nm: [(nm, h) for h in range(H)]
                for c in range(2):
                    csl = slice(c * 64, (c + 1) * 64)
                    m0 = mcur[0]
                    mnew = 1 - m0
                    for h in range(H):
                        MM(ps[0][0:64, h * 64:(h + 1) * 64], Mst[:, m0, h, :], RhatT[:, h, csl], True, True, [("Mst", m0), ("Rhat", h)], [("ps", 0)])
                    for h in range(H):
                        MM(ps[1][0:64, h * 64:(h + 1) * 64], GT[:, h, c, :], Mst[:, m0, h, :], True, True, [("Mst", m0), ("GT", h)], [("ps", 1)])
                    TT("dve", Mst[:, mnew, :, :], ps[1][0:64, 0:384].rearrange("p (h i) -> p h i", h=6), Hm[:, :, c, :], ALU.add,
                       [("ps", 1)] + allh("Hm"), [("Mst", mnew)])
                    TT("dve", yT[:, :, csl], ps[0][0:64, 0:384].rearrange("p (h t) -> p h t", h=6), Y0T[:, :, csl], ALU.add,
                       [("ps", 0)] + allh("Y0T"), [("yT", c)])
                    mcur[0] = mnew
                    advance(nxt, 1)
                advance(nxt, 1000)

            def post(tt, h):
                tsl = slice(tt * TW, (tt + 1) * TW)
                w3 = tt % 3
                ally = [("yT", c) for c in range(2)]
                osl = h
                kyc, kysq, kyrs = ("yc", osl), ("ysq", osl), ("yrs", osl)
                pb = ps[h % 2]
                pk = ("ps", h % 2)
                MM(pb[0:64, 0:TW], o64, yT[:, h, :], True, True, ["cst"] + ally, [pk])
                STT(yc[:, osl, :], pb[0:64, 0:TW], -1.0 / 64, yT[:, h, :], ALU.mult, ALU.add, [pk] + ally, [kyc])
                yield
                ACT(ysq[:, osl, :], yc[:, osl, :], AF.Square, [kyc], [kysq])
                yield
                MM(pb[0:64, 128:128 + TW], o64, ysq[:, osl, :], True, True, ["cst", kysq], [pk])
                ACT(yrs[:, osl, :], pb[0:64, 128:128 + TW], AF.Sqrt, [pk, "epsc"], [kyrs], scale=1.0 / 64, bias=epsc[0:64, 1:2])
                yield
                RECIP(yrs[:, osl, :], yrs[:, osl, :], [kyrs], [kyrs])
                yield
                TT("dve", yc[:, osl, :], yc[:, osl, :], yrs[:, osl, :], ALU.mult, [kyc, kyrs], [kyc])
                yield
                TS("dve", yc[:, osl, :], yc[:, osl, :], plg[:, h:h + 1], plb[:, h:h + 1], ALU.mult, ALU.add, [kyc, "plg", "plb"], [kyc])
                yield
                TT("pool", yc[:, osl, :], yc[:, osl, :], bon[:, w3, h, :], ALU.add, [kyc, ("bon", w3, h)], [kyc])
                yield
                TT("pool", obuf[:, osl, :], yc[:, osl, :], gal[:, w3, h, :], ALU.mult, [kyc, ("gal", w3, h)], [("obuf", osl)])
                DMA(catT[h * 64:(h + 1) * 64, tsl], obuf[:, osl, :], [("obuf", osl)], [("cat", "a", h, tt)])
                yield

            g0 = make_gens(0)
            advance(g0, 1000)
            for tt in range(NTB):
                pg = [post(tt - 1, h) for h in range(H)] if tt > 0 else []
                chain(tt, pg + make_gens(tt + 1))
            advance([post(NTB - 1, h) for h in range(H)], 1000)
            sc.barrier()
        if stop == "B":
            break

        with ExitStack() as es:
            wO = sb(es, "wO", [128, 2, 8, 128])
            catb = sb(es, "catb", [128, 2, 8, 512])
            mixb = sb(es, "mixb", [128, 8, 512])
            sq = sb(es, "sq", [128, 2, 512])
            rstd = sb(es, "rstd", [128, 2, 512])
            gP = colvec(es, "gP", PR["post_mix_g"][l], 8)
            w_out_l = PR["w_out"][l].rearrange("(k p) c -> p k c", p=128)
            cat_v = catT.rearrange("(k p) t -> p k t", p=128)
            wi = 0
            xe = sb(es, "xe", [128, 2, 8, 512])
            for tt in range(4):
                tsl = slice(tt * 512, (tt + 1) * 512)
                cs = tt % 2
                DMA(xe[:, cs, :, :], xD_v[:, :, tsl], [("xD", tt)], [("xe", cs, j) for j in range(8)])
                DMA(RR(catb[:, cs, :, :]), cat_v[:, :, tsl], [], [("catb", cs)], q="pool")
                for j in range(8):
                    sl = wi % 2
                    wi += 1
                    DMA(RR(wO[:, sl, :, :]), w_out_l[:, :, j * 128:(j + 1) * 128], [], [("wO", sl)], q="pool")
                    bi = j % 2
                    for k in range(8):
                        MM(ps[bi][:, :], wO[:, sl, k, :], catb[:, cs, k, :], k == 0, k == 7, [("wO", sl), ("catb", cs)], [("ps", bi)], r=True)
                    CP("act" if j % 2 else "dve", mixb[:, j, :], ps[bi][:, :], [("ps", bi)], [("mixb", j)])
                r = rms_stats(lambda k: mixb[:, k, :], lambda k: ("mixb", k), tt, (sq, rstd), ps[2 + tt % 2], ("ps", 2 + tt % 2))
                for j in range(8):
                    STT(mixb[:, j, :], mixb[:, j, :], gP[:, j:j + 1], r, ALU.mult, ALU.mult, [("mixb", j), ("rstd", tt % 2), "gP"], [("mixb", j)])
                    TT("dve", xe[:, cs, j, :], xe[:, cs, j, :], mixb[:, j, :], ALU.add, [("mixb", j), ("xe", cs, j)], [("xe", cs, j)])
                DMA(xD_v[:, :, tsl], xe[:, cs, :, :], [("xe", cs, j) for j in range(8)], [("xD", tt)])
            sc.barrier()
        if stop == "E":
            break

        with ExitStack() as es:
            TF = 1024
            hb = sb(es, "hb", [128, 8, TF])
            actb = sb(es, "actb", [128, NFF, TF])
            shm = sb(es, "shm", [128, 8 * TF])
            xf = shm[:, :].rearrange("p (k t) -> p k t", k=8)
            wD = shm[:, 0:2 * NFF * 128].rearrange("p (s j c) -> p s j c", s=2, j=NFF)
            wG = sb(es, "wG", [128, 2, 2, 8, 128])
            sq = sb(es, "sq", [128, 2, 512])
            rstd = sb(es, "rstd", [128, 2, 512])
            sil = sb(es, "sil", [128, 2, 512])
            xc = sb(es, "xc", [128, 2, TF])
            gF = colvec(es, "gF", PR["pre_ffn_g"][l], 8)
            gQ = colvec(es, "gQ", PR["post_ffn_g"][l], 8)
            w_fi = PR["w_ffn_in"][l].rearrange("(k p) c -> p k c", p=128)
            w_fo = PR["w_ffn_out"][l].rearrange("(j p) c -> p j c", p=128)
            wi = 0
            wdi = 0
            si = 0
            xci = 0
            SHK = ["sh", ("wD", 0), ("wD", 1)]
            for tt in range(S // TF):
                tsl = slice(tt * TF, (tt + 1) * TF)
                DMA(xf, xD_v[:, :, tsl], [("xD", 2 * tt), ("xD", 2 * tt + 1)], SHK)
                for hf in range(2):
                    hsl = slice(hf * 512, (hf + 1) * 512)
                    r = rms_stats(lambda k: xf[:, k, hsl], lambda k: "sh", si, (sq, rstd), ps[6 + si % 2], ("ps", 6 + si % 2))
                    for k in range(8):
                        STT(RR(hb[:, k, hsl]), xf[:, k, hsl], gF[:, k:k + 1], r, ALU.mult, ALU.mult, SHK + [("rstd", si % 2), "gF"], [("hb", k, hf)])
                    si += 1
                for j in range(NFF):
                    sl = wi % 2
                    wi += 1
                    DMA(RR(wG[:, sl, 0, :, :]), w_fi[:, :, j * 128:(j + 1) * 128], [], [("wG", sl, 0)], q="pool")
                    DMA(RR(wG[:, sl, 1, :, :]), w_fi[:, :, DFF + j * 128:DFF + (j + 1) * 128], [], [("wG", sl, 1)], q="pool")
                    for hf in range(2):
                        hsl = slice(hf * 512, (hf + 1) * 512)
                        bg, bu = hf * 2, hf * 2 + 1
                        for k in range(8):
                            MM(ps[bg][:, :], wG[:, sl, 0, k, :], hb[:, k, hsl], k == 0, k == 7, [("wG", sl, 0), ("hb", k, hf)], [("ps", bg)], r=True)
                        for k in range(8):
                            MM(ps[bu][:, :], wG[:, sl, 1, k, :], hb[:, k, hsl], k == 0, k == 7, [("wG", sl, 1), ("hb", k, hf)], [("ps", bu)], r=True)
                        ACT(sil[:, hf, :], ps[bg][:, :], AF.Silu, [("ps", bg)], [("sil", hf)])
                        TT("dve", RR(actb[:, j, hsl]), ps[bu][:, :], sil[:, hf, :], ALU.mult, [("ps", bu), ("sil", hf)], [("actb", j, hf)])
                for jo in range(8):
                    sl = wdi % 2
                    wdi += 1
                    DMA(RR(wD[:, sl, :, :]), w_fo[:, :, jo * 128:(jo + 1) * 128], [], [("wD", sl)], q="pool")
                    for hf in range(2):
                        hsl = slice(hf * 512, (hf + 1) * 512)
                        bi = 4 + hf
                        for j in range(NFF):
                            MM(ps[bi][:, :], wD[:, sl, j, :], actb[:, j, hsl], j == 0, j == NFF - 1, [("wD", sl), ("actb", j, hf)], [("ps", bi)], r=True)
                        CP("act" if hf else "dve", RR(hb[:, jo, hsl]), ps[bi][:, :], [("ps", bi)], [("hb", jo, hf)])
                for hf in range(2):
                    hsl = slice(hf * 512, (hf + 1) * 512)
                    r = rms_stats(lambda k: hb[:, k, hsl], lambda k: ("hb", k, hf), si, (sq, rstd), ps[6 + si % 2], ("ps", 6 + si % 2))
                    for j in range(8):
                        STT(RR(hb[:, j, hsl]), hb[:, j, hsl], gQ[:, j:j + 1], r, ALU.mult, ALU.mult, [("hb", j, hf), ("rstd", si % 2), "gQ"], [("hb", j, hf)])
                    si += 1
                for j in range(8):
                    cs = xci % 2
                    xci += 1
                    DMA(xc[:, cs, :], xD[j * 128:(j + 1) * 128, tsl], [("xD", 2 * tt), ("xD", 2 * tt + 1)], [("xc", cs)])
                    TT("pool" if j % 2 else "dve", xc[:, cs, :], xc[:, cs, :], hb[:, j, :], ALU.add, [("xc", cs), ("hb", j, 0), ("hb", j, 1)], [("xc", cs)])
                    DMA(xD[j * 128:(j + 1) * 128, tsl], xc[:, cs, :], [("xc", cs)], [("xDw", tt, j)])
            sc.barrier()
        if OPTS.get("xdbg") == l:
            DMA(xdbg[:, :], xD[:, :], [("xD", tt) for tt in range(4)], [("xdbg", 0)])

    if stop in (None, 'setup'):
        with ExitStack() as es:
            yo = sb(es, "yo", [128, 2, D])
            xo = sb(es, "xo", [128, 2, 8, 512])
            for t in range(16):
                sl = t % 2
                xsl = (t // 4) % 2
                if t % 4 == 0:
                    DMA(xo[:, xsl, :, :], xD_v[:, :, (t // 4) * 512:(t // 4 + 1) * 512], [("xD", t // 4)], [("xo", xsl)])
                for g in range(2):
                    bank = ps[(t * 2 + g) % 4]
                    bk = ("ps", (t * 2 + g) % 4)
                    for kk in range(4):
                        k = g * 4 + kk
                        TR(bank[:, kk * 128:(kk + 1) * 128], xo[:, xsl, k, (t % 4) * 128:(t % 4 + 1) * 128], ident, [("xo", xsl), "cst"], [bk])
                    CP("act" if g else "dve", yo[:, sl, g * 512:(g + 1) * 512], bank[:, :], [bk], [("yo", sl, g)])
                DMA(y_out[t * 128:(t + 1) * 128, :], yo[:, sl, :], [("yo", sl, 0), ("yo", sl, 1)], [("y", t)])
    sc.barrier()
    sc.emit()
    glob.close()
    return nc


_NC_CACHE = {}


def kernel(**inputs):
    if "nc" not in _NC_CACHE:
        _NC_CACHE["nc"] = build()
    nc = _NC_CACHE["nc"]
    x = np.ascontiguousarray(np.asarray(inputs["x"], dtype=np.float32))
    base = {n: np.ascontiguousarray(np.asarray(inputs[n], dtype=np.float32)) for n in PARAM_NAMES}
    base["cst"] = CONSTS["cst"]
    base["mobac"] = CONSTS["mobac"]
    base["blkind"] = CONSTS["blkind"]
    in_maps = []
    for b in range(8):
        m = dict(base)
        m["x"] = x[b]
        in_maps.append(m)
    res = run_bass_kernel_spmd(nc, in_maps, core_ids=list(range(8)))
    return np.stack([np.asarray(r["y"], dtype=np.float32) for r in res.results], 0)
```

```python
import numpy as np
from contextlib import ExitStack
import concourse.bass as bass
import concourse.mybir as mybir
from concourse.bass_utils import run_bass_kernel_spmd

F32 = mybir.dt.float32
F32R = mybir.dt.float32r
AF = mybir.ActivationFunctionType
ALU = mybir.AluOpType
AX = mybir.AxisListType

S = 2048
D = 1024
L = 2
DFF = 2816
NFF = 22
H = 6
C0 = float(np.exp(-0.5))
BIG = 30000.0


OPTS = {}


class Sched:
    def __init__(self, nc, n_dma=40):
        self.nc = nc
        self.names = ["pe", "act", "dve", "pool", "sp"]
        self.sem = {e: nc.alloc_semaphore("s_" + e) for e in ["pe", "act", "dve", "pool"]}
        self.cnt = {e: 0 for e in self.sem}
        self.dsem = [nc.alloc_semaphore("d%d" % i) for i in range(n_dma)]
        self.dcnt = [0] * n_dma
        self.drr = 0
        self.q = {e: [] for e in self.names}
        self.clock = {e: {} for e in self.names}
        self.evclock = {}
        self.evorder = {}
        self.nev = 0
        self.lastw = {}
        self.readers = {}

    def _deps(self, reads, writes):
        deps = {}

        def add(k, v):
            if deps.get(k, 0) < v:
                deps[k] = v

        for r in reads:
            ev = self.lastw.get(r)
            if ev is not None:
                add(*ev)
        for w in writes:
            ev = self.lastw.get(w)
            if ev is not None:
                add(*ev)
            for k, v in self.readers.get(w, {}).items():
                add(k, v)
        return deps

    def _commit(self, ev, reads, writes):
        k, v = ev
        for r in reads:
            d = self.readers.setdefault(r, {})
            if d.get(k, 0) < v:
                d[k] = v
        for w in writes:
            self.lastw[w] = ev
            self.readers[w] = {}

    def _waits(self, eng, deps):
        clk = self.clock.setdefault(eng, {})
        waits = []
        for k, v in sorted(deps.items(), key=lambda kv: -self.evorder.get(kv, 0)):
            if eng == "pe" and k == ("e", "pe"):
                continue
            if clk.get(k, 0) >= v:
                continue
            waits.append((k, v))
            for k2, v2 in self.evclock.get((k, v), {}).items():
                if clk.get(k2, 0) < v2:
                    clk[k2] = v2
            clk[k] = v
        return waits

    def op(self, eng, fn, reads=(), writes=()):
        banks = {("psx", k[1]) for k in list(reads) + list(writes) if isinstance(k, tuple) and k and k[0] == "ps"}
        if banks:
            writes = list(writes) + list(banks)
        deps = self._deps(reads, writes)
        waits = self._waits(eng, deps)
        self.cnt[eng] += 1
        ev = (("e", eng), self.cnt[eng])
        self.evclock[ev] = dict(self.clock[eng])
        self.nev += 1
        self.evorder[ev] = self.nev
        self.q[eng].append((waits, fn, "e"))
        self._commit(ev, reads, writes)

    def dma(self, qeng, out, in_, reads=(), writes=(), **kw):
        deps = self._deps(reads, writes)
        idx = self.drr
        self.drr = (self.drr + 1) % len(self.dsem)
        if self.dcnt[idx] > 0:
            k = ("d", idx)
            deps[k] = max(deps.get(k, 0), self.dcnt[idx])
        waits = self._waits(qeng, deps)
        self.dcnt[idx] += 16
        ev = (("d", idx), self.dcnt[idx])
        self.evclock[ev] = dict(self.clock[qeng])
        self.nev += 1
        self.evorder[ev] = self.nev
        self.q[qeng].append((waits, lambda e: e.dma_start(out=out, in_=in_, **kw), idx))
        self._commit(ev, reads, writes)

    def barrier(self):
        allev = [(("e", e), c) for e, c in self.cnt.items() if c > 0]
        allev += [(("d", i), c) for i, c in enumerate(self.dcnt) if c > 0]
        for eng in self.names:
            waits = self._waits(eng, dict(allev))
            if waits:
                self.q[eng].append((waits, None, None))
        self.lastw = {}
        self.readers = {}
        self.evclock = {}

    def emit(self):
        nc = self.nc
        engs = {"pe": "tensor", "act": "scalar", "dve": "vector", "pool": "gpsimd", "sp": "sync"}
        with nc.Block() as block:
            for name in self.names:
                def body(eng, name=name):
                    for waits, fn, kind in self.q[name]:
                        emb = None
                        if fn is not None and waits and kind == "e" and not OPTS.get("noemb"):
                            emb = waits[-1]
                            waits = waits[:-1]
                        for k, v in waits:
                            s = self.sem[k[1]] if k[0] == "e" else self.dsem[k[1]]
                            eng.wait_ge(s, v)
                        if fn is None:
                            continue
                        ins = fn(eng)
                        if emb is not None:
                            k, v = emb
                            ins._wait_ge(self.sem[k[1]] if k[0] == "e" else self.dsem[k[1]], v)
                        if kind == "e":
                            ins.then_inc(self.sem[name], 1)
                        else:
                            ins.then_inc(self.dsem[kind], 16)
                getattr(block, engs[name])(body)


def make_consts():
    c = {}
    i128 = np.arange(128)
    blk = (i128[:, None] // 64) == (i128[None, :] // 64)
    ident = np.eye(128, dtype=np.float32)
    ones = np.ones((128, 128), np.float32)
    SL = ((i128[:, None] > i128[None, :]) & blk).astype(np.float32)
    SU = ((i128[:, None] < i128[None, :]) & blk).astype(np.float32)
    IU = ((i128[:, None] <= i128[None, :]) & blk).astype(np.float32)
    IUfull = (i128[:, None] <= i128[None, :]).astype(np.float32)
    idst = np.concatenate([np.eye(64), np.eye(64)], 0).astype(np.float32)
    reset = np.ones((128, 256), np.float32)
    reset[:, ::64] = 0.0
    rowm = np.zeros((128, 2), np.float32)
    rowm[:64, 0] = 1.0
    rowm[64:, 1] = 1.0
    q512 = np.arange(512)
    cm = np.stack([(q512[None, :] >= (j * 128 + i128[:, None])).astype(np.float32) for j in range(4)], 1)
    parts = [ident, ones, SU, IU, SL, SU, IU, IUfull, idst, reset, rowm, cm.reshape(128, 2048)]
    offs = {}
    o = 0
    for nm, p in zip(["ident", "ones", "mE", "_1", "_2", "mY", "_3", "iuf", "idst", "reset", "rowm", "cm"], parts):
        offs[nm] = o
        o += p.shape[1]
    c["cst"] = np.ascontiguousarray(np.concatenate(parts, 1))
    c["offs"] = offs
    mb = np.zeros((128, 3, 16, 6, 8), np.float32)
    for t in range(16):
        b = t // 2
        for n in range(8):
            mb[:, 0, t, :, n] = 0.0 if n < b else -1e30
            mb[:, 1, t, :, n] = 1.0 if n < b else 0.0
            mb[:, 2, t, :, n] = 1.0 if n == b else 0.0
    c["mobac"] = mb.reshape(128, 3 * 16 * 48)
    bi = np.zeros((8, S), np.float32)
    for n in range(8):
        bi[n, n * 256:(n + 1) * 256] = 1.0
    c["blkind"] = bi
    return c


CONSTS = make_consts()
PARAM_NAMES = ["pre_mix_g", "w_in", "rwkv_mu", "rwkv_w0", "rwkv_w2", "rwkv_a0", "rwkv_a2", "rwkv_g2",
               "rwkv_k_k", "rwkv_k_a", "rwkv_r_k", "rwkv_lnx_g", "rwkv_lnx_b", "gmlp_ln_g", "gmlp_ln_b",
               "gmlp_w_s", "gmlp_b_s", "w_out", "post_mix_g", "pre_ffn_g", "w_ffn_in", "w_ffn_out", "post_ffn_g"]
PARAM_SHAPES = {"pre_mix_g": (L, D), "w_in": (L, D, 3072), "rwkv_mu": (L, 1408), "rwkv_w0": (L, 384),
                "rwkv_w2": (L, 64, 384), "rwkv_a0": (L, 384), "rwkv_a2": (L, 64, 384), "rwkv_g2": (L, 128, 384),
                "rwkv_k_k": (L, 384), "rwkv_k_a": (L, 384), "rwkv_r_k": (L, 6, 64), "rwkv_lnx_g": (L, 384),
                "rwkv_lnx_b": (L, 384), "gmlp_ln_g": (L, 256), "gmlp_ln_b": (L, 256), "gmlp_w_s": (L, 4, 128, 128),
                "gmlp_b_s": (L, 4, 128), "w_out": (L, D, D), "post_mix_g": (L, D), "pre_ffn_g": (L, D),
                "w_ffn_in": (L, D, 2 * DFF), "w_ffn_out": (L, DFF, D), "post_ffn_g": (L, D)}


def build(dbg=None, nlayers=L, stop=None):
    dbg = dbg or []
    nc = bass.Bass("TRN2", target_bir_lowering=False)
    sc = Sched(nc)
    OF = CONSTS["offs"]

    def dram(name, shape, kind="Internal"):
        if name in dbg:
            kind = "ExternalOutput"
        return nc.dram_tensor(name, list(shape), F32, kind=kind).ap()

    x_in = dram("x", [S, D], "ExternalInput")
    y_out = dram("y", [S, D], "ExternalOutput")
    cst_d = dram("cst", CONSTS["cst"].shape, "ExternalInput")
    if not OPTS.get("noparams"):
        PR = {n: dram(n, PARAM_SHAPES[n], "ExternalInput") for n in PARAM_NAMES}
        mobac_d = dram("mobac", CONSTS["mobac"].shape, "ExternalInput")
        blkind_d = dram("blkind", CONSTS["blkind"].shape, "ExternalInput")
    if OPTS.get("noscratch"):
        glob_scr = None
    rwkvT = dram("rwkvT", [1408, S]) if not OPTS.get("noscratch") else None
    qkT = dram("qkT", [768, S]) if not OPTS.get("noscratch") else None
    uT = dram("uT", [256, S]) if not OPTS.get("noscratch") else None
    vm_tm = dram("vm_tm", [S, 384]) if not OPTS.get("noscratch") else None
    vg_tm = dram("vg_tm", [S, 256]) if not OPTS.get("noscratch") else None
    catT = dram("catT", [D, S]) if not OPTS.get("noscratch") else None
    xdbg = dram("xdbg", [D, S]) if not OPTS.get("noscratch") else None
    xD = dram("xD", [D, S])
    xD_v = xD.rearrange("(k p) t -> p k t", p=128)

    uid = [0]

    def sb(es, name, shape):
        uid[0] += 1
        return es.enter_context(nc.sbuf_tensor("%s_%d" % (name, uid[0]), list(shape), F32))

    glob = ExitStack()
    cst = sb(glob, "cst_sb", [128, CONSTS["cst"].shape[1]])
    ps = [glob.enter_context(nc.psum_tensor("ps%d" % i, [128, 512], F32)) for i in range(8)]
    ident = cst[:, OF["ident"]:OF["ident"] + 128]
    ones = cst[:, OF["ones"]:OF["ones"] + 128]
    mE = cst[:, OF["mE"]:OF["mE"] + 384]
    mY = cst[:, OF["mY"]:OF["mY"] + 256]
    iuf = cst[:, OF["iuf"]:OF["iuf"] + 128]
    idst = cst[:, OF["idst"]:OF["idst"] + 64]
    resetm = cst[:, OF["reset"]:OF["reset"] + 256]
    cm = cst[:, OF["cm"]:OF["cm"] + 2048]
    epsc = sb(glob, "epsc", [128, 4])

    def ACT(out, in_, func, reads, writes, **kw):
        sc.op("act", lambda e: e.activation(out=out, in_=in_, func=func, **kw), reads, writes)

    def RR(ap):
        return ap if OPTS.get("nor") else ap.bitcast(F32R)

    def MM(out, lhsT, rhs, start, stop, reads, writes, r=False):
        if r and not OPTS.get("nor"):
            lhsT = lhsT.bitcast(F32R)
            rhs = rhs.bitcast(F32R)
        sc.op("pe", lambda e: e.matmul(out, lhsT=lhsT, rhs=rhs, start=start, stop=stop), reads, writes)

    def TR(out, in_, idn, reads, writes):
        sc.op("pe", lambda e: e.transpose(out, in_, idn), reads, writes)

    def TT(eng, out, in0, in1, op, reads, writes):
        sc.op(eng, lambda e: e.tensor_tensor(out=out, in0=in0, in1=in1, op=op), reads, writes)

    def TS(eng, out, in0, s1, s2, op0, op1, reads, writes):
        if s2 is None:
            sc.op(eng, lambda e: e.tensor_scalar(out=out, in0=in0, scalar1=s1, scalar2=None, op0=op0), reads, writes)
        else:
            sc.op(eng, lambda e: e.tensor_scalar(out=out, in0=in0, scalar1=s1, scalar2=s2, op0=op0, op1=op1), reads, writes)

    def STT(out, in0, scalar, in1, op0, op1, reads, writes):
        sc.op("dve", lambda e: e.scalar_tensor_tensor(out=out, in0=in0, scalar=scalar, in1=in1, op0=op0, op1=op1), reads, writes)

    def CP(eng, out, in_, reads, writes):
        if eng == "act":
            sc.op("act", lambda e: e.copy(out=out, in_=in_), reads, writes)
        else:
            sc.op(eng, lambda e: e.tensor_copy(out=out, in_=in_), reads, writes)

    def RECIP(out, in_, reads, writes):
        sc.op("dve", lambda e: e.reciprocal(out=out, in_=in_), reads, writes)

    def DMA(out, in_, reads, writes, q="sp", **kw):
        sc.dma(q, out, in_, reads, writes, **kw)

    def colvec(es, name, src_1d, ncol, p=128):
        t = sb(es, name, [p, ncol])
        DMA(t[:, :], src_1d.rearrange("(c p) -> p c", p=p), [], [name], allow_slow_non_contiguous=True)
        return t

    DMA(cst[:, :], cst_d[:, :], [], ["cst"])
    sc.op("dve", lambda e: e.memset(epsc[:, 0:1], 1e-6), [], ["epsc"])
    sc.op("dve", lambda e: e.memset(epsc[:, 1:2], 64e-5), [], ["epsc"])
    sc.op("dve", lambda e: e.memset(epsc[:, 2:3], 0.0), [], ["epsc"])
    with ExitStack() as es:
        xin = sb(es, "xin", [128, 2, D])
        xs = sb(es, "xs", [128, 2, 8, 512])
        for t in range(16):
            sl = t % 2
            xsl = (t // 4) % 2
            DMA(xin[:, sl, :], x_in[t * 128:(t + 1) * 128, :], [], [("xin", sl)])
            for g in range(2):
                bank = ps[(t * 2 + g) % 4]
                bk = ("ps", (t * 2 + g) % 4)
                for kk in range(4):
                    k = g * 4 + kk
                    TR(bank[:, kk * 128:(kk + 1) * 128], xin[:, sl, k * 128:(k + 1) * 128], ident, [("xin", sl), "cst"], [bk])
                CP("act" if g else "dve", xs[:, xsl, g * 4:(g + 1) * 4, (t % 4) * 128:(t % 4 + 1) * 128],
                   bank[:, :].rearrange("p (k c) -> p k c", k=4), [bk], [("xs", xsl, t % 4, g)])
            if t % 4 == 3:
                DMA(xD_v[:, :, (t // 4) * 512:(t // 4 + 1) * 512], xs[:, xsl, :, :], [("xs", xsl, q, g) for q in range(4) for g in range(2)], [("xD", t // 4)])
        sc.barrier()

    def rms_stats(src_fn, src_keys, tt, es_tiles, pbank, pkey):
        sq, rstd = es_tiles
        for k in range(8):
            ACT(sq[:, k % 2, :], src_fn(k), AF.Square, [src_keys(k)], [("sq", k % 2)])
            MM(pbank[:, :], ones, sq[:, k % 2, :], k == 0, k == 7, [("sq", k % 2), "cst"], [pkey])
        ACT(rstd[:, tt % 2, :], pbank[:, :], AF.Sqrt, [pkey, "epsc"], [("rstd", tt % 2)], scale=1.0 / D, bias=epsc[:, 0:1])
        RECIP(rstd[:, tt % 2, :], rstd[:, tt % 2, :], [("rstd", tt % 2)], [("rstd", tt % 2)])
        return rstd[:, tt % 2, :]

    for l in range(nlayers if stop != 'setup' else 0):
        with ExitStack() as es:
            hbuf = sb(es, "hbuf", [128, 8, S])
            sq = sb(es, "sq", [128, 2, 512])
            rstd = sb(es, "rstd", [128, 2, 512])
            gA = colvec(es, "gA", PR["pre_mix_g"][l], 8)
            muA = colvec(es, "muA", PR["rwkv_mu"][l], 11)
            xa = sb(es, "xa", [128, 2, 8, 512])
            for tt in range(4):
                tsl = slice(tt * 512, (tt + 1) * 512)
                xsl = tt % 2
                DMA(xa[:, xsl, :, :], xD_v[:, :, tsl], [("xD", tt)], [("xa", xsl)])
                r = rms_stats(lambda k: xa[:, xsl, k, :], lambda k: ("xa", xsl), tt, (sq, rstd), ps[4 + tt % 2], ("ps", 4 + tt % 2))
                for k in range(8):
                    STT(RR(hbuf[:, k, tsl]), xa[:, xsl, k, :], gA[:, k:k + 1], r, ALU.mult, ALU.mult,
                        [("xa", xsl), ("rstd", tt % 2), "gA"], [("h", k, tt)])
            es_main = es
            es = ExitStack()
            wA = sb(es, "wA", [128, 2, 8, 128])
            stg = sb(es, "stg", [128, 2, S])
            stg2 = sb(es, "stg2", [128, 2, S])
            w_in_l = PR["w_in"][l].rearrange("(k p) c -> p k c", p=128)
            fm_chunks = [(c * 128, rwkvT, c * 128, True) for c in range(11)]
            fm_chunks += [(1408 + c * 128, qkT, c * 128, False) for c in range(6)]
            fm_chunks += [(2560 + c * 128, uT, c * 128, False) for c in range(2)]
            def load_wA(ci):
                col0 = fm_chunks[ci][0]
                DMA(RR(wA[:, ci % 2, :, :]), w_in_l[:, :, col0:col0 + 128], [], [("wA", ci % 2)], q="pool")

            load_wA(0)
            for ci, (col0, dst, row0, shift) in enumerate(fm_chunks):
                sl = ci % 2
                if ci + 1 < len(fm_chunks):
                    load_wA(ci + 1)
                for tt in range(4):
                    tsl = slice(tt * 512, (tt + 1) * 512)
                    bi = (ci * 4 + tt) % 4
                    for k in range(8):
                        MM(ps[bi][:, :], wA[:, sl, k, :], hbuf[:, k, tsl], k == 0, k == 7,
                           [("wA", sl), ("h", k, tt)], [("ps", bi)], r=True)
                    CP("act" if tt % 2 else "dve", stg[:, sl, tsl], ps[bi][:, :], [("ps", bi)], [("stg", sl, tt)])
                allst = [("stg", sl, tt) for tt in range(4)]
                if shift:
                    TT("pool", stg2[:, sl, 1:S], stg[:, sl, 0:S - 1], stg[:, sl, 1:S], ALU.subtract, allst, [("stg2", sl)])
                    TS("pool", stg2[:, sl, 0:1], stg[:, sl, 0:1], -1.0, None, ALU.mult, None, allst, [("stg2", sl)])
                    STT(stg2[:, sl, :], stg2[:, sl, :], muA[:, ci:ci + 1], stg[:, sl, :], ALU.mult, ALU.add,
                        allst + [("stg2", sl), "muA"], [("stg2", sl)])
                    DMA(dst[row0:row0 + 128, :], stg2[:, sl, :], [("stg2", sl)], [("dr", id(dst), row0)], q="pool")
                else:
                    DMA(dst[row0:row0 + 128, :], stg[:, sl, :], allst, [("dr", id(dst), row0)], q="pool")
            sc.barrier()
            es.close()
            es = es_main
            wB = sb(es, "wB", [128, 8, 640])
            DMA(RR(wB[:, :, 0:384]), w_in_l[:, :, 2176:2560], [], ["wB"], q="pool")
            DMA(RR(wB[:, :, 384:640]), w_in_l[:, :, 2816:3072], [], ["wB"], q="pool")
            vst = sb(es, "vst", [128, 2, 640])
            for t in range(16):
                sl = t % 2
                b0, b1 = 4 + (t % 2) * 2, 5 + (t % 2) * 2
                for k in range(8):
                    MM(ps[b0][:, 0:384], hbuf[:, k, t * 128:(t + 1) * 128], wB[:, k, 0:384], k == 0, k == 7,
                       [("h", k, t // 4), "wB"], [("ps", b0)], r=True)
                for k in range(8):
                    MM(ps[b1][:, 0:256], hbuf[:, k, t * 128:(t + 1) * 128], wB[:, k, 384:640], k == 0, k == 7,
                       [("h", k, t // 4), "wB"], [("ps", b1)], r=True)
                CP("act", vst[:, sl, 0:384], ps[b0][:, 0:384], [("ps", b0)], [("vst", sl, 0)])
                CP("dve", vst[:, sl, 384:640], ps[b1][:, 0:256], [("ps", b1)], [("vst", sl, 1)])
                DMA(vm_tm[t * 128:(t + 1) * 128, :], vst[:, sl, 0:384], [("vst", sl, 0)], [("vm", t)], q="pool")
                DMA(vg_tm[t * 128:(t + 1) * 128, :], vst[:, sl, 384:640], [("vst", sl, 1)], [("vg", t)], q="pool")
            sc.barrier()
        if stop == "A":
            break

        with ExitStack() as es:
            lng = sb(es, "lng", [128, 256])
            lnb = sb(es, "lnb", [128, 256])
            DMA(lng[:, :], PR["gmlp_ln_g"][l].partition_broadcast(128), [], ["lng"])
            DMA(lnb[:, :], PR["gmlp_ln_b"][l].partition_broadcast(128), [], ["lnb"])
            wsn = sb(es, "wsn", [128, 4, 128])
            wsT = sb(es, "wsT", [128, 4, 128])
            bsr = sb(es, "bsr", [1, 512])
            DMA(wsn[:, :, :], PR["gmlp_w_s"][l].rearrange("g t s -> t g s"), [], ["wsn"])
            DMA(bsr[:, :], PR["gmlp_b_s"][l].rearrange("g t -> (g t)").partition_broadcast(1), [], ["bsr"])
            for g in range(4):
                TR(ps[0][:, g * 128:(g + 1) * 128], wsn[:, g, :], ident, ["wsn", "cst"], [("ps", 0)])
            for g in range(4):
                TT("dve", wsT[:, g, :], ps[0][:, g * 128:(g + 1) * 128], iuf, ALU.mult, [("ps", 0), "cst"], ["wsT"])
            gu = sb(es, "gu", [128, 2, S])
            t1 = sb(es, "t1", [128, S])
            cout = sb(es, "cout", [128, 2, S])
            for pp in range(2):
                DMA(gu[:, pp, :], uT[pp * 128:(pp + 1) * 128, :], [], [("gu", pp)])
                ACT(t1[:, :], gu[:, pp, :], AF.Square, [("gu", pp)], ["t1"])
                TS("pool", t1[:, :], t1[:, :], 0.044715, 1.0, ALU.mult, ALU.add, ["t1"], ["t1"])
                TT("dve", t1[:, :], t1[:, :], gu[:, pp, :], ALU.mult, ["t1", ("gu", pp)], ["t1"])
                ACT(t1[:, :], t1[:, :], AF.Sigmoid, ["t1"], ["t1"], scale=2.0 * 0.7978845608028654)
                TT("dve", gu[:, pp, :], gu[:, pp, :], t1[:, :], ALU.mult, ["t1", ("gu", pp)], [("gu", pp)])
            vb = sb(es, "vb", [128, 2, 256])
            t2 = sb(es, "t2", [128, 2, 256])
            st6 = sb(es, "st6", [128, 2, 8])
            for c in range(16):
                sl = c % 2
                DMA(vb[:, sl, :], vg_tm[c * 128:(c + 1) * 128, :], [], [("vb", sl)])
                kv, kt = ("vb", sl), ("t2", sl)
                ACT(t2[:, sl, :], vb[:, sl, :], AF.Square, [kv], [kt])
                TS("pool", t2[:, sl, :], t2[:, sl, :], 0.044715, 1.0, ALU.mult, ALU.add, [kt], [kt])
                TT("dve", t2[:, sl, :], t2[:, sl, :], vb[:, sl, :], ALU.mult, [kt, kv], [kt])
                ACT(t2[:, sl, :], t2[:, sl, :], AF.Sigmoid, [kt], [kt], scale=2.0 * 0.7978845608028654)
                TT("dve", vb[:, sl, :], vb[:, sl, :], t2[:, sl, :], ALU.mult, [kt, kv], [kv])
                ks = ("st6", sl)
                sc.op("dve", lambda e, sl=sl: e.bn_stats(out=st6[:, sl, 0:6], in_=vb[:, sl, :]), [kv], [ks])
                sc.op("dve", lambda e, sl=sl: e.bn_aggr(out=st6[:, sl, 6:8], in_=st6[:, sl, 0:6]), [ks], [ks])
                ACT(st6[:, sl, 7:8], st6[:, sl, 7:8], AF.Sqrt, [ks, "epsc"], [ks], bias=epsc[:, 0:1], scale=1.0)
                RECIP(st6[:, sl, 7:8], st6[:, sl, 7:8], [ks], [ks])
                TS("dve", vb[:, sl, :], vb[:, sl, :], st6[:, sl, 6:7], st6[:, sl, 7:8], ALU.subtract, ALU.mult, [kv, ks], [kv])
                TT("dve", vb[:, sl, :], vb[:, sl, :], lng[:, :], ALU.mult, [kv, "lng"], [kv])
                TT("dve", vb[:, sl, :], vb[:, sl, :], lnb[:, :], ALU.add, [kv, "lnb"], [kv])
                for pp in range(2):
                    bi = 1 + (c % 2) * 2 + pp
                    for gg in range(2):
                        g = pp * 2 + gg
                        MM(ps[bi][:, gg * 128:(gg + 1) * 128], vb[:, sl, pp * 128:(pp + 1) * 128], wsT[:, g, :], True, False, [kv, "wsT"], [("ps", bi)])
                        MM(ps[bi][:, gg * 128:(gg + 1) * 128], ones[0:1, :], bsr[0:1, g * 128:(g + 1) * 128], False, True, ["cst", "bsr"], [("ps", bi)])
                    for gg in range(2):
                        TT("dve", cout[gg * 64:(gg + 1) * 64, pp, c * 128:(c + 1) * 128], ps[bi][gg * 64:(gg + 1) * 64, gg * 128:(gg + 1) * 128],
                           gu[gg * 64:(gg + 1) * 64, pp, c * 128:(c + 1) * 128], ALU.mult, [("ps", bi), ("gu", pp)], [("cout", pp)])
            for pp in range(2):
                DMA(catT[768 + pp * 128:768 + (pp + 1) * 128, :], cout[:, pp, :], [("cout", pp)], [("cat", 6 + pp)], q="pool")
            sc.barrier()
        if stop == "D":
            break

        with ExitStack() as es:
            qa = sb(es, "qa", [72, 2, S])
            ka = sb(es, "ka", [72, 2, S])
            vt = sb(es, "vt", [128, 2, 16, 128])
            ones_r = sb(es, "ones_r", [128, 128])
            CP("dve", RR(ones_r[:, :]), ones, ["cst"], ["ones_r"])
            kbar = sb(es, "kbar", [64, 2, 8])
            mobc = sb(es, "mobc", [128, 3, 16, 48])
            NP = sb(es, "NP", [128, 2, 16, 72])
            sm = sb(es, "sm", [128, 16, 8])
            top8 = sb(es, "top8", [128, 16, 8])
            al = sb(es, "al", [128, 16, 8])
            pt = sb(es, "pt", [128, 6, 512])
            pacc = sb(es, "pacc", [128, 512])
            rden = sb(es, "rden", [128, 512])
            ob = sb(es, "ob", [128, 2, 512])
            DMA(mobc[:, :, :, :], mobac_d.rearrange("p (a t c) -> p a t c", a=3, t=16), [], ["mobc"])
            for q in range(2):
                DMA(RR(ka[64:72, q, :]), blkind_d[:, :], [], [("kaB", q)], q="pool")
            sc.op("pool", lambda e: e.memset(NP[:, :, :, :], 0.0), [], [("NP", 0), ("NP", 1)])
            pti = [0]

            def prepA(h):
                q = h % 2
                DMA(RR(qa[0:64, q, :]), qkT[h * 64:(h + 1) * 64, :], [], [("qaQ", q)], q="pool")
                DMA(RR(ka[0:64, q, :]), qkT[384 + h * 64:384 + (h + 1) * 64, :], [], [("kaK", q)], q="pool")
                if h % 2 == 0:
                    vq = (h // 2) % 2
                    DMA(RR(vt[:, vq, :, :]), vm_tm.rearrange("(t p) c -> p t c", p=128)[:, :, h * 64:(h + 2) * 64], [], [("vt", vq)], q="pool")
                sc.op("dve", lambda e: e.tensor_reduce(out=kbar[:, q, :], in_=ka[0:64, q, :].rearrange("p (n k) -> p n k", n=8), axis=AX.X, op=ALU.add), [("kaK", q)], [("kbar", q)])
                for t in range(16):
                    MM(ps[0][:, t * 8:(t + 1) * 8], qa[0:64, q, t * 128:(t + 1) * 128], kbar[:, q, :], True, True, [("qaQ", q), ("kbar", q)], [("ps", 0)])

            def prepB(h):
                q = h % 2
                hs = slice(h * 8, (h + 1) * 8)
                TT("dve", sm[:, :, :], ps[0][:, 0:128].rearrange("p (t n) -> p t n", t=16), mobc[:, 0, :, hs], ALU.add, [("ps", 0), "mobc"], ["sm"])
                for t in range(16):
                    sc.op("dve", lambda e, t=t: e.max(out=top8[:, t, :], in_=sm[:, t, :]), ["sm"], [("top8", t)])
                for t in range(16):
                    TS("dve", al[:, t, :], sm[:, t, :], top8[:, t, 2:3], None, ALU.is_ge, None, ["sm", ("top8", t)], [("al", t)])
                allal = [("al", t) for t in range(16)]
                TT("dve", al[:, :, :], al[:, :, :], mobc[:, 1, :, hs], ALU.mult, allal + ["mobc"], ["al2"])
                TT("dve", al[:, :, :], al[:, :, :], mobc[:, 2, :, hs], ALU.add, ["al2", "mobc"], ["al2"])
                TS("dve", NP[:, q, :, 64:72], al[:, :, :], -1.0, BIG, ALU.add, ALU.mult, ["al2"], [("NP", q)])

            def prepC(h):
                q = h % 2
                for t4 in range(4):
                    for tq in range(4):
                        t = t4 * 4 + tq
                        MM(ps[1][0:72, tq * 128:(tq + 1) * 128], NP[:, q, t, :], ident, True, True, [("NP", q), "cst"], [("ps", 1)])
                    CP("act", RR(qa[64:72, q, t4 * 512:(t4 + 1) * 512]), ps[1][64:72, :], [("ps", 1)], [("qaM", q)])

            def attn(h, qt):
                q = h % 2
                vq = (h // 2) % 2
                hp = slice((h % 2) * 64, (h % 2) * 64 + 64)
                qsl = slice(qt * 512, (qt + 1) * 512)
                nk = (qt + 1) * 4
                osl = qt % 2
                pis = {}

                def qk(kt):
                    sb_i = (2, 3, 6, 7)[kt % 4]
                    pi = pti[0] % 6
                    pti[0] += 1
                    pis[kt] = pi
                    MM(ps[sb_i][:, :], ka[0:72, q, kt * 128:(kt + 1) * 128], qa[0:72, q, qsl], True, True,
                       [("kaK", q), ("kaB", q), ("qaQ", q), ("qaM", q)], [("ps", sb_i)], r=True)
                    ACT(RR(pt[:, pi, :]), ps[sb_i][:, :], AF.Exp, [("ps", sb_i)], [("pt", pi)], scale=0.125)
                    if kt >= qt * 4:
                        j = kt - qt * 4
                        TT("dve", RR(pt[:, pi, :]), pt[:, pi, :], cm[:, j * 512:(j + 1) * 512], ALU.mult, [("pt", pi), "cst"], [("pt", pi)])

                def pv(kt):
                    pi = pis[kt]
                    MM(ps[4][:, :], vt[:, vq, kt, :], pt[:, pi, :], kt == 0, kt == nk - 1, [("vt", vq), ("pt", pi)], [("ps", 4)], r=True)
                    if kt == 0:
                        CP("pool", RR(pacc[:, :]), pt[:, pi, :], [("pt", pi)], ["pacc"])
                    else:
                        TT("pool", RR(pacc[:, :]), pacc[:, :], pt[:, pi, :], ALU.add, [("pt", pi), "pacc"], ["pacc"])

                DEPTH = 3
                for kt in range(min(DEPTH, nk)):
                    qk(kt)
                for kt in range(nk):
                    pv(kt)
                    if kt + DEPTH < nk:
                        qk(kt + DEPTH)
                MM(ps[5][:, :], ones_r[:, :], pacc[:, :], True, True, ["ones_r", "pacc"], [("ps", 5)], r=True)
                RECIP(rden[hp, :], ps[5][hp, :], [("ps", 5)], ["rden"])
                TT("dve", ob[hp, osl, :], ps[4][hp, :], rden[hp, :], ALU.mult, [("ps", 4), "rden"], [("ob", osl)])
                DMA(catT[384 + h * 64:384 + (h + 1) * 64, qsl], ob[hp, osl, :], [("ob", osl)], [("cat", "b", h, qt)])

            prepA(0)
            prepB(0)
            prepC(0)
            for h in range(H):
                for qt in range(4):
                    attn(h, qt)
                    if h + 1 < H:
                        if qt == 0:
                            prepA(h + 1)
                        elif qt == 1:
                            prepB(h + 1)
                        elif qt == 2:
                            prepC(h + 1)
            sc.barrier()
        if stop == "C":
            break

        with ExitStack() as es:
            TW = 128
            NTB = S // TW
            w2s = sb(es, "w2s", [64, 384])
            a2s = sb(es, "a2s", [64, 384])
            g2s = sb(es, "g2s", [128, 384])
            DMA(w2s[:, :], PR["rwkv_w2"][l], [], ["w2s"])
            DMA(a2s[:, :], PR["rwkv_a2"][l], [], ["a2s"])
            DMA(g2s[:, :], PR["rwkv_g2"][l], [], ["g2s"])
            pw0 = colvec(es, "pw0", PR["rwkv_w0"][l], 6, p=64)
            pa0 = colvec(es, "pa0", PR["rwkv_a0"][l], 6, p=64)
            pkk = colvec(es, "pkk", PR["rwkv_k_k"][l], 6, p=64)
            pka = colvec(es, "pka", PR["rwkv_k_a"][l], 6, p=64)
            prk = colvec(es, "prk", PR["rwkv_r_k"][l].rearrange("h d -> (h d)"), 6, p=64)
            plg = colvec(es, "plg", PR["rwkv_lnx_g"][l], 6, p=64)
            plb = colvec(es, "plb", PR["rwkv_lnx_b"][l], 6, p=64)
            pok = sb(es, "pok", [64, 6])
            TS("dve", pok[:, :], pka[:, :], -1.0, 1.0, ALU.mult, ALU.add, ["pka"], ["pok"])
            i64 = ident[0:64, 0:64]
            o64 = ones[0:64, 0:64]
            rowm = cst[:, OF["rowm"]:OF["rowm"] + 2]
            Mst = sb(es, "Mst", [64, 2, 6, 64])
            sc.op("dve", lambda e: e.memset(Mst[:, 0, :, :], 0.0), [], [("Mst", 0)])
            mcur = [0]
            RhatT = sb(es, "RhatT", [64, 6, TW])
            Y0T = sb(es, "Y0T", [64, 6, TW])
            yT = sb(es, "yT", [64, 6, TW])
            GT = sb(es, "GT", [64, 6, 2, 64])
            Hm = sb(es, "Hm", [64, 6, 2, 64])
            bon = sb(es, "bon", [64, 3, 6, TW]); gal = sb(es, "gal", [64, 3, 6, TW])
            ARh = sb(es, "ARh", [64, 2, 6, 2, TW]); BKh = sb(es, "BKh", [64, 2, 6, 2, TW]); BPh = sb(es, "BPh", [64, 2, 6, 2, TW])
            vvh = sb(es, "vvh", [64, 2, 6, TW]); pC = sb(es, "pC", [64, 2, 6, 2])
            wd = sb(es, "wd", [64, 2, TW]); ad = sb(es, "ad", [64, 2, TW]); gd = sb(es, "gd", [128, 2, TW])
            T6 = lambda nm: sb(es, nm, [64, 6, TW])
            rr = T6("rr"); kq = T6("kq"); sig = T6("sig"); cum = T6("cum"); cpv = T6("cpv")
            epos = T6("epos"); eneg = T6("eneg"); eprv = T6("eprv"); eend = T6("eend")
            aa = T6("aa"); kk = T6("kk"); kk2 = T6("kk2"); rn = T6("rn"); kka = T6("kka"); kp = T6("kp"); rkr = T6("rkr")
            nbc = sb(es, "nbc", [64, 6, 2])
            tm = sb(es, "tm", [128, 6, 256]); Eb = sb(es, "Eb", [128, 6, 512]); YS = sb(es, "YS", [128, 6, 256])
            Lb = sb(es, "Lb", [128, 6, 2, 384]); B2 = sb(es, "B2", [128, 6, 128]); K2 = sb(es, "K2", [128, 6, 128])
            yc = sb(es, "yc", [64, 6, TW]); ysq = sb(es, "ysq", [64, 6, TW]); yrs = sb(es, "yrs", [64, 6, TW])
            obuf = sb(es, "obuf", [64, 6, TW])

            def tile_pro(tt):
                tsl = slice(tt * TW, (tt + 1) * TW)
                ws = tt % 2
                DMA(wd[:, ws, :], rwkvT[1152:1216, tsl], [], [("wd", ws)])
                DMA(ad[:, ws, :], rwkvT[1216:1280, tsl], [], [("ad", ws)])
                DMA(gd[:, ws, :], rwkvT[1280:1408, tsl], [], [("gd", ws)])
                yield
                ACT(wd[:, ws, :], wd[:, ws, :], AF.Tanh, [("wd", ws)], [("wd", ws)])
                ACT(gd[:, ws, :], gd[:, ws, :], AF.Sigmoid, [("gd", ws)], [("gd", ws)])
                yield

            def stage0(tt, h):
                tsl = slice(tt * TW, (tt + 1) * TW)
                ws = tt % 2
                hc = slice(h * 64, (h + 1) * 64)
                K_ = lambda nm: (nm, h)
                pb = ps[h % 2]
                pk = ("ps", h % 2)
                DMA(rr[:, h, :], rwkvT[h * 64:(h + 1) * 64, tsl], [], [K_("rr")])
                DMA(kq[:, h, :], rwkvT[384 + h * 64:384 + (h + 1) * 64, tsl], [], [K_("kq")])
                DMA(vvh[:, ws, h, :], rwkvT[768 + h * 64:768 + (h + 1) * 64, tsl], [], [("vv", ws, h)])
                yield
                MM(pb[0:64, 0:TW], w2s[:, hc], wd[:, ws, :], True, True, ["w2s", ("wd", ws)], [pk])
                ACT(sig[:, h, :], pb[0:64, 0:TW], AF.Sigmoid, [pk, "pw0"], [K_("sig")], bias=pw0[:, h:h + 1])
                yield
                sc.op("dve", lambda e, h=h: e.tensor_tensor_scan(out=cum[:, h, :], data0=resetm[0:64, 0:TW], data1=sig[:, h, :], initial=0.0, op0=ALU.mult, op1=ALU.add), [K_("sig"), "cst"], [K_("cum")])
                yield
                TT("pool", cpv[:, h, :], cum[:, h, :], sig[:, h, :], ALU.subtract, [K_("cum"), K_("sig")], [K_("cpv")])
                ACT(epos[:, h, :], cum[:, h, :], AF.Exp, [K_("cum")], [K_("epos")], scale=-C0)
                ACT(eneg[:, h, :], cum[:, h, :], AF.Exp, [K_("cum")], [K_("eneg")], scale=C0)
                cum3 = cum[:, h, :].rearrange("p (c t) -> p c t", c=2)
                epos3 = epos[:, h, :].rearrange("p (c t) -> p c t", c=2)
                TS("dve", nbc[:, h, :], cum3[:, :, 63], -C0, None, ALU.mult, None, [K_("cum")], [K_("nbc")])
                yield
                ACT(eprv[:, h, :], cpv[:, h, :], AF.Exp, [K_("cpv")], [K_("eprv")], scale=-C0)
                CP("pool", pC[:, ws, h, :], epos3[:, :, 63], [K_("epos")], [("pC", ws, h)])
                for c in range(2):
                    ACT(eend[:, h, c * 64:(c + 1) * 64], cum[:, h, c * 64:(c + 1) * 64], AF.Exp, [K_("cum"), K_("nbc")], [K_("eend")], scale=C0, bias=nbc[:, h, c:c + 1])
                yield
                MM(pb[0:64, 128:128 + TW], a2s[:, hc], ad[:, ws, :], True, True, ["a2s", ("ad", ws)], [pk])
                ACT(aa[:, h, :], pb[0:64, 128:128 + TW], AF.Sigmoid, [pk, "pa0"], [K_("aa")], bias=pa0[:, h:h + 1])
                yield
                MM(pb[0:64, 256:256 + TW], g2s[:, hc], gd[:, ws, :], True, True, ["g2s", ("gd", ws)], [pk])
                CP("act", gal[:, tt % 3, h, :], pb[0:64, 256:256 + TW], [pk], [("gal", tt % 3, h)])
                yield
                TS("dve", kk[:, h, :], kq[:, h, :], pkk[:, h:h + 1], None, ALU.mult, None, [K_("kq"), "pkk"], [K_("kk")])
                yield
                ACT(kk2[:, h, :], kk[:, h, :], AF.Square, [K_("kk")], [K_("kk2")])
                yield
                MM(pb[0:64, 384:384 + TW], o64, kk2[:, h, :], True, True, ["cst", K_("kk2")], [pk])
                ACT(rn[:, h, :], pb[0:64, 384:384 + TW], AF.Sqrt, [pk], [K_("rn")])
                yield
                TS("dve", rn[:, h, :], rn[:, h, :], 1e-12, None, ALU.max, None, [K_("rn")], [K_("rn")])
                yield
                RECIP(rn[:, h, :], rn[:, h, :], [K_("rn")], [K_("rn")])
                yield
                TT("dve", kk[:, h, :], kk[:, h, :], rn[:, h, :], ALU.mult, [K_("kk"), K_("rn")], [K_("kk")])
                yield
                TT("pool", kka[:, h, :], kk[:, h, :], aa[:, h, :], ALU.mult, [K_("kk"), K_("aa")], [K_("kka")])
                TS("dve", rn[:, h, :], aa[:, h, :], pka[:, h:h + 1], pok[:, h:h + 1], ALU.mult, ALU.add, [K_("aa"), "pka", "pok", K_("rn")], [K_("rn")])
                STT(RR(ARh[:, ws, h, 0, :]), kk[:, h, :], -1.0, eprv[:, h, :], ALU.mult, ALU.mult, [K_("kk"), K_("eprv")], [("AR0", ws, h)])
                yield
                TT("dve", kp[:, h, :], kq[:, h, :], rn[:, h, :], ALU.mult, [K_("kq"), K_("rn")], [K_("kp")])
                TT("pool", RR(ARh[:, ws, h, 1, :]), rr[:, h, :], epos[:, h, :], ALU.mult, [K_("rr"), K_("epos")], [("AR1", ws, h)])
                yield
                STT(rkr[:, h, :], rr[:, h, :], prk[:, h:h + 1], kp[:, h, :], ALU.mult, ALU.mult, [K_("rr"), "prk", K_("kp")], [K_("rkr")])
                TT("pool", RR(BKh[:, ws, h, 0, :]), kka[:, h, :], eneg[:, h, :], ALU.mult, [K_("kka"), K_("eneg")], [("BK0", ws, h)])
                TT("pool", RR(BKh[:, ws, h, 1, :]), kp[:, h, :], eneg[:, h, :], ALU.mult, [K_("kp"), K_("eneg")], [("BK1", ws, h)])
                yield
                MM(pb[0:64, 0:TW], o64, rkr[:, h, :], True, True, ["cst", K_("rkr")], [pk])
                TT("dve", bon[:, tt % 3, h, :], pb[0:64, 0:TW], vvh[:, ws, h, :], ALU.mult, [pk, ("vv", ws, h)], [("bon", tt % 3, h)])
                TT("pool", BPh[:, ws, h, 0, :], kka[:, h, :], eend[:, h, :], ALU.mult, [K_("kka"), K_("eend")], [("BP", ws, h)])
                TT("pool", BPh[:, ws, h, 1, :], kp[:, h, :], eend[:, h, :], ALU.mult, [K_("kp"), K_("eend")], [("BP", ws, h)])
                yield

            def make_gens(tt):
                if tt >= NTB:
                    return []
                return [tile_pro(tt)] + [stage0(tt, h) for h in range(H)]

            def advance(gl, n):
                for _ in range(n):
                    for g in list(gl):
                        try:
                            next(g)
                        except StopIteration:
                            gl.remove(g)

            def chain(tt, nxt):
                ws = tt % 2
                tsl = slice(tt * TW, (tt + 1) * TW)
                HS = range(H)
                bk = lambda h: ps[2 + h]
                bkk = lambda h: ("ps", 2 + h)
                A0 = lambda h: ("AR0", ws, h)
                A1 = lambda h: ("AR1", ws, h)
                for h in HS:
                    for i, (src, kx) in enumerate([(ARh[:, ws, h, 0, :], A0(h)), (BPh[:, ws, h, 0, :], ("BP", ws, h)), (BPh[:, ws, h, 1, :], ("BP", ws, h)), (vvh[:, ws, h, :], ("vv", ws, h))]):
                        TR(bk(h)[:, i * 64:(i + 1) * 64], src, i64, [kx, "cst"], [bkk(h)])
                for h in HS:
                    CP("act", tm[:, h, :], bk(h)[:, 0:256], [bkk(h)], [("tm", h)])
                advance(nxt, 2)
                for h in HS:
                    MM(bk(h)[:, 0:256], BKh[:, ws, h, 0, :], ARh[:, ws, h, :, :], True, True, [("BK0", ws, h), A0(h), A1(h)], [bkk(h)], r=True)
                    MM(bk(h)[:, 256:384], ARh[:, ws, h, 0, :], BKh[:, ws, h, 0, :], True, True, [("BK0", ws, h), A0(h)], [bkk(h)], r=True)
                for h in HS:
                    TT("dve", RR(Eb[:, h, 0:384]), bk(h)[:, 0:384], mE, ALU.mult, [bkk(h), "cst"], [("E", h, "a")])
                advance(nxt, 2)
                for h in HS:
                    MM(bk(h)[:, 0:256], BKh[:, ws, h, 1, :], ARh[:, ws, h, :, :], True, True, [("BK1", ws, h), A0(h), A1(h)], [bkk(h)], r=True)
                for h in HS:
                    TT("dve", YS[:, h, :], bk(h)[:, 0:256], mY, ALU.mult, [bkk(h), "cst"], [("YS", h)])
                advance(nxt, 2)
                for h in HS:
                    MM(bk(h)[:, 256:320], YS[:, h, 0:128], tm[:, h, 192:256], True, True, [("YS", h), ("tm", h)], [bkk(h)])
                    CP("pool", RR(Eb[:, h, 384:448]), tm[:, h, 0:64], [("tm", h)], [("E", h, "b")])
                for h in HS:
                    CP("act", RR(Eb[:, h, 448:512]), bk(h)[:, 256:320], [bkk(h)], [("E", h, "c")])
                advance(nxt, 2)
                for lev in range(6):
                    def views(h):
                        if lev == 0:
                            return (Eb[:, h, 0:128], Eb[:, h, 256:384], Eb[:, h, 256:512], Eb[:, h, 384:512],
                                    [("E", h, "a"), ("E", h, "b"), ("E", h, "c")])
                        Lp = Lb[:, h, (lev - 1) % 2, :]
                        return (Lp[:, 0:128], Lp[:, 128:256], Lp[:, 128:384], Lp[:, 256:384],
                                [("L", h, (lev - 1) % 2, "p"), ("L", h, (lev - 1) % 2, "z")])
                    for h in HS:
                        PT_, P_, PZ_, Z_, rk = views(h)
                        MM(bk(h)[:, 128:384], PT_, PZ_, True, True, rk, [bkk(h)], r=True)
                        if lev < 5:
                            MM(bk(h)[:, 0:128], P_, PT_, True, True, rk, [bkk(h)], r=True)
                    for h in HS:
                        PT_, P_, PZ_, Z_, rk = views(h)
                        Ln = Lb[:, h, lev % 2, :]
                        TT("dve", RR(Ln[:, 256:384]), bk(h)[:, 256:384], Z_, ALU.add, [bkk(h)] + rk, [("L", h, lev % 2, "z")])
                        if lev < 5:
                            CP("act", RR(Ln[:, 0:256]), bk(h)[:, 0:256], [bkk(h)], [("L", h, lev % 2, "p")])
                    advance(nxt, 3)
                for h in HS:
                    for hf in range(2):
                        TS("pool", B2[:, h, hf * 64:(hf + 1) * 64], tm[:, h, 64:128], rowm[:, hf:hf + 1], None, ALU.mult, None, [("tm", h), "cst"], [("B2", h)])
                        TS("pool", K2[:, h, hf * 64:(hf + 1) * 64], tm[:, h, 128:192], rowm[:, hf:hf + 1], None, ALU.mult, None, [("tm", h), "cst"], [("K2", h)])
                for h in HS:
                    Lf = Lb[:, h, 1, :]
                    kLf = ("L", h, 1, "z")
                    W_, U0_ = Lf[:, 256:320], Lf[:, 320:384]
                    b_ = bk(h)
                    MM(b_[0:64, 0:128], W_, B2[:, h, :], True, True, [kLf, ("B2", h)], [bkk(h)])
                    for hf in range(2):
                        MM(b_[0:64, 128 + hf * 64:128 + (hf + 1) * 64], K2[:, h, hf * 64:(hf + 1) * 64], tm[:, h, 192:256], True, False, [("K2", h), ("tm", h)], [bkk(h)])
                        MM(b_[0:64, 128 + hf * 64:128 + (hf + 1) * 64], B2[:, h, hf * 64:(hf + 1) * 64], U0_, False, True, [("B2", h), kLf], [bkk(h)])
                    MM(b_[0:64, 256:384], W_, Eb[:, h, 128:256], True, True, [kLf, ("E", h, "a")], [bkk(h)])
                    MM(b_[0:64, 384:512], tm[:, h, 192:256], YS[:, h, 128:256], True, False, [("tm", h), ("YS", h)], [bkk(h)])
                    MM(b_[0:64, 384:512], U0_, Eb[:, h, 128:256], False, True, [kLf, ("E", h, "a")], [bkk(h)])
                advance(nxt, 2)
                for h in HS:
                    b_ = bk(h)
                    for hf in range(2):
                        STT(GT[:, h, hf, :], i64, pC[:, ws, h, hf:hf + 1], b_[0:64, hf * 64:(hf + 1) * 64], ALU.mult, ALU.add,
                            [bkk(h), ("pC", ws, h), "cst"], [("GT", h)])
                    TT("dve", RhatT[:, h, :], b_[0:64, 256:384], ARh[:, ws, h, 1, :], ALU.add, [bkk(h), A1(h)], [("Rhat", h)])
                for h in HS:
                    b_ = bk(h)
                    CP("act", Hm[:, h, :, :], b_[0:64, 128:256].rearrange("p (c i) -> p c i", c=2), [bkk(h)], [("Hm", h)])
                    CP("act", Y0T[:, h, :], b_[0:64, 384:512], [bkk(h)], [("Y0T", h)])
                advance(nxt, 2)
                allh = lambda nm: [(nm, h) for h in range(H)]
                for c in range(2):
                    csl = slice(c * 64, (c + 1) * 64)
                    m0 = mcur[0]
                    mnew = 1 - m0
                    for h in range(H):
                        MM(ps[0][0:64, h * 64:(h + 1) * 64], Mst[:, m0, h, :], RhatT[:, h, csl], True, True, [("Mst", m0), ("Rhat", h)], [("ps", 0)])
                    for h in range(H):
                        MM(ps[1][0:64, h * 64:(h + 1) * 64], GT[:, h, c, :], Mst[:, m0, h, :], True, True, [("Mst", m0), ("GT", h)], [("ps", 1)])
                    TT("dve", Mst[:, mnew, :, :], ps[1][0:64, 0:384].rearrange("p (h i) -> p h i", h=6), Hm[:, :, c, :], ALU.add,
                       [("ps", 1)] + allh("Hm"), [("Mst", mnew)])
                    TT("dve", yT[:, :, csl], ps[0][0:64, 0:384].rearrange("p (h t) -> p h t", h=6), Y0T[:, :, csl], ALU.add,
                       [("ps", 0)] + allh("Y0T"), [("yT", c)])
                    mcur[0] = mnew
                    advance(nxt, 1)
                advance(nxt, 1000)

            def post(tt, h):
                tsl = slice(tt * TW, (tt + 1) * TW)
                w3 = tt % 3
                ally = [("yT", c) for c in range(2)]
                osl = h
                kyc, kysq, kyrs = ("yc", osl), ("ysq", osl), ("yrs", osl)
                pb = ps[h % 2]
                pk = ("ps", h % 2)
                MM(pb[0:64, 0:TW], o64, yT[:, h, :], True, True, ["cst"] + ally, [pk])
                STT(yc[:, osl, :], pb[0:64, 0:TW], -1.0 / 64, yT[:, h, :], ALU.mult, ALU.add, [pk] + ally, [kyc])
                yield
                ACT(ysq[:, osl, :], yc[:, osl, :], AF.Square, [kyc], [kysq])
                yield
                MM(pb[0:64, 128:128 + TW], o64, ysq[:, osl, :], True, True, ["cst", kysq], [pk])
                ACT(yrs[:, osl, :], pb[0:64, 128:128 + TW], AF.Sqrt, [pk, "epsc"], [kyrs], scale=1.0 / 64, bias=epsc[0:64, 1:2])
                yield
                RECIP(yrs[:, osl, :], yrs[:, osl, :], [kyrs], [kyrs])
                yield
                TT("dve", yc[:, osl, :], yc[:, osl, :], yrs[:, osl, :], ALU.mult, [kyc, kyrs], [kyc])
                yield
                TS("dve", yc[:, osl, :], yc[:, osl, :], plg[:, h:h + 1], plb[:, h:h + 1], ALU.mult, ALU.add, [kyc, "plg", "plb"], [kyc])
                yield
                TT("pool", yc[:, osl, :], yc[:, osl, :], bon[:, w3, h, :], ALU.add, [kyc, ("bon", w3, h)], [kyc])
                yield
                TT("pool", obuf[:, osl, :], yc[:, osl, :], gal[:, w3, h, :], ALU.mult, [kyc, ("gal", w3, h)], [("obuf", osl)])
                DMA(catT[h * 64:(h + 1) * 64, tsl], obuf[:, osl, :], [("obuf", osl)], [("cat", "a", h, tt)])
                yield

            g0 = make_gens(0)
            advance(g0, 1000)
            for tt in range(NTB):
                pg = [post(tt - 1, h) for h in range(H)] if tt > 0 else []
                chain(tt, pg + make_gens(tt + 1))
            advance([post(NTB - 1, h) for h in range(H)], 1000)
            sc.barrier()
        if stop == "B":
            break

        with ExitStack() as es:
            wO = sb(es, "wO", [128, 2, 8, 128])
            catb = sb(es, "catb", [128, 2, 8, 512])
            mixb = sb(es, "mixb", [128, 8, 512])
            sq = sb(es, "sq", [128, 2, 512])
            rstd = sb(es, "rstd", [128, 2, 512])
            gP = colvec(es, "gP", PR["post_mix_g"][l], 8)
            w_out_l = PR["w_out"][l].rearrange("(k p) c -> p k c", p=128)
            cat_v = catT.rearrange("(k p) t -> p k t", p=128)
            wi = 0
            xe = sb(es, "xe", [128, 2, 8, 512])
            for tt in range(4):
                tsl = slice(tt * 512, (tt + 1) * 512)
                cs = tt % 2
                DMA(xe[:, cs, :, :], xD_v[:, :, tsl], [("xD", tt)], [("xe", cs, j) for j in range(8)])
                DMA(RR(catb[:, cs, :, :]), cat_v[:, :, tsl], [], [("catb", cs)], q="pool")
                for j in range(8):
                    sl = wi % 2
                    wi += 1
                    DMA(RR(wO[:, sl, :, :]), w_out_l[:, :, j * 128:(j + 1) * 128], [], [("wO", sl)], q="pool")
                    bi = j % 2
                    for k in range(8):
                        MM(ps[bi][:, :], wO[:, sl, k, :], catb[:, cs, k, :], k == 0, k == 7, [("wO", sl), ("catb", cs)], [("ps", bi)], r=True)
                    CP("act" if j % 2 else "dve", mixb[:, j, :], ps[bi][:, :], [("ps", bi)], [("mixb", j)])
                r = rms_stats(lambda k: mixb[:, k, :], lambda k: ("mixb", k), tt, (sq, rstd), ps[2 + tt % 2], ("ps", 2 + tt % 2))
                for j in range(8):
                    STT(mixb[:, j, :], mixb[:, j, :], gP[:, j:j + 1], r, ALU.mult, ALU.mult, [("mixb", j), ("rstd", tt % 2), "gP"], [("mixb", j)])
                    TT("dve", xe[:, cs, j, :], xe[:, cs, j, :], mixb[:, j, :], ALU.add, [("mixb", j), ("xe", cs, j)], [("xe", cs, j)])
                DMA(xD_v[:, :, tsl], xe[:, cs, :, :], [("xe", cs, j) for j in range(8)], [("xD", tt)])
            sc.barrier()
        if stop == "E":
            break

        with ExitStack() as es:
            TF = 1024
            hb = sb(es, "hb", [128, 8, TF])
            actb = sb(es, "actb", [128, NFF, TF])
            shm = sb(es, "shm", [128, 8 * TF])
            xf = shm[:, :].rearrange("p (k t) -> p k t", k=8)
            wD = shm[:, 0:2 * NFF * 128].rearrange("p (s j c) -> p s j c", s=2, j=NFF)
            wG = sb(es, "wG", [128, 2, 2, 8, 128])
            sq = sb(es, "sq", [128, 2, 512])
            rstd = sb(es, "rstd", [128, 2, 512])
            sil = sb(es, "sil", [128, 2, 512])
            xc = sb(es, "xc", [128, 2, TF])
            gF = colvec(es, "gF", PR["pre_ffn_g"][l], 8)
            gQ = colvec(es, "gQ", PR["post_ffn_g"][l], 8)
            w_fi = PR["w_ffn_in"][l].rearrange("(k p) c -> p k c", p=128)
            w_fo = PR["w_ffn_out"][l].rearrange("(j p) c -> p j c", p=128)
            wi = 0
            wdi = 0
            si = 0
            xci = 0
            SHK = ["sh", ("wD", 0), ("wD", 1)]
            for tt in range(S // TF):
                tsl = slice(tt * TF, (tt + 1) * TF)
                DMA(xf, xD_v[:, :, tsl], [("xD", 2 * tt), ("xD", 2 * tt + 1)], SHK)
                for hf in range(2):
                    hsl = slice(hf * 512, (hf + 1) * 512)
                    r = rms_stats(lambda k: xf[:, k, hsl], lambda k: "sh", si, (sq, rstd), ps[6 + si % 2], ("ps", 6 + si % 2))
                    for k in range(8):
                        STT(RR(hb[:, k, hsl]), xf[:, k, hsl], gF[:, k:k + 1], r, ALU.mult, ALU.mult, SHK + [("rstd", si % 2), "gF"], [("hb", k, hf)])
                    si += 1
                for j in range(NFF):
                    sl = wi % 2
                    wi += 1
                    DMA(RR(wG[:, sl, 0, :, :]), w_fi[:, :, j * 128:(j + 1) * 128], [], [("wG", sl, 0)], q="pool")
                    DMA(RR(wG[:, sl, 1, :, :]), w_fi[:, :, DFF + j * 128:DFF + (j + 1) * 128], [], [("wG", sl, 1)], q="pool")
                    for hf in range(2):
                        hsl = slice(hf * 512, (hf + 1) * 512)
                        bg, bu = hf * 2, hf * 2 + 1
                        for k in range(8):
                            MM(ps[bg][:, :], wG[:, sl, 0, k, :], hb[:, k, hsl], k == 0, k == 7, [("wG", sl, 0), ("hb", k, hf)], [("ps", bg)], r=True)
                        for k in range(8):
                            MM(ps[bu][:, :], wG[:, sl, 1, k, :], hb[:, k, hsl], k == 0, k == 7, [("wG", sl, 1), ("hb", k, hf)], [("ps", bu)], r=True)
                        ACT(sil[:, hf, :], ps[bg][:, :], AF.Silu, [("ps", bg)], [("sil", hf)])
                        TT("dve", RR(actb[:, j, hsl]), ps[bu][:, :], sil[:, hf, :], ALU.mult, [("ps", bu), ("sil", hf)], [("actb", j, hf)])
                for jo in range(8):
                    sl = wdi % 2
                    wdi += 1
                    DMA(RR(wD[:, sl, :, :]), w_fo[:, :, jo * 128:(jo + 1) * 128], [], [("wD", sl)], q="pool")
                    for hf in range(2):
                        hsl = slice(hf * 512, (hf + 1) * 512)
                        bi = 4 + hf
                        for j in range(NFF):
                            MM(ps[bi][:, :], wD[:, sl, j, :], actb[:, j, hsl], j == 0, j == NFF - 1, [("wD", sl), ("actb", j, hf)], [("ps", bi)], r=True)
                        CP("act" if hf else "dve", RR(hb[:, jo, hsl]), ps[bi][:, :], [("ps", bi)], [("hb", jo, hf)])
                for hf in range(2):
                    hsl = slice(hf * 512, (hf + 1) * 512)
                    r = rms_stats(lambda k: hb[:, k, hsl], lambda k: ("hb", k, hf), si, (sq, rstd), ps[6 + si % 2], ("ps", 6 + si % 2))
                    for j in range(8):
                        STT(RR(hb[:, j, hsl]), hb[:, j, hsl], gQ[:, j:j + 1], r, ALU.mult, ALU.mult, [("hb", j, hf), ("rstd", si % 2), "gQ"], [("hb", j, hf)])
                    si += 1
                for j in range(8):
                    cs = xci % 2
                    xci += 1
                    DMA(xc[:, cs, :], xD[j * 128:(j + 1) * 128, tsl], [("xD", 2 * tt), ("xD", 2 * tt + 1)], [("xc", cs)])
                    TT("pool" if j % 2 else "dve", xc[:, cs, :], xc[:, cs, :], hb[:, j, :], ALU.add, [("xc", cs), ("hb", j, 0), ("hb", j, 1)], [("xc", cs)])
                    DMA(xD[j * 128:(j + 1) * 128, tsl], xc[:, cs, :], [("xc", cs)], [("xDw", tt, j)])
            sc.barrier()
        if OPTS.get("xdbg") == l:
            DMA(xdbg[:, :], xD[:, :], [("xD", tt) for tt in range(4)], [("xdbg", 0)])

    if stop in (None, 'setup'):
        with ExitStack() as es:
            yo = sb(es, "yo", [128, 2, D])
            xo = sb(es, "xo", [128, 2, 8, 512])
            for t in range(16):
                sl = t % 2
                xsl = (t // 4) % 2
                if t % 4 == 0:
                    DMA(xo[:, xsl, :, :], xD_v[:, :, (t // 4) * 512:(t // 4 + 1) * 512], [("xD", t // 4)], [("xo", xsl)])
                for g in range(2):
                    bank = ps[(t * 2 + g) % 4]
                    bk = ("ps", (t * 2 + g) % 4)
                    for kk in range(4):
                        k = g * 4 + kk
                        TR(bank[:, kk * 128:(kk + 1) * 128], xo[:, xsl, k, (t % 4) * 128:(t % 4 + 1) * 128], ident, [("xo", xsl), "cst"], [bk])
                    CP("act" if g else "dve", yo[:, sl, g * 512:(g + 1) * 512], bank[:, :], [bk], [("yo", sl, g)])
                DMA(y_out[t * 128:(t + 1) * 128, :], yo[:, sl, :], [("yo", sl, 0), ("yo", sl, 1)], [("y", t)])
    sc.barrier()
    sc.emit()
    glob.close()
    return nc


_NC_CACHE = {}


def kernel(**inputs):
    if "nc" not in _NC_CACHE:
        _NC_CACHE["nc"] = build()
    nc = _NC_CACHE["nc"]
    x = np.ascontiguousarray(np.asarray(inputs["x"], dtype=np.float32))
    base = {n: np.ascontiguousarray(np.asarray(inputs[n], dtype=np.float32)) for n in PARAM_NAMES}
    base["cst"] = CONSTS["cst"]
    base["mobac"] = CONSTS["mobac"]
    base["blkind"] = CONSTS["blkind"]
    in_maps = []
    for b in range(8):
        m = dict(base)
        m["x"] = x[b]
        in_maps.append(m)
    res = run_bass_kernel_spmd(nc, in_maps, core_ids=list(range(8)))
    return np.stack([np.asarray(r["y"], dtype=np.float32) for r in res.results], 0)
```

```python
import numpy as np
from contextlib import ExitStack
import concourse.bass as bass
import concourse.mybir as mybir
from concourse.bass_utils import run_bass_kernel_spmd

F32 = mybir.dt.float32
F32R = mybir.dt.float32r
AF = mybir.ActivationFunctionType
ALU = mybir.AluOpType
AX = mybir.AxisListType

S = 2048
D = 1024
L = 2
DFF = 2816
NFF = 22
H = 6
C0 = float(np.exp(-0.5))
BIG = 30000.0


OPTS = {}


class Sched:
    def __init__(self, nc, n_dma=40):
        self.nc = nc
        self.names = ["pe", "act", "dve", "pool", "sp"]
        self.sem = {e: nc.alloc_semaphore("s_" + e) for e in ["pe", "act", "dve", "pool"]}
        self.cnt = {e: 0 for e in self.sem}
        self.dsem = [nc.alloc_semaphore("d%d" % i) for i in range(n_dma)]
        self.dcnt = [0] * n_dma
        self.drr = 0
        self.q = {e: [] for e in self.names}
        self.clock = {e: {} for e in self.names}
        self.evclock = {}
        self.evorder = {}
        self.nev = 0
        self.lastw = {}
        self.readers = {}

    def _deps(self, reads, writes):
        deps = {}

        def add(k, v):
            if deps.get(k, 0) < v:
                deps[k] = v

        for r in reads:
            ev = self.lastw.get(r)
            if ev is not None:
                add(*ev)
        for w in writes:
            ev = self.lastw.get(w)
            if ev is not None:
                add(*ev)
            for k, v in self.readers.get(w, {}).items():
                add(k, v)
        return deps

    def _commit(self, ev, reads, writes):
        k, v = ev
        for r in reads:
            d = self.readers.setdefault(r, {})
            if d.get(k, 0) < v:
                d[k] = v
        for w in writes:
            self.lastw[w] = ev
            self.readers[w] = {}

    def _waits(self, eng, deps):
        clk = self.clock.setdefault(eng, {})
        waits = []
        for k, v in sorted(deps.items(), key=lambda kv: -self.evorder.get(kv, 0)):
            if eng == "pe" and k == ("e", "pe"):
                continue
            if clk.get(k, 0) >= v:
                continue
            waits.append((k, v))
            for k2, v2 in self.evclock.get((k, v), {}).items():
                if clk.get(k2, 0) < v2:
                    clk[k2] = v2
            clk[k] = v
        return waits

    def op(self, eng, fn, reads=(), writes=()):
        banks = {("psx", k[1]) for k in list(reads) + list(writes) if isinstance(k, tuple) and k and k[0] == "ps"}
        if banks:
            writes = list(writes) + list(banks)
        deps = self._deps(reads, writes)
        waits = self._waits(eng, deps)
        self.cnt[eng] += 1
        ev = (("e", eng), self.cnt[eng])
        self.evclock[ev] = dict(self.clock[eng])
        self.nev += 1
        self.evorder[ev] = self.nev
        self.q[eng].append((waits, fn, "e"))
        self._commit(ev, reads, writes)

    def dma(self, qeng, out, in_, reads=(), writes=(), **kw):
        deps = self._deps(reads, writes)
        idx = self.drr
        self.drr = (self.drr + 1) % len(self.dsem)
        if self.dcnt[idx] > 0:
            k = ("d", idx)
            deps[k] = max(deps.get(k, 0), self.dcnt[idx])
        waits = self._waits(qeng, deps)
        self.dcnt[idx] += 16
        ev = (("d", idx), self.dcnt[idx])
        self.evclock[ev] = dict(self.clock[qeng])
        self.nev += 1
        self.evorder[ev] = self.nev
        self.q[qeng].append((waits, lambda e: e.dma_start(out=out, in_=in_, **kw), idx))
        self._commit(ev, reads, writes)

    def barrier(self):
        allev = [(("e", e), c) for e, c in self.cnt.items() if c > 0]
        allev += [(("d", i), c) for i, c in enumerate(self.dcnt) if c > 0]
        for eng in self.names:
            waits = self._waits(eng, dict(allev))
            if waits:
                self.q[eng].append((waits, None, None))
        self.lastw = {}
        self.readers = {}
        self.evclock = {}

    def emit(self):
        nc = self.nc
        engs = {"pe": "tensor", "act": "scalar", "dve": "vector", "pool": "gpsimd", "sp": "sync"}
        with nc.Block() as block:
            for name in self.names:
                def body(eng, name=name):
                    for waits, fn, kind in self.q[name]:
                        emb = None
                        if fn is not None and waits and kind == "e" and not OPTS.get("noemb"):
                            emb = waits[-1]
                            waits = waits[:-1]
                        for k, v in waits:
                            s = self.sem[k[1]] if k[0] == "e" else self.dsem[k[1]]
                            eng.wait_ge(s, v)
                        if fn is None:
                            continue
                        ins = fn(eng)
                        if emb is not None:
                            k, v = emb
                            ins._wait_ge(self.sem[k[1]] if k[0] == "e" else self.dsem[k[1]], v)
                        if kind == "e":
                            ins.then_inc(self.sem[name], 1)
                        else:
                            ins.then_inc(self.dsem[kind], 16)
                getattr(block, engs[name])(body)


def make_consts():
    c = {}
    i128 = np.arange(128)
    blk = (i128[:, None] // 64) == (i128[None, :] // 64)
    ident = np.eye(128, dtype=np.float32)
    ones = np.ones((128, 128), np.float32)
    SL = ((i128[:, None] > i128[None, :]) & blk).astype(np.float32)
    SU = ((i128[:, None] < i128[None, :]) & blk).astype(np.float32)
    IU = ((i128[:, None] <= i128[None, :]) & blk).astype(np.float32)
    IUfull = (i128[:, None] <= i128[None, :]).astype(np.float32)
    idst = np.concatenate([np.eye(64), np.eye(64)], 0).astype(np.float32)
    reset = np.ones((128, 768), np.float32)
    reset[:, ::64] = 0.0
    rowm = np.zeros((128, 2), np.float32)
    rowm[:64, 0] = 1.0
    rowm[64:, 1] = 1.0
    q512 = np.arange(512)
    cm = np.stack([(q512[None, :] >= (j * 128 + i128[:, None])).astype(np.float32) for j in range(4)], 1)
    parts = [ident, ones, SU, IU, SL, SU, IU, IUfull, idst, reset, rowm, cm.reshape(128, 2048)]
    offs = {}
    o = 0
    for nm, p in zip(["ident", "ones", "mE", "_1", "_2", "mY", "_3", "iuf", "idst", "reset", "rowm", "cm"], parts):
        offs[nm] = o
        o += p.shape[1]
    c["cst"] = np.ascontiguousarray(np.concatenate(parts, 1))
    c["offs"] = offs
    mb = np.zeros((128, 3, 16, 6, 8), np.float32)
    for t in range(16):
        b = t // 2
        for n in range(8):
            mb[:, 0, t, :, n] = 0.0 if n < b else -1e30
            mb[:, 1, t, :, n] = 1.0 if n < b else 0.0
            mb[:, 2, t, :, n] = 1.0 if n == b else 0.0
    c["mobac"] = mb.reshape(128, 3 * 16 * 48)
    bi = np.zeros((8, S), np.float32)
    for n in range(8):
        bi[n, n * 256:(n + 1) * 256] = 1.0
    c["blkind"] = bi
    return c


CONSTS = make_consts()
PARAM_NAMES = ["pre_mix_g", "w_in", "rwkv_mu", "rwkv_w0", "rwkv_w2", "rwkv_a0", "rwkv_a2", "rwkv_g2",
               "rwkv_k_k", "rwkv_k_a", "rwkv_r_k", "rwkv_lnx_g", "rwkv_lnx_b", "gmlp_ln_g", "gmlp_ln_b",
               "gmlp_w_s", "gmlp_b_s", "w_out", "post_mix_g", "pre_ffn_g", "w_ffn_in", "w_ffn_out", "post_ffn_g"]
PARAM_SHAPES = {"pre_mix_g": (L, D), "w_in": (L, D, 3072), "rwkv_mu": (L, 1408), "rwkv_w0": (L, 384),
                "rwkv_w2": (L, 64, 384), "rwkv_a0": (L, 384), "rwkv_a2": (L, 64, 384), "rwkv_g2": (L, 128, 384),
                "rwkv_k_k": (L, 384), "rwkv_k_a": (L, 384), "rwkv_r_k": (L, 6, 64), "rwkv_lnx_g": (L, 384),
                "rwkv_lnx_b": (L, 384), "gmlp_ln_g": (L, 256), "gmlp_ln_b": (L, 256), "gmlp_w_s": (L, 4, 128, 128),
                "gmlp_b_s": (L, 4, 128), "w_out": (L, D, D), "post_mix_g": (L, D), "pre_ffn_g": (L, D),
                "w_ffn_in": (L, D, 2 * DFF), "w_ffn_out": (L, DFF, D), "post_ffn_g": (L, D)}


def build(dbg=None, nlayers=L, stop=None):
    dbg = dbg or []
    nc = bass.Bass("TRN2", target_bir_lowering=False)
    sc = Sched(nc)
    OF = CONSTS["offs"]

    def dram(name, shape, kind="Internal"):
        if name in dbg:
            kind = "ExternalOutput"
        return nc.dram_tensor(name, list(shape), F32, kind=kind).ap()

    x_in = dram("x", [S, D], "ExternalInput")
    y_out = dram("y", [S, D], "ExternalOutput")
    cst_d = dram("cst", CONSTS["cst"].shape, "ExternalInput")
    if not OPTS.get("noparams"):
        PR = {n: dram(n, PARAM_SHAPES[n], "ExternalInput") for n in PARAM_NAMES}
        mobac_d = dram("mobac", CONSTS["mobac"].shape, "ExternalInput")
        blkind_d = dram("blkind", CONSTS["blkind"].shape, "ExternalInput")
    if OPTS.get("noscratch"):
        glob_scr = None
    rwkvT = dram("rwkvT", [1408, S]) if not OPTS.get("noscratch") else None
    qkT = dram("qkT", [768, S]) if not OPTS.get("noscratch") else None
    uT = dram("uT", [256, S]) if not OPTS.get("noscratch") else None
    vm_tm = dram("vm_tm", [S, 384]) if not OPTS.get("noscratch") else None
    vg_tm = dram("vg_tm", [S, 256]) if not OPTS.get("noscratch") else None
    catT = dram("catT", [D, S]) if not OPTS.get("noscratch") else None
    xdbg = dram("xdbg", [D, S]) if not OPTS.get("noscratch") else None
    xD = dram("xD", [D, S])
    xD_v = xD.rearrange("(k p) t -> p k t", p=128)

    uid = [0]

    def sb(es, name, shape):
        uid[0] += 1
        return es.enter_context(nc.sbuf_tensor("%s_%d" % (name, uid[0]), list(shape), F32))

    glob = ExitStack()
    cst = sb(glob, "cst_sb", [128, CONSTS["cst"].shape[1]])
    ps = [glob.enter_context(nc.psum_tensor("ps%d" % i, [128, 512], F32)) for i in range(8)]
    ident = cst[:, OF["ident"]:OF["ident"] + 128]
    ones = cst[:, OF["ones"]:OF["ones"] + 128]
    mE = cst[:, OF["mE"]:OF["mE"] + 384]
    mY = cst[:, OF["mY"]:OF["mY"] + 256]
    iuf = cst[:, OF["iuf"]:OF["iuf"] + 128]
    idst = cst[:, OF["idst"]:OF["idst"] + 64]
    resetm = cst[:, OF["reset"]:OF["reset"] + 768]
    cm = cst[:, OF["cm"]:OF["cm"] + 2048]
    epsc = sb(glob, "epsc", [128, 4])

    def ACT(out, in_, func, reads, writes, **kw):
        sc.op("act", lambda e: e.activation(out=out, in_=in_, func=func, **kw), reads, writes)

    def RR(ap):
        return ap if OPTS.get("nor") else ap.bitcast(F32R)

    def MM(out, lhsT, rhs, start, stop, reads, writes, r=False):
        if r and not OPTS.get("nor"):
            lhsT = lhsT.bitcast(F32R)
            rhs = rhs.bitcast(F32R)
        sc.op("pe", lambda e: e.matmul(out, lhsT=lhsT, rhs=rhs, start=start, stop=stop), reads, writes)

    def TR(out, in_, idn, reads, writes):
        sc.op("pe", lambda e: e.transpose(out, in_, idn), reads, writes)

    def TT(eng, out, in0, in1, op, reads, writes):
        sc.op(eng, lambda e: e.tensor_tensor(out=out, in0=in0, in1=in1, op=op), reads, writes)

    def TS(eng, out, in0, s1, s2, op0, op1, reads, writes):
        if s2 is None:
            sc.op(eng, lambda e: e.tensor_scalar(out=out, in0=in0, scalar1=s1, scalar2=None, op0=op0), reads, writes)
        else:
            sc.op(eng, lambda e: e.tensor_scalar(out=out, in0=in0, scalar1=s1, scalar2=s2, op0=op0, op1=op1), reads, writes)

    def STT(out, in0, scalar, in1, op0, op1, reads, writes):
        sc.op("dve", lambda e: e.scalar_tensor_tensor(out=out, in0=in0, scalar=scalar, in1=in1, op0=op0, op1=op1), reads, writes)

    def CP(eng, out, in_, reads, writes):
        if eng == "act":
            sc.op("act", lambda e: e.copy(out=out, in_=in_), reads, writes)
        else:
            sc.op(eng, lambda e: e.tensor_copy(out=out, in_=in_), reads, writes)

    def RECIP(out, in_, reads, writes):
        sc.op("dve", lambda e: e.reciprocal(out=out, in_=in_), reads, writes)

    def DMA(out, in_, reads, writes, q="sp", **kw):
        sc.dma(q, out, in_, reads, writes, **kw)

    def colvec(es, name, src_1d, ncol, p=128):
        t = sb(es, name, [p, ncol])
        DMA(t[:, :], src_1d.rearrange("(c p) -> p c", p=p), [], [name], allow_slow_non_contiguous=True)
        return t

    DMA(cst[:, :], cst_d[:, :], [], ["cst"])
    sc.op("dve", lambda e: e.memset(epsc[:, 0:1], 1e-6), [], ["epsc"])
    sc.op("dve", lambda e: e.memset(epsc[:, 1:2], 64e-5), [], ["epsc"])
    sc.op("dve", lambda e: e.memset(epsc[:, 2:3], 0.0), [], ["epsc"])
    with ExitStack() as es:
        xin = sb(es, "xin", [128, 2, D])
        xs = sb(es, "xs", [128, 2, 8, 512])
        for t in range(16):
            sl = t % 2
            xsl = (t // 4) % 2
            DMA(xin[:, sl, :], x_in[t * 128:(t + 1) * 128, :], [], [("xin", sl)])
            for g in range(2):
                bank = ps[(t * 2 + g) % 4]
                bk = ("ps", (t * 2 + g) % 4)
                for kk in range(4):
                    k = g * 4 + kk
                    TR(bank[:, kk * 128:(kk + 1) * 128], xin[:, sl, k * 128:(k + 1) * 128], ident, [("xin", sl), "cst"], [bk])
                CP("act" if g else "dve", xs[:, xsl, g * 4:(g + 1) * 4, (t % 4) * 128:(t % 4 + 1) * 128],
                   bank[:, :].rearrange("p (k c) -> p k c", k=4), [bk], [("xs", xsl, t % 4, g)])
            if t % 4 == 3:
                DMA(xD_v[:, :, (t // 4) * 512:(t // 4 + 1) * 512], xs[:, xsl, :, :], [("xs", xsl, q, g) for q in range(4) for g in range(2)], [("xD", t // 4)])
        sc.barrier()

    def rms_stats(src_fn, src_keys, tt, es_tiles, pbank, pkey):
        sq, rstd = es_tiles
        for k in range(8):
            ACT(sq[:, k % 2, :], src_fn(k), AF.Square, [src_keys(k)], [("sq", k % 2)])
            MM(pbank[:, :], ones, sq[:, k % 2, :], k == 0, k == 7, [("sq", k % 2), "cst"], [pkey])
        ACT(rstd[:, tt % 2, :], pbank[:, :], AF.Sqrt, [pkey, "epsc"], [("rstd", tt % 2)], scale=1.0 / D, bias=epsc[:, 0:1])
        RECIP(rstd[:, tt % 2, :], rstd[:, tt % 2, :], [("rstd", tt % 2)], [("rstd", tt % 2)])
        return rstd[:, tt % 2, :]

    for l in range(nlayers if stop != 'setup' else 0):
        with ExitStack() as es:
            hbuf = sb(es, "hbuf", [128, 8, S])
            sq = sb(es, "sq", [128, 2, 512])
            rstd = sb(es, "rstd", [128, 2, 512])
            gA = colvec(es, "gA", PR["pre_mix_g"][l], 8)
            muA = colvec(es, "muA", PR["rwkv_mu"][l], 11)
            xa = sb(es, "xa", [128, 2, 8, 512])
            for tt in range(4):
                tsl = slice(tt * 512, (tt + 1) * 512)
                xsl = tt % 2
                DMA(xa[:, xsl, :, :], xD_v[:, :, tsl], [("xD", tt)], [("xa", xsl)])
                r = rms_stats(lambda k: xa[:, xsl, k, :], lambda k: ("xa", xsl), tt, (sq, rstd), ps[4 + tt % 2], ("ps", 4 + tt % 2))
                for k in range(8):
                    STT(RR(hbuf[:, k, tsl]), xa[:, xsl, k, :], gA[:, k:k + 1], r, ALU.mult, ALU.mult,
                        [("xa", xsl), ("rstd", tt % 2), "gA"], [("h", k, tt)])
            es_main = es
            es = ExitStack()
            wA = sb(es, "wA", [128, 2, 8, 128])
            stg = sb(es, "stg", [128, 2, S])
            stg2 = sb(es, "stg2", [128, 2, S])
            w_in_l = PR["w_in"][l].rearrange("(k p) c -> p k c", p=128)
            fm_chunks = [(c * 128, rwkvT, c * 128, True) for c in range(11)]
            fm_chunks += [(1408 + c * 128, qkT, c * 128, False) for c in range(6)]
            fm_chunks += [(2560 + c * 128, uT, c * 128, False) for c in range(2)]
            def load_wA(ci):
                col0 = fm_chunks[ci][0]
                DMA(RR(wA[:, ci % 2, :, :]), w_in_l[:, :, col0:col0 + 128], [], [("wA", ci % 2)], q="pool")

            load_wA(0)
            for ci, (col0, dst, row0, shift) in enumerate(fm_chunks):
                sl = ci % 2
                if ci + 1 < len(fm_chunks):
                    load_wA(ci + 1)
                for tt in range(4):
                    tsl = slice(tt * 512, (tt + 1) * 512)
                    bi = (ci * 4 + tt) % 4
                    for k in range(8):
                        MM(ps[bi][:, :], wA[:, sl, k, :], hbuf[:, k, tsl], k == 0, k == 7,
                           [("wA", sl), ("h", k, tt)], [("ps", bi)], r=True)
                    CP("act" if tt % 2 else "dve", stg[:, sl, tsl], ps[bi][:, :], [("ps", bi)], [("stg", sl, tt)])
                allst = [("stg", sl, tt) for tt in range(4)]
                if shift:
                    TT("pool", stg2[:, sl, 1:S], stg[:, sl, 0:S - 1], stg[:, sl, 1:S], ALU.subtract, allst, [("stg2", sl)])
                    TS("pool", stg2[:, sl, 0:1], stg[:, sl, 0:1], -1.0, None, ALU.mult, None, allst, [("stg2", sl)])
                    STT(stg2[:, sl, :], stg2[:, sl, :], muA[:, ci:ci + 1], stg[:, sl, :], ALU.mult, ALU.add,
                        allst + [("stg2", sl), "muA"], [("stg2", sl)])
                    DMA(dst[row0:row0 + 128, :], stg2[:, sl, :], [("stg2", sl)], [("dr", id(dst), row0)], q="pool")
                else:
                    DMA(dst[row0:row0 + 128, :], stg[:, sl, :], allst, [("dr", id(dst), row0)], q="pool")
            sc.barrier()
            es.close()
            es = es_main
            wB = sb(es, "wB", [128, 8, 640])
            DMA(RR(wB[:, :, 0:384]), w_in_l[:, :, 2176:2560], [], ["wB"], q="pool")
            DMA(RR(wB[:, :, 384:640]), w_in_l[:, :, 2816:3072], [], ["wB"], q="pool")
            vst = sb(es, "vst", [128, 2, 640])
            for t in range(16):
                sl = t % 2
                b0, b1 = 4 + (t % 2) * 2, 5 + (t % 2) * 2
                for k in range(8):
                    MM(ps[b0][:, 0:384], hbuf[:, k, t * 128:(t + 1) * 128], wB[:, k, 0:384], k == 0, k == 7,
                       [("h", k, t // 4), "wB"], [("ps", b0)], r=True)
                for k in range(8):
                    MM(ps[b1][:, 0:256], hbuf[:, k, t * 128:(t + 1) * 128], wB[:, k, 384:640], k == 0, k == 7,
                       [("h", k, t // 4), "wB"], [("ps", b1)], r=True)
                CP("act", vst[:, sl, 0:384], ps[b0][:, 0:384], [("ps", b0)], [("vst", sl, 0)])
                CP("dve", vst[:, sl, 384:640], ps[b1][:, 0:256], [("ps", b1)], [("vst", sl, 1)])
                DMA(vm_tm[t * 128:(t + 1) * 128, :], vst[:, sl, 0:384], [("vst", sl, 0)], [("vm", t)], q="pool")
                DMA(vg_tm[t * 128:(t + 1) * 128, :], vst[:, sl, 384:640], [("vst", sl, 1)], [("vg", t)], q="pool")
            sc.barrier()
        if stop == "A":
            break

        with ExitStack() as es:
            lng = sb(es, "lng", [128, 256])
            lnb = sb(es, "lnb", [128, 256])
            DMA(lng[:, :], PR["gmlp_ln_g"][l].partition_broadcast(128), [], ["lng"])
            DMA(lnb[:, :], PR["gmlp_ln_b"][l].partition_broadcast(128), [], ["lnb"])
            wsn = sb(es, "wsn", [128, 4, 128])
            wsT = sb(es, "wsT", [128, 4, 128])
            bsr = sb(es, "bsr", [1, 512])
            DMA(wsn[:, :, :], PR["gmlp_w_s"][l].rearrange("g t s -> t g s"), [], ["wsn"])
            DMA(bsr[:, :], PR["gmlp_b_s"][l].rearrange("g t -> (g t)").partition_broadcast(1), [], ["bsr"])
            for g in range(4):
                TR(ps[0][:, g * 128:(g + 1) * 128], wsn[:, g, :], ident, ["wsn", "cst"], [("ps", 0)])
            for g in range(4):
                TT("dve", wsT[:, g, :], ps[0][:, g * 128:(g + 1) * 128], iuf, ALU.mult, [("ps", 0), "cst"], ["wsT"])
            gu = sb(es, "gu", [128, 2, S])
            t1 = sb(es, "t1", [128, S])
            cout = sb(es, "cout", [128, 2, S])
            for pp in range(2):
                DMA(gu[:, pp, :], uT[pp * 128:(pp + 1) * 128, :], [], [("gu", pp)])
                ACT(t1[:, :], gu[:, pp, :], AF.Square, [("gu", pp)], ["t1"])
                TS("pool", t1[:, :], t1[:, :], 0.044715, 1.0, ALU.mult, ALU.add, ["t1"], ["t1"])
                TT("dve", t1[:, :], t1[:, :], gu[:, pp, :], ALU.mult, ["t1", ("gu", pp)], ["t1"])
                ACT(t1[:, :], t1[:, :], AF.Sigmoid, ["t1"], ["t1"], scale=2.0 * 0.7978845608028654)
                TT("dve", gu[:, pp, :], gu[:, pp, :], t1[:, :], ALU.mult, ["t1", ("gu", pp)], [("gu", pp)])
            vb = sb(es, "vb", [128, 2, 256])
            t2 = sb(es, "t2", [128, 2, 256])
            st6 = sb(es, "st6", [128, 2, 8])
            for c in range(16):
                sl = c % 2
                DMA(vb[:, sl, :], vg_tm[c * 128:(c + 1) * 128, :], [], [("vb", sl)])
                kv, kt = ("vb", sl), ("t2", sl)
                ACT(t2[:, sl, :], vb[:, sl, :], AF.Square, [kv], [kt])
                TS("pool", t2[:, sl, :], t2[:, sl, :], 0.044715, 1.0, ALU.mult, ALU.add, [kt], [kt])
                TT("dve", t2[:, sl, :], t2[:, sl, :], vb[:, sl, :], ALU.mult, [kt, kv], [kt])
                ACT(t2[:, sl, :], t2[:, sl, :], AF.Sigmoid, [kt], [kt], scale=2.0 * 0.7978845608028654)
                TT("dve", vb[:, sl, :], vb[:, sl, :], t2[:, sl, :], ALU.mult, [kt, kv], [kv])
                ks = ("st6", sl)
                sc.op("dve", lambda e, sl=sl: e.bn_stats(out=st6[:, sl, 0:6], in_=vb[:, sl, :]), [kv], [ks])
                sc.op("dve", lambda e, sl=sl: e.bn_aggr(out=st6[:, sl, 6:8], in_=st6[:, sl, 0:6]), [ks], [ks])
                ACT(st6[:, sl, 7:8], st6[:, sl, 7:8], AF.Sqrt, [ks, "epsc"], [ks], bias=epsc[:, 0:1], scale=1.0)
                RECIP(st6[:, sl, 7:8], st6[:, sl, 7:8], [ks], [ks])
                TS("dve", vb[:, sl, :], vb[:, sl, :], st6[:, sl, 6:7], st6[:, sl, 7:8], ALU.subtract, ALU.mult, [kv, ks], [kv])
                TT("dve", vb[:, sl, :], vb[:, sl, :], lng[:, :], ALU.mult, [kv, "lng"], [kv])
                TT("dve", vb[:, sl, :], vb[:, sl, :], lnb[:, :], ALU.add, [kv, "lnb"], [kv])
                for pp in range(2):
                    bi = 1 + (c % 2) * 2 + pp
                    for gg in range(2):
                        g = pp * 2 + gg
                        MM(ps[bi][:, gg * 128:(gg + 1) * 128], vb[:, sl, pp * 128:(pp + 1) * 128], wsT[:, g, :], True, False, [kv, "wsT"], [("ps", bi)])
                        MM(ps[bi][:, gg * 128:(gg + 1) * 128], ones[0:1, :], bsr[0:1, g * 128:(g + 1) * 128], False, True, ["cst", "bsr"], [("ps", bi)])
                    for gg in range(2):
                        TT("dve", cout[gg * 64:(gg + 1) * 64, pp, c * 128:(c + 1) * 128], ps[bi][gg * 64:(gg + 1) * 64, gg * 128:(gg + 1) * 128],
                           gu[gg * 64:(gg + 1) * 64, pp, c * 128:(c + 1) * 128], ALU.mult, [("ps", bi), ("gu", pp)], [("cout", pp)])
            for pp in range(2):
                DMA(catT[768 + pp * 128:768 + (pp + 1) * 128, :], cout[:, pp, :], [("cout", pp)], [("cat", 6 + pp)], q="pool")
            sc.barrier()
        if stop == "D":
            break

        with ExitStack() as es:
            qa = sb(es, "qa", [72, 2, S])
            ka = sb(es, "ka", [72, 2, S])
            vt = sb(es, "vt", [128, 2, 16, 128])
            ones_r = sb(es, "ones_r", [128, 128])
            CP("dve", RR(ones_r[:, :]), ones, ["cst"], ["ones_r"])
            kbar = sb(es, "kbar", [64, 2, 8])
            mobc = sb(es, "mobc", [128, 3, 16, 48])
            NP = sb(es, "NP", [128, 2, 16, 72])
            sm = sb(es, "sm", [128, 16, 8])
            top8 = sb(es, "top8", [128, 16, 8])
            al = sb(es, "al", [128, 16, 8])
            pt = sb(es, "pt", [128, 6, 512])
            pacc = sb(es, "pacc", [128, 512])
            rden = sb(es, "rden", [128, 512])
            ob = sb(es, "ob", [128, 2, 512])
            DMA(mobc[:, :, :, :], mobac_d.rearrange("p (a t c) -> p a t c", a=3, t=16), [], ["mobc"])
            for q in range(2):
                DMA(RR(ka[64:72, q, :]), blkind_d[:, :], [], [("kaB", q)], q="pool")
            sc.op("pool", lambda e: e.memset(NP[:, :, :, :], 0.0), [], [("NP", 0), ("NP", 1)])
            pti = [0]

            def prepA(h):
                q = h % 2
                DMA(RR(qa[0:64, q, :]), qkT[h * 64:(h + 1) * 64, :], [], [("qaQ", q)], q="pool")
                DMA(RR(ka[0:64, q, :]), qkT[384 + h * 64:384 + (h + 1) * 64, :], [], [("kaK", q)], q="pool")
                if h % 2 == 0:
                    vq = (h // 2) % 2
                    DMA(RR(vt[:, vq, :, :]), vm_tm.rearrange("(t p) c -> p t c", p=128)[:, :, h * 64:(h + 2) * 64], [], [("vt", vq)], q="pool")
                sc.op("dve", lambda e: e.tensor_reduce(out=kbar[:, q, :], in_=ka[0:64, q, :].rearrange("p (n k) -> p n k", n=8), axis=AX.X, op=ALU.add), [("kaK", q)], [("kbar", q)])
                for t in range(16):
                    MM(ps[0][:, t * 8:(t + 1) * 8], qa[0:64, q, t * 128:(t + 1) * 128], kbar[:, q, :], True, True, [("qaQ", q), ("kbar", q)], [("ps", 0)])

            def prepB(h):
                q = h % 2
                hs = slice(h * 8, (h + 1) * 8)
                TT("dve", sm[:, :, :], ps[0][:, 0:128].rearrange("p (t n) -> p t n", t=16), mobc[:, 0, :, hs], ALU.add, [("ps", 0), "mobc"], ["sm"])
                for t in range(16):
                    sc.op("dve", lambda e, t=t: e.max(out=top8[:, t, :], in_=sm[:, t, :]), ["sm"], [("top8", t)])
                for t in range(16):
                    TS("dve", al[:, t, :], sm[:, t, :], top8[:, t, 2:3], None, ALU.is_ge, None, ["sm", ("top8", t)], [("al", t)])
                allal = [("al", t) for t in range(16)]
                TT("dve", al[:, :, :], al[:, :, :], mobc[:, 1, :, hs], ALU.mult, allal + ["mobc"], ["al2"])
                TT("dve", al[:, :, :], al[:, :, :], mobc[:, 2, :, hs], ALU.add, ["al2", "mobc"], ["al2"])
                TS("dve", NP[:, q, :, 64:72], al[:, :, :], -1.0, BIG, ALU.add, ALU.mult, ["al2"], [("NP", q)])

            def prepC(h):
                q = h % 2
                for t4 in range(4):
                    for tq in range(4):
                        t = t4 * 4 + tq
                        MM(ps[1][0:72, tq * 128:(tq + 1) * 128], NP[:, q, t, :], ident, True, True, [("NP", q), "cst"], [("ps", 1)])
                    CP("act", RR(qa[64:72, q, t4 * 512:(t4 + 1) * 512]), ps[1][64:72, :], [("ps", 1)], [("qaM", q)])

            def attn(h, qt):
                q = h % 2
                vq = (h // 2) % 2
                hp = slice((h % 2) * 64, (h % 2) * 64 + 64)
                qsl = slice(qt * 512, (qt + 1) * 512)
                nk = (qt + 1) * 4
                osl = qt % 2
                pis = {}

                def qk(kt):
                    sb_i = (2, 3, 6, 7)[kt % 4]
                    pi = pti[0] % 6
                    pti[0] += 1
                    pis[kt] = pi
                    MM(ps[sb_i][:, :], ka[0:72, q, kt * 128:(kt + 1) * 128], qa[0:72, q, qsl], True, True,
                       [("kaK", q), ("kaB", q), ("qaQ", q), ("qaM", q)], [("ps", sb_i)], r=True)
                    ACT(RR(pt[:, pi, :]), ps[sb_i][:, :], AF.Exp, [("ps", sb_i)], [("pt", pi)], scale=0.125)
                    if kt >= qt * 4:
                        j = kt - qt * 4
                        TT("dve", RR(pt[:, pi, :]), pt[:, pi, :], cm[:, j * 512:(j + 1) * 512], ALU.mult, [("pt", pi), "cst"], [("pt", pi)])

                def pv(kt):
                    pi = pis[kt]
                    MM(ps[4][:, :], vt[:, vq, kt, :], pt[:, pi, :], kt == 0, kt == nk - 1, [("vt", vq), ("pt", pi)], [("ps", 4)], r=True)
                    if kt == 0:
                        CP("pool", RR(pacc[:, :]), pt[:, pi, :], [("pt", pi)], ["pacc"])
                    else:
                        TT("pool", RR(pacc[:, :]), pacc[:, :], pt[:, pi, :], ALU.add, [("pt", pi), "pacc"], ["pacc"])

                DEPTH = 3
                for kt in range(min(DEPTH, nk)):
                    qk(kt)
                for kt in range(nk):
                    pv(kt)
                    if kt + DEPTH < nk:
                        qk(kt + DEPTH)
                MM(ps[5][:, :], ones_r[:, :], pacc[:, :], True, True, ["ones_r", "pacc"], [("ps", 5)], r=True)
                RECIP(rden[hp, :], ps[5][hp, :], [("ps", 5)], ["rden"])
                TT("dve", ob[hp, osl, :], ps[4][hp, :], rden[hp, :], ALU.mult, [("ps", 4), "rden"], [("ob", osl)])
                DMA(catT[384 + h * 64:384 + (h + 1) * 64, qsl], ob[hp, osl, :], [("ob", osl)], [("cat", "b", h, qt)])

            prepA(0)
            prepB(0)
            prepC(0)
            for h in range(H):
                for qt in range(4):
                    attn(h, qt)
                    if h + 1 < H:
                        if qt == 0:
                            prepA(h + 1)
                        elif qt == 1:
                            prepB(h + 1)
                        elif qt == 2:
                            prepC(h + 1)
            sc.barrier()
        if stop == "C":
            break

        with ExitStack() as es:
            TW = 128
            NTB = S // TW
            w2s = sb(es, "w2s", [64, 384])
            a2s = sb(es, "a2s", [64, 384])
            g2s = sb(es, "g2s", [128, 384])
            DMA(w2s[:, :], PR["rwkv_w2"][l], [], ["w2s"])
            DMA(a2s[:, :], PR["rwkv_a2"][l], [], ["a2s"])
            DMA(g2s[:, :], PR["rwkv_g2"][l], [], ["g2s"])
            pw0 = colvec(es, "pw0", PR["rwkv_w0"][l], 6, p=64)
            pa0 = colvec(es, "pa0", PR["rwkv_a0"][l], 6, p=64)
            pkk = colvec(es, "pkk", PR["rwkv_k_k"][l], 6, p=64)
            pka = colvec(es, "pka", PR["rwkv_k_a"][l], 6, p=64)
            prk = colvec(es, "prk", PR["rwkv_r_k"][l].rearrange("h d -> (h d)"), 6, p=64)
            plg = colvec(es, "plg", PR["rwkv_lnx_g"][l], 6, p=64)
            plb = colvec(es, "plb", PR["rwkv_lnx_b"][l], 6, p=64)
            pok = sb(es, "pok", [64, 6])
            TS("dve", pok[:, :], pka[:, :], -1.0, 1.0, ALU.mult, ALU.add, ["pka"], ["pok"])
            i64 = ident[0:64, 0:64]
            o64 = ones[0:64, 0:64]
            rowm = cst[:, OF["rowm"]:OF["rowm"] + 2]
            Mst = sb(es, "Mst", [64, 2, 6, 64])
            sc.op("dve", lambda e: e.memset(Mst[:, 0, :, :], 0.0), [], [("Mst", 0)])
            mcur = [0]
            RhatT = sb(es, "RhatT", [64, 6, TW])
            Y0T = sb(es, "Y0T", [64, 6, TW])
            yT = sb(es, "yT", [64, 6, TW])
            GT = sb(es, "GT", [64, 6, 2, 64])
            Hm = sb(es, "Hm", [64, 6, 2, 64])
            bon = sb(es, "bon", [64, 3, 6, TW]); gal = sb(es, "gal", [64, 3, 6, TW])
            ARh = sb(es, "ARh", [64, 2, 6, 2, TW]); BKh = sb(es, "BKh", [64, 2, 6, 2, TW]); BPh = sb(es, "BPh", [64, 2, 6, 2, TW])
            vvh = sb(es, "vvh", [64, 2, 6, TW]); pC = sb(es, "pC", [64, 2, 6, 2])
            wd = sb(es, "wd", [64, 2, TW]); ad = sb(es, "ad", [64, 2, TW]); gd = sb(es, "gd", [128, 2, TW])
            T6 = lambda nm: sb(es, nm, [64, 6, TW])
            rr = T6("rr"); kq = T6("kq"); sig = T6("sig"); cum = T6("cum"); cpv = T6("cpv")
            epos = T6("epos"); eneg = T6("eneg"); eprv = T6("eprv"); eend = T6("eend")
            aa = T6("aa"); kk = T6("kk"); kk2 = cpv; rn = T6("rn"); kka = T6("kka"); kp = T6("kp"); rkr = T6("rkr")
            nbc = sb(es, "nbc", [64, 6, 2])
            tm = sb(es, "tm", [128, 6, 256]); Eb = sb(es, "Eb", [128, 6, 512]); YS = sb(es, "YS", [128, 6, 256])
            Lb = sb(es, "Lb", [128, 6, 2, 384]); B2 = sb(es, "B2", [128, 6, 128]); K2 = sb(es, "K2", [128, 6, 128])
            yc = sb(es, "yc", [64, 6, TW]); ysq = sb(es, "ysq", [64, 6, TW]); yrs = sb(es, "yrs", [64, 6, TW])
            obuf = sb(es, "obuf", [64, 6, TW])

            def tile_pro(tt):
                tsl = slice(tt * TW, (tt + 1) * TW)
                ws = tt % 2
                DMA(wd[:, ws, :], rwkvT[1152:1216, tsl], [], [("wd", ws)])
                DMA(ad[:, ws, :], rwkvT[1216:1280, tsl], [], [("ad", ws)])
                DMA(gd[:, ws, :], rwkvT[1280:1408, tsl], [], [("gd", ws)])
                yield
                ACT(wd[:, ws, :], wd[:, ws, :], AF.Tanh, [("wd", ws)], [("wd", ws)])
                ACT(gd[:, ws, :], gd[:, ws, :], AF.Sigmoid, [("gd", ws)], [("gd", ws)])
                yield

            def stage0_all(tt):
                tsl = slice(tt * TW, (tt + 1) * TW)
                ws = tt % 2
                w3 = tt % 3
                HH = range(H)
                fl = lambda t: t[:, :, :].rearrange("p h t -> p (h t)")
                A6 = lambda nm: [(nm, ws, h) for h in HH]
                hv = lambda base: rwkvT[base:base + 384, tsl].rearrange("(h p) t -> p h t", p=64)
                DMA(rr[:, :, :], hv(0), [], ["rr"])
                DMA(kq[:, :, :], hv(384), [], ["kq"])
                DMA(vvh[:, ws, :, :], hv(768), [], A6("vv"))
                yield
                pbh = lambda h: (ps[0], ("ps", 0), h * TW) if h < 4 else (ps[1], ("ps", 1), (h - 4) * TW)
                for h in HH:
                    pb, pk, c0 = pbh(h)
                    MM(pb[0:64, c0:c0 + TW], w2s[:, h * 64:(h + 1) * 64], wd[:, ws, :], True, True, ["w2s", ("wd", ws)], [pk])
                for h in HH:
                    pb, pk, c0 = pbh(h)
                    ACT(sig[:, h, :], pb[0:64, c0:c0 + TW], AF.Sigmoid, [pk, "pw0"], [("sig", h)], bias=pw0[:, h:h + 1])
                yield
                allsig = [("sig", h) for h in HH]
                sc.op("dve", lambda e: e.tensor_tensor_scan(out=fl(cum), data0=resetm[0:64, 0:6 * TW], data1=fl(sig), initial=0.0, op0=ALU.mult, op1=ALU.add), allsig + ["cst"], ["cum"])
                yield
                TT("pool", fl(cpv), fl(cum), fl(sig), ALU.subtract, ["cum"] + allsig, ["cpv"])
                ACT(fl(epos), fl(cum), AF.Exp, ["cum"], ["epos"], scale=-C0)
                ACT(fl(eneg), fl(cum), AF.Exp, ["cum"], ["eneg"], scale=C0)
                cum12 = fl(cum).rearrange("p (c t) -> p c t", t=64)
                epos12 = fl(epos).rearrange("p (c t) -> p c t", t=64)
                nbc12 = nbc[:, :, :].rearrange("p h c -> p (h c)")
                TS("dve", nbc12, cum12[:, :, 63], -C0, None, ALU.mult, None, ["cum"], ["nbc"])
                yield
                ACT(fl(eprv), fl(cpv), AF.Exp, ["cpv"], ["eprv"], scale=-C0)
                CP("pool", pC[:, ws, :, :].rearrange("p h c -> p (h c)"), epos12[:, :, 63], ["epos"], A6("pC"))
                for h in HH:
                    for c in range(2):
                        ACT(eend[:, h, c * 64:(c + 1) * 64], cum[:, h, c * 64:(c + 1) * 64], AF.Exp, ["cum", "nbc"], [("eend", h)], scale=C0, bias=nbc[:, h, c:c + 1])
                    if h % 2:
                        yield
                for h in HH:
                    pb, pk, c0 = pbh(h)
                    MM(pb[0:64, c0:c0 + TW], a2s[:, h * 64:(h + 1) * 64], ad[:, ws, :], True, True, ["a2s", ("ad", ws)], [pk])
                for h in HH:
                    pb, pk, c0 = pbh(h)
                    ACT(aa[:, h, :], pb[0:64, c0:c0 + TW], AF.Sigmoid, [pk, "pa0"], [("aa", h)], bias=pa0[:, h:h + 1])
                yield
                for h in HH:
                    pb, pk, c0 = pbh(h)
                    MM(pb[0:64, c0:c0 + TW], g2s[:, h * 64:(h + 1) * 64], gd[:, ws, :], True, True, ["g2s", ("gd", ws)], [pk])
                CP("act", gal[:, w3, 0:4, :].rearrange("p h t -> p (h t)"), ps[0][0:64, 0:512], [("ps", 0)], [("gal", w3, h) for h in range(4)])
                CP("act", gal[:, w3, 4:6, :].rearrange("p h t -> p (h t)"), ps[1][0:64, 0:256], [("ps", 1)], [("gal", w3, h) for h in range(4, 6)])
                yield
                for h in HH:
                    TS("dve", kk[:, h, :], kq[:, h, :], pkk[:, h:h + 1], None, ALU.mult, None, ["kq", "pkk"], [("kk", h)])
                yield
                allkk = [("kk", h) for h in HH]
                ACT(fl(kk2), fl(kk), AF.Square, allkk + ["cpv"], ["cpv"])
                yield
                for h in HH:
                    pb, pk, c0 = pbh(h)
                    MM(pb[0:64, c0:c0 + TW], o64, kk2[:, h, :], True, True, ["cst", "cpv"], [pk])
                ACT(rn[:, 0:4, :].rearrange("p h t -> p (h t)"), ps[0][0:64, 0:512], AF.Sqrt, [("ps", 0)], [("rn", 0)])
                ACT(rn[:, 4:6, :].rearrange("p h t -> p (h t)"), ps[1][0:64, 0:256], AF.Sqrt, [("ps", 1)], [("rn", 1)])
                yield
                krn = [("rn", 0), ("rn", 1)]
                TS("dve", fl(rn), fl(rn), 1e-12, None, ALU.max, None, krn, krn)
                yield
                RECIP(fl(rn), fl(rn), krn, krn)
                yield
                TT("dve", fl(kk), fl(kk), fl(rn), ALU.mult, allkk + krn, ["kkn"])
                yield
                allaa = [("aa", h) for h in HH]
                TT("pool", fl(kka), fl(kk), fl(aa), ALU.mult, ["kkn"] + allaa, ["kka"])
                for h in HH:
                    TS("dve", rn[:, h, :], aa[:, h, :], pka[:, h:h + 1], pok[:, h:h + 1], ALU.mult, ALU.add, [("aa", h), "pka", "pok", "kkn"] + krn, [("fac", h)])
                STT(RR(ARh[:, ws, :, 0, :]), kk[:, :, :], -1.0, eprv[:, :, :], ALU.mult, ALU.mult, ["kkn", "eprv"], A6("AR0"))
                yield
                allfac = [("fac", h) for h in HH]
                TT("dve", fl(kp), fl(kq), fl(rn), ALU.mult, ["kq"] + allfac, ["kp"])
                TT("pool", RR(ARh[:, ws, :, 1, :]), rr[:, :, :], epos[:, :, :], ALU.mult, ["rr", "epos"], A6("AR1"))
                yield
                for h in HH:
                    STT(rkr[:, h, :], rr[:, h, :], prk[:, h:h + 1], kp[:, h, :], ALU.mult, ALU.mult, ["rr", "prk", "kp"], [("rkr", h)])
                TT("pool", RR(BKh[:, ws, :, 0, :]), kka[:, :, :], eneg[:, :, :], ALU.mult, ["kka", "eneg"], A6("BK0"))
                TT("pool", RR(BKh[:, ws, :, 1, :]), kp[:, :, :], eneg[:, :, :], ALU.mult, ["kp", "eneg"], A6("BK1"))
                yield
                for h in HH:
                    pb, pk, c0 = pbh(h)
                    MM(pb[0:64, c0:c0 + TW], o64, rkr[:, h, :], True, True, ["cst", ("rkr", h)], [pk])
                TT("dve", bon[:, w3, 0:4, :].rearrange("p h t -> p (h t)"), ps[0][0:64, 0:512], vvh[:, ws, 0:4, :].rearrange("p h t -> p (h t)"), ALU.mult,
                   [("ps", 0)] + A6("vv"), [("bon", w3, h) for h in range(4)])
                TT("dve", bon[:, w3, 4:6, :].rearrange("p h t -> p (h t)"), ps[1][0:64, 0:256], vvh[:, ws, 4:6, :].rearrange("p h t -> p (h t)"), ALU.mult,
                   [("ps", 1)] + A6("vv"), [("bon", w3, h) for h in range(4, 6)])
                alleend = [("eend", h) for h in HH]
                TT("pool", BPh[:, ws, :, 0, :], kka[:, :, :], eend[:, :, :], ALU.mult, ["kka"] + alleend, A6("BP"))
                TT("pool", BPh[:, ws, :, 1, :], kp[:, :, :], eend[:, :, :], ALU.mult, ["kp"] + alleend, A6("BP"))
                yield

            def make_gens(tt):
                if tt >= NTB:
                    return []
                return [tile_pro(tt), stage0_all(tt)]

            def advance(gl, n):
                for _ in range(n):
                    for g in list(gl):
                        try:
                            next(g)
                        except StopIteration:
                            gl.remove(g)

            def chain(tt, nxt):
                ws = tt % 2
                tsl = slice(tt * TW, (tt + 1) * TW)
                HS = range(H)
                bk = lambda h: ps[2 + h]
                bkk = lambda h: ("ps", 2 + h)
                A0 = lambda h: ("AR0", ws, h)
                A1 = lambda h: ("AR1", ws, h)
                for h in HS:
                    for i, (src, kx) in enumerate([(ARh[:, ws, h, 0, :], A0(h)), (BPh[:, ws, h, 0, :], ("BP", ws, h)), (BPh[:, ws, h, 1, :], ("BP", ws, h)), (vvh[:, ws, h, :], ("vv", ws, h))]):
                        TR(bk(h)[:, i * 64:(i + 1) * 64], src, i64, [kx, "cst"], [bkk(h)])
                for h in HS:
                    CP("act", tm[:, h, :], bk(h)[:, 0:256], [bkk(h)], [("tm", h)])
                advance(nxt, 2)
                for h in HS:
                    MM(bk(h)[:, 0:256], BKh[:, ws, h, 0, :], ARh[:, ws, h, :, :], True, True, [("BK0", ws, h), A0(h), A1(h)], [bkk(h)], r=True)
                    MM(bk(h)[:, 256:384], ARh[:, ws, h, 0, :], BKh[:, ws, h, 0, :], True, True, [("BK0", ws, h), A0(h)], [bkk(h)], r=True)
                for h in HS:
                    TT("dve", RR(Eb[:, h, 0:384]), bk(h)[:, 0:384], mE, ALU.mult, [bkk(h), "cst"], [("E", h, "a")])
                advance(nxt, 2)
                for h in HS:
                    MM(bk(h)[:, 0:256], BKh[:, ws, h, 1, :], ARh[:, ws, h, :, :], True, True, [("BK1", ws, h), A0(h), A1(h)], [bkk(h)], r=True)
                for h in HS:
                    TT("dve", YS[:, h, :], bk(h)[:, 0:256], mY, ALU.mult, [bkk(h), "cst"], [("YS", h)])
                advance(nxt, 2)
                for h in HS:
                    MM(bk(h)[:, 256:320], YS[:, h, 0:128], tm[:, h, 192:256], True, True, [("YS", h), ("tm", h)], [bkk(h)])
                    CP("pool", RR(Eb[:, h, 384:448]), tm[:, h, 0:64], [("tm", h)], [("E", h, "b")])
                for h in HS:
                    CP("act", RR(Eb[:, h, 448:512]), bk(h)[:, 256:320], [bkk(h)], [("E", h, "c")])
                advance(nxt, 2)
                for lev in range(6):
                    def views(h):
                        if lev == 0:
                            return (Eb[:, h, 0:128], Eb[:, h, 256:384], Eb[:, h, 256:512], Eb[:, h, 384:512],
                                    [("E", h, "a"), ("E", h, "b"), ("E", h, "c")])
                        Lp = Lb[:, h, (lev - 1) % 2, :]
                        return (Lp[:, 0:128], Lp[:, 128:256], Lp[:, 128:384], Lp[:, 256:384],
                                [("L", h, (lev - 1) % 2, "p"), ("L", h, (lev - 1) % 2, "z")])
                    for h in HS:
                        PT_, P_, PZ_, Z_, rk = views(h)
                        MM(bk(h)[:, 128:384], PT_, PZ_, True, True, rk, [bkk(h)], r=True)
                        if lev < 5:
                            MM(bk(h)[:, 0:128], P_, PT_, True, True, rk, [bkk(h)], r=True)
                    for h in HS:
                        PT_, P_, PZ_, Z_, rk = views(h)
                        Ln = Lb[:, h, lev % 2, :]
                        TT("dve", RR(Ln[:, 256:384]), bk(h)[:, 256:384], Z_, ALU.add, [bkk(h)] + rk, [("L", h, lev % 2, "z")])
                        if lev < 5:
                            CP("act", RR(Ln[:, 0:256]), bk(h)[:, 0:256], [bkk(h)], [("L", h, lev % 2, "p")])
                    advance(nxt, 3)
                for h in HS:
                    for hf in range(2):
                        TS("pool", B2[:, h, hf * 64:(hf + 1) * 64], tm[:, h, 64:128], rowm[:, hf:hf + 1], None, ALU.mult, None, [("tm", h), "cst"], [("B2", h)])
                        TS("pool", K2[:, h, hf * 64:(hf + 1) * 64], tm[:, h, 128:192], rowm[:, hf:hf + 1], None, ALU.mult, None, [("tm", h), "cst"], [("K2", h)])
                for h in HS:
                    Lf = Lb[:, h, 1, :]
                    kLf = ("L", h, 1, "z")
                    W_, U0_ = Lf[:, 256:320], Lf[:, 320:384]
                    b_ = bk(h)
                    MM(b_[0:64, 0:128], W_, B2[:, h, :], True, True, [kLf, ("B2", h)], [bkk(h)])
                    for hf in range(2):
                        MM(b_[0:64, 128 + hf * 64:128 + (hf + 1) * 64], K2[:, h, hf * 64:(hf + 1) * 64], tm[:, h, 192:256], True, False, [("K2", h), ("tm", h)], [bkk(h)])
                        MM(b_[0:64, 128 + hf * 64:128 + (hf + 1) * 64], B2[:, h, hf * 64:(hf + 1) * 64], U0_, False, True, [("B2", h), kLf], [bkk(h)])
                    MM(b_[0:64, 256:384], W_, Eb[:, h, 128:256], True, True, [kLf, ("E", h, "a")], [bkk(h)])
                    MM(b_[0:64, 384:512], tm[:, h, 192:256], YS[:, h, 128:256], True, False, [("tm", h), ("YS", h)], [bkk(h)])
                    MM(b_[0:64, 384:512], U0_, Eb[:, h, 128:256], False, True, [kLf, ("E", h, "a")], [bkk(h)])
                advance(nxt, 2)
                for h in HS:
                    b_ = bk(h)
                    for hf in range(2):
                        STT(GT[:, h, hf, :], i64, pC[:, ws, h, hf:hf + 1], b_[0:64, hf * 64:(hf + 1) * 64], ALU.mult, ALU.add,
                            [bkk(h), ("pC", ws, h), "cst"], [("GT", h)])
                    TT("dve", RhatT[:, h, :], b_[0:64, 256:384], ARh[:, ws, h, 1, :], ALU.add, [bkk(h), A1(h)], [("Rhat", h)])
                for h in HS:
                    b_ = bk(h)
                    CP("act", Hm[:, h, :, :], b_[0:64, 128:256].rearrange("p (c i) -> p c i", c=2), [bkk(h)], [("Hm", h)])
                    CP("act", Y0T[:, h, :], b_[0:64, 384:512], [bkk(h)], [("Y0T", h)])
                advance(nxt, 2)
                allh = lambda nm: [(nm, h) for h in range(H)]
                for c in range(2):
                    csl = slice(c * 64, (c + 1) * 64)
                    m0 = mcur[0]
                    mnew = 1 - m0
                    for h in range(H):
                        MM(ps[0][0:64, h * 64:(h + 1) * 64], Mst[:, m0, h, :], RhatT[:, h, csl], True, True, [("Mst", m0), ("Rhat", h)], [("ps", 0)])
                    for h in range(H):
                        MM(ps[1][0:64, h * 64:(h + 1) * 64], GT[:, h, c, :], Mst[:, m0, h, :], True, True, [("Mst", m0), ("GT", h)], [("ps", 1)])
                    TT("dve", Mst[:, mnew, :, :], ps[1][0:64, 0:384].rearrange("p (h i) -> p h i", h=6), Hm[:, :, c, :], ALU.add,
                       [("ps", 1)] + allh("Hm"), [("Mst", mnew)])
                    TT("dve", yT[:, :, csl], ps[0][0:64, 0:384].rearrange("p (h t) -> p h t", h=6), Y0T[:, :, csl], ALU.add,
                       [("ps", 0)] + allh("Y0T"), [("yT", c)])
                    mcur[0] = mnew
                    advance(nxt, 1)
                advance(nxt, 1000)

            def post(tt, h):
                tsl = slice(tt * TW, (tt + 1) * TW)
                w3 = tt % 3
                ally = [("yT", c) for c in range(2)]
                osl = h
                kyc, kysq, kyrs = ("yc", osl), ("ysq", osl), ("yrs", osl)
                pb = ps[h % 2]
                pk = ("ps", h % 2)
                MM(pb[0:64, 0:TW], o64, yT[:, h, :], True, True, ["cst"] + ally, [pk])
                STT(yc[:, osl, :], pb[0:64, 0:TW], -1.0 / 64, yT[:, h, :], ALU.mult, ALU.add, [pk] + ally, [kyc])
                yield
                ACT(ysq[:, osl, :], yc[:, osl, :], AF.Square, [kyc], [kysq])
                yield
                MM(pb[0:64, 128:128 + TW], o64, ysq[:, osl, :], True, True, ["cst", kysq], [pk])
                ACT(yrs[:, osl, :], pb[0:64, 128:128 + TW], AF.Sqrt, [pk, "epsc"], [kyrs], scale=1.0 / 64, bias=epsc[0:64, 1:2])
                yield
                RECIP(yrs[:, osl, :], yrs[:, osl, :], [kyrs], [kyrs])
                yield
                TT("dve", yc[:, osl, :], yc[:, osl, :], yrs[:, osl, :], ALU.mult, [kyc, kyrs], [kyc])
                yield
                TS("dve", yc[:, osl, :], yc[:, osl, :], plg[:, h:h + 1], plb[:, h:h + 1], ALU.mult, ALU.add, [kyc, "plg", "plb"], [kyc])
                yield
                TT("pool", yc[:, osl, :], yc[:, osl, :], bon[:, w3, h, :], ALU.add, [kyc, ("bon", w3, h)], [kyc])
                yield
                TT("pool", obuf[:, osl, :], yc[:, osl, :], gal[:, w3, h, :], ALU.mult, [kyc, ("gal", w3, h)], [("obuf", osl)])
                DMA(catT[h * 64:(h + 1) * 64, tsl], obuf[:, osl, :], [("obuf", osl)], [("cat", "a", h, tt)])
                yield

            g0 = make_gens(0)
            advance(g0, 1000)
            for tt in range(NTB):
                pg = [post(tt - 1, h) for h in range(H)] if tt > 0 else []
                chain(tt, pg + make_gens(tt + 1))
            advance([post(NTB - 1, h) for h in range(H)], 1000)
            sc.barrier()
        if stop == "B":
            break

        with ExitStack() as es:
            wO = sb(es, "wO", [128, 2, 8, 128])
            catb = sb(es, "catb", [128, 2, 8, 512])
            mixb = sb(es, "mixb", [128, 8, 512])
            sq = sb(es, "sq", [128, 2, 512])
            rstd = sb(es, "rstd", [128, 2, 512])
            gP = colvec(es, "gP", PR["post_mix_g"][l], 8)
            w_out_l = PR["w_out"][l].rearrange("(k p) c -> p k c", p=128)
            cat_v = catT.rearrange("(k p) t -> p k t", p=128)
            wi = 0
            xe = sb(es, "xe", [128, 2, 8, 512])
            for tt in range(4):
                tsl = slice(tt * 512, (tt + 1) * 512)
                cs = tt % 2
                DMA(xe[:, cs, :, :], xD_v[:, :, tsl], [("xD", tt)], [("xe", cs, j) for j in range(8)])
                DMA(RR(catb[:, cs, :, :]), cat_v[:, :, tsl], [], [("catb", cs)], q="pool")
                for j in range(8):
                    sl = wi % 2
                    wi += 1
                    DMA(RR(wO[:, sl, :, :]), w_out_l[:, :, j * 128:(j + 1) * 128], [], [("wO", sl)], q="pool")
                    bi = j % 2
                    for k in range(8):
                        MM(ps[bi][:, :], wO[:, sl, k, :], catb[:, cs, k, :], k == 0, k == 7, [("wO", sl), ("catb", cs)], [("ps", bi)], r=True)
                    CP("act" if j % 2 else "dve", mixb[:, j, :], ps[bi][:, :], [("ps", bi)], [("mixb", j)])
                r = rms_stats(lambda k: mixb[:, k, :], lambda k: ("mixb", k), tt, (sq, rstd), ps[2 + tt % 2], ("ps", 2 + tt % 2))
                for j in range(8):
                    STT(mixb[:, j, :], mixb[:, j, :], gP[:, j:j + 1], r, ALU.mult, ALU.mult, [("mixb", j), ("rstd", tt % 2), "gP"], [("mixb", j)])
                    TT("dve", xe[:, cs, j, :], xe[:, cs, j, :], mixb[:, j, :], ALU.add, [("mixb", j), ("xe", cs, j)], [("xe", cs, j)])
                DMA(xD_v[:, :, tsl], xe[:, cs, :, :], [("xe", cs, j) for j in range(8)], [("xD", tt)])
            sc.barrier()
        if stop == "E":
            break

        with ExitStack() as es:
            TF = 1024
            hb = sb(es, "hb", [128, 8, TF])
            actb = sb(es, "actb", [128, NFF, TF])
            shm = sb(es, "shm", [128, 8 * TF])
            xf = shm[:, :].rearrange("p (k t) -> p k t", k=8)
            wD = shm[:, 0:2 * NFF * 128].rearrange("p (s j c) -> p s j c", s=2, j=NFF)
            wG = sb(es, "wG", [128, 2, 2, 8, 128])
            sq = sb(es, "sq", [128, 2, 512])
            rstd = sb(es, "rstd", [128, 2, 512])
            sil = sb(es, "sil", [128, 2, 512])
            xc = sb(es, "xc", [128, 2, TF])
            gF = colvec(es, "gF", PR["pre_ffn_g"][l], 8)
            gQ = colvec(es, "gQ", PR["post_ffn_g"][l], 8)
            w_fi = PR["w_ffn_in"][l].rearrange("(k p) c -> p k c", p=128)
            w_fo = PR["w_ffn_out"][l].rearrange("(j p) c -> p j c", p=128)
            wi = 0
            wdi = 0
            si = 0
            xci = 0
            SHK = ["sh", ("wD", 0), ("wD", 1)]
            for tt in range(S // TF):
                tsl = slice(tt * TF, (tt + 1) * TF)
                DMA(xf, xD_v[:, :, tsl], [("xD", 2 * tt), ("xD", 2 * tt + 1)], SHK)
                for hf in range(2):
                    hsl = slice(hf * 512, (hf + 1) * 512)
                    r = rms_stats(lambda k: xf[:, k, hsl], lambda k: "sh", si, (sq, rstd), ps[6 + si % 2], ("ps", 6 + si % 2))
                    for k in range(8):
                        STT(RR(hb[:, k, hsl]), xf[:, k, hsl], gF[:, k:k + 1], r, ALU.mult, ALU.mult, SHK + [("rstd", si % 2), "gF"], [("hb", k, hf)])
                    si += 1
                for j in range(NFF):
                    sl = wi % 2
                    wi += 1
                    DMA(RR(wG[:, sl, 0, :, :]), w_fi[:, :, j * 128:(j + 1) * 128], [], [("wG", sl, 0)], q="pool")
                    DMA(RR(wG[:, sl, 1, :, :]), w_fi[:, :, DFF + j * 128:DFF + (j + 1) * 128], [], [("wG", sl, 1)], q="pool")
                    for hf in range(2):
                        hsl = slice(hf * 512, (hf + 1) * 512)
                        bg, bu = hf * 2, hf * 2 + 1
                        for k in range(8):
                            MM(ps[bg][:, :], wG[:, sl, 0, k, :], hb[:, k, hsl], k == 0, k == 7, [("wG", sl, 0), ("hb", k, hf)], [("ps", bg)], r=True)
                        for k in range(8):
                            MM(ps[bu][:, :], wG[:, sl, 1, k, :], hb[:, k, hsl], k == 0, k == 7, [("wG", sl, 1), ("hb", k, hf)], [("ps", bu)], r=True)
                        ACT(sil[:, hf, :], ps[bg][:, :], AF.Silu, [("ps", bg)], [("sil", hf)])
                        TT("dve", RR(actb[:, j, hsl]), ps[bu][:, :], sil[:, hf, :], ALU.mult, [("ps", bu), ("sil", hf)], [("actb", j, hf)])
                for jo in range(8):
                    sl = wdi % 2
                    wdi += 1
                    DMA(RR(wD[:, sl, :, :]), w_fo[:, :, jo * 128:(jo + 1) * 128], [], [("wD", sl)], q="pool")
                    for hf in range(2):
                        hsl = slice(hf * 512, (hf + 1) * 512)
                        bi = 4 + hf
                        for j in range(NFF):
                            MM(ps[bi][:, :], wD[:, sl, j, :], actb[:, j, hsl], j == 0, j == NFF - 1, [("wD", sl), ("actb", j, hf)], [("ps", bi)], r=True)
                        CP("act" if hf else "dve", RR(hb[:, jo, hsl]), ps[bi][:, :], [("ps", bi)], [("hb", jo, hf)])
                for hf in range(2):
                    hsl = slice(hf * 512, (hf + 1) * 512)
                    r = rms_stats(lambda k: hb[:, k, hsl], lambda k: ("hb", k, hf), si, (sq, rstd), ps[6 + si % 2], ("ps", 6 + si % 2))
                    for j in range(8):
                        STT(RR(hb[:, j, hsl]), hb[:, j, hsl], gQ[:, j:j + 1], r, ALU.mult, ALU.mult, [("hb", j, hf), ("rstd", si % 2), "gQ"], [("hb", j, hf)])
                    si += 1
                for j in range(8):
                    cs = xci % 2
                    xci += 1
                    DMA(xc[:, cs, :], xD[j * 128:(j + 1) * 128, tsl], [("xD", 2 * tt), ("xD", 2 * tt + 1)], [("xc", cs)])
                    TT("pool" if j % 2 else "dve", xc[:, cs, :], xc[:, cs, :], hb[:, j, :], ALU.add, [("xc", cs), ("hb", j, 0), ("hb", j, 1)], [("xc", cs)])
                    DMA(xD[j * 128:(j + 1) * 128, tsl], xc[:, cs, :], [("xc", cs)], [("xDw", tt, j)])
            sc.barrier()
        if OPTS.get("xdbg") == l:
            DMA(xdbg[:, :], xD[:, :], [("xD", tt) for tt in range(4)], [("xdbg", 0)])

    if stop in (None, 'setup'):
        with ExitStack() as es:
            yo = sb(es, "yo", [128, 2, D])
            xo = sb(es, "xo", [128, 2, 8, 512])
            for t in range(16):
                sl = t % 2
                xsl = (t // 4) % 2
                if t % 4 == 0:
                    DMA(xo[:, xsl, :, :], xD_v[:, :, (t // 4) * 512:(t // 4 + 1) * 512], [("xD", t // 4)], [("xo", xsl)])
                for g in range(2):
                    bank = ps[(t * 2 + g) % 4]
                    bk = ("ps", (t * 2 + g) % 4)
                    for kk in range(4):
                        k = g * 4 + kk
                        TR(bank[:, kk * 128:(kk + 1) * 128], xo[:, xsl, k, (t % 4) * 128:(t % 4 + 1) * 128], ident, [("xo", xsl), "cst"], [bk])
                    CP("act" if g else "dve", yo[:, sl, g * 512:(g + 1) * 512], bank[:, :], [bk], [("yo", sl, g)])
                DMA(y_out[t * 128:(t + 1) * 128, :], yo[:, sl, :], [("yo", sl, 0), ("yo", sl, 1)], [("y", t)])
    sc.barrier()
    sc.emit()
    glob.close()
    return nc


_NC_CACHE = {}


def kernel(**inputs):
    if "nc" not in _NC_CACHE:
        _NC_CACHE["nc"] = build()
    nc = _NC_CACHE["nc"]
    x = np.ascontiguousarray(np.asarray(inputs["x"], dtype=np.float32))
    base = {n: np.ascontiguousarray(np.asarray(inputs[n], dtype=np.float32)) for n in PARAM_NAMES}
    base["cst"] = CONSTS["cst"]
    base["mobac"] = CONSTS["mobac"]
    base["blkind"] = CONSTS["blkind"]
    in_maps = []
    for b in range(8):
        m = dict(base)
        m["x"] = x[b]
        in_maps.append(m)
    res = run_bass_kernel_spmd(nc, in_maps, core_ids=list(range(8)))
    return np.stack([np.asarray(r["y"], dtype=np.float32) for r in res.results], 0)
```

```python
import numpy as np
from contextlib import ExitStack
import concourse.bass as bass
import concourse.mybir as mybir
from concourse.bass_utils import run_bass_kernel_spmd

F32 = mybir.dt.float32
F32R = mybir.dt.float32r
AF = mybir.ActivationFunctionType
ALU = mybir.AluOpType
AX = mybir.AxisListType

S = 2048
D = 1024
L = 2
DFF = 2816
NFF = 22
H = 6
C0 = float(np.exp(-0.5))
BIG = 30000.0


OPTS = {}


class Sched:
    def __init__(self, nc, n_dma=40):
        self.nc = nc
        self.names = ["pe", "act", "dve", "pool", "sp"]
        self.sem = {e: nc.alloc_semaphore("s_" + e) for e in ["pe", "act", "dve", "pool"]}
        self.cnt = {e: 0 for e in self.sem}
        self.dsem = [nc.alloc_semaphore("d%d" % i) for i in range(n_dma)]
        self.dcnt = [0] * n_dma
        self.drr = 0
        self.q = {e: [] for e in self.names}
        self.clock = {e: {} for e in self.names}
        self.evclock = {}
        self.evorder = {}
        self.nev = 0
        self.lastw = {}
        self.readers = {}

    def _deps(self, reads, writes):
        deps = {}

        def add(k, v):
            if deps.get(k, 0) < v:
                deps[k] = v

        for r in reads:
            ev = self.lastw.get(r)
            if ev is not None:
                add(*ev)
        for w in writes:
            ev = self.lastw.get(w)
            if ev is not None:
                add(*ev)
            for k, v in self.readers.get(w, {}).items():
                add(k, v)
        return deps

    def _commit(self, ev, reads, writes):
        k, v = ev
        for r in reads:
            d = self.readers.setdefault(r, {})
            if d.get(k, 0) < v:
                d[k] = v
        for w in writes:
            self.lastw[w] = ev
            self.readers[w] = {}

    def _waits(self, eng, deps):
        clk = self.clock.setdefault(eng, {})
        waits = []
        for k, v in sorted(deps.items(), key=lambda kv: -self.evorder.get(kv, 0)):
            if eng == "pe" and k == ("e", "pe"):
                continue
            if clk.get(k, 0) >= v:
                continue
            waits.append((k, v))
            for k2, v2 in self.evclock.get((k, v), {}).items():
                if clk.get(k2, 0) < v2:
                    clk[k2] = v2
            clk[k] = v
        return waits

    def op(self, eng, fn, reads=(), writes=()):
        banks = {("psx", k[1]) for k in list(reads) + list(writes) if isinstance(k, tuple) and k and k[0] == "ps"}
        if banks:
            writes = list(writes) + list(banks)
        deps = self._deps(reads, writes)
        waits = self._waits(eng, deps)
        self.cnt[eng] += 1
        ev = (("e", eng), self.cnt[eng])
        self.evclock[ev] = dict(self.clock[eng])
        self.nev += 1
        self.evorder[ev] = self.nev
        self.q[eng].append((waits, fn, "e"))
        self._commit(ev, reads, writes)

    def dma(self, qeng, out, in_, reads=(), writes=(), **kw):
        deps = self._deps(reads, writes)
        idx = self.drr
        self.drr = (self.drr + 1) % len(self.dsem)
        if self.dcnt[idx] > 0:
            k = ("d", idx)
            deps[k] = max(deps.get(k, 0), self.dcnt[idx])
        waits = self._waits(qeng, deps)
        self.dcnt[idx] += 16
        ev = (("d", idx), self.dcnt[idx])
        self.evclock[ev] = dict(self.clock[qeng])
        self.nev += 1
        self.evorder[ev] = self.nev
        self.q[qeng].append((waits, lambda e: e.dma_start(out=out, in_=in_, **kw), idx))
        self._commit(ev, reads, writes)

    def barrier(self):
        allev = [(("e", e), c) for e, c in self.cnt.items() if c > 0]
        allev += [(("d", i), c) for i, c in enumerate(self.dcnt) if c > 0]
        for eng in self.names:
            waits = self._waits(eng, dict(allev))
            if waits:
                self.q[eng].append((waits, None, None))
        self.lastw = {}
        self.readers = {}
        self.evclock = {}

    def emit(self):
        nc = self.nc
        engs = {"pe": "tensor", "act": "scalar", "dve": "vector", "pool": "gpsimd", "sp": "sync"}
        with nc.Block() as block:
            for name in self.names:
                def body(eng, name=name):
                    for waits, fn, kind in self.q[name]:
                        emb = None
                        if fn is not None and waits and kind == "e" and not OPTS.get("noemb"):
                            emb = waits[-1]
                            waits = waits[:-1]
                        for k, v in waits:
                            s = self.sem[k[1]] if k[0] == "e" else self.dsem[k[1]]
                            eng.wait_ge(s, v)
                        if fn is None:
                            continue
                        ins = fn(eng)
                        if emb is not None:
                            k, v = emb
                            ins._wait_ge(self.sem[k[1]] if k[0] == "e" else self.dsem[k[1]], v)
                        if kind == "e":
                            ins.then_inc(self.sem[name], 1)
                        else:
                            ins.then_inc(self.dsem[kind], 16)
                getattr(block, engs[name])(body)


def make_consts():
    c = {}
    i128 = np.arange(128)
    blk = (i128[:, None] // 64) == (i128[None, :] // 64)
    ident = np.eye(128, dtype=np.float32)
    ones = np.ones((128, 128), np.float32)
    SL = ((i128[:, None] > i128[None, :]) & blk).astype(np.float32)
    SU = ((i128[:, None] < i128[None, :]) & blk).astype(np.float32)
    IU = ((i128[:, None] <= i128[None, :]) & blk).astype(np.float32)
    IUfull = (i128[:, None] <= i128[None, :]).astype(np.float32)
    idst = np.concatenate([np.eye(64), np.eye(64)], 0).astype(np.float32)
    reset = np.ones((128, 768), np.float32)
    reset[:, ::64] = 0.0
    rowm = np.zeros((128, 2), np.float32)
    rowm[:64, 0] = 1.0
    rowm[64:, 1] = 1.0
    q512 = np.arange(512)
    cm = np.stack([(q512[None, :] >= (j * 128 + i128[:, None])).astype(np.float32) for j in range(4)], 1)
    parts = [ident, ones, SU, IU, SL, SU, IU, IUfull, idst, reset, rowm, cm.reshape(128, 2048)]
    offs = {}
    o = 0
    for nm, p in zip(["ident", "ones", "mE", "_1", "_2", "mY", "_3", "iuf", "idst", "reset", "rowm", "cm"], parts):
        offs[nm] = o
        o += p.shape[1]
    c["cst"] = np.ascontiguousarray(np.concatenate(parts, 1))
    c["offs"] = offs
    mb = np.zeros((128, 3, 16, 6, 8), np.float32)
    for t in range(16):
        b = t // 2
        for n in range(8):
            mb[:, 0, t, :, n] = 0.0 if n < b else -1e30
            mb[:, 1, t, :, n] = 1.0 if n < b else 0.0
            mb[:, 2, t, :, n] = 1.0 if n == b else 0.0
    c["mobac"] = mb.reshape(128, 3 * 16 * 48)
    bi = np.zeros((8, S), np.float32)
    for n in range(8):
        bi[n, n * 256:(n + 1) * 256] = 1.0
    c["blkind"] = bi
    return c


CONSTS = make_consts()
PARAM_NAMES = ["pre_mix_g", "w_in", "rwkv_mu", "rwkv_w0", "rwkv_w2", "rwkv_a0", "rwkv_a2", "rwkv_g2",
               "rwkv_k_k", "rwkv_k_a", "rwkv_r_k", "rwkv_lnx_g", "rwkv_lnx_b", "gmlp_ln_g", "gmlp_ln_b",
               "gmlp_w_s", "gmlp_b_s", "w_out", "post_mix_g", "pre_ffn_g", "w_ffn_in", "w_ffn_out", "post_ffn_g"]
PARAM_SHAPES = {"pre_mix_g": (L, D), "w_in": (L, D, 3072), "rwkv_mu": (L, 1408), "rwkv_w0": (L, 384),
                "rwkv_w2": (L, 64, 384), "rwkv_a0": (L, 384), "rwkv_a2": (L, 64, 384), "rwkv_g2": (L, 128, 384),
                "rwkv_k_k": (L, 384), "rwkv_k_a": (L, 384), "rwkv_r_k": (L, 6, 64), "rwkv_lnx_g": (L, 384),
                "rwkv_lnx_b": (L, 384), "gmlp_ln_g": (L, 256), "gmlp_ln_b": (L, 256), "gmlp_w_s": (L, 4, 128, 128),
                "gmlp_b_s": (L, 4, 128), "w_out": (L, D, D), "post_mix_g": (L, D), "pre_ffn_g": (L, D),
                "w_ffn_in": (L, D, 2 * DFF), "w_ffn_out": (L, DFF, D), "post_ffn_g": (L, D)}


def build(dbg=None, nlayers=L, stop=None):
    dbg = dbg or []
    nc = bass.Bass("TRN2", target_bir_lowering=False)
    sc = Sched(nc)
    OF = CONSTS["offs"]

    def dram(name, shape, kind="Internal"):
        if name in dbg:
            kind = "ExternalOutput"
        return nc.dram_tensor(name, list(shape), F32, kind=kind).ap()

    x_in = dram("x", [S, D], "ExternalInput")
    y_out = dram("y", [S, D], "ExternalOutput")
    cst_d = dram("cst", CONSTS["cst"].shape, "ExternalInput")
    if not OPTS.get("noparams"):
        PR = {n: dram(n, PARAM_SHAPES[n], "ExternalInput") for n in PARAM_NAMES}
        mobac_d = dram("mobac", CONSTS["mobac"].shape, "ExternalInput")
        blkind_d = dram("blkind", CONSTS["blkind"].shape, "ExternalInput")
    if OPTS.get("noscratch"):
        glob_scr = None
    rwkvT = dram("rwkvT", [1408, S]) if not OPTS.get("noscratch") else None
    qkT = dram("qkT", [768, S]) if not OPTS.get("noscratch") else None
    uT = dram("uT", [256, S]) if not OPTS.get("noscratch") else None
    vm_tm = dram("vm_tm", [S, 384]) if not OPTS.get("noscratch") else None
    vg_tm = dram("vg_tm", [S, 256]) if not OPTS.get("noscratch") else None
    catT = dram("catT", [D, S]) if not OPTS.get("noscratch") else None
    xdbg = dram("xdbg", [D, S]) if not OPTS.get("noscratch") else None
    xD = dram("xD", [D, S])
    xD_v = xD.rearrange("(k p) t -> p k t", p=128)

    uid = [0]

    def sb(es, name, shape):
        uid[0] += 1
        return es.enter_context(nc.sbuf_tensor("%s_%d" % (name, uid[0]), list(shape), F32))

    glob = ExitStack()
    cst = sb(glob, "cst_sb", [128, CONSTS["cst"].shape[1]])
    ps = [glob.enter_context(nc.psum_tensor("ps%d" % i, [128, 512], F32)) for i in range(2)]
    psbig = glob.enter_context(nc.psum_tensor("psbig", [128, 6, 512], F32))
    ps = ps + [psbig[:, h, :] for h in range(6)]
    ident = cst[:, OF["ident"]:OF["ident"] + 128]
    ones = cst[:, OF["ones"]:OF["ones"] + 128]
    mE = cst[:, OF["mE"]:OF["mE"] + 384]
    mY = cst[:, OF["mY"]:OF["mY"] + 256]
    iuf = cst[:, OF["iuf"]:OF["iuf"] + 128]
    idst = cst[:, OF["idst"]:OF["idst"] + 64]
    resetm = cst[:, OF["reset"]:OF["reset"] + 768]
    cm = cst[:, OF["cm"]:OF["cm"] + 2048]
    epsc = sb(glob, "epsc", [128, 4])

    def ACT(out, in_, func, reads, writes, **kw):
        sc.op("act", lambda e: e.activation(out=out, in_=in_, func=func, **kw), reads, writes)

    def RR(ap):
        return ap if OPTS.get("nor") else ap.bitcast(F32R)

    def MM(out, lhsT, rhs, start, stop, reads, writes, r=False):
        if r and not OPTS.get("nor"):
            lhsT = lhsT.bitcast(F32R)
            rhs = rhs.bitcast(F32R)
        sc.op("pe", lambda e: e.matmul(out, lhsT=lhsT, rhs=rhs, start=start, stop=stop), reads, writes)

    def TR(out, in_, idn, reads, writes):
        sc.op("pe", lambda e: e.transpose(out, in_, idn), reads, writes)

    def TT(eng, out, in0, in1, op, reads, writes):
        sc.op(eng, lambda e: e.tensor_tensor(out=out, in0=in0, in1=in1, op=op), reads, writes)

    def TS(eng, out, in0, s1, s2, op0, op1, reads, writes):
        if s2 is None:
            sc.op(eng, lambda e: e.tensor_scalar(out=out, in0=in0, scalar1=s1, scalar2=None, op0=op0), reads, writes)
        else:
            sc.op(eng, lambda e: e.tensor_scalar(out=out, in0=in0, scalar1=s1, scalar2=s2, op0=op0, op1=op1), reads, writes)

    def STT(out, in0, scalar, in1, op0, op1, reads, writes):
        sc.op("dve", lambda e: e.scalar_tensor_tensor(out=out, in0=in0, scalar=scalar, in1=in1, op0=op0, op1=op1), reads, writes)

    def CP(eng, out, in_, reads, writes):
        if eng == "act":
            sc.op("act", lambda e: e.copy(out=out, in_=in_), reads, writes)
        else:
            sc.op(eng, lambda e: e.tensor_copy(out=out, in_=in_), reads, writes)

    def RECIP(out, in_, reads, writes):
        sc.op("dve", lambda e: e.reciprocal(out=out, in_=in_), reads, writes)

    def DMA(out, in_, reads, writes, q="sp", **kw):
        sc.dma(q, out, in_, reads, writes, **kw)

    def colvec(es, name, src_1d, ncol, p=128):
        t = sb(es, name, [p, ncol])
        DMA(t[:, :], src_1d.rearrange("(c p) -> p c", p=p), [], [name], allow_slow_non_contiguous=True)
        return t

    DMA(cst[:, :], cst_d[:, :], [], ["cst"])
    sc.op("dve", lambda e: e.memset(epsc[:, 0:1], 1e-6), [], ["epsc"])
    sc.op("dve", lambda e: e.memset(epsc[:, 1:2], 64e-5), [], ["epsc"])
    sc.op("dve", lambda e: e.memset(epsc[:, 2:3], 0.0), [], ["epsc"])
    with ExitStack() as es:
        xin = sb(es, "xin", [128, 2, D])
        xs = sb(es, "xs", [128, 2, 8, 512])
        for t in range(16):
            sl = t % 2
            xsl = (t // 4) % 2
            DMA(xin[:, sl, :], x_in[t * 128:(t + 1) * 128, :], [], [("xin", sl)])
            for g in range(2):
                bank = ps[(t * 2 + g) % 4]
                bk = ("ps", (t * 2 + g) % 4)
                for kk in range(4):
                    k = g * 4 + kk
                    TR(bank[:, kk * 128:(kk + 1) * 128], xin[:, sl, k * 128:(k + 1) * 128], ident, [("xin", sl), "cst"], [bk])
                CP("act" if g else "dve", xs[:, xsl, g * 4:(g + 1) * 4, (t % 4) * 128:(t % 4 + 1) * 128],
                   bank[:, :].rearrange("p (k c) -> p k c", k=4), [bk], [("xs", xsl, t % 4, g)])
            if t % 4 == 3:
                DMA(xD_v[:, :, (t // 4) * 512:(t // 4 + 1) * 512], xs[:, xsl, :, :], [("xs", xsl, q, g) for q in range(4) for g in range(2)], [("xD", t // 4)])
        sc.barrier()

    def rms_stats(src_fn, src_keys, tt, es_tiles, pbank, pkey):
        sq, rstd = es_tiles
        for k in range(8):
            ACT(sq[:, k % 2, :], src_fn(k), AF.Square, [src_keys(k)], [("sq", k % 2)])
            MM(pbank[:, :], ones, sq[:, k % 2, :], k == 0, k == 7, [("sq", k % 2), "cst"], [pkey])
        ACT(rstd[:, tt % 2, :], pbank[:, :], AF.Sqrt, [pkey, "epsc"], [("rstd", tt % 2)], scale=1.0 / D, bias=epsc[:, 0:1])
        RECIP(rstd[:, tt % 2, :], rstd[:, tt % 2, :], [("rstd", tt % 2)], [("rstd", tt % 2)])
        return rstd[:, tt % 2, :]

    for l in range(nlayers if stop != 'setup' else 0):
        with ExitStack() as es:
            hbuf = sb(es, "hbuf", [128, 8, S])
            sq = sb(es, "sq", [128, 2, 512])
            rstd = sb(es, "rstd", [128, 2, 512])
            gA = colvec(es, "gA", PR["pre_mix_g"][l], 8)
            muA = colvec(es, "muA", PR["rwkv_mu"][l], 11)
            xa = sb(es, "xa", [128, 2, 8, 512])
            for tt in range(4):
                tsl = slice(tt * 512, (tt + 1) * 512)
                xsl = tt % 2
                DMA(xa[:, xsl, :, :], xD_v[:, :, tsl], [("xD", tt)], [("xa", xsl)])
                r = rms_stats(lambda k: xa[:, xsl, k, :], lambda k: ("xa", xsl), tt, (sq, rstd), ps[4 + tt % 2], ("ps", 4 + tt % 2))
                for k in range(8):
                    STT(RR(hbuf[:, k, tsl]), xa[:, xsl, k, :], gA[:, k:k + 1], r, ALU.mult, ALU.mult,
                        [("xa", xsl), ("rstd", tt % 2), "gA"], [("h", k, tt)])
            es_main = es
            es = ExitStack()
            wA = sb(es, "wA", [128, 2, 8, 128])
            stg = sb(es, "stg", [128, 2, S])
            stg2 = sb(es, "stg2", [128, 2, S])
            w_in_l = PR["w_in"][l].rearrange("(k p) c -> p k c", p=128)
            fm_chunks = [(c * 128, rwkvT, c * 128, True) for c in range(11)]
            fm_chunks += [(1408 + c * 128, qkT, c * 128, False) for c in range(6)]
            fm_chunks += [(2560 + c * 128, uT, c * 128, False) for c in range(2)]
            def load_wA(ci):
                col0 = fm_chunks[ci][0]
                DMA(RR(wA[:, ci % 2, :, :]), w_in_l[:, :, col0:col0 + 128], [], [("wA", ci % 2)], q="pool")

            load_wA(0)
            for ci, (col0, dst, row0, shift) in enumerate(fm_chunks):
                sl = ci % 2
                if ci + 1 < len(fm_chunks):
                    load_wA(ci + 1)
                for tt in range(4):
                    tsl = slice(tt * 512, (tt + 1) * 512)
                    bi = (ci * 4 + tt) % 4
                    for k in range(8):
                        MM(ps[bi][:, :], wA[:, sl, k, :], hbuf[:, k, tsl], k == 0, k == 7,
                           [("wA", sl), ("h", k, tt)], [("ps", bi)], r=True)
                    CP("act" if tt % 2 else "dve", stg[:, sl, tsl], ps[bi][:, :], [("ps", bi)], [("stg", sl, tt)])
                allst = [("stg", sl, tt) for tt in range(4)]
                if shift:
                    TT("pool", stg2[:, sl, 1:S], stg[:, sl, 0:S - 1], stg[:, sl, 1:S], ALU.subtract, allst, [("stg2", sl)])
                    TS("pool", stg2[:, sl, 0:1], stg[:, sl, 0:1], -1.0, None, ALU.mult, None, allst, [("stg2", sl)])
                    STT(stg2[:, sl, :], stg2[:, sl, :], muA[:, ci:ci + 1], stg[:, sl, :], ALU.mult, ALU.add,
                        allst + [("stg2", sl), "muA"], [("stg2", sl)])
                    DMA(dst[row0:row0 + 128, :], stg2[:, sl, :], [("stg2", sl)], [("dr", id(dst), row0)], q="pool")
                else:
                    DMA(dst[row0:row0 + 128, :], stg[:, sl, :], allst, [("dr", id(dst), row0)], q="pool")
            sc.barrier()
            es.close()
            es = es_main
            wB = sb(es, "wB", [128, 8, 640])
            DMA(RR(wB[:, :, 0:384]), w_in_l[:, :, 2176:2560], [], ["wB"], q="pool")
            DMA(RR(wB[:, :, 384:640]), w_in_l[:, :, 2816:3072], [], ["wB"], q="pool")
            vst = sb(es, "vst", [128, 2, 640])
            for t in range(16):
                sl = t % 2
                b0, b1 = 4 + (t % 2) * 2, 5 + (t % 2) * 2
                for k in range(8):
                    MM(ps[b0][:, 0:384], hbuf[:, k, t * 128:(t + 1) * 128], wB[:, k, 0:384], k == 0, k == 7,
                       [("h", k, t // 4), "wB"], [("ps", b0)], r=True)
                for k in range(8):
                    MM(ps[b1][:, 0:256], hbuf[:, k, t * 128:(t + 1) * 128], wB[:, k, 384:640], k == 0, k == 7,
                       [("h", k, t // 4), "wB"], [("ps", b1)], r=True)
                CP("act", vst[:, sl, 0:384], ps[b0][:, 0:384], [("ps", b0)], [("vst", sl, 0)])
                CP("dve", vst[:, sl, 384:640], ps[b1][:, 0:256], [("ps", b1)], [("vst", sl, 1)])
                DMA(vm_tm[t * 128:(t + 1) * 128, :], vst[:, sl, 0:384], [("vst", sl, 0)], [("vm", t)], q="pool")
                DMA(vg_tm[t * 128:(t + 1) * 128, :], vst[:, sl, 384:640], [("vst", sl, 1)], [("vg", t)], q="pool")
            sc.barrier()
        if stop == "A":
            break

        with ExitStack() as es:
            lng = sb(es, "lng", [128, 256])
            lnb = sb(es, "lnb", [128, 256])
            DMA(lng[:, :], PR["gmlp_ln_g"][l].partition_broadcast(128), [], ["lng"])
            DMA(lnb[:, :], PR["gmlp_ln_b"][l].partition_broadcast(128), [], ["lnb"])
            wsn = sb(es, "wsn", [128, 4, 128])
            wsT = sb(es, "wsT", [128, 4, 128])
            bsr = sb(es, "bsr", [1, 512])
            DMA(wsn[:, :, :], PR["gmlp_w_s"][l].rearrange("g t s -> t g s"), [], ["wsn"])
            DMA(bsr[:, :], PR["gmlp_b_s"][l].rearrange("g t -> (g t)").partition_broadcast(1), [], ["bsr"])
            for g in range(4):
                TR(ps[0][:, g * 128:(g + 1) * 128], wsn[:, g, :], ident, ["wsn", "cst"], [("ps", 0)])
            for g in range(4):
                TT("dve", wsT[:, g, :], ps[0][:, g * 128:(g + 1) * 128], iuf, ALU.mult, [("ps", 0), "cst"], ["wsT"])
            gu = sb(es, "gu", [128, 2, S])
            t1 = sb(es, "t1", [128, S])
            cout = sb(es, "cout", [128, 2, S])
            for pp in range(2):
                DMA(gu[:, pp, :], uT[pp * 128:(pp + 1) * 128, :], [], [("gu", pp)])
                ACT(t1[:, :], gu[:, pp, :], AF.Square, [("gu", pp)], ["t1"])
                TS("pool", t1[:, :], t1[:, :], 0.044715, 1.0, ALU.mult, ALU.add, ["t1"], ["t1"])
                TT("dve", t1[:, :], t1[:, :], gu[:, pp, :], ALU.mult, ["t1", ("gu", pp)], ["t1"])
                ACT(t1[:, :], t1[:, :], AF.Sigmoid, ["t1"], ["t1"], scale=2.0 * 0.7978845608028654)
                TT("dve", gu[:, pp, :], gu[:, pp, :], t1[:, :], ALU.mult, ["t1", ("gu", pp)], [("gu", pp)])
            vb = sb(es, "vb", [128, 2, 256])
            t2 = sb(es, "t2", [128, 2, 256])
            st6 = sb(es, "st6", [128, 2, 8])
            for c in range(16):
                sl = c % 2
                DMA(vb[:, sl, :], vg_tm[c * 128:(c + 1) * 128, :], [], [("vb", sl)])
                kv, kt = ("vb", sl), ("t2", sl)
                ACT(t2[:, sl, :], vb[:, sl, :], AF.Square, [kv], [kt])
                TS("pool", t2[:, sl, :], t2[:, sl, :], 0.044715, 1.0, ALU.mult, ALU.add, [kt], [kt])
                TT("dve", t2[:, sl, :], t2[:, sl, :], vb[:, sl, :], ALU.mult, [kt, kv], [kt])
                ACT(t2[:, sl, :], t2[:, sl, :], AF.Sigmoid, [kt], [kt], scale=2.0 * 0.7978845608028654)
                TT("dve", vb[:, sl, :], vb[:, sl, :], t2[:, sl, :], ALU.mult, [kt, kv], [kv])
                ks = ("st6", sl)
                sc.op("dve", lambda e, sl=sl: e.bn_stats(out=st6[:, sl, 0:6], in_=vb[:, sl, :]), [kv], [ks])
                sc.op("dve", lambda e, sl=sl: e.bn_aggr(out=st6[:, sl, 6:8], in_=st6[:, sl, 0:6]), [ks], [ks])
                ACT(st6[:, sl, 7:8], st6[:, sl, 7:8], AF.Sqrt, [ks, "epsc"], [ks], bias=epsc[:, 0:1], scale=1.0)
                RECIP(st6[:, sl, 7:8], st6[:, sl, 7:8], [ks], [ks])
                TS("dve", vb[:, sl, :], vb[:, sl, :], st6[:, sl, 6:7], st6[:, sl, 7:8], ALU.subtract, ALU.mult, [kv, ks], [kv])
                TT("dve", vb[:, sl, :], vb[:, sl, :], lng[:, :], ALU.mult, [kv, "lng"], [kv])
                TT("dve", vb[:, sl, :], vb[:, sl, :], lnb[:, :], ALU.add, [kv, "lnb"], [kv])
                for pp in range(2):
                    bi = 1 + (c % 2) * 2 + pp
                    for gg in range(2):
                        g = pp * 2 + gg
                        MM(ps[bi][:, gg * 128:(gg + 1) * 128], vb[:, sl, pp * 128:(pp + 1) * 128], wsT[:, g, :], True, False, [kv, "wsT"], [("ps", bi)])
                        MM(ps[bi][:, gg * 128:(gg + 1) * 128], ones[0:1, :], bsr[0:1, g * 128:(g + 1) * 128], False, True, ["cst", "bsr"], [("ps", bi)])
                    for gg in range(2):
                        TT("dve", cout[gg * 64:(gg + 1) * 64, pp, c * 128:(c + 1) * 128], ps[bi][gg * 64:(gg + 1) * 64, gg * 128:(gg + 1) * 128],
                           gu[gg * 64:(gg + 1) * 64, pp, c * 128:(c + 1) * 128], ALU.mult, [("ps", bi), ("gu", pp)], [("cout", pp)])
            for pp in range(2):
                DMA(catT[768 + pp * 128:768 + (pp + 1) * 128, :], cout[:, pp, :], [("cout", pp)], [("cat", 6 + pp)], q="pool")
            sc.barrier()
        if stop == "D":
            break

        with ExitStack() as es:
            qa = sb(es, "qa", [72, 2, S])
            ka = sb(es, "ka", [72, 2, S])
            vt = sb(es, "vt", [128, 2, 16, 128])
            ones_r = sb(es, "ones_r", [128, 128])
            CP("dve", RR(ones_r[:, :]), ones, ["cst"], ["ones_r"])
            kbar = sb(es, "kbar", [64, 2, 8])
            mobc = sb(es, "mobc", [128, 3, 16, 48])
            NP = sb(es, "NP", [128, 2, 16, 72])
            sm = sb(es, "sm", [128, 16, 8])
            top8 = sb(es, "top8", [128, 16, 8])
            al = sb(es, "al", [128, 16, 8])
            pt = sb(es, "pt", [128, 6, 512])
            pacc = sb(es, "pacc", [128, 512])
            rden = sb(es, "rden", [128, 512])
            ob = sb(es, "ob", [128, 2, 512])
            DMA(mobc[:, :, :, :], mobac_d.rearrange("p (a t c) -> p a t c", a=3, t=16), [], ["mobc"])
            for q in range(2):
                DMA(RR(ka[64:72, q, :]), blkind_d[:, :], [], [("kaB", q)], q="pool")
            sc.op("pool", lambda e: e.memset(NP[:, :, :, :], 0.0), [], [("NP", 0), ("NP", 1)])
            pti = [0]

            def prepA(h):
                q = h % 2
                DMA(RR(qa[0:64, q, :]), qkT[h * 64:(h + 1) * 64, :], [], [("qaQ", q)], q="pool")
                DMA(RR(ka[0:64, q, :]), qkT[384 + h * 64:384 + (h + 1) * 64, :], [], [("kaK", q)], q="pool")
                if h % 2 == 0:
                    vq = (h // 2) % 2
                    DMA(RR(vt[:, vq, :, :]), vm_tm.rearrange("(t p) c -> p t c", p=128)[:, :, h * 64:(h + 2) * 64], [], [("vt", vq)], q="pool")
                sc.op("dve", lambda e: e.tensor_reduce(out=kbar[:, q, :], in_=ka[0:64, q, :].rearrange("p (n k) -> p n k", n=8), axis=AX.X, op=ALU.add), [("kaK", q)], [("kbar", q)])
                for t in range(16):
                    MM(ps[0][:, t * 8:(t + 1) * 8], qa[0:64, q, t * 128:(t + 1) * 128], kbar[:, q, :], True, True, [("qaQ", q), ("kbar", q)], [("ps", 0)])

            def prepB(h):
                q = h % 2
                hs = slice(h * 8, (h + 1) * 8)
                TT("dve", sm[:, :, :], ps[0][:, 0:128].rearrange("p (t n) -> p t n", t=16), mobc[:, 0, :, hs], ALU.add, [("ps", 0), "mobc"], ["sm"])
                for t in range(16):
                    sc.op("dve", lambda e, t=t: e.max(out=top8[:, t, :], in_=sm[:, t, :]), ["sm"], [("top8", t)])
                for t in range(16):
                    TS("dve", al[:, t, :], sm[:, t, :], top8[:, t, 2:3], None, ALU.is_ge, None, ["sm", ("top8", t)], [("al", t)])
                allal = [("al", t) for t in range(16)]
                TT("dve", al[:, :, :], al[:, :, :], mobc[:, 1, :, hs], ALU.mult, allal + ["mobc"], ["al2"])
                TT("dve", al[:, :, :], al[:, :, :], mobc[:, 2, :, hs], ALU.add, ["al2", "mobc"], ["al2"])
                TS("dve", NP[:, q, :, 64:72], al[:, :, :], -1.0, BIG, ALU.add, ALU.mult, ["al2"], [("NP", q)])

            def prepC(h):
                q = h % 2
                for t4 in range(4):
                    for tq in range(4):
                        t = t4 * 4 + tq
                        MM(ps[1][0:72, tq * 128:(tq + 1) * 128], NP[:, q, t, :], ident, True, True, [("NP", q), "cst"], [("ps", 1)])
                    CP("act", RR(qa[64:72, q, t4 * 512:(t4 + 1) * 512]), ps[1][64:72, :], [("ps", 1)], [("qaM", q)])

            def attn(h, qt):
                q = h % 2
                vq = (h // 2) % 2
                hp = slice((h % 2) * 64, (h % 2) * 64 + 64)
                qsl = slice(qt * 512, (qt + 1) * 512)
                nk = (qt + 1) * 4
                osl = qt % 2
                pis = {}

                def qk(kt):
                    sb_i = (2, 3, 6, 7)[kt % 4]
                    pi = pti[0] % 6
                    pti[0] += 1
                    pis[kt] = pi
                    MM(ps[sb_i][:, :], ka[0:72, q, kt * 128:(kt + 1) * 128], qa[0:72, q, qsl], True, True,
                       [("kaK", q), ("kaB", q), ("qaQ", q), ("qaM", q)], [("ps", sb_i)], r=True)
                    ACT(RR(pt[:, pi, :]), ps[sb_i][:, :], AF.Exp, [("ps", sb_i)], [("pt", pi)], scale=0.125)
                    if kt >= qt * 4:
                        j = kt - qt * 4
                        TT("dve", RR(pt[:, pi, :]), pt[:, pi, :], cm[:, j * 512:(j + 1) * 512], ALU.mult, [("pt", pi), "cst"], [("pt", pi)])

                def pv(kt):
                    pi = pis[kt]
                    MM(ps[4][:, :], vt[:, vq, kt, :], pt[:, pi, :], kt == 0, kt == nk - 1, [("vt", vq), ("pt", pi)], [("ps", 4)], r=True)
                    if kt == 0:
                        CP("pool", RR(pacc[:, :]), pt[:, pi, :], [("pt", pi)], ["pacc"])
                    else:
                        TT("pool", RR(pacc[:, :]), pacc[:, :], pt[:, pi, :], ALU.add, [("pt", pi), "pacc"], ["pacc"])

                DEPTH = 3
                for kt in range(min(DEPTH, nk)):
                    qk(kt)
                for kt in range(nk):
                    pv(kt)
                    if kt + DEPTH < nk:
                        qk(kt + DEPTH)
                MM(ps[5][:, :], ones_r[:, :], pacc[:, :], True, True, ["ones_r", "pacc"], [("ps", 5)], r=True)
                RECIP(rden[hp, :], ps[5][hp, :], [("ps", 5)], ["rden"])
                TT("dve", ob[hp, osl, :], ps[4][hp, :], rden[hp, :], ALU.mult, [("ps", 4), "rden"], [("ob", osl)])
                DMA(catT[384 + h * 64:384 + (h + 1) * 64, qsl], ob[hp, osl, :], [("ob", osl)], [("cat", "b", h, qt)])

            prepA(0)
            prepB(0)
            prepC(0)
            for h in range(H):
                for qt in range(4):
                    attn(h, qt)
                    if h + 1 < H:
                        if qt == 0:
                            prepA(h + 1)
                        elif qt == 1:
                            prepB(h + 1)
                        elif qt == 2:
                            prepC(h + 1)
            sc.barrier()
        if stop == "C":
            break

        with ExitStack() as es:
            TW = 128
            NTB = S // TW
            w2s = sb(es, "w2s", [64, 384])
            a2s = sb(es, "a2s", [64, 384])
            g2s = sb(es, "g2s", [128, 384])
            DMA(w2s[:, :], PR["rwkv_w2"][l], [], ["w2s"])
            DMA(a2s[:, :], PR["rwkv_a2"][l], [], ["a2s"])
            DMA(g2s[:, :], PR["rwkv_g2"][l], [], ["g2s"])
            pw0 = colvec(es, "pw0", PR["rwkv_w0"][l], 6, p=64)
            pa0 = colvec(es, "pa0", PR["rwkv_a0"][l], 6, p=64)
            pkk = colvec(es, "pkk", PR["rwkv_k_k"][l], 6, p=64)
            pka = colvec(es, "pka", PR["rwkv_k_a"][l], 6, p=64)
            prk = colvec(es, "prk", PR["rwkv_r_k"][l].rearrange("h d -> (h d)"), 6, p=64)
            plg = colvec(es, "plg", PR["rwkv_lnx_g"][l], 6, p=64)
            plb = colvec(es, "plb", PR["rwkv_lnx_b"][l], 6, p=64)
            pok = sb(es, "pok", [64, 6])
            TS("dve", pok[:, :], pka[:, :], -1.0, 1.0, ALU.mult, ALU.add, ["pka"], ["pok"])
            i64 = ident[0:64, 0:64]
            o64 = ones[0:64, 0:64]
            rowm = cst[:, OF["rowm"]:OF["rowm"] + 2]
            Mst = sb(es, "Mst", [64, 2, 6, 64])
            sc.op("dve", lambda e: e.memset(Mst[:, 0, :, :], 0.0), [], [("Mst", 0)])
            mcur = [0]
            RhatT = sb(es, "RhatT", [64, 6, TW])
            Y0T = sb(es, "Y0T", [64, 6, TW])
            yT = sb(es, "yT", [64, 6, TW])
            GT = sb(es, "GT", [64, 6, 2, 64])
            Hm = sb(es, "Hm", [64, 6, 2, 64])
            bon = sb(es, "bon", [64, 3, 6, TW]); gal = sb(es, "gal", [64, 3, 6, TW])
            ARh = sb(es, "ARh", [64, 2, 6, 2, TW]); BKh = sb(es, "BKh", [64, 2, 6, 2, TW]); BPh = sb(es, "BPh", [64, 2, 6, 2, TW])
            vvh = sb(es, "vvh", [64, 2, 6, TW]); pC = sb(es, "pC", [64, 2, 6, 2])
            wd = sb(es, "wd", [64, 2, TW]); ad = sb(es, "ad", [64, 2, TW]); gd = sb(es, "gd", [128, 2, TW])
            T6 = lambda nm: sb(es, nm, [64, 6, TW])
            rr = T6("rr"); kq = T6("kq"); sig = T6("sig"); cum = T6("cum"); cpv = T6("cpv")
            epos = T6("epos"); eneg = T6("eneg"); eprv = T6("eprv"); eend = T6("eend")
            aa = T6("aa"); kk = T6("kk"); kk2 = cpv; rn = T6("rn"); kka = T6("kka"); kp = T6("kp"); rkr = T6("rkr")
            nbc = sb(es, "nbc", [64, 6, 2])
            tm = sb(es, "tm", [128, 6, 256]); Eb = sb(es, "Eb", [128, 6, 512]); YS = sb(es, "YS", [128, 6, 256])
            Lb = sb(es, "Lb", [128, 6, 2, 384]); B2 = sb(es, "B2", [128, 6, 128]); K2 = sb(es, "K2", [128, 6, 128])
            yc = sb(es, "yc", [64, 6, TW]); ysq = sb(es, "ysq", [64, 6, TW]); yrs = sb(es, "yrs", [64, 6, TW])
            obuf = sb(es, "obuf", [64, 6, TW])

            def tile_pro(tt):
                tsl = slice(tt * TW, (tt + 1) * TW)
                ws = tt % 2
                DMA(wd[:, ws, :], rwkvT[1152:1216, tsl], [], [("wd", ws)])
                DMA(ad[:, ws, :], rwkvT[1216:1280, tsl], [], [("ad", ws)])
                DMA(gd[:, ws, :], rwkvT[1280:1408, tsl], [], [("gd", ws)])
                yield
                ACT(wd[:, ws, :], wd[:, ws, :], AF.Tanh, [("wd", ws)], [("wd", ws)])
                ACT(gd[:, ws, :], gd[:, ws, :], AF.Sigmoid, [("gd", ws)], [("gd", ws)])
                yield

            def stage0_all(tt):
                tsl = slice(tt * TW, (tt + 1) * TW)
                ws = tt % 2
                w3 = tt % 3
                HH = range(H)
                fl = lambda t: t[:, :, :].rearrange("p h t -> p (h t)")
                A6 = lambda nm: [(nm, ws, h) for h in HH]
                hv = lambda base: rwkvT[base:base + 384, tsl].rearrange("(h p) t -> p h t", p=64)
                DMA(rr[:, :, :], hv(0), [], ["rr"])
                DMA(kq[:, :, :], hv(384), [], ["kq"])
                DMA(vvh[:, ws, :, :], hv(768), [], A6("vv"))
                yield
                pbh = lambda h: (ps[0], ("ps", 0), h * TW) if h < 4 else (ps[1], ("ps", 1), (h - 4) * TW)
                for h in HH:
                    pb, pk, c0 = pbh(h)
                    MM(pb[0:64, c0:c0 + TW], w2s[:, h * 64:(h + 1) * 64], wd[:, ws, :], True, True, ["w2s", ("wd", ws)], [pk])
                for h in HH:
                    pb, pk, c0 = pbh(h)
                    ACT(sig[:, h, :], pb[0:64, c0:c0 + TW], AF.Sigmoid, [pk, "pw0"], [("sig", h)], bias=pw0[:, h:h + 1])
                yield
                allsig = [("sig", h) for h in HH]
                sc.op("dve", lambda e: e.tensor_tensor_scan(out=fl(cum), data0=resetm[0:64, 0:6 * TW], data1=fl(sig), initial=0.0, op0=ALU.mult, op1=ALU.add), allsig + ["cst"], ["cum"])
                yield
                TT("pool", fl(cpv), fl(cum), fl(sig), ALU.subtract, ["cum"] + allsig, ["cpv"])
                ACT(fl(epos), fl(cum), AF.Exp, ["cum"], ["epos"], scale=-C0)
                ACT(fl(eneg), fl(cum), AF.Exp, ["cum"], ["eneg"], scale=C0)
                cum12 = fl(cum).rearrange("p (c t) -> p c t", t=64)
                epos12 = fl(epos).rearrange("p (c t) -> p c t", t=64)
                nbc12 = nbc[:, :, :].rearrange("p h c -> p (h c)")
                TS("dve", nbc12, cum12[:, :, 63], -C0, None, ALU.mult, None, ["cum"], ["nbc"])
                yield
                ACT(fl(eprv), fl(cpv), AF.Exp, ["cpv"], ["eprv"], scale=-C0)
                CP("pool", pC[:, ws, :, :].rearrange("p h c -> p (h c)"), epos12[:, :, 63], ["epos"], A6("pC"))
                for h in HH:
                    for c in range(2):
                        ACT(eend[:, h, c * 64:(c + 1) * 64], cum[:, h, c * 64:(c + 1) * 64], AF.Exp, ["cum", "nbc"], [("eend", h)], scale=C0, bias=nbc[:, h, c:c + 1])
                    if h % 2:
                        yield
                for h in HH:
                    pb, pk, c0 = pbh(h)
                    MM(pb[0:64, c0:c0 + TW], a2s[:, h * 64:(h + 1) * 64], ad[:, ws, :], True, True, ["a2s", ("ad", ws)], [pk])
                for h in HH:
                    pb, pk, c0 = pbh(h)
                    ACT(aa[:, h, :], pb[0:64, c0:c0 + TW], AF.Sigmoid, [pk, "pa0"], [("aa", h)], bias=pa0[:, h:h + 1])
                yield
                for h in HH:
                    pb, pk, c0 = pbh(h)
                    MM(pb[0:64, c0:c0 + TW], g2s[:, h * 64:(h + 1) * 64], gd[:, ws, :], True, True, ["g2s", ("gd", ws)], [pk])
                CP("act", gal[:, w3, 0:4, :].rearrange("p h t -> p (h t)"), ps[0][0:64, 0:512], [("ps", 0)], [("gal", w3, h) for h in range(4)])
                CP("act", gal[:, w3, 4:6, :].rearrange("p h t -> p (h t)"), ps[1][0:64, 0:256], [("ps", 1)], [("gal", w3, h) for h in range(4, 6)])
                yield
                for h in HH:
                    TS("dve", kk[:, h, :], kq[:, h, :], pkk[:, h:h + 1], None, ALU.mult, None, ["kq", "pkk"], [("kk", h)])
                yield
                allkk = [("kk", h) for h in HH]
                ACT(fl(kk2), fl(kk), AF.Square, allkk + ["cpv"], ["cpv"])
                yield
                for h in HH:
                    pb, pk, c0 = pbh(h)
                    MM(pb[0:64, c0:c0 + TW], o64, kk2[:, h, :], True, True, ["cst", "cpv"], [pk])
                ACT(rn[:, 0:4, :].rearrange("p h t -> p (h t)"), ps[0][0:64, 0:512], AF.Sqrt, [("ps", 0)], [("rn", 0)])
                ACT(rn[:, 4:6, :].rearrange("p h t -> p (h t)"), ps[1][0:64, 0:256], AF.Sqrt, [("ps", 1)], [("rn", 1)])
                yield
                krn = [("rn", 0), ("rn", 1)]
                TS("dve", fl(rn), fl(rn), 1e-12, None, ALU.max, None, krn, krn)
                yield
                RECIP(fl(rn), fl(rn), krn, krn)
                yield
                TT("dve", fl(kk), fl(kk), fl(rn), ALU.mult, allkk + krn, ["kkn"])
                yield
                allaa = [("aa", h) for h in HH]
                TT("pool", fl(kka), fl(kk), fl(aa), ALU.mult, ["kkn"] + allaa, ["kka"])
                for h in HH:
                    TS("dve", rn[:, h, :], aa[:, h, :], pka[:, h:h + 1], pok[:, h:h + 1], ALU.mult, ALU.add, [("aa", h), "pka", "pok", "kkn"] + krn, [("fac", h)])
                STT(RR(ARh[:, ws, :, 0, :]), kk[:, :, :], -1.0, eprv[:, :, :], ALU.mult, ALU.mult, ["kkn", "eprv"], A6("AR0"))
                yield
                allfac = [("fac", h) for h in HH]
                TT("dve", fl(kp), fl(kq), fl(rn), ALU.mult, ["kq"] + allfac, ["kp"])
                TT("pool", RR(ARh[:, ws, :, 1, :]), rr[:, :, :], epos[:, :, :], ALU.mult, ["rr", "epos"], A6("AR1"))
                yield
                for h in HH:
                    STT(rkr[:, h, :], rr[:, h, :], prk[:, h:h + 1], kp[:, h, :], ALU.mult, ALU.mult, ["rr", "prk", "kp"], [("rkr", h)])
                TT("pool", RR(BKh[:, ws, :, 0, :]), kka[:, :, :], eneg[:, :, :], ALU.mult, ["kka", "eneg"], A6("BK0"))
                TT("pool", RR(BKh[:, ws, :, 1, :]), kp[:, :, :], eneg[:, :, :], ALU.mult, ["kp", "eneg"], A6("BK1"))
                yield
                for h in HH:
                    pb, pk, c0 = pbh(h)
                    MM(pb[0:64, c0:c0 + TW], o64, rkr[:, h, :], True, True, ["cst", ("rkr", h)], [pk])
                TT("dve", bon[:, w3, 0:4, :].rearrange("p h t -> p (h t)"), ps[0][0:64, 0:512], vvh[:, ws, 0:4, :].rearrange("p h t -> p (h t)"), ALU.mult,
                   [("ps", 0)] + A6("vv"), [("bon", w3, h) for h in range(4)])
                TT("dve", bon[:, w3, 4:6, :].rearrange("p h t -> p (h t)"), ps[1][0:64, 0:256], vvh[:, ws, 4:6, :].rearrange("p h t -> p (h t)"), ALU.mult,
                   [("ps", 1)] + A6("vv"), [("bon", w3, h) for h in range(4, 6)])
                alleend = [("eend", h) for h in HH]
                TT("pool", BPh[:, ws, :, 0, :], kka[:, :, :], eend[:, :, :], ALU.mult, ["kka"] + alleend, A6("BP"))
                TT("pool", BPh[:, ws, :, 1, :], kp[:, :, :], eend[:, :, :], ALU.mult, ["kp"] + alleend, A6("BP"))
                yield

            def make_gens(tt):
                if tt >= NTB:
                    return []
                return [tile_pro(tt), stage0_all(tt)]

            def advance(gl, n):
                for _ in range(n):
                    for g in list(gl):
                        try:
                            next(g)
                        except StopIteration:
                            gl.remove(g)

            def chain(tt, nxt):
                ws = tt % 2
                tsl = slice(tt * TW, (tt + 1) * TW)
                HS = range(H)
                GR = [(0, 3), (3, 6)]
                bk = lambda h: ps[2 + h]
                bkk = lambda h: ("ps", 2 + h)
                gk = lambda g: [bkk(h) for h in range(*g)]
                gkey = lambda nm, g: [(nm, h) for h in range(*g)]
                A0 = lambda h: ("AR0", ws, h)
                A1 = lambda h: ("AR1", ws, h)
                bc = lambda ap, n: ap.unsqueeze(1).to_broadcast([ap.shape[0], n, ap.shape[1]])
                for g in GR:
                    for h in range(*g):
                        for i, (src, kx) in enumerate([(ARh[:, ws, h, 0, :], A0(h)), (BPh[:, ws, h, 0, :], ("BP", ws, h)), (BPh[:, ws, h, 1, :], ("BP", ws, h)), (vvh[:, ws, h, :], ("vv", ws, h))]):
                            TR(bk(h)[:, i * 64:(i + 1) * 64], src, i64, [kx, "cst"], [bkk(h)])
                for g in GR:
                    CP("act", tm[:, g[0]:g[1], :], psbig[:, g[0]:g[1], 0:256], gk(g), gkey("tm", g))
                advance(nxt, 2)
                for g in GR:
                    for h in range(*g):
                        MM(bk(h)[:, 0:256], BKh[:, ws, h, 0, :], ARh[:, ws, h, :, :], True, True, [("BK0", ws, h), A0(h), A1(h)], [bkk(h)], r=True)
                        MM(bk(h)[:, 256:384], ARh[:, ws, h, 0, :], BKh[:, ws, h, 0, :], True, True, [("BK0", ws, h), A0(h)], [bkk(h)], r=True)
                for g in GR:
                    TT("dve", RR(Eb[:, g[0]:g[1], 0:384]), psbig[:, g[0]:g[1], 0:384], bc(mE, 3), ALU.mult, gk(g) + ["cst"], [("E", h, "a") for h in range(*g)])
                advance(nxt, 2)
                for g in GR:
                    for h in range(*g):
                        MM(bk(h)[:, 0:256], BKh[:, ws, h, 1, :], ARh[:, ws, h, :, :], True, True, [("BK1", ws, h), A0(h), A1(h)], [bkk(h)], r=True)
                for g in GR:
                    TT("dve", YS[:, g[0]:g[1], :], psbig[:, g[0]:g[1], 0:256], bc(mY, 3), ALU.mult, gk(g) + ["cst"], gkey("YS", g))
                advance(nxt, 2)
                for g in GR:
                    for h in range(*g):
                        MM(bk(h)[:, 256:320], YS[:, h, 0:128], tm[:, h, 192:256], True, True, [("YS", h), ("tm", h)], [bkk(h)])
                    CP("pool", RR(Eb[:, g[0]:g[1], 384:448]), tm[:, g[0]:g[1], 0:64], gkey("tm", g), [("E", h, "b") for h in range(*g)])
                for g in GR:
                    CP("act", RR(Eb[:, g[0]:g[1], 448:512]), psbig[:, g[0]:g[1], 256:320], gk(g), [("E", h, "c") for h in range(*g)])
                advance(nxt, 2)
                for lev in range(6):
                    def views(h):
                        if lev == 0:
                            return (Eb[:, h, 0:128], Eb[:, h, 256:384], Eb[:, h, 256:512],
                                    [("E", h, "a"), ("E", h, "b"), ("E", h, "c")])
                        Lp = Lb[:, h, (lev - 1) % 2, :]
                        return (Lp[:, 0:128], Lp[:, 128:256], Lp[:, 128:384],
                                [("L", h, (lev - 1) % 2, "p"), ("L", h, (lev - 1) % 2, "z")])
                    for g in GR:
                        for h in range(*g):
                            PT_, P_, PZ_, rk = views(h)
                            MM(bk(h)[:, 128:384], PT_, PZ_, True, True, rk, [bkk(h)], r=True)
                            if lev < 5:
                                MM(bk(h)[:, 0:128], P_, PT_, True, True, rk, [bkk(h)], r=True)
                    for g in GR:
                        gs = slice(g[0], g[1])
                        if lev == 0:
                            Zg = Eb[:, gs, 384:512]
                            rkg = [("E", h, x) for h in range(*g) for x in "abc"]
                        else:
                            Zg = Lb[:, gs, (lev - 1) % 2, 256:384]
                            rkg = [("L", h, (lev - 1) % 2, x) for h in range(*g) for x in "pz"]
                        TT("dve", RR(Lb[:, gs, lev % 2, 256:384]), psbig[:, gs, 256:384], Zg, ALU.add, gk(g) + rkg, [("L", h, lev % 2, "z") for h in range(*g)])
                        if lev < 5:
                            CP("act", RR(Lb[:, gs, lev % 2, 0:256]), psbig[:, gs, 0:256], gk(g), [("L", h, lev % 2, "p") for h in range(*g)])
                    advance(nxt, 3)
                for g in GR:
                    gs = slice(g[0], g[1])
                    for hf in range(2):
                        TS("pool", B2[:, gs, hf * 64:(hf + 1) * 64], tm[:, gs, 64:128], rowm[:, hf:hf + 1], None, ALU.mult, None, gkey("tm", g) + ["cst"], gkey("B2", g))
                        TS("pool", K2[:, gs, hf * 64:(hf + 1) * 64], tm[:, gs, 128:192], rowm[:, hf:hf + 1], None, ALU.mult, None, gkey("tm", g) + ["cst"], gkey("K2", g))
                for g in GR:
                    for h in range(*g):
                        Lf = Lb[:, h, 1, :]
                        kLf = ("L", h, 1, "z")
                        W_, U0_ = Lf[:, 256:320], Lf[:, 320:384]
                        b_ = bk(h)
                        MM(b_[0:64, 0:128], W_, B2[:, h, :], True, True, [kLf, ("B2", h)], [bkk(h)])
                        for hf in range(2):
                            MM(b_[0:64, 128 + hf * 64:128 + (hf + 1) * 64], K2[:, h, hf * 64:(hf + 1) * 64], tm[:, h, 192:256], True, False, [("K2", h), ("tm", h)], [bkk(h)])
                            MM(b_[0:64, 128 + hf * 64:128 + (hf + 1) * 64], B2[:, h, hf * 64:(hf + 1) * 64], U0_, False, True, [("B2", h), kLf], [bkk(h)])
                        MM(b_[0:64, 256:384], W_, Eb[:, h, 128:256], True, True, [kLf, ("E", h, "a")], [bkk(h)])
                        MM(b_[0:64, 384:512], tm[:, h, 192:256], YS[:, h, 128:256], True, False, [("tm", h), ("YS", h)], [bkk(h)])
                        MM(b_[0:64, 384:512], U0_, Eb[:, h, 128:256], False, True, [kLf, ("E", h, "a")], [bkk(h)])
                advance(nxt, 2)
                for g in GR:
                    gs = slice(g[0], g[1])
                    for h in range(*g):
                        b_ = bk(h)
                        for hf in range(2):
                            STT(GT[:, h, hf, :], i64, pC[:, ws, h, hf:hf + 1], b_[0:64, hf * 64:(hf + 1) * 64], ALU.mult, ALU.add,
                                [bkk(h), ("pC", ws, h), "cst"], [("GT", h)])
                    TT("dve", RhatT[:, gs, :], psbig[0:64, gs, 256:384], ARh[:, ws, gs, 1, :], ALU.add, gk(g) + [A1(h) for h in range(*g)], gkey("Rhat", g))
                for g in GR:
                    gs = slice(g[0], g[1])
                    CP("act", Hm[:, gs, :, :].rearrange("p h c i -> p h (c i)"), psbig[0:64, gs, 128:256], gk(g), gkey("Hm", g))
                    CP("act", Y0T[:, gs, :], psbig[0:64, gs, 384:512], gk(g), gkey("Y0T", g))
                advance(nxt, 2)
                allh = lambda nm: [(nm, h) for h in range(H)]
                for c in range(2):
                    csl = slice(c * 64, (c + 1) * 64)
                    m0 = mcur[0]
                    mnew = 1 - m0
                    for h in range(H):
                        MM(ps[0][0:64, h * 64:(h + 1) * 64], Mst[:, m0, h, :], RhatT[:, h, csl], True, True, [("Mst", m0), ("Rhat", h)], [("ps", 0)])
                    for h in range(H):
                        MM(ps[1][0:64, h * 64:(h + 1) * 64], GT[:, h, c, :], Mst[:, m0, h, :], True, True, [("Mst", m0), ("GT", h)], [("ps", 1)])
                    TT("dve", Mst[:, mnew, :, :], ps[1][0:64, 0:384].rearrange("p (h i) -> p h i", h=6), Hm[:, :, c, :], ALU.add,
                       [("ps", 1)] + allh("Hm"), [("Mst", mnew)])
                    TT("dve", yT[:, :, csl], ps[0][0:64, 0:384].rearrange("p (h t) -> p h t", h=6), Y0T[:, :, csl], ALU.add,
                       [("ps", 0)] + allh("Y0T"), [("yT", c)])
                    mcur[0] = mnew
                    advance(nxt, 1)
                advance(nxt, 1000)

            def post(tt, h):
                tsl = slice(tt * TW, (tt + 1) * TW)
                w3 = tt % 3
                ally = [("yT", c) for c in range(2)]
                osl = h
                kyc, kysq, kyrs = ("yc", osl), ("ysq", osl), ("yrs", osl)
                pb = ps[h % 2]
                pk = ("ps", h % 2)
                MM(pb[0:64, 0:TW], o64, yT[:, h, :], True, True, ["cst"] + ally, [pk])
                STT(yc[:, osl, :], pb[0:64, 0:TW], -1.0 / 64, yT[:, h, :], ALU.mult, ALU.add, [pk] + ally, [kyc])
                yield
                ACT(ysq[:, osl, :], yc[:, osl, :], AF.Square, [kyc], [kysq])
                yield
                MM(pb[0:64, 128:128 + TW], o64, ysq[:, osl, :], True, True, ["cst", kysq], [pk])
                ACT(yrs[:, osl, :], pb[0:64, 128:128 + TW], AF.Sqrt, [pk, "epsc"], [kyrs], scale=1.0 / 64, bias=epsc[0:64, 1:2])
                yield
                RECIP(yrs[:, osl, :], yrs[:, osl, :], [kyrs], [kyrs])
                yield
                TT("dve", yc[:, osl, :], yc[:, osl, :], yrs[:, osl, :], ALU.mult, [kyc, kyrs], [kyc])
                yield
                TS("dve", yc[:, osl, :], yc[:, osl, :], plg[:, h:h + 1], plb[:, h:h + 1], ALU.mult, ALU.add, [kyc, "plg", "plb"], [kyc])
                yield
                TT("pool", yc[:, osl, :], yc[:, osl, :], bon[:, w3, h, :], ALU.add, [kyc, ("bon", w3, h)], [kyc])
                yield
                TT("pool", obuf[:, osl, :], yc[:, osl, :], gal[:, w3, h, :], ALU.mult, [kyc, ("gal", w3, h)], [("obuf", osl)])
                DMA(catT[h * 64:(h + 1) * 64, tsl], obuf[:, osl, :], [("obuf", osl)], [("cat", "a", h, tt)])
                yield

            g0 = make_gens(0)
            advance(g0, 1000)
            for tt in range(NTB):
                pg = [post(tt - 1, h) for h in range(H)] if tt > 0 else []
                chain(tt, pg + make_gens(tt + 1))
            advance([post(NTB - 1, h) for h in range(H)], 1000)
            sc.barrier()
        if stop == "B":
            break

        with ExitStack() as es:
            wO = sb(es, "wO", [128, 2, 8, 128])
            catb = sb(es, "catb", [128, 2, 8, 512])
            mixb = sb(es, "mixb", [128, 8, 512])
            sq = sb(es, "sq", [128, 2, 512])
            rstd = sb(es, "rstd", [128, 2, 512])
            gP = colvec(es, "gP", PR["post_mix_g"][l], 8)
            w_out_l = PR["w_out"][l].rearrange("(k p) c -> p k c", p=128)
            cat_v = catT.rearrange("(k p) t -> p k t", p=128)
            wi = 0
            xe = sb(es, "xe", [128, 2, 8, 512])
            for tt in range(4):
                tsl = slice(tt * 512, (tt + 1) * 512)
                cs = tt % 2
                DMA(xe[:, cs, :, :], xD_v[:, :, tsl], [("xD", tt)], [("xe", cs, j) for j in range(8)])
                DMA(RR(catb[:, cs, :, :]), cat_v[:, :, tsl], [], [("catb", cs)], q="pool")
                for j in range(8):
                    sl = wi % 2
                    wi += 1
                    DMA(RR(wO[:, sl, :, :]), w_out_l[:, :, j * 128:(j + 1) * 128], [], [("wO", sl)], q="pool")
                    bi = j % 2
                    for k in range(8):
                        MM(ps[bi][:, :], wO[:, sl, k, :], catb[:, cs, k, :], k == 0, k == 7, [("wO", sl), ("catb", cs)], [("ps", bi)], r=True)
                    CP("act" if j % 2 else "dve", mixb[:, j, :], ps[bi][:, :], [("ps", bi)], [("mixb", j)])
                r = rms_stats(lambda k: mixb[:, k, :], lambda k: ("mixb", k), tt, (sq, rstd), ps[2 + tt % 2], ("ps", 2 + tt % 2))
                for j in range(8):
                    STT(mixb[:, j, :], mixb[:, j, :], gP[:, j:j + 1], r, ALU.mult, ALU.mult, [("mixb", j), ("rstd", tt % 2), "gP"], [("mixb", j)])
                    TT("dve", xe[:, cs, j, :], xe[:, cs, j, :], mixb[:, j, :], ALU.add, [("mixb", j), ("xe", cs, j)], [("xe", cs, j)])
                DMA(xD_v[:, :, tsl], xe[:, cs, :, :], [("xe", cs, j) for j in range(8)], [("xD", tt)])
            sc.barrier()
        if stop == "E":
            break

        with ExitStack() as es:
            TF = 1024
            hb = sb(es, "hb", [128, 8, TF])
            actb = sb(es, "actb", [128, NFF, TF])
            shm = sb(es, "shm", [128, 8 * TF])
            xf = shm[:, :].rearrange("p (k t) -> p k t", k=8)
            wD = shm[:, 0:2 * NFF * 128].rearrange("p (s j c) -> p s j c", s=2, j=NFF)
            wG = sb(es, "wG", [128, 2, 2, 8, 128])
            sq = sb(es, "sq", [128, 2, 512])
            rstd = sb(es, "rstd", [128, 2, 512])
            sil = sb(es, "sil", [128, 2, 512])
            xc = sb(es, "xc", [128, 2, TF])
            gF = colvec(es, "gF", PR["pre_ffn_g"][l], 8)
            gQ = colvec(es, "gQ", PR["post_ffn_g"][l], 8)
            w_fi = PR["w_ffn_in"][l].rearrange("(k p) c -> p k c", p=128)
            w_fo = PR["w_ffn_out"][l].rearrange("(j p) c -> p j c", p=128)
            wi = 0
            wdi = 0
            si = 0
            xci = 0
            SHK = ["sh", ("wD", 0), ("wD", 1)]
            for tt in range(S // TF):
                tsl = slice(tt * TF, (tt + 1) * TF)
                DMA(xf, xD_v[:, :, tsl], [("xD", 2 * tt), ("xD", 2 * tt + 1)], SHK)
                for hf in range(2):
                    hsl = slice(hf * 512, (hf + 1) * 512)
                    r = rms_stats(lambda k: xf[:, k, hsl], lambda k: "sh", si, (sq, rstd), ps[6 + si % 2], ("ps", 6 + si % 2))
                    for k in range(8):
                        STT(RR(hb[:, k, hsl]), xf[:, k, hsl], gF[:, k:k + 1], r, ALU.mult, ALU.mult, SHK + [("rstd", si % 2), "gF"], [("hb", k, hf)])
                    si += 1
                for j in range(NFF):
                    sl = wi % 2
                    wi += 1
                    DMA(RR(wG[:, sl, 0, :, :]), w_fi[:, :, j * 128:(j + 1) * 128], [], [("wG", sl, 0)], q="pool")
                    DMA(RR(wG[:, sl, 1, :, :]), w_fi[:, :, DFF + j * 128:DFF + (j + 1) * 128], [], [("wG", sl, 1)], q="pool")
                    for hf in range(2):
                        hsl = slice(hf * 512, (hf + 1) * 512)
                        bg, bu = hf * 2, hf * 2 + 1
                        for k in range(8):
                            MM(ps[bg][:, :], wG[:, sl, 0, k, :], hb[:, k, hsl], k == 0, k == 7, [("wG", sl, 0), ("hb", k, hf)], [("ps", bg)], r=True)
                        for k in range(8):
                            MM(ps[bu][:, :], wG[:, sl, 1, k, :], hb[:, k, hsl], k == 0, k == 7, [("wG", sl, 1), ("hb", k, hf)], [("ps", bu)], r=True)
                        ACT(sil[:, hf, :], ps[bg][:, :], AF.Silu, [("ps", bg)], [("sil", hf)])
                        TT("dve", RR(actb[:, j, hsl]), ps[bu][:, :], sil[:, hf, :], ALU.mult, [("ps", bu), ("sil", hf)], [("actb", j, hf)])
                for jo in range(8):
                    sl = wdi % 2
                    wdi += 1
                    DMA(RR(wD[:, sl, :, :]), w_fo[:, :, jo * 128:(jo + 1) * 128], [], [("wD", sl)], q="pool")
                    for hf in range(2):
                        hsl = slice(hf * 512, (hf + 1) * 512)
                        bi = 4 + hf
                        for j in range(NFF):
                            MM(ps[bi][:, :], wD[:, sl, j, :], actb[:, j, hsl], j == 0, j == NFF - 1, [("wD", sl), ("actb", j, hf)], [("ps", bi)], r=True)
                        CP("act" if hf else "dve", RR(hb[:, jo, hsl]), ps[bi][:, :], [("ps", bi)], [("hb", jo, hf)])
                for hf in range(2):
                    hsl = slice(hf * 512, (hf + 1) * 512)
                    r = rms_stats(lambda k: hb[:, k, hsl], lambda k: ("hb", k, hf), si, (sq, rstd), ps[6 + si % 2], ("ps", 6 + si % 2))
                    for j in range(8):
                        STT(RR(hb[:, j, hsl]), hb[:, j, hsl], gQ[:, j:j + 1], r, ALU.mult, ALU.mult, [("hb", j, hf), ("rstd", si % 2), "gQ"], [("hb", j, hf)])
                    si += 1
                for j in range(8):
                    cs = xci % 2
                    xci += 1
                    DMA(xc[:, cs, :], xD[j * 128:(j + 1) * 128, tsl], [("xD", 2 * tt), ("xD", 2 * tt + 1)], [("xc", cs)])
                    TT("pool" if j % 2 else "dve", xc[:, cs, :], xc[:, cs, :], hb[:, j, :], ALU.add, [("xc", cs), ("hb", j, 0), ("hb", j, 1)], [("xc", cs)])
                    DMA(xD[j * 128:(j + 1) * 128, tsl], xc[:, cs, :], [("xc", cs)], [("xDw", tt, j)])
            sc.barrier()
        if OPTS.get("xdbg") == l:
            DMA(xdbg[:, :], xD[:, :], [("xD", tt) for tt in range(4)], [("xdbg", 0)])

    if stop in (None, 'setup'):
        with ExitStack() as es:
            yo = sb(es, "yo", [128, 2, D])
            xo = sb(es, "xo", [128, 2, 8, 512])
            for t in range(16):
                sl = t % 2
                xsl = (t // 4) % 2
                if t % 4 == 0:
                    DMA(xo[:, xsl, :, :], xD_v[:, :, (t // 4) * 512:(t // 4 + 1) * 512], [("xD", t // 4)], [("xo", xsl)])
                for g in range(2):
                    bank = ps[(t * 2 + g) % 4]
                    bk = ("ps", (t * 2 + g) % 4)
                    for kk in range(4):
                        k = g * 4 + kk
                        TR(bank[:, kk * 128:(kk + 1) * 128], xo[:, xsl, k, (t % 4) * 128:(t % 4 + 1) * 128], ident, [("xo", xsl), "cst"], [bk])
                    CP("act" if g else "dve", yo[:, sl, g * 512:(g + 1) * 512], bank[:, :], [bk], [("yo", sl, g)])
                DMA(y_out[t * 128:(t + 1) * 128, :], yo[:, sl, :], [("yo", sl, 0), ("yo", sl, 1)], [("y", t)])
    sc.barrier()
    sc.emit()
    glob.close()
    return nc


_NC_CACHE = {}


def kernel(**inputs):
    if "nc" not in _NC_CACHE:
        _NC_CACHE["nc"] = build()
    nc = _NC_CACHE["nc"]
    x = np.ascontiguousarray(np.asarray(inputs["x"], dtype=np.float32))
    base = {n: np.ascontiguousarray(np.asarray(inputs[n], dtype=np.float32)) for n in PARAM_NAMES}
    base["cst"] = CONSTS["cst"]
    base["mobac"] = CONSTS["mobac"]
    base["blkind"] = CONSTS["blkind"]
    in_maps = []
    for b in range(8):
        m = dict(base)
        m["x"] = x[b]
        in_maps.append(m)
    res = run_bass_kernel_spmd(nc, in_maps, core_ids=list(range(8)))
    return np.stack([np.asarray(r["y"], dtype=np.float32) for r in res.results], 0)
```

```python
import numpy as np
from contextlib import ExitStack
import concourse.bass as bass
import concourse.mybir as mybir
from concourse.bass_utils import run_bass_kernel_spmd

F32 = mybir.dt.float32
F32R = mybir.dt.float32r
AF = mybir.ActivationFunctionType
ALU = mybir.AluOpType
AX = mybir.AxisListType

S = 2048
D = 1024
L = 2
DFF = 2816
NFF = 22
H = 6
C0 = float(np.exp(-0.5))
BIG = 30000.0


OPTS = {}


class Sched:
    def __init__(self, nc, n_dma=40):
        self.nc = nc
        self.names = ["pe", "act", "dve", "pool", "sp"]
        self.sem = {e: nc.alloc_semaphore("s_" + e) for e in ["pe", "act", "dve", "pool"]}
        self.cnt = {e: 0 for e in self.sem}
        self.dsem = [nc.alloc_semaphore("d%d" % i) for i in range(n_dma)]
        self.dcnt = [0] * n_dma
        self.drr = 0
        self.q = {e: [] for e in self.names}
        self.clock = {e: {} for e in self.names}
        self.evclock = {}
        self.evorder = {}
        self.nev = 0
        self.lastw = {}
        self.readers = {}

    def _deps(self, reads, writes):
        deps = {}

        def add(k, v):
            if deps.get(k, 0) < v:
                deps[k] = v

        for r in reads:
            ev = self.lastw.get(r)
            if ev is not None:
                add(*ev)
        for w in writes:
            ev = self.lastw.get(w)
            if ev is not None:
                add(*ev)
            for k, v in self.readers.get(w, {}).items():
                add(k, v)
        return deps

    def _commit(self, ev, reads, writes):
        k, v = ev
        for r in reads:
            d = self.readers.setdefault(r, {})
            if d.get(k, 0) < v:
                d[k] = v
        for w in writes:
            self.lastw[w] = ev
            self.readers[w] = {}

    def _waits(self, eng, deps):
        clk = self.clock.setdefault(eng, {})
        waits = []
        for k, v in sorted(deps.items(), key=lambda kv: -self.evorder.get(kv, 0)):
            if eng == "pe" and k == ("e", "pe"):
                continue
            if clk.get(k, 0) >= v:
                continue
            waits.append((k, v))
            for k2, v2 in self.evclock.get((k, v), {}).items():
                if clk.get(k2, 0) < v2:
                    clk[k2] = v2
            clk[k] = v
        return waits

    def op(self, eng, fn, reads=(), writes=()):
        banks = {("psx", k[1]) for k in list(reads) + list(writes) if isinstance(k, tuple) and k and k[0] == "ps"}
        if banks:
            writes = list(writes) + list(banks)
        deps = self._deps(reads, writes)
        waits = self._waits(eng, deps)
        self.cnt[eng] += 1
        ev = (("e", eng), self.cnt[eng])
        self.evclock[ev] = dict(self.clock[eng])
        self.nev += 1
        self.evorder[ev] = self.nev
        self.q[eng].append((waits, fn, "e"))
        self._commit(ev, reads, writes)

    def dma(self, qeng, out, in_, reads=(), writes=(), **kw):
        deps = self._deps(reads, writes)
        idx = self.drr
        self.drr = (self.drr + 1) % len(self.dsem)
        if self.dcnt[idx] > 0:
            k = ("d", idx)
            deps[k] = max(deps.get(k, 0), self.dcnt[idx])
        waits = self._waits(qeng, deps)
        self.dcnt[idx] += 16
        ev = (("d", idx), self.dcnt[idx])
        self.evclock[ev] = dict(self.clock[qeng])
        self.nev += 1
        self.evorder[ev] = self.nev
        self.q[qeng].append((waits, lambda e: e.dma_start(out=out, in_=in_, **kw), idx))
        self._commit(ev, reads, writes)

    def barrier(self):
        allev = [(("e", e), c) for e, c in self.cnt.items() if c > 0]
        allev += [(("d", i), c) for i, c in enumerate(self.dcnt) if c > 0]
        for eng in self.names:
            waits = self._waits(eng, dict(allev))
            if waits:
                self.q[eng].append((waits, None, None))
        self.lastw = {}
        self.readers = {}
        self.evclock = {}

    def emit(self):
        nc = self.nc
        engs = {"pe": "tensor", "act": "scalar", "dve": "vector", "pool": "gpsimd", "sp": "sync"}
        with nc.Block() as block:
            for name in self.names:
                def body(eng, name=name):
                    for waits, fn, kind in self.q[name]:
                        emb = None
                        if fn is not None and waits and kind == "e" and not OPTS.get("noemb"):
                            emb = waits[-1]
                            waits = waits[:-1]
                        for k, v in waits:
                            s = self.sem[k[1]] if k[0] == "e" else self.dsem[k[1]]
                            eng.wait_ge(s, v)
                        if fn is None:
                            continue
                        ins = fn(eng)
                        if emb is not None:
                            k, v = emb
                            ins._wait_ge(self.sem[k[1]] if k[0] == "e" else self.dsem[k[1]], v)
                        if kind == "e":
                            ins.then_inc(self.sem[name], 1)
                        else:
                            ins.then_inc(self.dsem[kind], 16)
                getattr(block, engs[name])(body)


def make_consts():
    c = {}
    i128 = np.arange(128)
    blk = (i128[:, None] // 64) == (i128[None, :] // 64)
    ident = np.eye(128, dtype=np.float32)
    ones = np.ones((128, 128), np.float32)
    SL = ((i128[:, None] > i128[None, :]) & blk).astype(np.float32)
    SU = ((i128[:, None] < i128[None, :]) & blk).astype(np.float32)
    IU = ((i128[:, None] <= i128[None, :]) & blk).astype(np.float32)
    IUfull = (i128[:, None] <= i128[None, :]).astype(np.float32)
    idst = np.concatenate([np.eye(64), np.eye(64)], 0).astype(np.float32)
    reset = np.ones((128, 768), np.float32)
    reset[:, ::64] = 0.0
    rowm = np.zeros((128, 2), np.float32)
    rowm[:64, 0] = 1.0
    rowm[64:, 1] = 1.0
    q512 = np.arange(512)
    cm = np.stack([(q512[None, :] >= (j * 128 + i128[:, None])).astype(np.float32) for j in range(4)], 1)
    parts = [ident, ones, SU, IU, SL, SU, IU, IUfull, idst, reset, rowm, cm.reshape(128, 2048)]
    offs = {}
    o = 0
    for nm, p in zip(["ident", "ones", "mE", "_1", "_2", "mY", "_3", "iuf", "idst", "reset", "rowm", "cm"], parts):
        offs[nm] = o
        o += p.shape[1]
    c["cst"] = np.ascontiguousarray(np.concatenate(parts, 1))
    c["offs"] = offs
    mb = np.zeros((128, 3, 16, 6, 8), np.float32)
    for t in range(16):
        b = t // 2
        for n in range(8):
            mb[:, 0, t, :, n] = 0.0 if n < b else -1e30
            mb[:, 1, t, :, n] = 1.0 if n < b else 0.0
            mb[:, 2, t, :, n] = 1.0 if n == b else 0.0
    c["mobac"] = mb.reshape(128, 3 * 16 * 48)
    bi = np.zeros((8, S), np.float32)
    for n in range(8):
        bi[n, n * 256:(n + 1) * 256] = 1.0
    c["blkind"] = bi
    return c


CONSTS = make_consts()
PARAM_NAMES = ["pre_mix_g", "w_in", "rwkv_mu", "rwkv_w0", "rwkv_w2", "rwkv_a0", "rwkv_a2", "rwkv_g2",
               "rwkv_k_k", "rwkv_k_a", "rwkv_r_k", "rwkv_lnx_g", "rwkv_lnx_b", "gmlp_ln_g", "gmlp_ln_b",
               "gmlp_w_s", "gmlp_b_s", "w_out", "post_mix_g", "pre_ffn_g", "w_ffn_in", "w_ffn_out", "post_ffn_g"]
PARAM_SHAPES = {"pre_mix_g": (L, D), "w_in": (L, D, 3072), "rwkv_mu": (L, 1408), "rwkv_w0": (L, 384),
                "rwkv_w2": (L, 64, 384), "rwkv_a0": (L, 384), "rwkv_a2": (L, 64, 384), "rwkv_g2": (L, 128, 384),
                "rwkv_k_k": (L, 384), "rwkv_k_a": (L, 384), "rwkv_r_k": (L, 6, 64), "rwkv_lnx_g": (L, 384),
                "rwkv_lnx_b": (L, 384), "gmlp_ln_g": (L, 256), "gmlp_ln_b": (L, 256), "gmlp_w_s": (L, 4, 128, 128),
                "gmlp_b_s": (L, 4, 128), "w_out": (L, D, D), "post_mix_g": (L, D), "pre_ffn_g": (L, D),
                "w_ffn_in": (L, D, 2 * DFF), "w_ffn_out": (L, DFF, D), "post_ffn_g": (L, D)}


def build(dbg=None, nlayers=L, stop=None):
    dbg = dbg or []
    nc = bass.Bass("TRN2", target_bir_lowering=False)
    sc = Sched(nc)
    OF = CONSTS["offs"]

    def dram(name, shape, kind="Internal"):
        if name in dbg:
            kind = "ExternalOutput"
        return nc.dram_tensor(name, list(shape), F32, kind=kind).ap()

    x_in = dram("x", [S, D], "ExternalInput")
    y_out = dram("y", [S, D], "ExternalOutput")
    cst_d = dram("cst", CONSTS["cst"].shape, "ExternalInput")
    if not OPTS.get("noparams"):
        PR = {n: dram(n, PARAM_SHAPES[n], "ExternalInput") for n in PARAM_NAMES}
        mobac_d = dram("mobac", CONSTS["mobac"].shape, "ExternalInput")
        blkind_d = dram("blkind", CONSTS["blkind"].shape, "ExternalInput")
    if OPTS.get("noscratch"):
        glob_scr = None
    rwkvT = dram("rwkvT", [1408, S]) if not OPTS.get("noscratch") else None
    qkT = dram("qkT", [768, S]) if not OPTS.get("noscratch") else None
    uT = dram("uT", [256, S]) if not OPTS.get("noscratch") else None
    vm_tm = dram("vm_tm", [S, 384]) if not OPTS.get("noscratch") else None
    vg_tm = dram("vg_tm", [S, 256]) if not OPTS.get("noscratch") else None
    catT = dram("catT", [D, S]) if not OPTS.get("noscratch") else None
    xdbg = dram("xdbg", [D, S]) if not OPTS.get("noscratch") else None
    xD = dram("xD", [D, S])
    xD_v = xD.rearrange("(k p) t -> p k t", p=128)

    uid = [0]

    def sb(es, name, shape):
        uid[0] += 1
        return es.enter_context(nc.sbuf_tensor("%s_%d" % (name, uid[0]), list(shape), F32))

    glob = ExitStack()
    cst = sb(glob, "cst_sb", [128, CONSTS["cst"].shape[1]])
    ps = [glob.enter_context(nc.psum_tensor("ps%d" % i, [128, 512], F32)) for i in range(2)]
    psbig = glob.enter_context(nc.psum_tensor("psbig", [128, 6, 512], F32))
    ps = ps + [psbig[:, h, :] for h in range(6)]
    ident = cst[:, OF["ident"]:OF["ident"] + 128]
    ones = cst[:, OF["ones"]:OF["ones"] + 128]
    mE = cst[:, OF["mE"]:OF["mE"] + 384]
    mY = cst[:, OF["mY"]:OF["mY"] + 256]
    iuf = cst[:, OF["iuf"]:OF["iuf"] + 128]
    idst = cst[:, OF["idst"]:OF["idst"] + 64]
    resetm = cst[:, OF["reset"]:OF["reset"] + 768]
    cm = cst[:, OF["cm"]:OF["cm"] + 2048]
    epsc = sb(glob, "epsc", [128, 4])

    def ACT(out, in_, func, reads, writes, **kw):
        sc.op("act", lambda e: e.activation(out=out, in_=in_, func=func, **kw), reads, writes)

    def RR(ap):
        return ap if OPTS.get("nor") else ap.bitcast(F32R)

    def MM(out, lhsT, rhs, start, stop, reads, writes, r=False):
        if r and not OPTS.get("nor"):
            lhsT = lhsT.bitcast(F32R)
            rhs = rhs.bitcast(F32R)
        sc.op("pe", lambda e: e.matmul(out, lhsT=lhsT, rhs=rhs, start=start, stop=stop), reads, writes)

    def TR(out, in_, idn, reads, writes):
        sc.op("pe", lambda e: e.transpose(out, in_, idn), reads, writes)

    def TT(eng, out, in0, in1, op, reads, writes):
        sc.op(eng, lambda e: e.tensor_tensor(out=out, in0=in0, in1=in1, op=op), reads, writes)

    def TS(eng, out, in0, s1, s2, op0, op1, reads, writes):
        if s2 is None:
            sc.op(eng, lambda e: e.tensor_scalar(out=out, in0=in0, scalar1=s1, scalar2=None, op0=op0), reads, writes)
        else:
            sc.op(eng, lambda e: e.tensor_scalar(out=out, in0=in0, scalar1=s1, scalar2=s2, op0=op0, op1=op1), reads, writes)

    def STT(out, in0, scalar, in1, op0, op1, reads, writes):
        sc.op("dve", lambda e: e.scalar_tensor_tensor(out=out, in0=in0, scalar=scalar, in1=in1, op0=op0, op1=op1), reads, writes)

    def CP(eng, out, in_, reads, writes):
        if eng == "act":
            sc.op("act", lambda e: e.copy(out=out, in_=in_), reads, writes)
        else:
            sc.op(eng, lambda e: e.tensor_copy(out=out, in_=in_), reads, writes)

    def RECIP(out, in_, reads, writes):
        sc.op("dve", lambda e: e.reciprocal(out=out, in_=in_), reads, writes)

    def DMA(out, in_, reads, writes, q="sp", **kw):
        sc.dma(q, out, in_, reads, writes, **kw)

    def colvec(es, name, src_1d, ncol, p=128):
        t = sb(es, name, [p, ncol])
        DMA(t[:, :], src_1d.rearrange("(c p) -> p c", p=p), [], [name], allow_slow_non_contiguous=True)
        return t

    DMA(cst[:, :], cst_d[:, :], [], ["cst"])
    sc.op("dve", lambda e: e.memset(epsc[:, 0:1], 1e-6), [], ["epsc"])
    sc.op("dve", lambda e: e.memset(epsc[:, 1:2], 64e-5), [], ["epsc"])
    sc.op("dve", lambda e: e.memset(epsc[:, 2:3], 0.0), [], ["epsc"])
    with ExitStack() as es:
        xin = sb(es, "xin", [128, 2, D])
        xs = sb(es, "xs", [128, 2, 8, 512])
        for t in range(16):
            sl = t % 2
            xsl = (t // 4) % 2
            DMA(xin[:, sl, :], x_in[t * 128:(t + 1) * 128, :], [], [("xin", sl)])
            for g in range(2):
                bank = ps[(t * 2 + g) % 4]
                bk = ("ps", (t * 2 + g) % 4)
                for kk in range(4):
                    k = g * 4 + kk
                    TR(bank[:, kk * 128:(kk + 1) * 128], xin[:, sl, k * 128:(k + 1) * 128], ident, [("xin", sl), "cst"], [bk])
                CP("act" if g else "dve", xs[:, xsl, g * 4:(g + 1) * 4, (t % 4) * 128:(t % 4 + 1) * 128],
                   bank[:, :].rearrange("p (k c) -> p k c", k=4), [bk], [("xs", xsl, t % 4, g)])
            if t % 4 == 3:
                DMA(xD_v[:, :, (t // 4) * 512:(t // 4 + 1) * 512], xs[:, xsl, :, :], [("xs", xsl, q, g) for q in range(4) for g in range(2)], [("xD", t // 4)])
        sc.barrier()

    def rms_stats(src_fn, src_keys, tt, es_tiles, pbank, pkey):
        sq, rstd = es_tiles
        for k in range(8):
            ACT(sq[:, k % 2, :], src_fn(k), AF.Square, [src_keys(k)], [("sq", k % 2)])
            MM(pbank[:, :], ones, sq[:, k % 2, :], k == 0, k == 7, [("sq", k % 2), "cst"], [pkey])
        ACT(rstd[:, tt % 2, :], pbank[:, :], AF.Sqrt, [pkey, "epsc"], [("rstd", tt % 2)], scale=1.0 / D, bias=epsc[:, 0:1])
        RECIP(rstd[:, tt % 2, :], rstd[:, tt % 2, :], [("rstd", tt % 2)], [("rstd", tt % 2)])
        return rstd[:, tt % 2, :]

    for l in range(nlayers if stop != 'setup' else 0):
        with ExitStack() as es:
            hbuf = sb(es, "hbuf", [128, 8, S])
            sq = sb(es, "sq", [128, 2, 512])
            rstd = sb(es, "rstd", [128, 2, 512])
            gA = colvec(es, "gA", PR["pre_mix_g"][l], 8)
            muA = colvec(es, "muA", PR["rwkv_mu"][l], 11)
            xa = sb(es, "xa", [128, 2, 8, 512])
            for tt in range(4):
                tsl = slice(tt * 512, (tt + 1) * 512)
                xsl = tt % 2
                DMA(xa[:, xsl, :, :], xD_v[:, :, tsl], [("xD", tt)], [("xa", xsl)])
                r = rms_stats(lambda k: xa[:, xsl, k, :], lambda k: ("xa", xsl), tt, (sq, rstd), ps[4 + tt % 2], ("ps", 4 + tt % 2))
                for k in range(8):
                    STT(RR(hbuf[:, k, tsl]), xa[:, xsl, k, :], gA[:, k:k + 1], r, ALU.mult, ALU.mult,
                        [("xa", xsl), ("rstd", tt % 2), "gA"], [("h", k, tt)])
            es_main = es
            es = ExitStack()
            wA = sb(es, "wA", [128, 2, 8, 128])
            stg = sb(es, "stg", [128, 2, S])
            stg2 = sb(es, "stg2", [128, 2, S])
            w_in_l = PR["w_in"][l].rearrange("(k p) c -> p k c", p=128)
            fm_chunks = [(c * 128, rwkvT, c * 128, True) for c in range(11)]
            fm_chunks += [(1408 + c * 128, qkT, c * 128, False) for c in range(6)]
            fm_chunks += [(2560 + c * 128, uT, c * 128, False) for c in range(2)]
            def load_wA(ci):
                col0 = fm_chunks[ci][0]
                DMA(RR(wA[:, ci % 2, :, :]), w_in_l[:, :, col0:col0 + 128], [], [("wA", ci % 2)], q="pool")

            load_wA(0)
            for ci, (col0, dst, row0, shift) in enumerate(fm_chunks):
                sl = ci % 2
                if ci + 1 < len(fm_chunks):
                    load_wA(ci + 1)
                for tt in range(4):
                    tsl = slice(tt * 512, (tt + 1) * 512)
                    bi = (ci * 4 + tt) % 4
                    for k in range(8):
                        MM(ps[bi][:, :], wA[:, sl, k, :], hbuf[:, k, tsl], k == 0, k == 7,
                           [("wA", sl), ("h", k, tt)], [("ps", bi)], r=True)
                    CP("act" if tt % 2 else "dve", stg[:, sl, tsl], ps[bi][:, :], [("ps", bi)], [("stg", sl, tt)])
                allst = [("stg", sl, tt) for tt in range(4)]
                if shift:
                    TT("pool", stg2[:, sl, 1:S], stg[:, sl, 0:S - 1], stg[:, sl, 1:S], ALU.subtract, allst, [("stg2", sl)])
                    TS("pool", stg2[:, sl, 0:1], stg[:, sl, 0:1], -1.0, None, ALU.mult, None, allst, [("stg2", sl)])
                    STT(stg2[:, sl, :], stg2[:, sl, :], muA[:, ci:ci + 1], stg[:, sl, :], ALU.mult, ALU.add,
                        allst + [("stg2", sl), "muA"], [("stg2", sl)])
                    DMA(dst[row0:row0 + 128, :], stg2[:, sl, :], [("stg2", sl)], [("dr", id(dst), row0)], q="pool")
                else:
                    DMA(dst[row0:row0 + 128, :], stg[:, sl, :], allst, [("dr", id(dst), row0)], q="pool")
            sc.barrier()
            es.close()
            es = es_main
            wB = sb(es, "wB", [128, 8, 640])
            DMA(RR(wB[:, :, 0:384]), w_in_l[:, :, 2176:2560], [], ["wB"], q="pool")
            DMA(RR(wB[:, :, 384:640]), w_in_l[:, :, 2816:3072], [], ["wB"], q="pool")
            vst = sb(es, "vst", [128, 2, 640])
            for t in range(16):
                sl = t % 2
                b0, b1 = 4 + (t % 2) * 2, 5 + (t % 2) * 2
                for k in range(8):
                    MM(ps[b0][:, 0:384], hbuf[:, k, t * 128:(t + 1) * 128], wB[:, k, 0:384], k == 0, k == 7,
                       [("h", k, t // 4), "wB"], [("ps", b0)], r=True)
                for k in range(8):
                    MM(ps[b1][:, 0:256], hbuf[:, k, t * 128:(t + 1) * 128], wB[:, k, 384:640], k == 0, k == 7,
                       [("h", k, t // 4), "wB"], [("ps", b1)], r=True)
                CP("act", vst[:, sl, 0:384], ps[b0][:, 0:384], [("ps", b0)], [("vst", sl, 0)])
                CP("dve", vst[:, sl, 384:640], ps[b1][:, 0:256], [("ps", b1)], [("vst", sl, 1)])
                DMA(vm_tm[t * 128:(t + 1) * 128, :], vst[:, sl, 0:384], [("vst", sl, 0)], [("vm", t)], q="pool")
                DMA(vg_tm[t * 128:(t + 1) * 128, :], vst[:, sl, 384:640], [("vst", sl, 1)], [("vg", t)], q="pool")
            sc.barrier()
        if stop == "A":
            break

        with ExitStack() as es:
            lng = sb(es, "lng", [128, 256])
            lnb = sb(es, "lnb", [128, 256])
            DMA(lng[:, :], PR["gmlp_ln_g"][l].partition_broadcast(128), [], ["lng"])
            DMA(lnb[:, :], PR["gmlp_ln_b"][l].partition_broadcast(128), [], ["lnb"])
            wsn = sb(es, "wsn", [128, 4, 128])
            wsT = sb(es, "wsT", [128, 4, 128])
            bsr = sb(es, "bsr", [1, 512])
            DMA(wsn[:, :, :], PR["gmlp_w_s"][l].rearrange("g t s -> t g s"), [], ["wsn"])
            DMA(bsr[:, :], PR["gmlp_b_s"][l].rearrange("g t -> (g t)").partition_broadcast(1), [], ["bsr"])
            for g in range(4):
                TR(ps[0][:, g * 128:(g + 1) * 128], wsn[:, g, :], ident, ["wsn", "cst"], [("ps", 0)])
            for g in range(4):
                TT("dve", wsT[:, g, :], ps[0][:, g * 128:(g + 1) * 128], iuf, ALU.mult, [("ps", 0), "cst"], ["wsT"])
            gu = sb(es, "gu", [128, 2, S])
            t1 = sb(es, "t1", [128, S])
            cout = sb(es, "cout", [128, 2, S])
            for pp in range(2):
                DMA(gu[:, pp, :], uT[pp * 128:(pp + 1) * 128, :], [], [("gu", pp)])
                ACT(t1[:, :], gu[:, pp, :], AF.Square, [("gu", pp)], ["t1"])
                TS("pool", t1[:, :], t1[:, :], 0.044715, 1.0, ALU.mult, ALU.add, ["t1"], ["t1"])
                TT("dve", t1[:, :], t1[:, :], gu[:, pp, :], ALU.mult, ["t1", ("gu", pp)], ["t1"])
                ACT(t1[:, :], t1[:, :], AF.Sigmoid, ["t1"], ["t1"], scale=2.0 * 0.7978845608028654)
                TT("dve", gu[:, pp, :], gu[:, pp, :], t1[:, :], ALU.mult, ["t1", ("gu", pp)], [("gu", pp)])
            vb = sb(es, "vb", [128, 2, 256])
            t2 = sb(es, "t2", [128, 2, 256])
            st6 = sb(es, "st6", [128, 2, 8])
            for c in range(16):
                sl = c % 2
                DMA(vb[:, sl, :], vg_tm[c * 128:(c + 1) * 128, :], [], [("vb", sl)])
                kv, kt = ("vb", sl), ("t2", sl)
                ACT(t2[:, sl, :], vb[:, sl, :], AF.Square, [kv], [kt])
                TS("pool", t2[:, sl, :], t2[:, sl, :], 0.044715, 1.0, ALU.mult, ALU.add, [kt], [kt])
                TT("dve", t2[:, sl, :], t2[:, sl, :], vb[:, sl, :], ALU.mult, [kt, kv], [kt])
                ACT(t2[:, sl, :], t2[:, sl, :], AF.Sigmoid, [kt], [kt], scale=2.0 * 0.7978845608028654)
                TT("dve", vb[:, sl, :], vb[:, sl, :], t2[:, sl, :], ALU.mult, [kt, kv], [kv])
                ks = ("st6", sl)
                sc.op("dve", lambda e, sl=sl: e.bn_stats(out=st6[:, sl, 0:6], in_=vb[:, sl, :]), [kv], [ks])
                sc.op("dve", lambda e, sl=sl: e.bn_aggr(out=st6[:, sl, 6:8], in_=st6[:, sl, 0:6]), [ks], [ks])
                ACT(st6[:, sl, 7:8], st6[:, sl, 7:8], AF.Sqrt, [ks, "epsc"], [ks], bias=epsc[:, 0:1], scale=1.0)
                RECIP(st6[:, sl, 7:8], st6[:, sl, 7:8], [ks], [ks])
                TS("dve", vb[:, sl, :], vb[:, sl, :], st6[:, sl, 6:7], st6[:, sl, 7:8], ALU.subtract, ALU.mult, [kv, ks], [kv])
                TT("dve", vb[:, sl, :], vb[:, sl, :], lng[:, :], ALU.mult, [kv, "lng"], [kv])
                TT("dve", vb[:, sl, :], vb[:, sl, :], lnb[:, :], ALU.add, [kv, "lnb"], [kv])
                for pp in range(2):
                    bi = 1 + (c % 2) * 2 + pp
                    for gg in range(2):
                        g = pp * 2 + gg
                        MM(ps[bi][:, gg * 128:(gg + 1) * 128], vb[:, sl, pp * 128:(pp + 1) * 128], wsT[:, g, :], True, False, [kv, "wsT"], [("ps", bi)])
                        MM(ps[bi][:, gg * 128:(gg + 1) * 128], ones[0:1, :], bsr[0:1, g * 128:(g + 1) * 128], False, True, ["cst", "bsr"], [("ps", bi)])
                    for gg in range(2):
                        TT("dve", cout[gg * 64:(gg + 1) * 64, pp, c * 128:(c + 1) * 128], ps[bi][gg * 64:(gg + 1) * 64, gg * 128:(gg + 1) * 128],
                           gu[gg * 64:(gg + 1) * 64, pp, c * 128:(c + 1) * 128], ALU.mult, [("ps", bi), ("gu", pp)], [("cout", pp)])
            for pp in range(2):
                DMA(catT[768 + pp * 128:768 + (pp + 1) * 128, :], cout[:, pp, :], [("cout", pp)], [("cat", 6 + pp)], q="pool")
            sc.barrier()
        if stop == "D":
            break

        with ExitStack() as es:
            qa = sb(es, "qa", [72, 2, S])
            ka = sb(es, "ka", [72, 2, S])
            vt = sb(es, "vt", [128, 2, 16, 128])
            ones_r = sb(es, "ones_r", [128, 128])
            CP("dve", RR(ones_r[:, :]), ones, ["cst"], ["ones_r"])
            kbar = sb(es, "kbar", [64, 2, 8])
            mobc = sb(es, "mobc", [128, 3, 16, 48])
            NP = sb(es, "NP", [128, 2, 16, 72])
            sm = sb(es, "sm", [128, 16, 8])
            top8 = sb(es, "top8", [128, 16, 8])
            al = sb(es, "al", [128, 16, 8])
            pt = sb(es, "pt", [128, 6, 512])
            pacc = sb(es, "pacc", [128, 512])
            rden = sb(es, "rden", [128, 512])
            ob = sb(es, "ob", [128, 2, 512])
            DMA(mobc[:, :, :, :], mobac_d.rearrange("p (a t c) -> p a t c", a=3, t=16), [], ["mobc"])
            for q in range(2):
                DMA(RR(ka[64:72, q, :]), blkind_d[:, :], [], [("kaB", q)], q="pool")
            sc.op("pool", lambda e: e.memset(NP[:, :, :, :], 0.0), [], [("NP", 0), ("NP", 1)])
            pti = [0]

            def prepA(h):
                q = h % 2
                DMA(RR(qa[0:64, q, :]), qkT[h * 64:(h + 1) * 64, :], [], [("qaQ", q)], q="pool")
                DMA(RR(ka[0:64, q, :]), qkT[384 + h * 64:384 + (h + 1) * 64, :], [], [("kaK", q)], q="pool")
                if h % 2 == 0:
                    vq = (h // 2) % 2
                    DMA(RR(vt[:, vq, :, :]), vm_tm.rearrange("(t p) c -> p t c", p=128)[:, :, h * 64:(h + 2) * 64], [], [("vt", vq)], q="pool")
                sc.op("dve", lambda e: e.tensor_reduce(out=kbar[:, q, :], in_=ka[0:64, q, :].rearrange("p (n k) -> p n k", n=8), axis=AX.X, op=ALU.add), [("kaK", q)], [("kbar", q)])
                for t in range(16):
                    MM(ps[0][:, t * 8:(t + 1) * 8], qa[0:64, q, t * 128:(t + 1) * 128], kbar[:, q, :], True, True, [("qaQ", q), ("kbar", q)], [("ps", 0)])

            def prepB(h):
                q = h % 2
                hs = slice(h * 8, (h + 1) * 8)
                TT("dve", sm[:, :, :], ps[0][:, 0:128].rearrange("p (t n) -> p t n", t=16), mobc[:, 0, :, hs], ALU.add, [("ps", 0), "mobc"], ["sm"])
                for t in range(16):
                    sc.op("dve", lambda e, t=t: e.max(out=top8[:, t, :], in_=sm[:, t, :]), ["sm"], [("top8", t)])
                for t in range(16):
                    TS("dve", al[:, t, :], sm[:, t, :], top8[:, t, 2:3], None, ALU.is_ge, None, ["sm", ("top8", t)], [("al", t)])
                allal = [("al", t) for t in range(16)]
                TT("dve", al[:, :, :], al[:, :, :], mobc[:, 1, :, hs], ALU.mult, allal + ["mobc"], ["al2"])
                TT("dve", al[:, :, :], al[:, :, :], mobc[:, 2, :, hs], ALU.add, ["al2", "mobc"], ["al2"])
                TS("dve", NP[:, q, :, 64:72], al[:, :, :], -1.0, BIG, ALU.add, ALU.mult, ["al2"], [("NP", q)])

            def prepC(h):
                q = h % 2
                for t4 in range(4):
                    for tq in range(4):
                        t = t4 * 4 + tq
                        MM(ps[1][0:72, tq * 128:(tq + 1) * 128], NP[:, q, t, :], ident, True, True, [("NP", q), "cst"], [("ps", 1)])
                    CP("act", RR(qa[64:72, q, t4 * 512:(t4 + 1) * 512]), ps[1][64:72, :], [("ps", 1)], [("qaM", q)])

            def attn(h, qt):
                q = h % 2
                vq = (h // 2) % 2
                hp = slice((h % 2) * 64, (h % 2) * 64 + 64)
                qsl = slice(qt * 512, (qt + 1) * 512)
                nk = (qt + 1) * 4
                osl = qt % 2
                pis = {}

                def qk(kt):
                    sb_i = (2, 3, 6, 7)[kt % 4]
                    pi = pti[0] % 6
                    pti[0] += 1
                    pis[kt] = pi
                    MM(ps[sb_i][:, :], ka[0:72, q, kt * 128:(kt + 1) * 128], qa[0:72, q, qsl], True, True,
                       [("kaK", q), ("kaB", q), ("qaQ", q), ("qaM", q)], [("ps", sb_i)], r=True)
                    ACT(RR(pt[:, pi, :]), ps[sb_i][:, :], AF.Exp, [("ps", sb_i)], [("pt", pi)], scale=0.125)
                    if kt >= qt * 4:
                        j = kt - qt * 4
                        TT("dve", RR(pt[:, pi, :]), pt[:, pi, :], cm[:, j * 512:(j + 1) * 512], ALU.mult, [("pt", pi), "cst"], [("pt", pi)])

                def pv(kt):
                    pi = pis[kt]
                    MM(ps[4][:, :], vt[:, vq, kt, :], pt[:, pi, :], kt == 0, kt == nk - 1, [("vt", vq), ("pt", pi)], [("ps", 4)], r=True)
                    MM(ps[5][:, :], ones_r[:, :], pt[:, pi, :], kt == 0, kt == nk - 1, ["ones_r", ("pt", pi)], [("ps", 5)], r=True)

                DEPTH = 3
                for kt in range(min(DEPTH, nk)):
                    qk(kt)
                for kt in range(nk):
                    pv(kt)
                    if kt + DEPTH < nk:
                        qk(kt + DEPTH)
                RECIP(rden[hp, :], ps[5][hp, :], [("ps", 5)], ["rden"])
                TT("dve", ob[hp, osl, :], ps[4][hp, :], rden[hp, :], ALU.mult, [("ps", 4), "rden"], [("ob", osl)])
                DMA(catT[384 + h * 64:384 + (h + 1) * 64, qsl], ob[hp, osl, :], [("ob", osl)], [("cat", "b", h, qt)])

            prepA(0)
            prepB(0)
            prepC(0)
            for h in range(H):
                for qt in range(4):
                    attn(h, qt)
                    if h + 1 < H:
                        if qt == 0:
                            prepA(h + 1)
                        elif qt == 1:
                            prepB(h + 1)
                        elif qt == 2:
                            prepC(h + 1)
            sc.barrier()
        if stop == "C":
            break

        with ExitStack() as es:
            TW = 128
            NTB = S // TW
            w2s = sb(es, "w2s", [64, 384])
            a2s = sb(es, "a2s", [64, 384])
            g2s = sb(es, "g2s", [128, 384])
            DMA(w2s[:, :], PR["rwkv_w2"][l], [], ["w2s"])
            DMA(a2s[:, :], PR["rwkv_a2"][l], [], ["a2s"])
            DMA(g2s[:, :], PR["rwkv_g2"][l], [], ["g2s"])
            pw0 = colvec(es, "pw0", PR["rwkv_w0"][l], 6, p=64)
            pa0 = colvec(es, "pa0", PR["rwkv_a0"][l], 6, p=64)
            pkk = colvec(es, "pkk", PR["rwkv_k_k"][l], 6, p=64)
            pka = colvec(es, "pka", PR["rwkv_k_a"][l], 6, p=64)
            prk = colvec(es, "prk", PR["rwkv_r_k"][l].rearrange("h d -> (h d)"), 6, p=64)
            plg = colvec(es, "plg", PR["rwkv_lnx_g"][l], 6, p=64)
            plb = colvec(es, "plb", PR["rwkv_lnx_b"][l], 6, p=64)
            pok = sb(es, "pok", [64, 6])
            TS("dve", pok[:, :], pka[:, :], -1.0, 1.0, ALU.mult, ALU.add, ["pka"], ["pok"])
            i64 = ident[0:64, 0:64]
            o64 = ones[0:64, 0:64]
            rowm = cst[:, OF["rowm"]:OF["rowm"] + 2]
            Mst = sb(es, "Mst", [64, 2, 6, 64])
            sc.op("dve", lambda e: e.memset(Mst[:, 0, :, :], 0.0), [], [("Mst", 0)])
            mcur = [0]
            RhatT = sb(es, "RhatT", [64, 6, TW])
            Y0T = sb(es, "Y0T", [64, 6, TW])
            yT = sb(es, "yT", [64, 6, TW])
            GT = sb(es, "GT", [64, 6, 2, 64])
            Hm = sb(es, "Hm", [64, 6, 2, 64])
            bon = sb(es, "bon", [64, 3, 6, TW]); gal = sb(es, "gal", [64, 3, 6, TW])
            ARh = sb(es, "ARh", [64, 2, 6, 2, TW]); BKh = sb(es, "BKh", [64, 2, 6, 2, TW]); BPh = sb(es, "BPh", [64, 2, 6, 2, TW])
            vvh = sb(es, "vvh", [64, 2, 6, TW]); pC = sb(es, "pC", [64, 2, 6, 2])
            wd = sb(es, "wd", [64, 2, TW]); ad = sb(es, "ad", [64, 2, TW]); gd = sb(es, "gd", [128, 2, TW])
            T6 = lambda nm: sb(es, nm, [64, 6, TW])
            rr = T6("rr"); kq = T6("kq"); sig = T6("sig"); cum = T6("cum"); cpv = T6("cpv")
            epos = T6("epos"); eneg = T6("eneg"); eprv = T6("eprv"); eend = T6("eend")
            aa = T6("aa"); kk = T6("kk"); kk2 = cpv; rn = T6("rn"); kka = T6("kka"); kp = T6("kp"); rkr = T6("rkr")
            nbc = sb(es, "nbc", [64, 6, 2])
            tm = sb(es, "tm", [128, 6, 256]); Eb = sb(es, "Eb", [128, 6, 512]); YS = sb(es, "YS", [128, 6, 256])
            Lb = sb(es, "Lb", [128, 6, 2, 384]); B2 = sb(es, "B2", [128, 6, 128]); K2 = sb(es, "K2", [128, 6, 128])
            yc = sb(es, "yc", [64, 6, TW]); ysq = sb(es, "ysq", [64, 6, TW]); yrs = sb(es, "yrs", [64, 6, TW])
            obuf = sb(es, "obuf", [64, 6, TW])

            def tile_pro(tt):
                tsl = slice(tt * TW, (tt + 1) * TW)
                ws = tt % 2
                DMA(wd[:, ws, :], rwkvT[1152:1216, tsl], [], [("wd", ws)])
                DMA(ad[:, ws, :], rwkvT[1216:1280, tsl], [], [("ad", ws)])
                DMA(gd[:, ws, :], rwkvT[1280:1408, tsl], [], [("gd", ws)])
                yield
                ACT(wd[:, ws, :], wd[:, ws, :], AF.Tanh, [("wd", ws)], [("wd", ws)])
                ACT(gd[:, ws, :], gd[:, ws, :], AF.Sigmoid, [("gd", ws)], [("gd", ws)])
                yield

            def stage0_all(tt):
                tsl = slice(tt * TW, (tt + 1) * TW)
                ws = tt % 2
                w3 = tt % 3
                HH = range(H)
                fl = lambda t: t[:, :, :].rearrange("p h t -> p (h t)")
                A6 = lambda nm: [(nm, ws, h) for h in HH]
                hv = lambda base: rwkvT[base:base + 384, tsl].rearrange("(h p) t -> p h t", p=64)
                DMA(rr[:, :, :], hv(0), [], ["rr"])
                DMA(kq[:, :, :], hv(384), [], ["kq"])
                DMA(vvh[:, ws, :, :], hv(768), [], A6("vv"))
                yield
                pbh = lambda h: (ps[0], ("ps", 0), h * TW) if h < 4 else (ps[1], ("ps", 1), (h - 4) * TW)
                for h in HH:
                    pb, pk, c0 = pbh(h)
                    MM(pb[0:64, c0:c0 + TW], w2s[:, h * 64:(h + 1) * 64], wd[:, ws, :], True, True, ["w2s", ("wd", ws)], [pk])
                for h in HH:
                    pb, pk, c0 = pbh(h)
                    ACT(sig[:, h, :], pb[0:64, c0:c0 + TW], AF.Sigmoid, [pk, "pw0"], [("sig", h)], bias=pw0[:, h:h + 1])
                yield
                allsig = [("sig", h) for h in HH]
                sc.op("dve", lambda e: e.tensor_tensor_scan(out=fl(cum), data0=resetm[0:64, 0:6 * TW], data1=fl(sig), initial=0.0, op0=ALU.mult, op1=ALU.add), allsig + ["cst"], ["cum"])
                yield
                TT("pool", fl(cpv), fl(cum), fl(sig), ALU.subtract, ["cum"] + allsig, ["cpv"])
                ACT(fl(epos), fl(cum), AF.Exp, ["cum"], ["epos"], scale=-C0)
                ACT(fl(eneg), fl(cum), AF.Exp, ["cum"], ["eneg"], scale=C0)
                cum12 = fl(cum).rearrange("p (c t) -> p c t", t=64)
                epos12 = fl(epos).rearrange("p (c t) -> p c t", t=64)
                nbc12 = nbc[:, :, :].rearrange("p h c -> p (h c)")
                TS("dve", nbc12, cum12[:, :, 63], -C0, None, ALU.mult, None, ["cum"], ["nbc"])
                yield
                ACT(fl(eprv), fl(cpv), AF.Exp, ["cpv"], ["eprv"], scale=-C0)
                CP("pool", pC[:, ws, :, :].rearrange("p h c -> p (h c)"), epos12[:, :, 63], ["epos"], A6("pC"))
                for h in HH:
                    for c in range(2):
                        ACT(eend[:, h, c * 64:(c + 1) * 64], cum[:, h, c * 64:(c + 1) * 64], AF.Exp, ["cum", "nbc"], [("eend", h)], scale=C0, bias=nbc[:, h, c:c + 1])
                    if h % 2:
                        yield
                for h in HH:
                    pb, pk, c0 = pbh(h)
                    MM(pb[0:64, c0:c0 + TW], a2s[:, h * 64:(h + 1) * 64], ad[:, ws, :], True, True, ["a2s", ("ad", ws)], [pk])
                for h in HH:
                    pb, pk, c0 = pbh(h)
                    ACT(aa[:, h, :], pb[0:64, c0:c0 + TW], AF.Sigmoid, [pk, "pa0"], [("aa", h)], bias=pa0[:, h:h + 1])
                yield
                for h in HH:
                    pb, pk, c0 = pbh(h)
                    MM(pb[0:64, c0:c0 + TW], g2s[:, h * 64:(h + 1) * 64], gd[:, ws, :], True, True, ["g2s", ("gd", ws)], [pk])
                CP("act", gal[:, w3, 0:4, :].rearrange("p h t -> p (h t)"), ps[0][0:64, 0:512], [("ps", 0)], [("gal", w3, h) for h in range(4)])
                CP("act", gal[:, w3, 4:6, :].rearrange("p h t -> p (h t)"), ps[1][0:64, 0:256], [("ps", 1)], [("gal", w3, h) for h in range(4, 6)])
                yield
                for h in HH:
                    TS("dve", kk[:, h, :], kq[:, h, :], pkk[:, h:h + 1], None, ALU.mult, None, ["kq", "pkk"], [("kk", h)])
                yield
                allkk = [("kk", h) for h in HH]
                ACT(fl(kk2), fl(kk), AF.Square, allkk + ["cpv"], ["cpv"])
                yield
                for h in HH:
                    pb, pk, c0 = pbh(h)
                    MM(pb[0:64, c0:c0 + TW], o64, kk2[:, h, :], True, True, ["cst", "cpv"], [pk])
                ACT(rn[:, 0:4, :].rearrange("p h t -> p (h t)"), ps[0][0:64, 0:512], AF.Sqrt, [("ps", 0)], [("rn", 0)])
                ACT(rn[:, 4:6, :].rearrange("p h t -> p (h t)"), ps[1][0:64, 0:256], AF.Sqrt, [("ps", 1)], [("rn", 1)])
                yield
                krn = [("rn", 0), ("rn", 1)]
                TS("dve", fl(rn), fl(rn), 1e-12, None, ALU.max, None, krn, krn)
                yield
                RECIP(fl(rn), fl(rn), krn, krn)
                yield
                TT("dve", fl(kk), fl(kk), fl(rn), ALU.mult, allkk + krn, ["kkn"])
                yield
                allaa = [("aa", h) for h in HH]
                TT("pool", fl(kka), fl(kk), fl(aa), ALU.mult, ["kkn"] + allaa, ["kka"])
                for h in HH:
                    TS("dve", rn[:, h, :], aa[:, h, :], pka[:, h:h + 1], pok[:, h:h + 1], ALU.mult, ALU.add, [("aa", h), "pka", "pok", "kkn"] + krn, [("fac", h)])
                STT(RR(ARh[:, ws, :, 0, :]), kk[:, :, :], -1.0, eprv[:, :, :], ALU.mult, ALU.mult, ["kkn", "eprv"], A6("AR0"))
                yield
                allfac = [("fac", h) for h in HH]
                TT("dve", fl(kp), fl(kq), fl(rn), ALU.mult, ["kq"] + allfac, ["kp"])
                TT("pool", RR(ARh[:, ws, :, 1, :]), rr[:, :, :], epos[:, :, :], ALU.mult, ["rr", "epos"], A6("AR1"))
                yield
                for h in HH:
                    STT(rkr[:, h, :], rr[:, h, :], prk[:, h:h + 1], kp[:, h, :], ALU.mult, ALU.mult, ["rr", "prk", "kp"], [("rkr", h)])
                TT("pool", RR(BKh[:, ws, :, 0, :]), kka[:, :, :], eneg[:, :, :], ALU.mult, ["kka", "eneg"], A6("BK0"))
                TT("pool", RR(BKh[:, ws, :, 1, :]), kp[:, :, :], eneg[:, :, :], ALU.mult, ["kp", "eneg"], A6("BK1"))
                yield
                for h in HH:
                    pb, pk, c0 = pbh(h)
                    MM(pb[0:64, c0:c0 + TW], o64, rkr[:, h, :], True, True, ["cst", ("rkr", h)], [pk])
                TT("dve", bon[:, w3, 0:4, :].rearrange("p h t -> p (h t)"), ps[0][0:64, 0:512], vvh[:, ws, 0:4, :].rearrange("p h t -> p (h t)"), ALU.mult,
                   [("ps", 0)] + A6("vv"), [("bon", w3, h) for h in range(4)])
                TT("dve", bon[:, w3, 4:6, :].rearrange("p h t -> p (h t)"), ps[1][0:64, 0:256], vvh[:, ws, 4:6, :].rearrange("p h t -> p (h t)"), ALU.mult,
                   [("ps", 1)] + A6("vv"), [("bon", w3, h) for h in range(4, 6)])
                alleend = [("eend", h) for h in HH]
                TT("pool", BPh[:, ws, :, 0, :], kka[:, :, :], eend[:, :, :], ALU.mult, ["kka"] + alleend, A6("BP"))
                TT("pool", BPh[:, ws, :, 1, :], kp[:, :, :], eend[:, :, :], ALU.mult, ["kp"] + alleend, A6("BP"))
                yield

            def make_gens(tt):
                if tt >= NTB:
                    return []
                return [tile_pro(tt), stage0_all(tt)]

            def advance(gl, n):
                for _ in range(n):
                    for g in list(gl):
                        try:
                            next(g)
                        except StopIteration:
                            gl.remove(g)

            def chain(tt, nxt):
                ws = tt % 2
                tsl = slice(tt * TW, (tt + 1) * TW)
                HS = range(H)
                GR = [(0, 3), (3, 6)]
                bk = lambda h: ps[2 + h]
                bkk = lambda h: ("ps", 2 + h)
                gk = lambda g: [bkk(h) for h in range(*g)]
                gkey = lambda nm, g: [(nm, h) for h in range(*g)]
                A0 = lambda h: ("AR0", ws, h)
                A1 = lambda h: ("AR1", ws, h)
                bc = lambda ap, n: ap.unsqueeze(1).to_broadcast([ap.shape[0], n, ap.shape[1]])
                for g in GR:
                    for h in range(*g):
                        for i, (src, kx) in enumerate([(ARh[:, ws, h, 0, :], A0(h)), (BPh[:, ws, h, 0, :], ("BP", ws, h)), (BPh[:, ws, h, 1, :], ("BP", ws, h)), (vvh[:, ws, h, :], ("vv", ws, h))]):
                            TR(bk(h)[:, i * 64:(i + 1) * 64], src, i64, [kx, "cst"], [bkk(h)])
                for g in GR:
                    CP("act", tm[:, g[0]:g[1], :], psbig[:, g[0]:g[1], 0:256], gk(g), gkey("tm", g))
                advance(nxt, 2)
                for g in GR:
                    for h in range(*g):
                        MM(bk(h)[:, 0:256], BKh[:, ws, h, 0, :], ARh[:, ws, h, :, :], True, True, [("BK0", ws, h), A0(h), A1(h)], [bkk(h)], r=True)
                        MM(bk(h)[:, 256:384], ARh[:, ws, h, 0, :], BKh[:, ws, h, 0, :], True, True, [("BK0", ws, h), A0(h)], [bkk(h)], r=True)
                for g in GR:
                    TT("dve", RR(Eb[:, g[0]:g[1], 0:384]), psbig[:, g[0]:g[1], 0:384], bc(mE, 3), ALU.mult, gk(g) + ["cst"], [("E", h, "a") for h in range(*g)])
                advance(nxt, 2)
                for g in GR:
                    for h in range(*g):
                        MM(bk(h)[:, 0:256], BKh[:, ws, h, 1, :], ARh[:, ws, h, :, :], True, True, [("BK1", ws, h), A0(h), A1(h)], [bkk(h)], r=True)
                for g in GR:
                    TT("dve", YS[:, g[0]:g[1], :], psbig[:, g[0]:g[1], 0:256], bc(mY, 3), ALU.mult, gk(g) + ["cst"], gkey("YS", g))
                advance(nxt, 2)
                for g in GR:
                    for h in range(*g):
                        MM(bk(h)[:, 256:320], YS[:, h, 0:128], tm[:, h, 192:256], True, True, [("YS", h), ("tm", h)], [bkk(h)])
                    CP("pool", RR(Eb[:, g[0]:g[1], 384:448]), tm[:, g[0]:g[1], 0:64], gkey("tm", g), [("E", h, "b") for h in range(*g)])
                for g in GR:
                    CP("act", RR(Eb[:, g[0]:g[1], 448:512]), psbig[:, g[0]:g[1], 256:320], gk(g), [("E", h, "c") for h in range(*g)])
                advance(nxt, 2)
                for lev in range(6):
                    def views(h):
                        if lev == 0:
                            return (Eb[:, h, 0:128], Eb[:, h, 256:384], Eb[:, h, 256:512],
                                    [("E", h, "a"), ("E", h, "b"), ("E", h, "c")])
                        Lp = Lb[:, h, (lev - 1) % 2, :]
                        return (Lp[:, 0:128], Lp[:, 128:256], Lp[:, 128:384],
                                [("L", h, (lev - 1) % 2, "p"), ("L", h, (lev - 1) % 2, "z")])
                    for g in GR:
                        for h in range(*g):
                            PT_, P_, PZ_, rk = views(h)
                            MM(bk(h)[:, 128:384], PT_, PZ_, True, True, rk, [bkk(h)], r=True)
                            if lev < 5:
                                MM(bk(h)[:, 0:128], P_, PT_, True, True, rk, [bkk(h)], r=True)
                    for g in GR:
                        gs = slice(g[0], g[1])
                        if lev == 0:
                            Zg = Eb[:, gs, 384:512]
                            rkg = [("E", h, x) for h in range(*g) for x in "abc"]
                        else:
                            Zg = Lb[:, gs, (lev - 1) % 2, 256:384]
                            rkg = [("L", h, (lev - 1) % 2, x) for h in range(*g) for x in "pz"]
                        TT("dve", RR(Lb[:, gs, lev % 2, 256:384]), psbig[:, gs, 256:384], Zg, ALU.add, gk(g) + rkg, [("L", h, lev % 2, "z") for h in range(*g)])
                        if lev < 5:
                            CP("act", RR(Lb[:, gs, lev % 2, 0:256]), psbig[:, gs, 0:256], gk(g), [("L", h, lev % 2, "p") for h in range(*g)])
                    advance(nxt, 3)
                for g in GR:
                    gs = slice(g[0], g[1])
                    for hf in range(2):
                        TS("pool", B2[:, gs, hf * 64:(hf + 1) * 64], tm[:, gs, 64:128], rowm[:, hf:hf + 1], None, ALU.mult, None, gkey("tm", g) + ["cst"], gkey("B2", g))
                        TS("pool", K2[:, gs, hf * 64:(hf + 1) * 64], tm[:, gs, 128:192], rowm[:, hf:hf + 1], None, ALU.mult, None, gkey("tm", g) + ["cst"], gkey("K2", g))
                for g in GR:
                    for h in range(*g):
                        Lf = Lb[:, h, 1, :]
                        kLf = ("L", h, 1, "z")
                        W_, U0_ = Lf[:, 256:320], Lf[:, 320:384]
                        b_ = bk(h)
                        MM(b_[0:64, 0:128], W_, B2[:, h, :], True, True, [kLf, ("B2", h)], [bkk(h)])
                        for hf in range(2):
                            MM(b_[0:64, 128 + hf * 64:128 + (hf + 1) * 64], K2[:, h, hf * 64:(hf + 1) * 64], tm[:, h, 192:256], True, False, [("K2", h), ("tm", h)], [bkk(h)])
                            MM(b_[0:64, 128 + hf * 64:128 + (hf + 1) * 64], B2[:, h, hf * 64:(hf + 1) * 64], U0_, False, True, [("B2", h), kLf], [bkk(h)])
                        MM(b_[0:64, 256:384], W_, Eb[:, h, 128:256], True, True, [kLf, ("E", h, "a")], [bkk(h)])
                        MM(b_[0:64, 384:512], tm[:, h, 192:256], YS[:, h, 128:256], True, False, [("tm", h), ("YS", h)], [bkk(h)])
                        MM(b_[0:64, 384:512], U0_, Eb[:, h, 128:256], False, True, [kLf, ("E", h, "a")], [bkk(h)])
                advance(nxt, 2)
                for g in GR:
                    gs = slice(g[0], g[1])
                    for h in range(*g):
                        b_ = bk(h)
                        for hf in range(2):
                            STT(GT[:, h, hf, :], i64, pC[:, ws, h, hf:hf + 1], b_[0:64, hf * 64:(hf + 1) * 64], ALU.mult, ALU.add,
                                [bkk(h), ("pC", ws, h), "cst"], [("GT", h)])
                    TT("dve", RhatT[:, gs, :], psbig[0:64, gs, 256:384], ARh[:, ws, gs, 1, :], ALU.add, gk(g) + [A1(h) for h in range(*g)], gkey("Rhat", g))
                for g in GR:
                    gs = slice(g[0], g[1])
                    CP("act", Hm[:, gs, :, :].rearrange("p h c i -> p h (c i)"), psbig[0:64, gs, 128:256], gk(g), gkey("Hm", g))
                    CP("act", Y0T[:, gs, :], psbig[0:64, gs, 384:512], gk(g), gkey("Y0T", g))
                advance(nxt, 2)
                allh = lambda nm: [(nm, h) for h in range(H)]
                for c in range(2):
                    csl = slice(c * 64, (c + 1) * 64)
                    m0 = mcur[0]
                    mnew = 1 - m0
                    for h in range(H):
                        MM(ps[0][0:64, h * 64:(h + 1) * 64], Mst[:, m0, h, :], RhatT[:, h, csl], True, True, [("Mst", m0), ("Rhat", h)], [("ps", 0)])
                    for h in range(H):
                        MM(ps[1][0:64, h * 64:(h + 1) * 64], GT[:, h, c, :], Mst[:, m0, h, :], True, True, [("Mst", m0), ("GT", h)], [("ps", 1)])
                    TT("dve", Mst[:, mnew, :, :], ps[1][0:64, 0:384].rearrange("p (h i) -> p h i", h=6), Hm[:, :, c, :], ALU.add,
                       [("ps", 1)] + allh("Hm"), [("Mst", mnew)])
                    TT("dve", yT[:, :, csl], ps[0][0:64, 0:384].rearrange("p (h t) -> p h t", h=6), Y0T[:, :, csl], ALU.add,
                       [("ps", 0)] + allh("Y0T"), [("yT", c)])
                    mcur[0] = mnew
                    advance(nxt, 1)
                advance(nxt, 1000)

            def post(tt, h):
                tsl = slice(tt * TW, (tt + 1) * TW)
                w3 = tt % 3
                ally = [("yT", c) for c in range(2)]
                osl = h
                kyc, kysq, kyrs = ("yc", osl), ("ysq", osl), ("yrs", osl)
                pb = ps[h % 2]
                pk = ("ps", h % 2)
                MM(pb[0:64, 0:TW], o64, yT[:, h, :], True, True, ["cst"] + ally, [pk])
                STT(yc[:, osl, :], pb[0:64, 0:TW], -1.0 / 64, yT[:, h, :], ALU.mult, ALU.add, [pk] + ally, [kyc])
                yield
                ACT(ysq[:, osl, :], yc[:, osl, :], AF.Square, [kyc], [kysq])
                yield
                MM(pb[0:64, 128:128 + TW], o64, ysq[:, osl, :], True, True, ["cst", kysq], [pk])
                ACT(yrs[:, osl, :], pb[0:64, 128:128 + TW], AF.Sqrt, [pk, "epsc"], [kyrs], scale=1.0 / 64, bias=epsc[0:64, 1:2])
                yield
                RECIP(yrs[:, osl, :], yrs[:, osl, :], [kyrs], [kyrs])
                yield
                TT("dve", yc[:, osl, :], yc[:, osl, :], yrs[:, osl, :], ALU.mult, [kyc, kyrs], [kyc])
                yield
                TS("dve", yc[:, osl, :], yc[:, osl, :], plg[:, h:h + 1], plb[:, h:h + 1], ALU.mult, ALU.add, [kyc, "plg", "plb"], [kyc])
                yield
                TT("pool", yc[:, osl, :], yc[:, osl, :], bon[:, w3, h, :], ALU.add, [kyc, ("bon", w3, h)], [kyc])
                yield
                TT("pool", obuf[:, osl, :], yc[:, osl, :], gal[:, w3, h, :], ALU.mult, [kyc, ("gal", w3, h)], [("obuf", osl)])
                DMA(catT[h * 64:(h + 1) * 64, tsl], obuf[:, osl, :], [("obuf", osl)], [("cat", "a", h, tt)])
                yield

            g0 = make_gens(0)
            advance(g0, 1000)
            for tt in range(NTB):
                pg = [post(tt - 1, h) for h in range(H)] if tt > 0 else []
                chain(tt, pg + make_gens(tt + 1))
            advance([post(NTB - 1, h) for h in range(H)], 1000)
            sc.barrier()
        if stop == "B":
            break

        with ExitStack() as es:
            wO = sb(es, "wO", [128, 2, 8, 128])
            catb = sb(es, "catb", [128, 2, 8, 512])
            mixb = sb(es, "mixb", [128, 8, 512])
            sq = sb(es, "sq", [128, 2, 512])
            rstd = sb(es, "rstd", [128, 2, 512])
            gP = colvec(es, "gP", PR["post_mix_g"][l], 8)
            w_out_l = PR["w_out"][l].rearrange("(k p) c -> p k c", p=128)
            cat_v = catT.rearrange("(k p) t -> p k t", p=128)
            wi = 0
            xe = sb(es, "xe", [128, 2, 8, 512])
            for tt in range(4):
                tsl = slice(tt * 512, (tt + 1) * 512)
                cs = tt % 2
                DMA(xe[:, cs, :, :], xD_v[:, :, tsl], [("xD", tt)], [("xe", cs, j) for j in range(8)])
                DMA(RR(catb[:, cs, :, :]), cat_v[:, :, tsl], [], [("catb", cs)], q="pool")
                for j in range(8):
                    sl = wi % 2
                    wi += 1
                    DMA(RR(wO[:, sl, :, :]), w_out_l[:, :, j * 128:(j + 1) * 128], [], [("wO", sl)], q="pool")
                    bi = j % 2
                    for k in range(8):
                        MM(ps[bi][:, :], wO[:, sl, k, :], catb[:, cs, k, :], k == 0, k == 7, [("wO", sl), ("catb", cs)], [("ps", bi)], r=True)
                    CP("act" if j % 2 else "dve", mixb[:, j, :], ps[bi][:, :], [("ps", bi)], [("mixb", j)])
                r = rms_stats(lambda k: mixb[:, k, :], lambda k: ("mixb", k), tt, (sq, rstd), ps[2 + tt % 2], ("ps", 2 + tt % 2))
                for j in range(8):
                    STT(mixb[:, j, :], mixb[:, j, :], gP[:, j:j + 1], r, ALU.mult, ALU.mult, [("mixb", j), ("rstd", tt % 2), "gP"], [("mixb", j)])
                    TT("dve", xe[:, cs, j, :], xe[:, cs, j, :], mixb[:, j, :], ALU.add, [("mixb", j), ("xe", cs, j)], [("xe", cs, j)])
                DMA(xD_v[:, :, tsl], xe[:, cs, :, :], [("xe", cs, j) for j in range(8)], [("xD", tt)])
            sc.barrier()
        if stop == "E":
            break

        with ExitStack() as es:
            TF = 1024
            hb = sb(es, "hb", [128, 8, TF])
            actb = sb(es, "actb", [128, NFF, TF])
            shm = sb(es, "shm", [128, 8 * TF])
            xf = shm[:, :].rearrange("p (k t) -> p k t", k=8)
            wD = shm[:, 0:2 * NFF * 128].rearrange("p (s j c) -> p s j c", s=2, j=NFF)
            wG = sb(es, "wG", [128, 2, 2, 8, 128])
            sq = sb(es, "sq", [128, 2, 512])
            rstd = sb(es, "rstd", [128, 2, 512])
            sil = sb(es, "sil", [128, 2, 512])
            xc = sb(es, "xc", [128, 2, TF])
            gF = colvec(es, "gF", PR["pre_ffn_g"][l], 8)
            gQ = colvec(es, "gQ", PR["post_ffn_g"][l], 8)
            w_fi = PR["w_ffn_in"][l].rearrange("(k p) c -> p k c", p=128)
            w_fo = PR["w_ffn_out"][l].rearrange("(j p) c -> p j c", p=128)
            wi = 0
            wdi = 0
            si = 0
            xci = 0
            SHK = ["sh", ("wD", 0), ("wD", 1)]
            for tt in range(S // TF):
                tsl = slice(tt * TF, (tt + 1) * TF)
                DMA(xf, xD_v[:, :, tsl], [("xD", 2 * tt), ("xD", 2 * tt + 1)], SHK)
                for hf in range(2):
                    hsl = slice(hf * 512, (hf + 1) * 512)
                    r = rms_stats(lambda k: xf[:, k, hsl], lambda k: "sh", si, (sq, rstd), ps[6 + si % 2], ("ps", 6 + si % 2))
                    for k in range(8):
                        STT(RR(hb[:, k, hsl]), xf[:, k, hsl], gF[:, k:k + 1], r, ALU.mult, ALU.mult, SHK + [("rstd", si % 2), "gF"], [("hb", k, hf)])
                    si += 1
                for j in range(NFF):
                    sl = wi % 2
                    wi += 1
                    DMA(RR(wG[:, sl, 0, :, :]), w_fi[:, :, j * 128:(j + 1) * 128], [], [("wG", sl, 0)], q="pool")
                    DMA(RR(wG[:, sl, 1, :, :]), w_fi[:, :, DFF + j * 128:DFF + (j + 1) * 128], [], [("wG", sl, 1)], q="pool")
                    for hf in range(2):
                        hsl = slice(hf * 512, (hf + 1) * 512)
                        bg, bu = hf * 2, hf * 2 + 1
                        for k in range(8):
                            MM(ps[bg][:, :], wG[:, sl, 0, k, :], hb[:, k, hsl], k == 0, k == 7, [("wG", sl, 0), ("hb", k, hf)], [("ps", bg)], r=True)
                        for k in range(8):
                            MM(ps[bu][:, :], wG[:, sl, 1, k, :], hb[:, k, hsl], k == 0, k == 7, [("wG", sl, 1), ("hb", k, hf)], [("ps", bu)], r=True)
                        ACT(sil[:, hf, :], ps[bg][:, :], AF.Silu, [("ps", bg)], [("sil", hf)])
                        TT("dve", RR(actb[:, j, hsl]), ps[bu][:, :], sil[:, hf, :], ALU.mult, [("ps", bu), ("sil", hf)], [("actb", j, hf)])
                for jo in range(8):
                    sl = wdi % 2
                    wdi += 1
                    DMA(RR(wD[:, sl, :, :]), w_fo[:, :, jo * 128:(jo + 1) * 128], [], [("wD", sl)], q="pool")
                    for hf in range(2):
                        hsl = slice(hf * 512, (hf + 1) * 512)
                        bi = 4 + hf
                        for j in range(NFF):
                            MM(ps[bi][:, :], wD[:, sl, j, :], actb[:, j, hsl], j == 0, j == NFF - 1, [("wD", sl), ("actb", j, hf)], [("ps", bi)], r=True)
                        CP("act" if hf else "dve", RR(hb[:, jo, hsl]), ps[bi][:, :], [("ps", bi)], [("hb", jo, hf)])
                for hf in range(2):
                    hsl = slice(hf * 512, (hf + 1) * 512)
                    r = rms_stats(lambda k: hb[:, k, hsl], lambda k: ("hb", k, hf), si, (sq, rstd), ps[6 + si % 2], ("ps", 6 + si % 2))
                    for j in range(8):
                        STT(RR(hb[:, j, hsl]), hb[:, j, hsl], gQ[:, j:j + 1], r, ALU.mult, ALU.mult, [("hb", j, hf), ("rstd", si % 2), "gQ"], [("hb", j, hf)])
                    si += 1
                for j in range(8):
                    cs = xci % 2
                    xci += 1
                    DMA(xc[:, cs, :], xD[j * 128:(j + 1) * 128, tsl], [("xD", 2 * tt), ("xD", 2 * tt + 1)], [("xc", cs)])
                    TT("pool" if j % 2 else "dve", xc[:, cs, :], xc[:, cs, :], hb[:, j, :], ALU.add, [("xc", cs), ("hb", j, 0), ("hb", j, 1)], [("xc", cs)])
                    DMA(xD[j * 128:(j + 1) * 128, tsl], xc[:, cs, :], [("xc", cs)], [("xDw", tt, j)])
            sc.barrier()
        if OPTS.get("xdbg") == l:
            DMA(xdbg[:, :], xD[:, :], [("xD", tt) for tt in range(4)], [("xdbg", 0)])

    if stop in (None, 'setup'):
        with ExitStack() as es:
            yo = sb(es, "yo", [128, 2, D])
            xo = sb(es, "xo", [128, 2, 8, 512])
            for t in range(16):
                sl = t % 2
                xsl = (t // 4) % 2
                if t % 4 == 0:
                    DMA(xo[:, xsl, :, :], xD_v[:, :, (t // 4) * 512:(t // 4 + 1) * 512], [("xD", t // 4)], [("xo", xsl)])
                for g in range(2):
                    bank = ps[(t * 2 + g) % 4]
                    bk = ("ps", (t * 2 + g) % 4)
                    for kk in range(4):
                        k = g * 4 + kk
                        TR(bank[:, kk * 128:(kk + 1) * 128], xo[:, xsl, k, (t % 4) * 128:(t % 4 + 1) * 128], ident, [("xo", xsl), "cst"], [bk])
                    CP("act" if g else "dve", yo[:, sl, g * 512:(g + 1) * 512], bank[:, :], [bk], [("yo", sl, g)])
                DMA(y_out[t * 128:(t + 1) * 128, :], yo[:, sl, :], [("yo", sl, 0), ("yo", sl, 1)], [("y", t)])
    sc.barrier()
    sc.emit()
    glob.close()
    return nc


_NC_CACHE = {}


def kernel(**inputs):
    if "nc" not in _NC_CACHE:
        _NC_CACHE["nc"] = build()
    nc = _NC_CACHE["nc"]
    x = np.ascontiguousarray(np.asarray(inputs["x"], dtype=np.float32))
    base = {n: np.ascontiguousarray(np.asarray(inputs[n], dtype=np.float32)) for n in PARAM_NAMES}
    base["cst"] = CONSTS["cst"]
    base["mobac"] = CONSTS["mobac"]
    base["blkind"] = CONSTS["blkind"]
    in_maps = []
    for b in range(8):
        m = dict(base)
        m["x"] = x[b]
        in_maps.append(m)
    res = run_bass_kernel_spmd(nc, in_maps, core_ids=list(range(8)))
    return np.stack([np.asarray(r["y"], dtype=np.float32) for r in res.results], 0)
```

```python
import numpy as np
from contextlib import ExitStack
import concourse.bass as bass
import concourse.mybir as mybir
from concourse.bass_utils import run_bass_kernel_spmd

F32 = mybir.dt.float32
F32R = mybir.dt.float32r
AF = mybir.ActivationFunctionType
ALU = mybir.AluOpType
AX = mybir.AxisListType

S = 2048
D = 1024
L = 2
DFF = 2816
NFF = 22
H = 6
C0 = float(np.exp(-0.5))
BIG = 30000.0


OPTS = {}


class Sched:
    def __init__(self, nc, n_dma=40):
        self.nc = nc
        self.names = ["pe", "act", "dve", "pool", "sp"]
        self.sem = {e: nc.alloc_semaphore("s_" + e) for e in ["pe", "act", "dve", "pool"]}
        self.cnt = {e: 0 for e in self.sem}
        self.dsem = [nc.alloc_semaphore("d%d" % i) for i in range(n_dma)]
        self.dcnt = [0] * n_dma
        self.drr = 0
        self.q = {e: [] for e in self.names}
        self.clock = {e: {} for e in self.names}
        self.evclock = {}
        self.evorder = {}
        self.nev = 0
        self.lastw = {}
        self.readers = {}

    def _deps(self, reads, writes):
        deps = {}

        def add(k, v):
            if deps.get(k, 0) < v:
                deps[k] = v

        for r in reads:
            ev = self.lastw.get(r)
            if ev is not None:
                add(*ev)
        for w in writes:
            ev = self.lastw.get(w)
            if ev is not None:
                add(*ev)
            for k, v in self.readers.get(w, {}).items():
                add(k, v)
        return deps

    def _commit(self, ev, reads, writes):
        k, v = ev
        for r in reads:
            d = self.readers.setdefault(r, {})
            if d.get(k, 0) < v:
                d[k] = v
        for w in writes:
            self.lastw[w] = ev
            self.readers[w] = {}

    def _waits(self, eng, deps):
        clk = self.clock.setdefault(eng, {})
        waits = []
        for k, v in sorted(deps.items(), key=lambda kv: -self.evorder.get(kv, 0)):
            if eng == "pe" and k == ("e", "pe"):
                continue
            if clk.get(k, 0) >= v:
                continue
            waits.append((k, v))
            for k2, v2 in self.evclock.get((k, v), {}).items():
                if clk.get(k2, 0) < v2:
                    clk[k2] = v2
            clk[k] = v
        return waits

    def op(self, eng, fn, reads=(), writes=()):
        banks = {("psx", k[1]) for k in list(reads) + list(writes) if isinstance(k, tuple) and k and k[0] == "ps"}
        if banks:
            writes = list(writes) + list(banks)
        deps = self._deps(reads, writes)
        waits = self._waits(eng, deps)
        self.cnt[eng] += 1
        ev = (("e", eng), self.cnt[eng])
        self.evclock[ev] = dict(self.clock[eng])
        self.nev += 1
        self.evorder[ev] = self.nev
        self.q[eng].append((waits, fn, "e"))
        self._commit(ev, reads, writes)

    def dma(self, qeng, out, in_, reads=(), writes=(), **kw):
        deps = self._deps(reads, writes)
        idx = self.drr
        self.drr = (self.drr + 1) % len(self.dsem)
        if self.dcnt[idx] > 0:
            k = ("d", idx)
            deps[k] = max(deps.get(k, 0), self.dcnt[idx])
        waits = self._waits(qeng, deps)
        self.dcnt[idx] += 16
        ev = (("d", idx), self.dcnt[idx])
        self.evclock[ev] = dict(self.clock[qeng])
        self.nev += 1
        self.evorder[ev] = self.nev
        self.q[qeng].append((waits, lambda e: e.dma_start(out=out, in_=in_, **kw), idx))
        self._commit(ev, reads, writes)

    def barrier(self):
        allev = [(("e", e), c) for e, c in self.cnt.items() if c > 0]
        allev += [(("d", i), c) for i, c in enumerate(self.dcnt) if c > 0]
        for eng in self.names:
            waits = self._waits(eng, dict(allev))
            if waits:
                self.q[eng].append((waits, None, None))
        self.lastw = {}
        self.readers = {}
        self.evclock = {}

    def emit(self):
        nc = self.nc
        engs = {"pe": "tensor", "act": "scalar", "dve": "vector", "pool": "gpsimd", "sp": "sync"}
        with nc.Block() as block:
            for name in self.names:
                def body(eng, name=name):
                    for waits, fn, kind in self.q[name]:
                        emb = None
                        if fn is not None and waits and kind == "e" and not OPTS.get("noemb"):
                            emb = waits[-1]
                            waits = waits[:-1]
                        for k, v in waits:
                            s = self.sem[k[1]] if k[0] == "e" else self.dsem[k[1]]
                            eng.wait_ge(s, v)
                        if fn is None:
                            continue
                        ins = fn(eng)
                        if emb is not None:
                            k, v = emb
                            ins._wait_ge(self.sem[k[1]] if k[0] == "e" else self.dsem[k[1]], v)
                        if kind == "e":
                            ins.then_inc(self.sem[name], 1)
                        else:
                            ins.then_inc(self.dsem[kind], 16)
                getattr(block, engs[name])(body)


def make_consts():
    c = {}
    i128 = np.arange(128)
    blk = (i128[:, None] // 64) == (i128[None, :] // 64)
    ident = np.eye(128, dtype=np.float32)
    ones = np.ones((128, 128), np.float32)
    SL = ((i128[:, None] > i128[None, :]) & blk).astype(np.float32)
    SU = ((i128[:, None] < i128[None, :]) & blk).astype(np.float32)
    IU = ((i128[:, None] <= i128[None, :]) & blk).astype(np.float32)
    IUfull = (i128[:, None] <= i128[None, :]).astype(np.float32)
    idst = np.concatenate([np.eye(64), np.eye(64)], 0).astype(np.float32)
    reset = np.ones((128, 768), np.float32)
    reset[:, ::64] = 0.0
    rowm = np.zeros((128, 2), np.float32)
    rowm[:64, 0] = 1.0
    rowm[64:, 1] = 1.0
    q512 = np.arange(512)
    cm = np.stack([(q512[None, :] >= (j * 128 + i128[:, None])).astype(np.float32) for j in range(4)], 1)
    parts = [ident, ones, SU, IU, SL, SU, IU, IUfull, idst, reset, rowm, cm.reshape(128, 2048)]
    offs = {}
    o = 0
    for nm, p in zip(["ident", "ones", "mE", "_1", "_2", "mY", "_3", "iuf", "idst", "reset", "rowm", "cm"], parts):
        offs[nm] = o
        o += p.shape[1]
    c["cst"] = np.ascontiguousarray(np.concatenate(parts, 1))
    c["offs"] = offs
    mb = np.zeros((128, 3, 16, 6, 8), np.float32)
    for t in range(16):
        b = t // 2
        for n in range(8):
            mb[:, 0, t, :, n] = 0.0 if n < b else -1e30
            mb[:, 1, t, :, n] = 1.0 if n < b else 0.0
            mb[:, 2, t, :, n] = 1.0 if n == b else 0.0
    c["mobac"] = mb.reshape(128, 3 * 16 * 48)
    bi = np.zeros((8, S), np.float32)
    for n in range(8):
        bi[n, n * 256:(n + 1) * 256] = 1.0
    c["blkind"] = bi
    return c


CONSTS = make_consts()
PARAM_NAMES = ["pre_mix_g", "w_in", "rwkv_mu", "rwkv_w0", "rwkv_w2", "rwkv_a0", "rwkv_a2", "rwkv_g2",
               "rwkv_k_k", "rwkv_k_a", "rwkv_r_k", "rwkv_lnx_g", "rwkv_lnx_b", "gmlp_ln_g", "gmlp_ln_b",
               "gmlp_w_s", "gmlp_b_s", "w_out", "post_mix_g", "pre_ffn_g", "w_ffn_in", "w_ffn_out", "post_ffn_g"]
PARAM_SHAPES = {"pre_mix_g": (L, D), "w_in": (L, D, 3072), "rwkv_mu": (L, 1408), "rwkv_w0": (L, 384),
                "rwkv_w2": (L, 64, 384), "rwkv_a0": (L, 384), "rwkv_a2": (L, 64, 384), "rwkv_g2": (L, 128, 384),
                "rwkv_k_k": (L, 384), "rwkv_k_a": (L, 384), "rwkv_r_k": (L, 6, 64), "rwkv_lnx_g": (L, 384),
                "rwkv_lnx_b": (L, 384), "gmlp_ln_g": (L, 256), "gmlp_ln_b": (L, 256), "gmlp_w_s": (L, 4, 128, 128),
                "gmlp_b_s": (L, 4, 128), "w_out": (L, D, D), "post_mix_g": (L, D), "pre_ffn_g": (L, D),
                "w_ffn_in": (L, D, 2 * DFF), "w_ffn_out": (L, DFF, D), "post_ffn_g": (L, D)}


def build(dbg=None, nlayers=L, stop=None):
    dbg = dbg or []
    nc = bass.Bass("TRN2", target_bir_lowering=False)
    sc = Sched(nc)
    OF = CONSTS["offs"]

    def dram(name, shape, kind="Internal"):
        if name in dbg:
            kind = "ExternalOutput"
        return nc.dram_tensor(name, list(shape), F32, kind=kind).ap()

    x_in = dram("x", [S, D], "ExternalInput")
    y_out = dram("y", [S, D], "ExternalOutput")
    cst_d = dram("cst", CONSTS["cst"].shape, "ExternalInput")
    if not OPTS.get("noparams"):
        PR = {n: dram(n, PARAM_SHAPES[n], "ExternalInput") for n in PARAM_NAMES}
        mobac_d = dram("mobac", CONSTS["mobac"].shape, "ExternalInput")
        blkind_d = dram("blkind", CONSTS["blkind"].shape, "ExternalInput")
    if OPTS.get("noscratch"):
        glob_scr = None
    rwkvT = dram("rwkvT", [1408, S]) if not OPTS.get("noscratch") else None
    qkT = dram("qkT", [768, S]) if not OPTS.get("noscratch") else None
    uT = dram("uT", [256, S]) if not OPTS.get("noscratch") else None
    vm_tm = dram("vm_tm", [S, 384]) if not OPTS.get("noscratch") else None
    vg_tm = dram("vg_tm", [S, 256]) if not OPTS.get("noscratch") else None
    catT = dram("catT", [D, S]) if not OPTS.get("noscratch") else None
    xdbg = dram("xdbg", [D, S]) if not OPTS.get("noscratch") else None
    xD = dram("xD", [D, S])
    xD_v = xD.rearrange("(k p) t -> p k t", p=128)

    uid = [0]

    def sb(es, name, shape):
        uid[0] += 1
        return es.enter_context(nc.sbuf_tensor("%s_%d" % (name, uid[0]), list(shape), F32))

    glob = ExitStack()
    cst = sb(glob, "cst_sb", [128, CONSTS["cst"].shape[1]])
    ps = [glob.enter_context(nc.psum_tensor("ps%d" % i, [128, 512], F32)) for i in range(2)]
    psbig = glob.enter_context(nc.psum_tensor("psbig", [128, 6, 512], F32))
    ps = ps + [psbig[:, h, :] for h in range(6)]
    ident = cst[:, OF["ident"]:OF["ident"] + 128]
    ones = cst[:, OF["ones"]:OF["ones"] + 128]
    mE = cst[:, OF["mE"]:OF["mE"] + 384]
    mY = cst[:, OF["mY"]:OF["mY"] + 256]
    iuf = cst[:, OF["iuf"]:OF["iuf"] + 128]
    idst = cst[:, OF["idst"]:OF["idst"] + 64]
    resetm = cst[:, OF["reset"]:OF["reset"] + 768]
    cm = cst[:, OF["cm"]:OF["cm"] + 2048]
    epsc = sb(glob, "epsc", [128, 4])

    def ACT(out, in_, func, reads, writes, **kw):
        sc.op("act", lambda e: e.activation(out=out, in_=in_, func=func, **kw), reads, writes)

    def RR(ap):
        return ap if OPTS.get("nor") else ap.bitcast(F32R)

    def MM(out, lhsT, rhs, start, stop, reads, writes, r=False):
        if r and not OPTS.get("nor"):
            lhsT = lhsT.bitcast(F32R)
            rhs = rhs.bitcast(F32R)
        sc.op("pe", lambda e: e.matmul(out, lhsT=lhsT, rhs=rhs, start=start, stop=stop), reads, writes)

    def TR(out, in_, idn, reads, writes):
        sc.op("pe", lambda e: e.transpose(out, in_, idn), reads, writes)

    def TT(eng, out, in0, in1, op, reads, writes):
        sc.op(eng, lambda e: e.tensor_tensor(out=out, in0=in0, in1=in1, op=op), reads, writes)

    def TS(eng, out, in0, s1, s2, op0, op1, reads, writes):
        if s2 is None:
            sc.op(eng, lambda e: e.tensor_scalar(out=out, in0=in0, scalar1=s1, scalar2=None, op0=op0), reads, writes)
        else:
            sc.op(eng, lambda e: e.tensor_scalar(out=out, in0=in0, scalar1=s1, scalar2=s2, op0=op0, op1=op1), reads, writes)

    def STT(out, in0, scalar, in1, op0, op1, reads, writes):
        sc.op("dve", lambda e: e.scalar_tensor_tensor(out=out, in0=in0, scalar=scalar, in1=in1, op0=op0, op1=op1), reads, writes)

    def CP(eng, out, in_, reads, writes):
        if eng == "act":
            sc.op("act", lambda e: e.copy(out=out, in_=in_), reads, writes)
        else:
            sc.op(eng, lambda e: e.tensor_copy(out=out, in_=in_), reads, writes)

    def RECIP(out, in_, reads, writes):
        sc.op("dve", lambda e: e.reciprocal(out=out, in_=in_), reads, writes)

    def DMA(out, in_, reads, writes, q="sp", **kw):
        sc.dma(q, out, in_, reads, writes, **kw)

    def colvec(es, name, src_1d, ncol, p=128):
        t = sb(es, name, [p, ncol])
        DMA(t[:, :], src_1d.rearrange("(c p) -> p c", p=p), [], [name], allow_slow_non_contiguous=True)
        return t

    DMA(cst[:, :], cst_d[:, :], [], ["cst"])
    sc.op("dve", lambda e: e.memset(epsc[:, 0:1], 1e-6), [], ["epsc"])
    sc.op("dve", lambda e: e.memset(epsc[:, 1:2], 64e-5), [], ["epsc"])
    sc.op("dve", lambda e: e.memset(epsc[:, 2:3], 0.0), [], ["epsc"])
    with ExitStack() as es:
        xin = sb(es, "xin", [128, 2, D])
        xs = sb(es, "xs", [128, 2, 8, 512])
        for t in range(16):
            sl = t % 2
            xsl = (t // 4) % 2
            DMA(xin[:, sl, :], x_in[t * 128:(t + 1) * 128, :], [], [("xin", sl)])
            for g in range(2):
                bank = ps[(t * 2 + g) % 4]
                bk = ("ps", (t * 2 + g) % 4)
                for kk in range(4):
                    k = g * 4 + kk
                    TR(bank[:, kk * 128:(kk + 1) * 128], xin[:, sl, k * 128:(k + 1) * 128], ident, [("xin", sl), "cst"], [bk])
                CP("act" if g else "dve", xs[:, xsl, g * 4:(g + 1) * 4, (t % 4) * 128:(t % 4 + 1) * 128],
                   bank[:, :].rearrange("p (k c) -> p k c", k=4), [bk], [("xs", xsl, t % 4, g)])
            if t % 4 == 3:
                DMA(xD_v[:, :, (t // 4) * 512:(t // 4 + 1) * 512], xs[:, xsl, :, :], [("xs", xsl, q, g) for q in range(4) for g in range(2)], [("xD", t // 4)])
        sc.barrier()

    def rms_stats(src_fn, src_keys, tt, es_tiles, pbank, pkey):
        sq, rstd = es_tiles
        for k in range(8):
            ACT(sq[:, k % 2, :], src_fn(k), AF.Square, [src_keys(k)], [("sq", k % 2)])
            MM(pbank[:, :], ones, sq[:, k % 2, :], k == 0, k == 7, [("sq", k % 2), "cst"], [pkey])
        ACT(rstd[:, tt % 2, :], pbank[:, :], AF.Sqrt, [pkey, "epsc"], [("rstd", tt % 2)], scale=1.0 / D, bias=epsc[:, 0:1])
        RECIP(rstd[:, tt % 2, :], rstd[:, tt % 2, :], [("rstd", tt % 2)], [("rstd", tt % 2)])
        return rstd[:, tt % 2, :]

    for l in range(nlayers if stop != 'setup' else 0):
        with ExitStack() as es:
            hbuf = sb(es, "hbuf", [128, 8, S])
            sq = sb(es, "sq", [128, 2, 512])
            rstd = sb(es, "rstd", [128, 2, 512])
            gA = colvec(es, "gA", PR["pre_mix_g"][l], 8)
            muA = colvec(es, "muA", PR["rwkv_mu"][l], 11)
            xa = sb(es, "xa", [128, 2, 8, 512])
            for tt in range(4):
                tsl = slice(tt * 512, (tt + 1) * 512)
                xsl = tt % 2
                DMA(xa[:, xsl, :, :], xD_v[:, :, tsl], [("xD", tt)], [("xa", xsl)])
                r = rms_stats(lambda k: xa[:, xsl, k, :], lambda k: ("xa", xsl), tt, (sq, rstd), ps[4 + tt % 2], ("ps", 4 + tt % 2))
                for k in range(8):
                    STT(RR(hbuf[:, k, tsl]), xa[:, xsl, k, :], gA[:, k:k + 1], r, ALU.mult, ALU.mult,
                        [("xa", xsl), ("rstd", tt % 2), "gA"], [("h", k, tt)])
            es_main = es
            es = ExitStack()
            wA = sb(es, "wA", [128, 3, 8, 128])
            stg = sb(es, "stg", [128, 2, S])
            stg2 = sb(es, "stg2", [128, 2, S])
            w_in_l = PR["w_in"][l].rearrange("(k p) c -> p k c", p=128)
            fm_chunks = [(c * 128, rwkvT, c * 128, True) for c in range(11)]
            fm_chunks += [(1408 + c * 128, qkT, c * 128, False) for c in range(6)]
            fm_chunks += [(2560 + c * 128, uT, c * 128, False) for c in range(2)]
            def load_wA(ci):
                col0 = fm_chunks[ci][0]
                DMA(RR(wA[:, ci % 3, :, :]), w_in_l[:, :, col0:col0 + 128], [], [("wA", ci % 3)], q="pool")

            load_wA(0)
            load_wA(1)
            for ci, (col0, dst, row0, shift) in enumerate(fm_chunks):
                sl = ci % 2
                if ci + 2 < len(fm_chunks):
                    load_wA(ci + 2)
                for tt in range(4):
                    tsl = slice(tt * 512, (tt + 1) * 512)
                    bi = (ci * 4 + tt) % 4
                    for k in range(8):
                        MM(ps[bi][:, :], wA[:, ci % 3, k, :], hbuf[:, k, tsl], k == 0, k == 7,
                           [("wA", ci % 3), ("h", k, tt)], [("ps", bi)], r=True)
                    CP("act" if tt % 2 else "dve", stg[:, sl, tsl], ps[bi][:, :], [("ps", bi)], [("stg", sl, tt)])
                allst = [("stg", sl, tt) for tt in range(4)]
                if shift:
                    TT("pool", stg2[:, sl, 1:S], stg[:, sl, 0:S - 1], stg[:, sl, 1:S], ALU.subtract, allst, [("stg2", sl)])
                    TS("pool", stg2[:, sl, 0:1], stg[:, sl, 0:1], -1.0, None, ALU.mult, None, allst, [("stg2", sl)])
                    STT(stg2[:, sl, :], stg2[:, sl, :], muA[:, ci:ci + 1], stg[:, sl, :], ALU.mult, ALU.add,
                        allst + [("stg2", sl), "muA"], [("stg2", sl)])
                    DMA(dst[row0:row0 + 128, :], stg2[:, sl, :], [("stg2", sl)], [("dr", id(dst), row0)], q="pool")
                else:
                    DMA(dst[row0:row0 + 128, :], stg[:, sl, :], allst, [("dr", id(dst), row0)], q="pool")
            sc.barrier()
            es.close()
            es = es_main
            wB = sb(es, "wB", [128, 8, 640])
            DMA(RR(wB[:, :, 0:384]), w_in_l[:, :, 2176:2560], [], ["wB"], q="pool")
            DMA(RR(wB[:, :, 384:640]), w_in_l[:, :, 2816:3072], [], ["wB"], q="pool")
            vst = sb(es, "vst", [128, 2, 640])
            for t in range(16):
                sl = t % 2
                b0, b1 = 4 + (t % 2) * 2, 5 + (t % 2) * 2
                for k in range(8):
                    MM(ps[b0][:, 0:384], hbuf[:, k, t * 128:(t + 1) * 128], wB[:, k, 0:384], k == 0, k == 7,
                       [("h", k, t // 4), "wB"], [("ps", b0)], r=True)
                for k in range(8):
                    MM(ps[b1][:, 0:256], hbuf[:, k, t * 128:(t + 1) * 128], wB[:, k, 384:640], k == 0, k == 7,
                       [("h", k, t // 4), "wB"], [("ps", b1)], r=True)
                CP("act", vst[:, sl, 0:384], ps[b0][:, 0:384], [("ps", b0)], [("vst", sl, 0)])
                CP("dve", vst[:, sl, 384:640], ps[b1][:, 0:256], [("ps", b1)], [("vst", sl, 1)])
                DMA(vm_tm[t * 128:(t + 1) * 128, :], vst[:, sl, 0:384], [("vst", sl, 0)], [("vm", t)], q="pool")
                DMA(vg_tm[t * 128:(t + 1) * 128, :], vst[:, sl, 384:640], [("vst", sl, 1)], [("vg", t)], q="pool")
            sc.barrier()
        if stop == "A":
            break

        with ExitStack() as es:
            lng = sb(es, "lng", [128, 256])
            lnb = sb(es, "lnb", [128, 256])
            DMA(lng[:, :], PR["gmlp_ln_g"][l].partition_broadcast(128), [], ["lng"])
            DMA(lnb[:, :], PR["gmlp_ln_b"][l].partition_broadcast(128), [], ["lnb"])
            wsn = sb(es, "wsn", [128, 4, 128])
            wsT = sb(es, "wsT", [128, 4, 128])
            bsr = sb(es, "bsr", [1, 512])
            DMA(wsn[:, :, :], PR["gmlp_w_s"][l].rearrange("g t s -> t g s"), [], ["wsn"])
            DMA(bsr[:, :], PR["gmlp_b_s"][l].rearrange("g t -> (g t)").partition_broadcast(1), [], ["bsr"])
            for g in range(4):
                TR(ps[0][:, g * 128:(g + 1) * 128], wsn[:, g, :], ident, ["wsn", "cst"], [("ps", 0)])
            for g in range(4):
                TT("dve", wsT[:, g, :], ps[0][:, g * 128:(g + 1) * 128], iuf, ALU.mult, [("ps", 0), "cst"], ["wsT"])
            gu = sb(es, "gu", [128, 2, S])
            t1 = sb(es, "t1", [128, S])
            cout = sb(es, "cout", [128, 2, S])
            for pp in range(2):
                DMA(gu[:, pp, :], uT[pp * 128:(pp + 1) * 128, :], [], [("gu", pp)])
                ACT(t1[:, :], gu[:, pp, :], AF.Square, [("gu", pp)], ["t1"])
                TS("pool", t1[:, :], t1[:, :], 0.044715, 1.0, ALU.mult, ALU.add, ["t1"], ["t1"])
                TT("dve", t1[:, :], t1[:, :], gu[:, pp, :], ALU.mult, ["t1", ("gu", pp)], ["t1"])
                ACT(t1[:, :], t1[:, :], AF.Sigmoid, ["t1"], ["t1"], scale=2.0 * 0.7978845608028654)
                TT("dve", gu[:, pp, :], gu[:, pp, :], t1[:, :], ALU.mult, ["t1", ("gu", pp)], [("gu", pp)])
            vb = sb(es, "vb", [128, 2, 256])
            t2 = sb(es, "t2", [128, 2, 256])
            st6 = sb(es, "st6", [128, 2, 8])
            for c in range(16):
                sl = c % 2
                DMA(vb[:, sl, :], vg_tm[c * 128:(c + 1) * 128, :], [], [("vb", sl)])
                kv, kt = ("vb", sl), ("t2", sl)
                ACT(t2[:, sl, :], vb[:, sl, :], AF.Square, [kv], [kt])
                TS("pool", t2[:, sl, :], t2[:, sl, :], 0.044715, 1.0, ALU.mult, ALU.add, [kt], [kt])
                TT("dve", t2[:, sl, :], t2[:, sl, :], vb[:, sl, :], ALU.mult, [kt, kv], [kt])
                ACT(t2[:, sl, :], t2[:, sl, :], AF.Sigmoid, [kt], [kt], scale=2.0 * 0.7978845608028654)
                TT("dve", vb[:, sl, :], vb[:, sl, :], t2[:, sl, :], ALU.mult, [kt, kv], [kv])
                ks = ("st6", sl)
                sc.op("dve", lambda e, sl=sl: e.bn_stats(out=st6[:, sl, 0:6], in_=vb[:, sl, :]), [kv], [ks])
                sc.op("dve", lambda e, sl=sl: e.bn_aggr(out=st6[:, sl, 6:8], in_=st6[:, sl, 0:6]), [ks], [ks])
                ACT(st6[:, sl, 7:8], st6[:, sl, 7:8], AF.Sqrt, [ks, "epsc"], [ks], bias=epsc[:, 0:1], scale=1.0)
                RECIP(st6[:, sl, 7:8], st6[:, sl, 7:8], [ks], [ks])
                TS("dve", vb[:, sl, :], vb[:, sl, :], st6[:, sl, 6:7], st6[:, sl, 7:8], ALU.subtract, ALU.mult, [kv, ks], [kv])
                TT("dve", vb[:, sl, :], vb[:, sl, :], lng[:, :], ALU.mult, [kv, "lng"], [kv])
                TT("dve", vb[:, sl, :], vb[:, sl, :], lnb[:, :], ALU.add, [kv, "lnb"], [kv])
                for pp in range(2):
                    bi = 1 + (c % 2) * 2 + pp
                    for gg in range(2):
                        g = pp * 2 + gg
                        MM(ps[bi][:, gg * 128:(gg + 1) * 128], vb[:, sl, pp * 128:(pp + 1) * 128], wsT[:, g, :], True, False, [kv, "wsT"], [("ps", bi)])
                        MM(ps[bi][:, gg * 128:(gg + 1) * 128], ones[0:1, :], bsr[0:1, g * 128:(g + 1) * 128], False, True, ["cst", "bsr"], [("ps", bi)])
                    for gg in range(2):
                        TT("dve", cout[gg * 64:(gg + 1) * 64, pp, c * 128:(c + 1) * 128], ps[bi][gg * 64:(gg + 1) * 64, gg * 128:(gg + 1) * 128],
                           gu[gg * 64:(gg + 1) * 64, pp, c * 128:(c + 1) * 128], ALU.mult, [("ps", bi), ("gu", pp)], [("cout", pp)])
            for pp in range(2):
                DMA(catT[768 + pp * 128:768 + (pp + 1) * 128, :], cout[:, pp, :], [("cout", pp)], [("cat", 6 + pp)], q="pool")
            sc.barrier()
        if stop == "D":
            break

        with ExitStack() as es:
            qa = sb(es, "qa", [72, 2, S])
            ka = sb(es, "ka", [72, 2, S])
            vt = sb(es, "vt", [128, 2, 16, 128])
            ones_r = sb(es, "ones_r", [128, 128])
            CP("dve", RR(ones_r[:, :]), ones, ["cst"], ["ones_r"])
            kbar = sb(es, "kbar", [64, 2, 8])
            mobc = sb(es, "mobc", [128, 3, 16, 48])
            NP = sb(es, "NP", [128, 2, 16, 72])
            sm = sb(es, "sm", [128, 16, 8])
            top8 = sb(es, "top8", [128, 16, 8])
            al = sb(es, "al", [128, 16, 8])
            pt = sb(es, "pt", [128, 6, 512])
            pacc = sb(es, "pacc", [128, 512])
            rden = sb(es, "rden", [128, 512])
            ob = sb(es, "ob", [128, 2, 512])
            DMA(mobc[:, :, :, :], mobac_d.rearrange("p (a t c) -> p a t c", a=3, t=16), [], ["mobc"])
            for q in range(2):
                DMA(RR(ka[64:72, q, :]), blkind_d[:, :], [], [("kaB", q)], q="pool")
            sc.op("pool", lambda e: e.memset(NP[:, :, :, :], 0.0), [], [("NP", 0), ("NP", 1)])
            pti = [0]

            def prepA(h):
                q = h % 2
                DMA(RR(qa[0:64, q, :]), qkT[h * 64:(h + 1) * 64, :], [], [("qaQ", q)], q="pool")
                DMA(RR(ka[0:64, q, :]), qkT[384 + h * 64:384 + (h + 1) * 64, :], [], [("kaK", q)], q="pool")
                if h % 2 == 0:
                    vq = (h // 2) % 2
                    DMA(RR(vt[:, vq, :, :]), vm_tm.rearrange("(t p) c -> p t c", p=128)[:, :, h * 64:(h + 2) * 64], [], [("vt", vq)], q="pool")
                sc.op("dve", lambda e: e.tensor_reduce(out=kbar[:, q, :], in_=ka[0:64, q, :].rearrange("p (n k) -> p n k", n=8), axis=AX.X, op=ALU.add), [("kaK", q)], [("kbar", q)])
                for t in range(16):
                    MM(ps[0][:, t * 8:(t + 1) * 8], qa[0:64, q, t * 128:(t + 1) * 128], kbar[:, q, :], True, True, [("qaQ", q), ("kbar", q)], [("ps", 0)])

            def prepB(h):
                q = h % 2
                hs = slice(h * 8, (h + 1) * 8)
                TT("dve", sm[:, :, :], ps[0][:, 0:128].rearrange("p (t n) -> p t n", t=16), mobc[:, 0, :, hs], ALU.add, [("ps", 0), "mobc"], ["sm"])
                for t in range(16):
                    sc.op("dve", lambda e, t=t: e.max(out=top8[:, t, :], in_=sm[:, t, :]), ["sm"], [("top8", t)])
                for t in range(16):
                    TS("dve", al[:, t, :], sm[:, t, :], top8[:, t, 2:3], None, ALU.is_ge, None, ["sm", ("top8", t)], [("al", t)])
                allal = [("al", t) for t in range(16)]
                TT("dve", al[:, :, :], al[:, :, :], mobc[:, 1, :, hs], ALU.mult, allal + ["mobc"], ["al2"])
                TT("dve", al[:, :, :], al[:, :, :], mobc[:, 2, :, hs], ALU.add, ["al2", "mobc"], ["al2"])
                TS("dve", NP[:, q, :, 64:72], al[:, :, :], -1.0, BIG, ALU.add, ALU.mult, ["al2"], [("NP", q)])

            def prepC(h):
                q = h % 2
                for t4 in range(4):
                    for tq in range(4):
                        t = t4 * 4 + tq
                        MM(ps[1][0:72, tq * 128:(tq + 1) * 128], NP[:, q, t, :], ident, True, True, [("NP", q), "cst"], [("ps", 1)])
                    CP("act", RR(qa[64:72, q, t4 * 512:(t4 + 1) * 512]), ps[1][64:72, :], [("ps", 1)], [("qaM", q)])

            def attn(h, qt):
                q = h % 2
                vq = (h // 2) % 2
                hp = slice((h % 2) * 64, (h % 2) * 64 + 64)
                qsl = slice(qt * 512, (qt + 1) * 512)
                nk = (qt + 1) * 4
                osl = qt % 2
                pis = {}

                def qk(kt):
                    sb_i = (2, 3, 6, 7)[kt % 4]
                    pi = pti[0] % 6
                    pti[0] += 1
                    pis[kt] = pi
                    MM(ps[sb_i][:, :], ka[0:72, q, kt * 128:(kt + 1) * 128], qa[0:72, q, qsl], True, True,
                       [("kaK", q), ("kaB", q), ("qaQ", q), ("qaM", q)], [("ps", sb_i)], r=True)
                    ACT(RR(pt[:, pi, :]), ps[sb_i][:, :], AF.Exp, [("ps", sb_i)], [("pt", pi)], scale=0.125)
                    if kt >= qt * 4:
                        j = kt - qt * 4
                        TT("dve", RR(pt[:, pi, :]), pt[:, pi, :], cm[:, j * 512:(j + 1) * 512], ALU.mult, [("pt", pi), "cst"], [("pt", pi)])

                def pv(kt):
                    pi = pis[kt]
                    MM(ps[4][:, :], vt[:, vq, kt, :], pt[:, pi, :], kt == 0, kt == nk - 1, [("vt", vq), ("pt", pi)], [("ps", 4)], r=True)
                    MM(ps[5][:, :], ones_r[:, :], pt[:, pi, :], kt == 0, kt == nk - 1, ["ones_r", ("pt", pi)], [("ps", 5)], r=True)

                DEPTH = 3
                for kt in range(min(DEPTH, nk)):
                    qk(kt)
                for kt in range(nk):
                    pv(kt)
                    if kt + DEPTH < nk:
                        qk(kt + DEPTH)
                RECIP(rden[hp, :], ps[5][hp, :], [("ps", 5)], ["rden"])
                TT("dve", ob[hp, osl, :], ps[4][hp, :], rden[hp, :], ALU.mult, [("ps", 4), "rden"], [("ob", osl)])
                DMA(catT[384 + h * 64:384 + (h + 1) * 64, qsl], ob[hp, osl, :], [("ob", osl)], [("cat", "b", h, qt)])

            prepA(0)
            prepB(0)
            prepC(0)
            for h in range(H):
                for qt in range(4):
                    attn(h, qt)
                    if h + 1 < H:
                        if qt == 0:
                            prepA(h + 1)
                        elif qt == 1:
                            prepB(h + 1)
                        elif qt == 2:
                            prepC(h + 1)
            sc.barrier()
        if stop == "C":
            break

        with ExitStack() as es:
            TW = 128
            NTB = S // TW
            w2s = sb(es, "w2s", [64, 384])
            a2s = sb(es, "a2s", [64, 384])
            g2s = sb(es, "g2s", [128, 384])
            DMA(w2s[:, :], PR["rwkv_w2"][l], [], ["w2s"])
            DMA(a2s[:, :], PR["rwkv_a2"][l], [], ["a2s"])
            DMA(g2s[:, :], PR["rwkv_g2"][l], [], ["g2s"])
            pw0 = colvec(es, "pw0", PR["rwkv_w0"][l], 6, p=64)
            pa0 = colvec(es, "pa0", PR["rwkv_a0"][l], 6, p=64)
            pkk = colvec(es, "pkk", PR["rwkv_k_k"][l], 6, p=64)
            pka = colvec(es, "pka", PR["rwkv_k_a"][l], 6, p=64)
            prk = colvec(es, "prk", PR["rwkv_r_k"][l].rearrange("h d -> (h d)"), 6, p=64)
            plg = colvec(es, "plg", PR["rwkv_lnx_g"][l], 6, p=64)
            plb = colvec(es, "plb", PR["rwkv_lnx_b"][l], 6, p=64)
            pok = sb(es, "pok", [64, 6])
            TS("dve", pok[:, :], pka[:, :], -1.0, 1.0, ALU.mult, ALU.add, ["pka"], ["pok"])
            i64 = ident[0:64, 0:64]
            o64 = ones[0:64, 0:64]
            rowm = cst[:, OF["rowm"]:OF["rowm"] + 2]
            Mst = sb(es, "Mst", [64, 2, 6, 64])
            sc.op("dve", lambda e: e.memset(Mst[:, 0, :, :], 0.0), [], [("Mst", 0)])
            mcur = [0]
            RhatT = sb(es, "RhatT", [64, 6, TW])
            Y0T = sb(es, "Y0T", [64, 6, TW])
            yT = sb(es, "yT", [64, 6, TW])
            GT = sb(es, "GT", [64, 6, 2, 64])
            Hm = sb(es, "Hm", [64, 6, 2, 64])
            bon = sb(es, "bon", [64, 3, 6, TW]); gal = sb(es, "gal", [64, 3, 6, TW])
            ARh = sb(es, "ARh", [64, 2, 6, 2, TW]); BKh = sb(es, "BKh", [64, 2, 6, 2, TW]); BPh = sb(es, "BPh", [64, 2, 6, 2, TW])
            vvh = sb(es, "vvh", [64, 2, 6, TW]); pC = sb(es, "pC", [64, 2, 6, 2])
            wd = sb(es, "wd", [64, 2, TW]); ad = sb(es, "ad", [64, 2, TW]); gd = sb(es, "gd", [128, 2, TW])
            T6 = lambda nm: sb(es, nm, [64, 6, TW])
            rr = T6("rr"); kq = T6("kq"); sig = T6("sig"); cum = T6("cum"); cpv = T6("cpv")
            epos = T6("epos"); eneg = T6("eneg"); eprv = T6("eprv"); eend = T6("eend")
            aa = T6("aa"); kk = T6("kk"); kk2 = cpv; rn = T6("rn"); kka = T6("kka"); kp = T6("kp"); rkr = T6("rkr")
            nbc = sb(es, "nbc", [64, 6, 2])
            tm = sb(es, "tm", [128, 6, 256]); Eb = sb(es, "Eb", [128, 6, 512]); YS = sb(es, "YS", [128, 6, 256])
            Lb = sb(es, "Lb", [128, 6, 2, 384]); B2 = sb(es, "B2", [128, 6, 128]); K2 = sb(es, "K2", [128, 6, 128])
            yc = sb(es, "yc", [64, 6, TW]); ysq = sb(es, "ysq", [64, 6, TW]); yrs = sb(es, "yrs", [64, 6, TW])
            obuf = sb(es, "obuf", [64, 6, TW])

            def tile_pro(tt):
                tsl = slice(tt * TW, (tt + 1) * TW)
                ws = tt % 2
                DMA(wd[:, ws, :], rwkvT[1152:1216, tsl], [], [("wd", ws)])
                DMA(ad[:, ws, :], rwkvT[1216:1280, tsl], [], [("ad", ws)])
                DMA(gd[:, ws, :], rwkvT[1280:1408, tsl], [], [("gd", ws)])
                yield
                ACT(wd[:, ws, :], wd[:, ws, :], AF.Tanh, [("wd", ws)], [("wd", ws)])
                ACT(gd[:, ws, :], gd[:, ws, :], AF.Sigmoid, [("gd", ws)], [("gd", ws)])
                yield

            def stage0_all(tt):
                tsl = slice(tt * TW, (tt + 1) * TW)
                ws = tt % 2
                w3 = tt % 3
                HH = range(H)
                fl = lambda t: t[:, :, :].rearrange("p h t -> p (h t)")
                A6 = lambda nm: [(nm, ws, h) for h in HH]
                hv = lambda base: rwkvT[base:base + 384, tsl].rearrange("(h p) t -> p h t", p=64)
                DMA(rr[:, :, :], hv(0), [], ["rr"])
                DMA(kq[:, :, :], hv(384), [], ["kq"])
                DMA(vvh[:, ws, :, :], hv(768), [], A6("vv"))
                yield
                pbh = lambda h: (ps[0], ("ps", 0), h * TW) if h < 4 else (ps[1], ("ps", 1), (h - 4) * TW)
                for h in HH:
                    pb, pk, c0 = pbh(h)
                    MM(pb[0:64, c0:c0 + TW], w2s[:, h * 64:(h + 1) * 64], wd[:, ws, :], True, True, ["w2s", ("wd", ws)], [pk])
                for h in HH:
                    pb, pk, c0 = pbh(h)
                    ACT(sig[:, h, :], pb[0:64, c0:c0 + TW], AF.Sigmoid, [pk, "pw0"], [("sig", h)], bias=pw0[:, h:h + 1])
                yield
                allsig = [("sig", h) for h in HH]
                sc.op("dve", lambda e: e.tensor_tensor_scan(out=fl(cum), data0=resetm[0:64, 0:6 * TW], data1=fl(sig), initial=0.0, op0=ALU.mult, op1=ALU.add), allsig + ["cst"], ["cum"])
                yield
                TT("pool", fl(cpv), fl(cum), fl(sig), ALU.subtract, ["cum"] + allsig, ["cpv"])
                ACT(fl(epos), fl(cum), AF.Exp, ["cum"], ["epos"], scale=-C0)
                ACT(fl(eneg), fl(cum), AF.Exp, ["cum"], ["eneg"], scale=C0)
                cum12 = fl(cum).rearrange("p (c t) -> p c t", t=64)
                epos12 = fl(epos).rearrange("p (c t) -> p c t", t=64)
                nbc12 = nbc[:, :, :].rearrange("p h c -> p (h c)")
                TS("dve", nbc12, cum12[:, :, 63], -C0, None, ALU.mult, None, ["cum"], ["nbc"])
                yield
                ACT(fl(eprv), fl(cpv), AF.Exp, ["cpv"], ["eprv"], scale=-C0)
                CP("pool", pC[:, ws, :, :].rearrange("p h c -> p (h c)"), epos12[:, :, 63], ["epos"], A6("pC"))
                for h in HH:
                    for c in range(2):
                        ACT(eend[:, h, c * 64:(c + 1) * 64], cum[:, h, c * 64:(c + 1) * 64], AF.Exp, ["cum", "nbc"], [("eend", h)], scale=C0, bias=nbc[:, h, c:c + 1])
                    if h % 2:
                        yield
                for h in HH:
                    pb, pk, c0 = pbh(h)
                    MM(pb[0:64, c0:c0 + TW], a2s[:, h * 64:(h + 1) * 64], ad[:, ws, :], True, True, ["a2s", ("ad", ws)], [pk])
                for h in HH:
                    pb, pk, c0 = pbh(h)
                    ACT(aa[:, h, :], pb[0:64, c0:c0 + TW], AF.Sigmoid, [pk, "pa0"], [("aa", h)], bias=pa0[:, h:h + 1])
                yield
                for h in HH:
                    pb, pk, c0 = pbh(h)
                    MM(pb[0:64, c0:c0 + TW], g2s[:, h * 64:(h + 1) * 64], gd[:, ws, :], True, True, ["g2s", ("gd", ws)], [pk])
                CP("act", gal[:, w3, 0:4, :].rearrange("p h t -> p (h t)"), ps[0][0:64, 0:512], [("ps", 0)], [("gal", w3, h) for h in range(4)])
                CP("act", gal[:, w3, 4:6, :].rearrange("p h t -> p (h t)"), ps[1][0:64, 0:256], [("ps", 1)], [("gal", w3, h) for h in range(4, 6)])
                yield
                for h in HH:
                    TS("dve", kk[:, h, :], kq[:, h, :], pkk[:, h:h + 1], None, ALU.mult, None, ["kq", "pkk"], [("kk", h)])
                yield
                allkk = [("kk", h) for h in HH]
                ACT(fl(kk2), fl(kk), AF.Square, allkk + ["cpv"], ["cpv"])
                yield
                for h in HH:
                    pb, pk, c0 = pbh(h)
                    MM(pb[0:64, c0:c0 + TW], o64, kk2[:, h, :], True, True, ["cst", "cpv"], [pk])
                ACT(rn[:, 0:4, :].rearrange("p h t -> p (h t)"), ps[0][0:64, 0:512], AF.Sqrt, [("ps", 0)], [("rn", 0)])
                ACT(rn[:, 4:6, :].rearrange("p h t -> p (h t)"), ps[1][0:64, 0:256], AF.Sqrt, [("ps", 1)], [("rn", 1)])
                yield
                krn = [("rn", 0), ("rn", 1)]
                TS("dve", fl(rn), fl(rn), 1e-12, None, ALU.max, None, krn, krn)
                yield
                RECIP(fl(rn), fl(rn), krn, krn)
                yield
                TT("dve", fl(kk), fl(kk), fl(rn), ALU.mult, allkk + krn, ["kkn"])
                yield
                allaa = [("aa", h) for h in HH]
                TT("pool", fl(kka), fl(kk), fl(aa), ALU.mult, ["kkn"] + allaa, ["kka"])
                for h in HH:
                    TS("dve", rn[:, h, :], aa[:, h, :], pka[:, h:h + 1], pok[:, h:h + 1], ALU.mult, ALU.add, [("aa", h), "pka", "pok", "kkn"] + krn, [("fac", h)])
                STT(RR(ARh[:, ws, :, 0, :]), kk[:, :, :], -1.0, eprv[:, :, :], ALU.mult, ALU.mult, ["kkn", "eprv"], A6("AR0"))
                yield
                allfac = [("fac", h) for h in HH]
                TT("dve", fl(kp), fl(kq), fl(rn), ALU.mult, ["kq"] + allfac, ["kp"])
                TT("pool", RR(ARh[:, ws, :, 1, :]), rr[:, :, :], epos[:, :, :], ALU.mult, ["rr", "epos"], A6("AR1"))
                yield
                for h in HH:
                    STT(rkr[:, h, :], rr[:, h, :], prk[:, h:h + 1], kp[:, h, :], ALU.mult, ALU.mult, ["rr", "prk", "kp"], [("rkr", h)])
                TT("pool", RR(BKh[:, ws, :, 0, :]), kka[:, :, :], eneg[:, :, :], ALU.mult, ["kka", "eneg"], A6("BK0"))
                TT("pool", RR(BKh[:, ws, :, 1, :]), kp[:, :, :], eneg[:, :, :], ALU.mult, ["kp", "eneg"], A6("BK1"))
                yield
                for h in HH:
                    pb, pk, c0 = pbh(h)
                    MM(pb[0:64, c0:c0 + TW], o64, rkr[:, h, :], True, True, ["cst", ("rkr", h)], [pk])
                TT("dve", bon[:, w3, 0:4, :].rearrange("p h t -> p (h t)"), ps[0][0:64, 0:512], vvh[:, ws, 0:4, :].rearrange("p h t -> p (h t)"), ALU.mult,
                   [("ps", 0)] + A6("vv"), [("bon", w3, h) for h in range(4)])
                TT("dve", bon[:, w3, 4:6, :].rearrange("p h t -> p (h t)"), ps[1][0:64, 0:256], vvh[:, ws, 4:6, :].rearrange("p h t -> p (h t)"), ALU.mult,
                   [("ps", 1)] + A6("vv"), [("bon", w3, h) for h in range(4, 6)])
                alleend = [("eend", h) for h in HH]
                TT("pool", BPh[:, ws, :, 0, :], kka[:, :, :], eend[:, :, :], ALU.mult, ["kka"] + alleend, A6("BP"))
                TT("pool", BPh[:, ws, :, 1, :], kp[:, :, :], eend[:, :, :], ALU.mult, ["kp"] + alleend, A6("BP"))
                yield

            def make_gens(tt):
                if tt >= NTB:
                    return []
                return [tile_pro(tt), stage0_all(tt)]

            def advance(gl, n):
                for _ in range(n):
                    for g in list(gl):
                        try:
                            next(g)
                        except StopIteration:
                            gl.remove(g)

            def chain(tt, nxt):
                ws = tt % 2
                tsl = slice(tt * TW, (tt + 1) * TW)
                HS = range(H)
                GR = [(0, 3), (3, 6)]
                bk = lambda h: ps[2 + h]
                bkk = lambda h: ("ps", 2 + h)
                gk = lambda g: [bkk(h) for h in range(*g)]
                gkey = lambda nm, g: [(nm, h) for h in range(*g)]
                A0 = lambda h: ("AR0", ws, h)
                A1 = lambda h: ("AR1", ws, h)
                bc = lambda ap, n: ap.unsqueeze(1).to_broadcast([ap.shape[0], n, ap.shape[1]])
                for g in GR:
                    for h in range(*g):
                        for i, (src, kx) in enumerate([(ARh[:, ws, h, 0, :], A0(h)), (BPh[:, ws, h, 0, :], ("BP", ws, h)), (BPh[:, ws, h, 1, :], ("BP", ws, h)), (vvh[:, ws, h, :], ("vv", ws, h))]):
                            TR(bk(h)[:, i * 64:(i + 1) * 64], src, i64, [kx, "cst"], [bkk(h)])
                for g in GR:
                    CP("act", tm[:, g[0]:g[1], :], psbig[:, g[0]:g[1], 0:256], gk(g), gkey("tm", g))
                advance(nxt, 2)
                for g in GR:
                    for h in range(*g):
                        MM(bk(h)[:, 0:256], BKh[:, ws, h, 0, :], ARh[:, ws, h, :, :], True, True, [("BK0", ws, h), A0(h), A1(h)], [bkk(h)], r=True)
                        MM(bk(h)[:, 256:384], ARh[:, ws, h, 0, :], BKh[:, ws, h, 0, :], True, True, [("BK0", ws, h), A0(h)], [bkk(h)], r=True)
                for g in GR:
                    TT("dve", RR(Eb[:, g[0]:g[1], 0:384]), psbig[:, g[0]:g[1], 0:384], bc(mE, 3), ALU.mult, gk(g) + ["cst"], [("E", h, "a") for h in range(*g)])
                advance(nxt, 2)
                for g in GR:
                    for h in range(*g):
                        MM(bk(h)[:, 0:256], BKh[:, ws, h, 1, :], ARh[:, ws, h, :, :], True, True, [("BK1", ws, h), A0(h), A1(h)], [bkk(h)], r=True)
                for g in GR:
                    TT("dve", YS[:, g[0]:g[1], :], psbig[:, g[0]:g[1], 0:256], bc(mY, 3), ALU.mult, gk(g) + ["cst"], gkey("YS", g))
                advance(nxt, 2)
                for g in GR:
                    for h in range(*g):
                        MM(bk(h)[:, 256:320], YS[:, h, 0:128], tm[:, h, 192:256], True, True, [("YS", h), ("tm", h)], [bkk(h)])
                    CP("pool", RR(Eb[:, g[0]:g[1], 384:448]), tm[:, g[0]:g[1], 0:64], gkey("tm", g), [("E", h, "b") for h in range(*g)])
                for g in GR:
                    CP("act", RR(Eb[:, g[0]:g[1], 448:512]), psbig[:, g[0]:g[1], 256:320], gk(g), [("E", h, "c") for h in range(*g)])
                advance(nxt, 2)
                for lev in range(6):
                    def views(h):
                        if lev == 0:
                            return (Eb[:, h, 0:128], Eb[:, h, 256:384], Eb[:, h, 256:512],
                                    [("E", h, "a"), ("E", h, "b"), ("E", h, "c")])
                        Lp = Lb[:, h, (lev - 1) % 2, :]
                        return (Lp[:, 0:128], Lp[:, 128:256], Lp[:, 128:384],
                                [("L", h, (lev - 1) % 2, "p"), ("L", h, (lev - 1) % 2, "z")])
                    for g in GR:
                        for h in range(*g):
                            PT_, P_, PZ_, rk = views(h)
                            MM(bk(h)[:, 128:384], PT_, PZ_, True, True, rk, [bkk(h)], r=True)
                            if lev < 5:
                                MM(bk(h)[:, 0:128], P_, PT_, True, True, rk, [bkk(h)], r=True)
                    for g in GR:
                        gs = slice(g[0], g[1])
                        if lev == 0:
                            Zg = Eb[:, gs, 384:512]
                            rkg = [("E", h, x) for h in range(*g) for x in "abc"]
                        else:
                            Zg = Lb[:, gs, (lev - 1) % 2, 256:384]
                            rkg = [("L", h, (lev - 1) % 2, x) for h in range(*g) for x in "pz"]
                        TT("dve", RR(Lb[:, gs, lev % 2, 256:384]), psbig[:, gs, 256:384], Zg, ALU.add, gk(g) + rkg, [("L", h, lev % 2, "z") for h in range(*g)])
                        if lev < 5:
                            CP("act", RR(Lb[:, gs, lev % 2, 0:256]), psbig[:, gs, 0:256], gk(g), [("L", h, lev % 2, "p") for h in range(*g)])
                    advance(nxt, 3)
                for g in GR:
                    gs = slice(g[0], g[1])
                    for hf in range(2):
                        TS("pool", B2[:, gs, hf * 64:(hf + 1) * 64], tm[:, gs, 64:128], rowm[:, hf:hf + 1], None, ALU.mult, None, gkey("tm", g) + ["cst"], gkey("B2", g))
                        TS("pool", K2[:, gs, hf * 64:(hf + 1) * 64], tm[:, gs, 128:192], rowm[:, hf:hf + 1], None, ALU.mult, None, gkey("tm", g) + ["cst"], gkey("K2", g))
                for g in GR:
                    for h in range(*g):
                        Lf = Lb[:, h, 1, :]
                        kLf = ("L", h, 1, "z")
                        W_, U0_ = Lf[:, 256:320], Lf[:, 320:384]
                        b_ = bk(h)
                        MM(b_[0:64, 0:128], W_, B2[:, h, :], True, True, [kLf, ("B2", h)], [bkk(h)])
                        for hf in range(2):
                            MM(b_[0:64, 128 + hf * 64:128 + (hf + 1) * 64], K2[:, h, hf * 64:(hf + 1) * 64], tm[:, h, 192:256], True, False, [("K2", h), ("tm", h)], [bkk(h)])
                            MM(b_[0:64, 128 + hf * 64:128 + (hf + 1) * 64], B2[:, h, hf * 64:(hf + 1) * 64], U0_, False, True, [("B2", h), kLf], [bkk(h)])
                        MM(b_[0:64, 256:384], W_, Eb[:, h, 128:256], True, True, [kLf, ("E", h, "a")], [bkk(h)])
                        MM(b_[0:64, 384:512], tm[:, h, 192:256], YS[:, h, 128:256], True, False, [("tm", h), ("YS", h)], [bkk(h)])
                        MM(b_[0:64, 384:512], U0_, Eb[:, h, 128:256], False, True, [kLf, ("E", h, "a")], [bkk(h)])
                advance(nxt, 2)
                for g in GR:
                    gs = slice(g[0], g[1])
                    for h in range(*g):
                        b_ = bk(h)
                        for hf in range(2):
                            STT(GT[:, h, hf, :], i64, pC[:, ws, h, hf:hf + 1], b_[0:64, hf * 64:(hf + 1) * 64], ALU.mult, ALU.add,
                                [bkk(h), ("pC", ws, h), "cst"], [("GT", h)])
                    TT("dve", RhatT[:, gs, :], psbig[0:64, gs, 256:384], ARh[:, ws, gs, 1, :], ALU.add, gk(g) + [A1(h) for h in range(*g)], gkey("Rhat", g))
                for g in GR:
                    gs = slice(g[0], g[1])
                    CP("act", Hm[:, gs, :, :].rearrange("p h c i -> p h (c i)"), psbig[0:64, gs, 128:256], gk(g), gkey("Hm", g))
                    CP("act", Y0T[:, gs, :], psbig[0:64, gs, 384:512], gk(g), gkey("Y0T", g))
                advance(nxt, 2)
                allh = lambda nm: [(nm, h) for h in range(H)]
                for c in range(2):
                    csl = slice(c * 64, (c + 1) * 64)
                    m0 = mcur[0]
                    mnew = 1 - m0
                    for h in range(H):
                        MM(ps[0][0:64, h * 64:(h + 1) * 64], Mst[:, m0, h, :], RhatT[:, h, csl], True, True, [("Mst", m0), ("Rhat", h)], [("ps", 0)])
                    for h in range(H):
                        MM(ps[1][0:64, h * 64:(h + 1) * 64], GT[:, h, c, :], Mst[:, m0, h, :], True, True, [("Mst", m0), ("GT", h)], [("ps", 1)])
                    TT("dve", Mst[:, mnew, :, :], ps[1][0:64, 0:384].rearrange("p (h i) -> p h i", h=6), Hm[:, :, c, :], ALU.add,
                       [("ps", 1)] + allh("Hm"), [("Mst", mnew)])
                    TT("dve", yT[:, :, csl], ps[0][0:64, 0:384].rearrange("p (h t) -> p h t", h=6), Y0T[:, :, csl], ALU.add,
                       [("ps", 0)] + allh("Y0T"), [("yT", c)])
                    mcur[0] = mnew
                    advance(nxt, 1)
                advance(nxt, 1000)

            def post(tt, h):
                tsl = slice(tt * TW, (tt + 1) * TW)
                w3 = tt % 3
                ally = [("yT", c) for c in range(2)]
                osl = h
                kyc, kysq, kyrs = ("yc", osl), ("ysq", osl), ("yrs", osl)
                pb = ps[h % 2]
                pk = ("ps", h % 2)
                MM(pb[0:64, 0:TW], o64, yT[:, h, :], True, True, ["cst"] + ally, [pk])
                STT(yc[:, osl, :], pb[0:64, 0:TW], -1.0 / 64, yT[:, h, :], ALU.mult, ALU.add, [pk] + ally, [kyc])
                yield
                ACT(ysq[:, osl, :], yc[:, osl, :], AF.Square, [kyc], [kysq])
                yield
                MM(pb[0:64, 128:128 + TW], o64, ysq[:, osl, :], True, True, ["cst", kysq], [pk])
                ACT(yrs[:, osl, :], pb[0:64, 128:128 + TW], AF.Sqrt, [pk, "epsc"], [kyrs], scale=1.0 / 64, bias=epsc[0:64, 1:2])
                yield
                RECIP(yrs[:, osl, :], yrs[:, osl, :], [kyrs], [kyrs])
                yield
                TT("dve", yc[:, osl, :], yc[:, osl, :], yrs[:, osl, :], ALU.mult, [kyc, kyrs], [kyc])
                yield
                TS("dve", yc[:, osl, :], yc[:, osl, :], plg[:, h:h + 1], plb[:, h:h + 1], ALU.mult, ALU.add, [kyc, "plg", "plb"], [kyc])
                yield
                TT("pool", yc[:, osl, :], yc[:, osl, :], bon[:, w3, h, :], ALU.add, [kyc, ("bon", w3, h)], [kyc])
                yield
                TT("pool", obuf[:, osl, :], yc[:, osl, :], gal[:, w3, h, :], ALU.mult, [kyc, ("gal", w3, h)], [("obuf", osl)])
                DMA(catT[h * 64:(h + 1) * 64, tsl], obuf[:, osl, :], [("obuf", osl)], [("cat", "a", h, tt)])
                yield

            g0 = make_gens(0)
            advance(g0, 1000)
            for tt in range(NTB):
                pg = [post(tt - 1, h) for h in range(H)] if tt > 0 else []
                chain(tt, pg + make_gens(tt + 1))
            advance([post(NTB - 1, h) for h in range(H)], 1000)
            sc.barrier()
        if stop == "B":
            break

        with ExitStack() as es:
            wO = sb(es, "wO", [128, 2, 8, 128])
            catb = sb(es, "catb", [128, 2, 8, 512])
            mixb = sb(es, "mixb", [128, 8, 512])
            sq = sb(es, "sq", [128, 2, 512])
            rstd = sb(es, "rstd", [128, 2, 512])
            gP = colvec(es, "gP", PR["post_mix_g"][l], 8)
            w_out_l = PR["w_out"][l].rearrange("(k p) c -> p k c", p=128)
            cat_v = catT.rearrange("(k p) t -> p k t", p=128)
            wi = 0
            xe = sb(es, "xe", [128, 2, 8, 512])
            for tt in range(4):
                tsl = slice(tt * 512, (tt + 1) * 512)
                cs = tt % 2
                DMA(xe[:, cs, :, :], xD_v[:, :, tsl], [("xD", tt)], [("xe", cs, j) for j in range(8)])
                DMA(RR(catb[:, cs, :, :]), cat_v[:, :, tsl], [], [("catb", cs)], q="pool")
                for j in range(8):
                    sl = wi % 2
                    wi += 1
                    DMA(RR(wO[:, sl, :, :]), w_out_l[:, :, j * 128:(j + 1) * 128], [], [("wO", sl)], q="pool")
                    bi = j % 2
                    for k in range(8):
                        MM(ps[bi][:, :], wO[:, sl, k, :], catb[:, cs, k, :], k == 0, k == 7, [("wO", sl), ("catb", cs)], [("ps", bi)], r=True)
                    CP("act" if j % 2 else "dve", mixb[:, j, :], ps[bi][:, :], [("ps", bi)], [("mixb", j)])
                r = rms_stats(lambda k: mixb[:, k, :], lambda k: ("mixb", k), tt, (sq, rstd), ps[2 + tt % 2], ("ps", 2 + tt % 2))
                for j in range(8):
                    STT(mixb[:, j, :], mixb[:, j, :], gP[:, j:j + 1], r, ALU.mult, ALU.mult, [("mixb", j), ("rstd", tt % 2), "gP"], [("mixb", j)])
                    TT("dve", xe[:, cs, j, :], xe[:, cs, j, :], mixb[:, j, :], ALU.add, [("mixb", j), ("xe", cs, j)], [("xe", cs, j)])
                DMA(xD_v[:, :, tsl], xe[:, cs, :, :], [("xe", cs, j) for j in range(8)], [("xD", tt)])
            sc.barrier()
        if stop == "E":
            break

        with ExitStack() as es:
            TF = 1024
            hb = sb(es, "hb", [128, 8, TF])
            actb = sb(es, "actb", [128, NFF, TF])
            shm = sb(es, "shm", [128, 8 * TF])
            xf = shm[:, :].rearrange("p (k t) -> p k t", k=8)
            wD = shm[:, 0:2 * NFF * 128].rearrange("p (s j c) -> p s j c", s=2, j=NFF)
            wG = sb(es, "wG", [128, 2, 2, 8, 128])
            sq = sb(es, "sq", [128, 2, 512])
            rstd = sb(es, "rstd", [128, 2, 512])
            sil = sb(es, "sil", [128, 2, 512])
            xc = sb(es, "xc", [128, 2, TF])
            gF = colvec(es, "gF", PR["pre_ffn_g"][l], 8)
            gQ = colvec(es, "gQ", PR["post_ffn_g"][l], 8)
            w_fi = PR["w_ffn_in"][l].rearrange("(k p) c -> p k c", p=128)
            w_fo = PR["w_ffn_out"][l].rearrange("(j p) c -> p j c", p=128)
            wi = 0
            wdi = 0
            si = 0
            xci = 0
            SHK = ["sh", ("wD", 0), ("wD", 1)]
            for tt in range(S // TF):
                tsl = slice(tt * TF, (tt + 1) * TF)
                DMA(xf, xD_v[:, :, tsl], [("xD", 2 * tt), ("xD", 2 * tt + 1)], SHK)
                for hf in range(2):
                    hsl = slice(hf * 512, (hf + 1) * 512)
                    r = rms_stats(lambda k: xf[:, k, hsl], lambda k: "sh", si, (sq, rstd), ps[6 + si % 2], ("ps", 6 + si % 2))
                    for k in range(8):
                        STT(RR(hb[:, k, hsl]), xf[:, k, hsl], gF[:, k:k + 1], r, ALU.mult, ALU.mult, SHK + [("rstd", si % 2), "gF"], [("hb", k, hf)])
                    si += 1
                for j in range(NFF):
                    sl = wi % 2
                    wi += 1
                    DMA(RR(wG[:, sl, 0, :, :]), w_fi[:, :, j * 128:(j + 1) * 128], [], [("wG", sl, 0)], q="pool")
                    DMA(RR(wG[:, sl, 1, :, :]), w_fi[:, :, DFF + j * 128:DFF + (j + 1) * 128], [], [("wG", sl, 1)], q="pool")
                    for hf in range(2):
                        hsl = slice(hf * 512, (hf + 1) * 512)
                        bg, bu = hf * 2, hf * 2 + 1
                        for k in range(8):
                            MM(ps[bg][:, :], wG[:, sl, 0, k, :], hb[:, k, hsl], k == 0, k == 7, [("wG", sl, 0), ("hb", k, hf)], [("ps", bg)], r=True)
                        for k in range(8):
                            MM(ps[bu][:, :], wG[:, sl, 1, k, :], hb[:, k, hsl], k == 0, k == 7, [("wG", sl, 1), ("hb", k, hf)], [("ps", bu)], r=True)
                        ACT(sil[:, hf, :], ps[bg][:, :], AF.Silu, [("ps", bg)], [("sil", hf)])
                        TT("dve", RR(actb[:, j, hsl]), ps[bu][:, :], sil[:, hf, :], ALU.mult, [("ps", bu), ("sil", hf)], [("actb", j, hf)])
                for jo in range(8):
                    sl = wdi % 2
                    wdi += 1
                    DMA(RR(wD[:, sl, :, :]), w_fo[:, :, jo * 128:(jo + 1) * 128], [], [("wD", sl)], q="pool")
                    for hf in range(2):
                        hsl = slice(hf * 512, (hf + 1) * 512)
                        bi = 4 + hf
                        for j in range(NFF):
                            MM(ps[bi][:, :], wD[:, sl, j, :], actb[:, j, hsl], j == 0, j == NFF - 1, [("wD", sl), ("actb", j, hf)], [("ps", bi)], r=True)
                        CP("act" if hf else "dve", RR(hb[:, jo, hsl]), ps[bi][:, :], [("ps", bi)], [("hb", jo, hf)])
                for hf in range(2):
                    hsl = slice(hf * 512, (hf + 1) * 512)
                    r = rms_stats(lambda k: hb[:, k, hsl], lambda k: ("hb", k, hf), si, (sq, rstd), ps[6 + si % 2], ("ps", 6 + si % 2))
                    for j in range(8):
                        STT(RR(hb[:, j, hsl]), hb[:, j, hsl], gQ[:, j:j + 1], r, ALU.mult, ALU.mult, [("hb", j, hf), ("rstd", si % 2), "gQ"], [("hb", j, hf)])
                    si += 1
                for j in range(8):
                    cs = xci % 2
                    xci += 1
                    DMA(xc[:, cs, :], xD[j * 128:(j + 1) * 128, tsl], [("xD", 2 * tt), ("xD", 2 * tt + 1)], [("xc", cs)])
                    TT("pool" if j % 2 else "dve", xc[:, cs, :], xc[:, cs, :], hb[:, j, :], ALU.add, [("xc", cs), ("hb", j, 0), ("hb", j, 1)], [("xc", cs)])
                    DMA(xD[j * 128:(j + 1) * 128, tsl], xc[:, cs, :], [("xc", cs)], [("xDw", tt, j)])
            sc.barrier()
        if OPTS.get("xdbg") == l:
            DMA(xdbg[:, :], xD[:, :], [("xD", tt) for tt in range(4)], [("xdbg", 0)])

    if stop in (None, 'setup'):
        with ExitStack() as es:
            yo = sb(es, "yo", [128, 2, D])
            xo = sb(es, "xo", [128, 2, 8, 512])
            for t in range(16):
                sl = t % 2
                xsl = (t // 4) % 2
                if t % 4 == 0:
                    DMA(xo[:, xsl, :, :], xD_v[:, :, (t // 4) * 512:(t // 4 + 1) * 512], [("xD", t // 4)], [("xo", xsl)])
                for g in range(2):
                    bank = ps[(t * 2 + g) % 4]
                    bk = ("ps", (t * 2 + g) % 4)
                    for kk in range(4):
                        k = g * 4 + kk
                        TR(bank[:, kk * 128:(kk + 1) * 128], xo[:, xsl, k, (t % 4) * 128:(t % 4 + 1) * 128], ident, [("xo", xsl), "cst"], [bk])
                    CP("act" if g else "dve", yo[:, sl, g * 512:(g + 1) * 512], bank[:, :], [bk], [("yo", sl, g)])
                DMA(y_out[t * 128:(t + 1) * 128, :], yo[:, sl, :], [("yo", sl, 0), ("yo", sl, 1)], [("y", t)])
    sc.barrier()
    sc.emit()
    glob.close()
    return nc


_NC_CACHE = {}


def kernel(**inputs):
    if "nc" not in _NC_CACHE:
        _NC_CACHE["nc"] = build()
    nc = _NC_CACHE["nc"]
    x = np.ascontiguousarray(np.asarray(inputs["x"], dtype=np.float32))
    base = {n: np.ascontiguousarray(np.asarray(inputs[n], dtype=np.float32)) for n in PARAM_NAMES}
    base["cst"] = CONSTS["cst"]
    base["mobac"] = CONSTS["mobac"]
    base["blkind"] = CONSTS["blkind"]
    in_maps = []
    for b in range(8):
        m = dict(base)
        m["x"] = x[b]
        in_maps.append(m)
    res = run_bass_kernel_spmd(nc, in_maps, core_ids=list(range(8)))
    return np.stack([np.asarray(r["y"], dtype=np.float32) for r in res.results], 0)
```
